# Optimizing a Trainium2 kernel written in Bass

```python
import math
import jax, jax.numpy as jnp
from jax import lax
import numpy as np

D_MODEL = 2048
BATCH = 2
SEQ = 4096
DEPTH = 4
DEC_BATCH = 32
DEC_SEQ = 4
PAST_LEN = 16384
PAGE_SIZE = 128

ATTN_WIDTH = D_MODEL // 2
SSM_WIDTH = D_MODEL - ATTN_WIDTH
HEAD_DIM = 64
N_HEADS = ATTN_WIDTH // HEAD_DIM
N_KV_HEADS = max(1, N_HEADS // 8)
GQA_GROUP = N_HEADS // N_KV_HEADS
KV_WIDTH = N_KV_HEADS * HEAD_DIM
WINDOW = 128
SSM_GROUP = 16
N_SSM_GROUPS = SSM_WIDTH // SSM_GROUP
SSM_STATE = 64
D_IN = ATTN_WIDTH + 2 * KV_WIDTH + SSM_WIDTH
D_FF = ((8 * D_MODEL + 3 * 256 - 1) // (3 * 256)) * 256
EPS = 1e-5
DT_MIN = 1e-3
DT_MAX = 1e-1

kernel_name = 'hymba_swa_sink_s5_decoder_step'


def rms_norm(x, g):
    xf = x.astype(jnp.float32)
    y = xf * lax.rsqrt(jnp.mean(xf * xf, axis=-1, keepdims=True) + EPS)
    return (y * g.astype(jnp.float32)).astype(x.dtype)


def band_mask(tq, tk):
    i = jnp.arange(tq)[:, None]
    j = jnp.arange(tk)[None, :]
    return (j > i) & (j <= i + WINDOW)


def sink_attend(q, k, v, mask, sink):
    s = jnp.einsum('...qkgd,...jkd->...kgqj', q, k).astype(jnp.float32) * (HEAD_DIM ** -0.5)
    s = jnp.where(mask, s, -jnp.inf)
    sk = sink.astype(jnp.float32)[:, :, None, None]
    m = jnp.maximum(jnp.max(s, axis=-1, keepdims=True), sk)
    p = jnp.exp(s - m)
    den = jnp.sum(p, axis=-1, keepdims=True) + jnp.exp(sk - m)
    p = (p / den).astype(v.dtype)
    return jnp.einsum('...kgqj,...jkd->...qkgd', p, v)


def prompt_window_attention(q, k, v, sink):
    b, s = q.shape[0], q.shape[1]
    nb = s // WINDOW
    qb = q.reshape(b, nb, WINDOW, N_KV_HEADS, GQA_GROUP, HEAD_DIM)
    kb = k.reshape(b, nb, WINDOW, N_KV_HEADS, HEAD_DIM)
    vb = v.reshape(b, nb, WINDOW, N_KV_HEADS, HEAD_DIM)
    pad = ((0, 0), (1, 0), (0, 0), (0, 0), (0, 0))
    kk = jnp.concatenate([jnp.pad(kb[:, :-1], pad), kb], axis=2)
    vv = jnp.concatenate([jnp.pad(vb[:, :-1], pad), vb], axis=2)
    first_ok = (jnp.arange(nb) > 0)[:, None, None] | (jnp.arange(2 * WINDOW) >= WINDOW)[None, None, :]
    mask = (band_mask(WINDOW, 2 * WINDOW)[None] & first_ok)[:, None, None]
    out = sink_attend(qb, kk, vv, mask, sink).reshape(b, s, ATTN_WIDTH)
    return out, k[:, -WINDOW:], v[:, -WINDOW:]


def sample_window_attention(q, k, v, sink, kc, vc):
    b, t = q.shape[0], q.shape[1]
    kk = jnp.concatenate([kc.astype(k.dtype), k], axis=1)
    vv = jnp.concatenate([vc.astype(v.dtype), v], axis=1)
    mask = band_mask(t, WINDOW + t)
    out = sink_attend(q, kk, vv, mask, sink).reshape(b, t, ATTN_WIDTH)
    return out, kk[:, -WINDOW:], vv[:, -WINDOW:]


def complex_affine_combine(e1, e2):
    a1r, a1i, b1r, b1i = e1
    a2r, a2i, b2r, b2i = e2
    ar = a1r * a2r - a1i * a2i
    ai = a1r * a2i + a1i * a2r
    br = a2r * b1r - a2i * b1i + b2r
    bi = a2r * b1i + a2i * b1r + b2i
    return (ar, ai, br, bi)


def s5_branch(u, h0r, h0i, a_re, a_im, log_dt, b_re, b_im, c_re, c_im, d, w_glu):
    f32 = jnp.float32
    bsz, t = u.shape[0], u.shape[1]
    uf = u.astype(f32)
    ug = uf.reshape(bsz, t, N_SSM_GROUPS, SSM_GROUP)
    dt = jnp.exp(log_dt.astype(f32))[:, None]
    lr = a_re.astype(f32)
    li = a_im.astype(f32)
    mag = jnp.exp(lr * dt)
    ang = li * dt
    abr = mag * jnp.cos(ang)
    abi = mag * jnp.sin(ang)
    nr = abr - 1.0
    den = lr * lr + li * li
    cr = (nr * lr + abi * li) / den
    ci = (abi * lr - nr * li) / den
    br_ = b_re.astype(f32)
    bi_ = b_im.astype(f32)
    bbr = cr[..., None] * br_ - ci[..., None] * bi_
    bbi = cr[..., None] * bi_ + ci[..., None] * br_
    bur = jnp.einsum('blgc,gpc->blgp', ug, bbr)
    bui = jnp.einsum('blgc,gpc->blgp', ug, bbi)
    ar = jnp.broadcast_to(abr, bur.shape)
    ai = jnp.broadcast_to(abi, bur.shape)
    acr, aci, bcr, bci = lax.associative_scan(complex_affine_combine, (ar, ai, bur, bui), axis=1)
    h0r_ = h0r.astype(f32)[:, None]
    h0i_ = h0i.astype(f32)[:, None]
    hr = acr * h0r_ - aci * h0i_ + bcr
    hi = acr * h0i_ + aci * h0r_ + bci
    y = (jnp.einsum('blgp,gcp->blgc', hr, c_re.astype(f32))
         - jnp.einsum('blgp,gcp->blgc', hi, c_im.astype(f32)))
    y = y.reshape(bsz, t, SSM_WIDTH) + d.astype(f32) * uf
    z = jax.nn.gelu(y, approximate=False).astype(u.dtype)
    out = z * jax.nn.sigmoid(z @ w_glu)
    return out, hr[:, -1], hi[:, -1]


def hybrid_layer(x, l, p, kc, vc, h0r, h0i):
    b, t = x.shape[0], x.shape[1]
    h = rms_norm(x, p['norm_mix'][l])
    z = h @ p['w_in'][l]
    q = z[..., :ATTN_WIDTH].reshape(b, t, N_KV_HEADS, GQA_GROUP, HEAD_DIM)
    k = z[..., ATTN_WIDTH:ATTN_WIDTH + KV_WIDTH].reshape(b, t, N_KV_HEADS, HEAD_DIM)
    v = z[..., ATTN_WIDTH + KV_WIDTH:ATTN_WIDTH + 2 * KV_WIDTH].reshape(b, t, N_KV_HEADS, HEAD_DIM)
    u = z[..., ATTN_WIDTH + 2 * KV_WIDTH:]
    sink = p['attn_sink'][l].reshape(N_KV_HEADS, GQA_GROUP)
    if kc is None:
        a, kn, vn = prompt_window_attention(q, k, v, sink)
    else:
        a, kn, vn = sample_window_attention(q, k, v, sink, kc, vc)
    s, hr, hi = s5_branch(u, h0r, h0i, p['ssm_a_re'][l], p['ssm_a_im'][l], p['ssm_log_dt'][l],
                          p['ssm_b_re'][l], p['ssm_b_im'][l], p['ssm_c_re'][l], p['ssm_c_im'][l],
                          p['ssm_d'][l], p['w_glu'][l])
    mixed = jnp.concatenate([rms_norm(a, p['norm_attn_out'][l]),
                             rms_norm(s, p['norm_ssm_out'][l])], axis=-1) @ p['w_out'][l]
    x = x + mixed
    hf = rms_norm(x, p['norm_ffn'][l])
    x = x + (jax.nn.silu(hf @ p['w_gate'][l]) * (hf @ p['w_up'][l])) @ p['w_down'][l]
    return x, kn, vn, hr, hi


def run_trunk(x, p, cache_k, cache_v, h0_re, h0_im):
    ks, vs, hrs, his = [], [], [], []
    for l in range(DEPTH):
        kc = None if cache_k is None else cache_k[l]
        vc = None if cache_v is None else cache_v[l]
        x, kn, vn, hr, hi = hybrid_layer(x, l, p, kc, vc, h0_re[l], h0_im[l])
        ks.append(kn)
        vs.append(vn)
        hrs.append(hr)
        his.append(hi)
    y = rms_norm(x, p['norm_final'])
    return y, jnp.stack(ks), jnp.stack(vs), jnp.stack(hrs), jnp.stack(his)


def setup_inputs(seed: int = 0) -> dict:
    key = jax.random.key(seed)
    ks = jax.random.split(key, 32)
    f32 = jnp.float32

    def nrm(k, shape, scale):
        return jax.random.normal(k, shape, f32) * scale

    G, P = N_SSM_GROUPS, SSM_STATE
    return {
        'x_prompt': nrm(ks[0], (BATCH, SEQ, D_MODEL), 1.0),
        'x_sample': nrm(ks[1], (DEC_BATCH, DEC_SEQ, D_MODEL), 1.0),
        'cache_k': nrm(ks[2], (DEPTH, DEC_BATCH, WINDOW, N_KV_HEADS, HEAD_DIM), 1.0),
        'cache_v': nrm(ks[3], (DEPTH, DEC_BATCH, WINDOW, N_KV_HEADS, HEAD_DIM), 1.0),
        'state_ssm_re': nrm(ks[4], (DEPTH, DEC_BATCH, G, P), 0.1),
        'state_ssm_im': nrm(ks[5], (DEPTH, DEC_BATCH, G, P), 0.1),
        'norm_mix': 1.0 + nrm(ks[6], (DEPTH, D_MODEL), 0.02),
        'w_in': nrm(ks[7], (DEPTH, D_MODEL, D_IN), D_MODEL ** -0.5),
        'attn_sink': nrm(ks[8], (DEPTH, N_HEADS), 0.5),
        'ssm_a_re': -0.5 * jnp.exp(nrm(ks[9], (DEPTH, G, P), 0.05)),
        'ssm_a_im': math.pi * jnp.arange(P, dtype=f32) + nrm(ks[10], (DEPTH, G, P), 0.01),
        'ssm_log_dt': jax.random.uniform(ks[11], (DEPTH, G), f32, math.log(DT_MIN), math.log(DT_MAX)),
        'ssm_b_re': nrm(ks[12], (DEPTH, G, P, SSM_GROUP), (2 * SSM_GROUP) ** -0.5),
        'ssm_b_im': nrm(ks[13], (DEPTH, G, P, SSM_GROUP), (2 * SSM_GROUP) ** -0.5),
        'ssm_c_re': nrm(ks[14], (DEPTH, G, SSM_GROUP, P), (2 * P) ** -0.5),
        'ssm_c_im': nrm(ks[15], (DEPTH, G, SSM_GROUP, P), (2 * P) ** -0.5),
        'ssm_d': nrm(ks[16], (DEPTH, SSM_WIDTH), 1.0),
        'w_glu': nrm(ks[17], (DEPTH, SSM_WIDTH, SSM_WIDTH), SSM_WIDTH ** -0.5),
        'norm_attn_out': 1.0 + nrm(ks[18], (DEPTH, ATTN_WIDTH), 0.02),
        'norm_ssm_out': 1.0 + nrm(ks[19], (DEPTH, SSM_WIDTH), 0.02),
        'w_out': nrm(ks[20], (DEPTH, ATTN_WIDTH + SSM_WIDTH, D_MODEL), (ATTN_WIDTH + SSM_WIDTH) ** -0.5),
        'norm_ffn': 1.0 + nrm(ks[21], (DEPTH, D_MODEL), 0.02),
        'w_gate': nrm(ks[22], (DEPTH, D_MODEL, D_FF), D_MODEL ** -0.5),
        'w_up': nrm(ks[23], (DEPTH, D_MODEL, D_FF), D_MODEL ** -0.5),
        'w_down': nrm(ks[24], (DEPTH, D_FF, D_MODEL), D_FF ** -0.5),
        'norm_final': 1.0 + nrm(ks[25], (D_MODEL,), 0.02),
    }


def reference(x_prompt, x_sample, cache_k, cache_v, state_ssm_re, state_ssm_im,
              norm_mix, w_in, attn_sink, ssm_a_re, ssm_a_im, ssm_log_dt,
              ssm_b_re, ssm_b_im, ssm_c_re, ssm_c_im, ssm_d, w_glu,
              norm_attn_out, norm_ssm_out, w_out, norm_ffn, w_gate, w_up, w_down,
              norm_final):
    p = dict(norm_mix=norm_mix, w_in=w_in, attn_sink=attn_sink,
             ssm_a_re=ssm_a_re, ssm_a_im=ssm_a_im, ssm_log_dt=ssm_log_dt,
             ssm_b_re=ssm_b_re, ssm_b_im=ssm_b_im, ssm_c_re=ssm_c_re, ssm_c_im=ssm_c_im,
             ssm_d=ssm_d, w_glu=w_glu, norm_attn_out=norm_attn_out, norm_ssm_out=norm_ssm_out,
             w_out=w_out, norm_ffn=norm_ffn, w_gate=w_gate, w_up=w_up, w_down=w_down,
             norm_final=norm_final)
    zeros_state = jnp.zeros((DEPTH, x_prompt.shape[0], N_SSM_GROUPS, SSM_STATE), jnp.float32)
    y_prompt, k_p, v_p, hr_p, hi_p = run_trunk(x_prompt, p, None, None, zeros_state, zeros_state)
    y_sample, k_s, v_s, hr_s, hi_s = run_trunk(x_sample, p, cache_k, cache_v, state_ssm_re, state_ssm_im)
    return (y_prompt, y_sample, k_p, v_p, hr_p, hi_p, k_s, v_s, hr_s, hi_s)
```

```python
import contextlib
import math
import numpy as np
import concourse.bass as bass
import concourse.mybir as mybir
from concourse.bass import AP
from concourse.bass_utils import run_bass_kernel_spmd

F32 = mybir.dt.float32
BF16 = mybir.dt.bfloat16
I32 = mybir.dt.int32
AF = mybir.ActivationFunctionType
ALU = mybir.AluOpType
AX = mybir.AxisListType

D = 2048
DEPTH = 4
SEQ = 4096
DIN = 2304
DFF = 5632
NP = 512
NS = 4
NT = NP + NS
NB = NP // 128
NCHK = NP // 8
NCH1 = NCHK + 1
EPS = 1e-5
NEG = -30000.0
TWO_PI = 2.0 * math.pi


class Reg:
    __slots__ = ("name", "w", "rs")

    def __init__(self, name):
        self.name = name
        self.w = None
        self.rs = []


class Op:
    __slots__ = ("eng", "fn", "deps", "marked", "count", "is_dma", "sem", "semval")

    def __init__(self, eng, fn, is_dma=False):
        self.eng = eng
        self.fn = fn
        self.deps = []
        self.marked = False
        self.count = 0
        self.is_dma = is_dma
        self.sem = None
        self.semval = 0


ENGS = ("pe", "act", "dve", "pool", "sp")
N_DMA_SEMS = 32


class Prog:
    def __init__(self, nc):
        self.nc = nc
        self.ops = {e: [] for e in ENGS}
        self.dma_last = [None] * N_DMA_SEMS
        self.dma_cum = [0] * N_DMA_SEMS
        self.dma_rr = 0
        self.regs = {}

    def R(self, name):
        r = self.regs.get(name)
        if r is None:
            r = self.regs[name] = Reg(name)
        return r

    def _deps(self, op, reads, writes):
        deps = []
        for r in reads:
            if r.w is not None:
                deps.append(r.w)
        for w in writes:
            if w.w is not None:
                deps.append(w.w)
            deps.extend(w.rs)
        for r in reads:
            r.rs.append(op)
        for w in writes:
            w.w = op
            w.rs = []
        seen = set()
        out = []
        for d in deps:
            if id(d) in seen or d is op:
                continue
            seen.add(id(d))
            if op.eng == "pe" and d.eng == "pe" and not d.is_dma and not op.is_dma:
                continue
            out.append(d)
            d.marked = True
        op.deps = out

    def op(self, eng, fn, reads=(), writes=()):
        o = Op(eng, fn)
        writes = list(writes) + [x for x in reads if isinstance(x, str) and x.startswith("ps")]
        reads = [x for x in reads if not (isinstance(x, str) and x.startswith("ps"))]
        self._deps(o, [self.R(x) if isinstance(x, str) else x for x in reads],
                   [self.R(x) if isinstance(x, str) else x for x in writes])
        self.ops[eng].append(o)
        return o

    def dma(self, queue, fn, reads=(), writes=()):
        o = Op(queue, fn, is_dma=True)
        s = self.dma_rr
        self.dma_rr = (self.dma_rr + 1) % N_DMA_SEMS
        o.sem = s
        self.dma_cum[s] += 16
        o.semval = self.dma_cum[s]
        self._deps(o, [self.R(x) if isinstance(x, str) else x for x in reads],
                   [self.R(x) if isinstance(x, str) else x for x in writes])
        if self.dma_last[s] is not None:
            o.deps.append(self.dma_last[s])
        self.dma_last[s] = o
        self.ops[queue].append(o)
        return o

    def emit(self):
        nc = self.nc
        for e in ENGS:
            c = 0
            for o in self.ops[e]:
                if o.is_dma:
                    continue
                if o.marked:
                    c += 1
                o.count = c
        with contextlib.ExitStack() as st:
            esem = {e: st.enter_context(nc.semaphore("es_" + e)) for e in ENGS}
            dsem = [st.enter_context(nc.semaphore("ds_%d" % i)) for i in range(N_DMA_SEMS)]
            block = st.enter_context(nc.Block())
            handles = {"pe": block.tensor, "act": block.scalar, "dve": block.vector,
                       "pool": block.gpsimd, "sp": block.sync}
            for e in ENGS:
                ops = self.ops[e]

                def body(eh, ops=ops, e=e):
                    waited = {}
                    for o in ops:
                        for d in o.deps:
                            if d.is_dma:
                                key, sem, val = ("d", d.sem), dsem[d.sem], d.semval
                            else:
                                key, sem, val = ("e", d.eng), esem[d.eng], d.count
                            if waited.get(key, 0) >= val:
                                continue
                            waited[key] = val
                            eh.wait_ge(sem, val)
                        inst = o.fn(eh)
                        if o.is_dma:
                            inst.then_inc(dsem[o.sem], 16)
                        elif o.marked:
                            inst.then_inc(esem[e], 1)

                handles[e](body)


def Rec(name, *a, **k):
    return lambda e: getattr(e, name)(*a, **k)


def sap(base, dims):
    return AP(tensor=base.tensor, offset=base.offset, ap=[list(base.ap[0])] + [[int(a), int(b)] for a, b in dims])


def dap(t, offset, dims):
    return AP(tensor=t, offset=int(offset), ap=[[int(a), int(b)] for a, b in dims])


class _Stop(Exception):
    pass


def build(npass=8, depth=DEPTH, dbg=None, stop=None):
    nc = bass.Bass("TRN2", target_bir_lowering=False)
    P = Prog(nc)
    dt_in = lambda n, s: nc.dram_tensor(n, list(s), F32, kind="ExternalInput")
    dt_out = lambda n, s: nc.dram_tensor(n, list(s), F32, kind="ExternalOutput")
    xp = dt_in("xp", [SEQ, D])
    xs = dt_in("xs", [16, D])
    ck = dt_in("ck", [DEPTH, 4, 128, 128])
    cv = dt_in("cv", [DEPTH, 4, 128, 128])
    hr0 = dt_in("hr0", [DEPTH, 4, 64, 64])
    hi0 = dt_in("hi0", [DEPTH, 4, 64, 64])
    W = {}
    for n, s in [("norm_mix", [DEPTH, D]), ("w_in", [DEPTH, D, DIN]), ("attn_sink", [DEPTH, 16]),
                 ("ssm_a_re", [DEPTH, 64, 64]), ("ssm_a_im", [DEPTH, 64, 64]), ("ssm_log_dt", [DEPTH, 64]),
                 ("ssm_b_re", [DEPTH, 64, 64, 16]), ("ssm_b_im", [DEPTH, 64, 64, 16]),
                 ("ssm_c_re", [DEPTH, 64, 16, 64]), ("ssm_c_im", [DEPTH, 64, 16, 64]),
                 ("ssm_d", [DEPTH, 1024]), ("w_glu", [DEPTH, 1024, 1024]),
                 ("norm_attn_out", [DEPTH, 1024]), ("norm_ssm_out", [DEPTH, 1024]),
                 ("w_out", [DEPTH, D, D]), ("norm_ffn", [DEPTH, D]),
                 ("w_gate", [DEPTH, D, DFF]), ("w_up", [DEPTH, D, DFF]), ("w_down", [DEPTH, DFF, D]),
                 ("norm_final", [D])]:
        W[n] = dt_in(n, s)
    cst = dt_in("cst", [128, 1100])
    yp = dt_out("yp", [SEQ, D])
    ys = dt_out("ys", [16, D])
    okp = dt_out("okp", [DEPTH, 128, 128])
    ovp = dt_out("ovp", [DEPTH, 128, 128])
    ohrp = dt_out("ohrp", [DEPTH, 64, 64])
    ohip = dt_out("ohip", [DEPTH, 64, 64])
    oks = dt_out("oks", [DEPTH, 4, 128, 128])
    ovs = dt_out("ovs", [DEPTH, 4, 128, 128])
    ohrs = dt_out("ohrs", [DEPTH, 4, 64, 64])
    ohis = dt_out("ohis", [DEPTH, 4, 64, 64])
    dbg_t = nc.dram_tensor("dbg", [128, 16 * NT], BF16, kind="ExternalOutput") if dbg else None

    st = contextlib.ExitStack()
    with st:
        st.enter_context(nc.allow_non_contiguous_dma(reason="small strided parameter loads"))
        sb = lambda n, s, d=F32: st.enter_context(nc.sbuf_tensor(n, list(s), d))
        X = sb("X", [128, 16, NT])
        Hb = sb("Hb", [128, 16, NT], BF16)
        Ao = sb("Ao", [128, 8, NT], BF16)
        FB = sb("FB", [128, 6 * NT])
        ACT_ = FB[:].bitcast(BF16).rearrange("p (a b) -> p a b", a=12)
        Zf = ACT_
        stage = FB[:, 0:D]
        rstd = sb("rstd", [128, NT])
        sqs = [sb("sq%d" % i, [128, NT], BF16) for i in range(2)]
        NSLOT = 4
        ring = [sb("ring%d" % i, [128, 16, 256], BF16) for i in range(NSLOT)]
        cs = sb("cs", [128, 1100])
        identb = sb("identb", [128, 128], BF16)
        onesb = sb("onesb", [128, 128], BF16)
        maskb = sb("maskb", [128, 256], BF16)
        maskfb = sb("maskfb", [128, 256], BF16)
        masksb = sb("masksb", [128, 132], BF16)
        gains = sb("gains", [128, DEPTH, 64])
        gfin = sb("gfin", [128, 16])
        sinkb = sb("sinkb", [128, DEPTH * 16])
        Kb = [sb("Kb%d" % i, [128, 128 + NP], BF16) for i in range(2)]
        KS = [sb("KS%d" % i, [128, 132], BF16) for i in range(2)]
        Vb = sb("Vb", [128, NB + 1, 4, 128], BF16)
        VSc = sb("VSc", [128, 4, 128], BF16)
        VSn = sb("VSn", [4, 4, 128], BF16)
        Khalo = sb("Khalo", [128, DEPTH, 2, 128], BF16)
        Vhalo = sb("Vhalo", [128, DEPTH, 4, 128], BF16)
        kvst = sb("kvst", [128, 256])
        kvss = sb("kvss", [4, 256])
        cpy = kvst
        cks = sb("cks", [128, 256], BF16)
        Pf = [sb("Pf%d" % i, [128, 256]) for i in range(2)]
        Pn = [sb("Pn%d" % i, [128, 256], BF16) for i in range(2)]
        PT = [sb("PT%d" % i, [128, 2, 128], BF16) for i in range(2)]
        sm = sb("sm", [128, 16])
        S1 = lambda n: sb(n, [128, 32])
        lam_r, lam_i, dtt, mag, th, abr, abi, t1, t2, t3, t4, crr, cii, rho, phi, iar, iai = [
            S1(n) for n in "lam_r lam_i dtt mag th abr abi t1 t2 t3 t4 crr cii rho phi iar iai".split()]
        ti = sb("ti", [128, 32], I32)
        Br = sb("Br", [128, 32, 16]); Bi = sb("Bi", [128, 32, 16])
        Bbr = Br; Bbi = Bi
        Cn = sb("Cn", [16, 16, 64])
        Cr = sb("Cr", [128, 32, 16]); Ci = sb("Ci", [128, 32, 16])
        PCr = sb("PCr", [128, 32, 16]); PCi = sb("PCi", [128, 32, 16])
        PBr = sb("PBr", [128, 32, 8]); PBi = sb("PBi", [128, 32, 8])
        Drow = sb("Drow", [128, 1024])
        Hr = sb("Hr", [128, DEPTH, 32]); Hi = sb("Hi", [128, DEPTH, 32])
        h0r = sb("h0r", [128, 32]); h0i = sb("h0i", [128, 32])
        Vu = sb("Vu", [128, 16, 8, 16], BF16)
        Ug = [sb("Ug%d" % i, [128, NCH1], BF16) for i in range(16)]
        MBt = [sb("MBt%d" % i, [128, 128], BF16) for i in range(2)]
        MBp = [sb("MBp%d" % i, [128, 128], BF16) for i in range(4)]
        TgS = [sb("TgS%d" % i, [128, 128], BF16) for i in range(16)]
        MCB = [[sb("MCB%d_%d" % (i, r), [128, 256], BF16) for r in range(2)] for i in range(8)]
        tA = sb("tA", [128, 256]); tB = sb("tB", [128, 256])
        tmask = sb("tmask", [128, 128])
        NPB = 8
        Tc = sb("Tc", [128, NPB, 65]); Ts = sb("Ts", [128, NPB, 65])
        Dr = sb("Dr", [128, NPB, 65]); Di = sb("Di", [128, NPB, 65])
        d0 = sb("d0", [128, NPB, 65])
        ang = sb("ang", [128, NPB, 65])
        u1 = sb("u1", [128, NPB, 65]); u2 = sb("u2", [128, NPB, 65])
        Sbr = sb("Sbr", [128, NPB, 65], BF16); Sbi = sb("Sbi", [128, NPB, 65], BF16)
        Xs = sb("Xs", [128, 2, NPB])
        hend = sb("hend", [128, 2, 32])
        vd = sb("vd", [128, 4, 8, 16])
        ypre = sb("ypre", [128, 4, 8, 16])
        zcm = sb("zcm", [128, 8, 128], BF16)
        ps = [st.enter_context(nc.psum_tensor("ps%d" % i, [128, 512], F32)) for i in range(8)]
        RP = ["ps%d" % i for i in range(8)]
        STG = ["fb%d" % i for i in range(8)]

        def psb(b):
            return ps[b][:].bitcast(BF16)

        P.dma("sp", Rec("dma_start", out=cs[:], in_=cst.ap()), [], ["cs"])
        P.op("dve", Rec("tensor_copy", out=identb[:], in_=cs[:, 0:128]), ["cs"], ["identb"])
        P.op("dve", Rec("tensor_copy", out=maskb[:], in_=cs[:, 128:384]), ["cs"], ["maskb"])
        P.op("dve", Rec("tensor_copy", out=maskfb[:], in_=cs[:, 384:640]), ["cs"], ["maskfb"])
        P.op("dve", Rec("tensor_copy", out=masksb[:], in_=cs[:, 640:772]), ["cs"], ["masksb"])
        P.op("dve", Rec("tensor_copy", out=tmask[:], in_=cs[:, 772:900]), ["cs"], ["tmask"])
        P.op("pool", Rec("memset", onesb[:], 1.0), [], ["onesb"])
        identf = cs[:, 0:128]
        posf = cs[:, 900:965]
        for l in range(depth):
            for nm, off, nk in [("norm_mix", 0, 16), ("norm_ffn", 16, 16), ("norm_attn_out", 32, 8), ("norm_ssm_out", 40, 8)]:
                src = dap(W[nm], l * nk * 128, [[1, 128], [128, nk]])
                P.dma("sp", Rec("dma_start", out=gains[:, l, off:off + nk], in_=src), [], ["gains"])
        P.dma("sp", Rec("dma_start", out=gfin[:], in_=dap(W["norm_final"], 0, [[1, 128], [128, 16]])), [], ["gfin"])
        P.dma("sp", Rec("dma_start", out=sinkb[:], in_=dap(W["attn_sink"], 0, [[0, 128], [1, DEPTH * 16]])), [], ["sinkb"])
        P.op("pool", Rec("memset", Hr[:], 0.0), [], ["H"])
        P.op("pool", Rec("memset", Hi[:], 0.0), [], ["H"])
        P.op("pool", Rec("memset", Khalo[:], 0.0), [], ["Khalo"])
        P.op("pool", Rec("memset", Vhalo[:], 0.0), [], ["Vhalo"])
        P.op("pool", Rec("memset", Vb[:], 0.0), [], ["Vb"])
        P.op("pool", Rec("memset", VSc[:], 0.0), [], ["VSc"])
        P.op("pool", Rec("memset", VSn[:], 0.0), [], ["VSn"])
        P.op("pool", Rec("memset", Vu[:], 0.0), [], ["Vu"])
        for i in range(4):
            P.op("pool", Rec("memset", MBp[i][:], 0.0), [], ["MBp"])

        sched = []
        for ps_i in range(npass):
            for l in range(depth):
                sched.append(("w_in", l, 0, 16, 1024, 256))
                for j in range(4):
                    sched.append(("w_in", l, 0, 16, j * 256, 256))
                for j in range(4):
                    sched.append(("w_in", l, 0, 16, 1280 + j * 256, 256))
                for j in range(4):
                    sched.append(("w_glu", l, 0, 8, j * 256, 256))
                for j in range(8):
                    sched.append(("w_out", l, 0, 16, j * 256, 256))
                for fp in range(4):
                    c0 = fp * 1536
                    nsl = 6 if fp < 3 else 4
                    for j in range(nsl):
                        sched.append(("w_gate", l, 0, 16, c0 + j * 256, 256))
                        sched.append(("w_up", l, 0, 16, c0 + j * 256, 256))
                    nk = nsl * 2
                    for j in range(8):
                        sched.append(("w_down", l, c0, nk, j * 256, 256))
        wcur = [0, 0]
        NCOLS = {"w_in": DIN, "w_glu": 1024, "w_out": D, "w_gate": DFF, "w_up": DFF, "w_down": D}
        NROWS = {"w_in": D, "w_glu": 1024, "w_out": D, "w_gate": D, "w_up": D, "w_down": DFF}

        def w_issue(upto):
            while wcur[1] < min(upto, len(sched)):
                i = wcur[1]
                nm, l, r0, nk, c0, ncl = sched[i]
                slot = i % NSLOT
                if nm == "w_krep":
                    for kvh in range(2):
                        for r_ in range(2):
                            src = dap(W["w_in"], l * D * DIN + 1024 + kvh * 64, [[DIN, 128], [128 * DIN, 16], [1, 64]])
                            c_ = kvh * 128 + r_ * 64
                            P.dma("pool", Rec("dma_start", out=ring[slot][:, :, c_:c_ + 64], in_=src),
                                  [], ["ring%d" % slot])
                    wcur[1] += 1
                    continue
                ncols = NCOLS[nm]
                src = dap(W[nm], l * NROWS[nm] * ncols + r0 * ncols + c0, [[ncols, 128], [128 * ncols, nk], [1, ncl]])
                P.dma("pool", Rec("dma_start", out=ring[slot][:, 0:nk, 0:ncl], in_=src),
                      [], ["ring%d" % slot])
                wcur[1] += 1

        def w_next(nm, l, c0):
            i = wcur[0]
            assert sched[i][0] == nm and sched[i][1] == l and sched[i][4] == c0, (sched[i], nm, l, c0)
            w_issue(i + NSLOT - 1)
            wcur[0] += 1
            return ring[i % NSLOT], "ring%d" % (i % NSLOT)

        pcnt = [0]
        HS = [True]

        def proj_chunk(slab, rslab, nk, mcol, rhs_fn, rhs_regs, evac_fn, lhs_rep=False):
            b = pcnt[0] % 2
            sbk = 2 + pcnt[0] % 2
            so = ((pcnt[0] // 2) % 8) * 4
            pcnt[0] += 1
            for k in range(nk):
                lhs = slab[:, k, mcol:mcol + 128]
                r = rhs_fn(k)
                P.op("pe", Rec("matmul", ps[b][:, 0:NP], lhsT=lhs, rhs=r[:, 0:NP], start=(k == 0), stop=(k == nk - 1)),
                     [rslab] + rhs_regs(k), [RP[b]])
                if HS[0]:
                    P.op("pe", Rec("matmul", ps[sbk][:, so:so + NS], lhsT=lhs, rhs=r[:, NP:NT], start=(k == 0), stop=(k == nk - 1)),
                         [rslab] + rhs_regs(k), [RP[sbk]])
            evac_fn(ps[b][:, 0:NP], ps[sbk][:, so:so + NS] if HS[0] else None, [RP[b]], [RP[sbk]])

        def rmsnorm(srcs, src_regs, gain_fn, dsts, dst_regs, dim):
            nk = len(srcs)
            for k in range(nk):
                q = sqs[k % 2]
                P.op("act", Rec("activation", out=q[:], in_=srcs[k], func=AF.Square), [src_regs[k]], ["sq%d" % (k % 2)])
                P.op("pe", Rec("matmul", ps[3][:, 0:NP], lhsT=onesb[:], rhs=q[:, 0:NP], start=(k == 0), stop=(k == nk - 1)),
                     ["sq%d" % (k % 2), "onesb"], [RP[3]])
                if HS[0]:
                    P.op("pe", Rec("matmul", ps[2][:, 480:480 + NS], lhsT=onesb[:], rhs=q[:, NP:NT], start=(k == 0), stop=(k == nk - 1)),
                         ["sq%d" % (k % 2), "onesb"], [RP[2]])
            P.op("act", Rec("activation", out=rstd[:, 0:NP], in_=ps[3][:, 0:NP], func=AF.Sqrt, scale=1.0 / dim, bias=epsb[:, 0:1]), [RP[3], "epsb"], ["rstd"])
            if HS[0]:
                P.op("act", Rec("activation", out=rstd[:, NP:NT], in_=ps[2][:, 480:480 + NS], func=AF.Sqrt, scale=1.0 / dim, bias=epsb[:, 0:1]), [RP[2], "epsb"], ["rstd"])
            P.op("dve", Rec("reciprocal", out=rstd[:], in_=rstd[:]), ["rstd"], ["rstd"])
            for k in range(nk):
                P.op("dve", Rec("scalar_tensor_tensor", out=dsts[k], in0=srcs[k], scalar=gain_fn(k), in1=rstd[:], op0=ALU.mult, op1=ALU.mult),
                     [src_regs[k], "rstd", "gains", "gfin"], [dst_regs[k]])

        epsb = sb("epsb", [128, 1])
        P.op("pool", Rec("memset", epsb[:], EPS), [], ["epsb"])
        halfpi = sb("halfpi", [128, 1])
        P.op("pool", Rec("memset", halfpi[:], math.pi / 2), [], ["halfpi"])

        def sincos(eng_v, angle, itmp, a1, a2, out_c, out_s, regs_in, rtag):
            P.op("dve", Rec("tensor_scalar", out=a1, in0=angle, scalar1=1.0 / TWO_PI, scalar2=None, op0=ALU.mult), regs_in, [rtag + "a1"])
            P.op("dve", Rec("tensor_copy", out=itmp, in_=a1), [rtag + "a1"], [rtag + "a2"])
            P.op("dve", Rec("tensor_copy", out=a1, in_=itmp), [rtag + "a2"], [rtag + "a1"])
            P.op("dve", Rec("scalar_tensor_tensor", out=angle, in0=a1, scalar=-TWO_PI, in1=angle, op0=ALU.mult, op1=ALU.add), [rtag + "a1"] + regs_in, regs_in)
            P.op("act", Rec("activation", out=a1, in_=angle, func=AF.Sin, scale=0.5), regs_in, [rtag + "a1"])
            P.op("act", Rec("activation", out=a2, in_=angle, func=AF.Sin, scale=0.5, bias=halfpi[:, 0:1]), regs_in + ["halfpi"], [rtag + "a2"])
            P.op("dve", Rec("scalar_tensor_tensor", out=out_s, in0=a1, scalar=2.0, in1=a2, op0=ALU.mult, op1=ALU.mult), [rtag + "a1", rtag + "a2"], [rtag + "s"])
            P.op("dve", Rec("tensor_tensor", out=a2, in0=a1, in1=a1, op=ALU.mult), [rtag + "a1"], [rtag + "a2"])
            P.op("dve", Rec("tensor_scalar", out=out_c, in0=a2, scalar1=-2.0, scalar2=1.0, op0=ALU.mult, op1=ALU.add), [rtag + "a2"], [rtag + "c"])

        def cmul(eng, o_r, o_i, a_r, a_i, b_r, b_i, tmp1, tmp2, rin, rout):
            P.op(eng, Rec("tensor_tensor", out=tmp1, in0=a_i, in1=b_i, op=ALU.mult), rin, rout)
            P.op(eng, Rec("tensor_tensor", out=tmp2, in0=a_i, in1=b_r, op=ALU.mult), rin, rout)
            P.op(eng, Rec("tensor_tensor", out=o_r, in0=a_r, in1=b_r, op=ALU.mult), rin, rout)
            P.op(eng, Rec("tensor_tensor", out=o_i, in0=a_r, in1=b_i, op=ALU.mult), rin, rout)
            P.op(eng, Rec("tensor_tensor", out=o_r, in0=o_r, in1=tmp1, op=ALU.subtract), rin, rout)
            P.op(eng, Rec("tensor_tensor", out=o_i, in0=o_i, in1=tmp2, op=ALU.add), rin, rout)

        def ckpt(n):
            if stop is not None and n >= stop:
                raise _Stop()

        try:
          for pi in range(npass):
              sidx = pi % 4
              HS[0] = pi < 4
              for blk in range(NB):
                  r0 = pi * NP + blk * 128
                  P.dma("sp", Rec("dma_start", out=stage[:], in_=dap(xp, r0 * D, [[D, 128], [1, D]])), [], STG)
                  for kq in range(4):
                      for kk in range(4):
                          k = kq * 4 + kk
                          P.op("pe", Rec("transpose", ps[4][:, kk * 128:(kk + 1) * 128], stage[:, k * 128:(k + 1) * 128], identf),
                               STG + ["cs"], [RP[4]])
                      P.op("dve", Rec("tensor_copy", out=X[:, kq * 4:kq * 4 + 4, blk * 128:(blk + 1) * 128],
                                                                      in_=ps[4][:].rearrange("p (a b) -> p a b", a=4)),
                           [RP[4]], ["X%d" % k for k in range(kq * 4, kq * 4 + 4)])
              P.dma("sp", Rec("dma_start", out=stage[0:4, :], in_=dap(xs, sidx * 4 * D, [[D, 4], [1, D]])), [], STG)
              for k in range(16 if HS[0] else 0):
                  P.op("pe", Rec("transpose", ps[4][:, k * 4:(k + 1) * 4], stage[0:4, k * 128:(k + 1) * 128], identf[0:4, 0:4]),
                       STG + ["cs"], [RP[4]])
              if HS[0]:
                  P.op("dve", Rec("tensor_copy", out=X[:, :, NP:NT], in_=ps[4][:, 0:64].rearrange("p (a b) -> p a b", a=16)),
                       [RP[4]], ["X%d" % k for k in range(16)])

              ckpt(1)
              for l in range(depth):
                  rmsnorm([X[:, k, :] for k in range(16)], ["X%d" % k for k in range(16)], lambda k, l=l: gains[:, l, k:k + 1],
                          [Hb[:, k, :] for k in range(16)], ["Hb%d" % k for k in range(16)], D)
                  hb_rhs = lambda k: Hb[:, k, :]
                  hb_regs = lambda k: ["Hb%d" % k]
                  ckpt(2)
                  for kvh in range(2):
                      P.op("pool", Rec("tensor_copy", out=Kb[kvh][:, 0:128], in_=Khalo[:, l, kvh, :]), ["Khalo"], ["Kb%d" % kvh])
                  P.op("pool", Rec("tensor_copy", out=Vb[:, 0, :, :], in_=Vhalo[:, l, :, :]), ["Vhalo"], ["Vb"])
                  ckpt(2.1)
                  slab, rslab = w_next("w_in", l, 1024)

                  def ev_k(pm, psm, rpm, rps):
                      for kvh in range(2):
                          hs = slice(kvh * 64, kvh * 64 + 64)
                          P.op("act", Rec("activation", out=Kb[kvh][hs, 128:128 + NP], in_=pm[hs, :], func=AF.Copy), rpm, ["Kb%d" % kvh])
                          if psm is not None:
                              P.op("act", Rec("activation", out=KS[kvh][hs, 128:132], in_=psm[hs, :], func=AF.Copy), rps, ["KS%d" % kvh])
                  proj_chunk(slab, rslab, 16, 0, hb_rhs, hb_regs, ev_k)
                  for kvh in range(2):
                      hs = slice(kvh * 64, kvh * 64 + 64)
                      ho = slice((1 - kvh) * 64, (1 - kvh) * 64 + 64)
                      P.dma("sp", Rec("dma_start", out=Kb[kvh][ho, 128:128 + NP], in_=Kb[kvh][hs, 128:128 + NP]), ["Kb%d" % kvh], ["Kb%d" % kvh])
                  ckpt(2.2)
                  import os as _os
                  for blk in range(NB + (1 if HS[0] else 0)):
                      if blk < NB:
                          cols = slice(blk * 128, (blk + 1) * 128); mrows = 128
                      else:
                          cols = slice(NP, NT); mrows = NS
                      for k in range(16):
                          P.op("pe", Rec("matmul", ps[5][0:mrows, 0:256], lhsT=Hb[:, k, cols], rhs=slab[:, k, 0:256], start=(k == 0), stop=(k == 15)),
                               [rslab, "Hb%d" % k], [RP[5]])
                      if blk < NB:
                          for kvh in range(2 - 2 * int(_os.environ.get("SKIP_A", "0"))):
                              for odd in range(2):
                                  P.op("dve", Rec("tensor_copy", out=Vb[:, blk + 1, kvh * 2 + odd, odd * 64:odd * 64 + 64], in_=ps[5][:, 128 + kvh * 64:128 + kvh * 64 + 64]),
                                       [RP[5]], ["Vb"])
                          if blk == NB - 1 and not int(_os.environ.get("SKIP_B", "0")):
                              P.op("act", Rec("activation", out=kvst[:], in_=ps[5][:, 0:256], func=AF.Copy), [RP[5]], ["kvst"])
                              P.dma("sp", Rec("dma_start", out=okp.ap()[l], in_=kvst[:, 0:128]), ["kvst"], ["okp"])
                              P.dma("sp", Rec("dma_start", out=ovp.ap()[l], in_=kvst[:, 128:256]), ["kvst"], ["ovp"])
                      else:
                          for kvh in range(2):
                              for odd in range(2):
                                  P.op("dve", Rec("tensor_copy", out=VSn[0:4, kvh * 2 + odd, odd * 64:odd * 64 + 64], in_=ps[5][0:4, 128 + kvh * 64:128 + kvh * 64 + 64]),
                                       [RP[5]], ["VSn"])
                          P.op("act", Rec("activation", out=kvss[:], in_=ps[5][0:4, 0:256], func=AF.Copy), [RP[5]], ["kvss"])
                          P.dma("sp", Rec("dma_start", out=oks.ap()[l, sidx, 124:128, :], in_=kvss[:, 0:128]), ["kvss"], ["oks"])
                          P.dma("sp", Rec("dma_start", out=ovs.ap()[l, sidx, 124:128, :], in_=kvss[:, 128:256]), ["kvss"], ["ovs"])
                          P.dma("sp", Rec("dma_start", out=cpy[0:124, 0:128], in_=ck.ap()[l, sidx, 4:128, :]), [], ["kvst"])
                          P.dma("sp", Rec("dma_start", out=cpy[0:124, 128:256], in_=cv.ap()[l, sidx, 4:128, :]), [], ["kvst"])
                          P.dma("sp", Rec("dma_start", out=oks.ap()[l, sidx, 0:124, :], in_=cpy[0:124, 0:128]), ["kvst"], ["oks"])
                          P.dma("sp", Rec("dma_start", out=ovs.ap()[l, sidx, 0:124, :], in_=cpy[0:124, 128:256]), ["kvst"], ["ovs"])
                  ckpt(2.3)
                  for kvh in range(2):
                      P.op("pool", Rec("tensor_copy", out=Khalo[:, l, kvh, :], in_=Kb[kvh][:, NP:NP + 128]), ["Kb%d" % kvh], ["Khalo"])
                  P.op("pool", Rec("tensor_copy", out=Vhalo[:, l, :, :], in_=Vb[:, NB, :, :]), ["Vb"], ["Vhalo"])
                  ckpt(2.4)
                  if HS[0]:
                      P.dma("pool", Rec("dma_start", out=cks[:, 0:128], in_=ck.ap()[l, sidx]), [], ["cks"])
                      P.op("pe", Rec("matmul", ps[5][:, 256:384], lhsT=cks[:, 0:128], rhs=identb[:], start=True, stop=True), ["cks", "identb"], [RP[5]])
                  for kvh in range(2 if HS[0] else 0):
                      hs = slice(kvh * 64, kvh * 64 + 64)
                      ho = slice((1 - kvh) * 64, (1 - kvh) * 64 + 64)
                      P.op("act", Rec("activation", out=KS[kvh][hs, 0:128], in_=ps[5][hs, 256:384], func=AF.Copy), [RP[5]], ["KS%d" % kvh])
                      P.dma("sp", Rec("dma_start", out=KS[kvh][ho, :], in_=KS[kvh][hs, :]), ["KS%d" % kvh], ["KS%d" % kvh])
                  for kvh in range(2 if HS[0] else 0):
                      for odd in range(2):
                          P.dma("pool", Rec("dma_start", out=VSc[:, kvh * 2 + odd, odd * 64:odd * 64 + 64], in_=cv.ap()[l, sidx, :, kvh * 64:kvh * 64 + 64]), [], ["VSc"])
                  ckpt(3)
                  for j in range(4):
                      slab, rslab = w_next("w_in", l, j * 256)
                      for mm in range(2):
                          m = j * 2 + mm
                          def ev_q(pm, psm, rpm, rps, m=m):
                              P.op("act", Rec("activation", out=Ao[:, m, 0:NP], in_=pm, func=AF.Copy, scale=0.125), rpm, ["Ao%d" % m])
                              if psm is not None:
                                  P.op("act", Rec("activation", out=Ao[:, m, NP:NT], in_=psm, func=AF.Copy, scale=0.125), rps, ["Ao%d" % m])
                          proj_chunk(slab, rslab, 16, mm * 128, hb_rhs, hb_regs, ev_q)
                  ckpt(4)
                  ucnt = [0]

                  def attn_unit(m, nq, qcols, keyfn, nkeys, mk, segs, l=l):
                      for odd in range(2):
                          h = 2 * m + odd
                          kvh = h // 8
                          hp = odd * 64
                          u = ucnt[0] % 2
                          ucnt[0] += 1
                          sbk = 6 + u
                          P.op("pe", Rec("matmul", ps[sbk][0:nq, 0:nkeys], lhsT=Ao[hp:hp + 64, m, qcols], rhs=keyfn(kvh)[hp:hp + 64, :], start=True, stop=False),
                               ["Ao%d" % m, "Kb%d" % kvh, "KS%d" % kvh], [RP[sbk]])
                          P.op("pe", Rec("matmul", ps[sbk][0:nq, 0:nkeys], lhsT=identb[0:nq, 0:nq], rhs=mk[0:nq, 0:nkeys], start=False, stop=True),
                               ["identb", "maskb", "maskfb", "masksb"], [RP[sbk]])
                          c0 = u * 8
                          scol = sinkb[0:nq, l * 16 + h:l * 16 + h + 1]
                          P.op("dve", Rec("reduce_max", out=sm[0:nq, c0:c0 + 1], in_=ps[sbk][0:nq, 0:nkeys], axis=AX.X), [RP[sbk]], ["sm%d" % u])
                          P.op("dve", Rec("tensor_scalar", out=sm[0:nq, c0 + 1:c0 + 2], in0=sm[0:nq, c0:c0 + 1], scalar1=scol, scalar2=-1.0, op0=ALU.max, op1=ALU.mult),
                               ["sm%d" % u, "sinkb"], ["sm%d" % u])
                          P.op("act", Rec("activation", out=Pf[u][0:nq, 0:nkeys], in_=ps[sbk][0:nq, 0:nkeys], func=AF.Exp, bias=sm[0:nq, c0 + 1:c0 + 2], accum_out=sm[0:nq, c0 + 2:c0 + 3]),
                               [RP[sbk], "sm%d" % u], ["Pf%d" % u, "sm%d" % u])
                          P.op("act", Rec("activation", out=sm[0:nq, c0 + 3:c0 + 4], in_=sm[0:nq, c0 + 1:c0 + 2], func=AF.Exp, bias=scol),
                               ["sm%d" % u, "sinkb"], ["sm%d" % u])
                          P.op("dve", Rec("tensor_tensor", out=sm[0:nq, c0 + 4:c0 + 5], in0=sm[0:nq, c0 + 2:c0 + 3], in1=sm[0:nq, c0 + 3:c0 + 4], op=ALU.add), ["sm%d" % u], ["sm%d" % u])
                          P.op("dve", Rec("reciprocal", out=sm[0:nq, c0 + 5:c0 + 6], in_=sm[0:nq, c0 + 4:c0 + 5]), ["sm%d" % u], ["sm%d" % u])
                          P.op("dve", Rec("tensor_scalar", out=Pn[u][0:nq, 0:nkeys], in0=Pf[u][0:nq, 0:nkeys], scalar1=sm[0:nq, c0 + 5:c0 + 6], scalar2=None, op0=ALU.mult),
                               ["Pf%d" % u, "sm%d" % u], ["Pn%d" % u])
                          ptb = psb(6 + u)
                          ko = 0
                          for si, (nk_, vfn) in enumerate(segs):
                              P.op("pe", Rec("transpose", ptb[0:nk_, si * 128:si * 128 + nq], Pn[u][0:nq, ko:ko + nk_], identb[0:nq, 0:nq]),
                                   ["Pn%d" % u, "identb"], [RP[6 + u]])
                              P.op("act", Rec("activation", out=PT[u][0:nk_, si, 0:nq], in_=ptb[0:nk_, si * 128:si * 128 + nq], func=AF.Copy),
                                   [RP[6 + u]], ["PT%d" % u])
                              ko += nk_
                          for si, (nk_, vfn) in enumerate(segs):
                              first = (odd == 0 and si == 0)
                              last = (odd == 1 and si == len(segs) - 1)
                              P.op("pe", Rec("matmul", ps[3][:, 0:nq], lhsT=vfn(kvh * 2 + odd)[0:nk_, :], rhs=PT[u][0:nk_, si, 0:nq], start=first, stop=last),
                                   ["PT%d" % u, "Vb", "VSc", "VSn"], [RP[3]])
                      P.op("dve", Rec("tensor_copy", out=Ao[:, m, qcols], in_=ps[3][:, 0:nq]), [RP[3]], ["Ao%d" % m])

                  def gen_attn():
                      for nb in range(NB):
                          for m in range(8):
                              mk = maskfb if (pi == 0 and nb == 0) else maskb
                              attn_unit(m, 128, slice(nb * 128, (nb + 1) * 128),
                                        lambda kvh, nb=nb: Kb[kvh][:, nb * 128:nb * 128 + 256], 256, mk,
                                        [(128, lambda v, nb=nb: Vb[:, nb, v, :]), (128, lambda v, nb=nb: Vb[:, nb + 1, v, :])])
                              yield
                      for m in range(8 if HS[0] else 0):
                          attn_unit(m, NS, slice(NP, NT), lambda kvh: KS[kvh][:, 0:132], 132, masksb,
                                    [(128, lambda v: VSc[:, v, :]), (NS, lambda v: VSn[:, v, :])])
                          yield
                  ckpt(5)
                  def gen_ssm():
                      for nm, dst in [("ssm_a_re", lam_r), ("ssm_a_im", lam_i)]:
                          P.dma("sp", Rec("dma_start", out=dst[:], in_=dap(W[nm], l * 4096, [[1, 128], [128, 32]])), [], ["ssmp"])
                      for gl in range(2):
                          P.dma("sp", Rec("dma_start", out=dtt[gl * 64:(gl + 1) * 64, :], in_=dap(W["ssm_log_dt"], l * 64 + gl, [[0, 64], [2, 32]])), [], ["ssmp"])
                      for nm, dst in [("ssm_b_re", Br), ("ssm_b_im", Bi)]:
                          P.dma("sp", Rec("dma_start", out=dst[:], in_=dap(W[nm], l * 65536, [[16, 128], [2048, 32], [1, 16]])), [], ["ssmB"])
                      for nm, dst, rc in [("ssm_c_re", Cr, "Cr"), ("ssm_c_im", Ci, "Ci")]:
                          for jq in range(4):
                              P.dma("sp", Rec("dma_start", out=Cn[:], in_=dap(W[nm], l * 65536 + jq * 16384, [[64, 16], [1024, 16], [1, 64]])), [], ["Cn"])
                              for jj in range(8):
                                  P.op("pe", Rec("transpose", ps[4][:, jj * 16:(jj + 1) * 16], Cn[:, 2 * jj:2 * jj + 2, :], identf[0:16, 0:16]), ["Cn", "cs"], [RP[4]])
                              P.op("dve", Rec("tensor_copy", out=dst[:, jq * 8:(jq + 1) * 8, :], in_=ps[4][:, 0:128].rearrange("p (a b) -> p a b", a=8)), [RP[4]], ["ssmC"])
                      P.dma("sp", Rec("dma_start", out=Drow[:], in_=dap(W["ssm_d"], l * 1024, [[0, 128], [1, 1024]])), [], ["Drow"])
                      for src_t, dst in [(hr0, h0r), (hi0, h0i)]:
                          P.dma("sp", Rec("dma_start", out=dst[:], in_=dap(src_t, (l * 4 + sidx) * 4096, [[1, 128], [128, 32]])), [], ["h0"])
                      sp_ = ["ssmp"]
                      P.op("act", Rec("activation", out=dtt[:], in_=dtt[:], func=AF.Exp), sp_, sp_)
                      P.op("dve", Rec("tensor_tensor", out=t1[:], in0=lam_r[:], in1=dtt[:], op=ALU.mult), sp_, sp_)
                      P.op("act", Rec("activation", out=mag[:], in_=t1[:], func=AF.Exp), sp_, sp_)
                      P.op("act", Rec("activation", out=rho[:], in_=t1[:], func=AF.Exp, scale=8.0), sp_, sp_)
                      P.op("dve", Rec("tensor_tensor", out=th[:], in0=lam_i[:], in1=dtt[:], op=ALU.mult), sp_, sp_)
                      P.op("dve", Rec("tensor_scalar", out=phi[:], in0=th[:], scalar1=8.0, scalar2=None, op0=ALU.mult), sp_, sp_)
                      sincos("dve", th[:], t3[:].bitcast(I32), t2[:], t3[:], abr[:], abi[:], sp_, "tr1")
                      P.op("dve", Rec("tensor_tensor", out=abr[:], in0=abr[:], in1=mag[:], op=ALU.mult), sp_ + ["tr1c"], sp_)
                      P.op("dve", Rec("tensor_tensor", out=abi[:], in0=abi[:], in1=mag[:], op=ALU.mult), sp_ + ["tr1s"], sp_)
                      P.op("dve", Rec("tensor_scalar", out=t2[:], in0=phi[:], scalar1=1.0 / TWO_PI, scalar2=None, op0=ALU.mult), sp_ + ["tr1a1"], sp_ + ["tr1a1"])
                      P.op("dve", Rec("tensor_copy", out=ti[:], in_=t2[:]), sp_ + ["tr1a1", "tr1a2"], sp_ + ["tr1a2"])
                      P.op("dve", Rec("tensor_copy", out=t2[:], in_=ti[:]), sp_ + ["tr1a1", "tr1a2"], sp_ + ["tr1a1"])
                      P.op("dve", Rec("scalar_tensor_tensor", out=phi[:], in0=t2[:], scalar=-TWO_PI, in1=phi[:], op0=ALU.mult, op1=ALU.add), sp_ + ["tr1a1"], sp_)
                      P.op("dve", Rec("tensor_scalar", out=t1[:], in0=abr[:], scalar1=-1.0, scalar2=None, op0=ALU.add), sp_, sp_)
                      P.op("dve", Rec("tensor_tensor", out=t2[:], in0=lam_r[:], in1=lam_r[:], op=ALU.mult), sp_ + ["tr1a1"], sp_ + ["tr1a1"])
                      P.op("dve", Rec("tensor_tensor", out=t3[:], in0=lam_i[:], in1=lam_i[:], op=ALU.mult), sp_ + ["tr1a2"], sp_ + ["tr1a2"])
                      P.op("dve", Rec("tensor_tensor", out=t2[:], in0=t2[:], in1=t3[:], op=ALU.add), sp_ + ["tr1a1", "tr1a2"], sp_ + ["tr1a1"])
                      P.op("dve", Rec("reciprocal", out=t2[:], in_=t2[:]), sp_ + ["tr1a1"], sp_ + ["tr1a1"])
                      P.op("dve", Rec("tensor_tensor", out=crr[:], in0=t1[:], in1=lam_r[:], op=ALU.mult), sp_, sp_)
                      P.op("dve", Rec("tensor_tensor", out=t3[:], in0=abi[:], in1=lam_i[:], op=ALU.mult), sp_ + ["tr1a2"], sp_ + ["tr1a2"])
                      P.op("dve", Rec("tensor_tensor", out=crr[:], in0=crr[:], in1=t3[:], op=ALU.add), sp_ + ["tr1a2"], sp_)
                      P.op("dve", Rec("tensor_tensor", out=crr[:], in0=crr[:], in1=t2[:], op=ALU.mult), sp_ + ["tr1a1"], sp_)
                      P.op("dve", Rec("tensor_tensor", out=cii[:], in0=abi[:], in1=lam_r[:], op=ALU.mult), sp_, sp_)
                      P.op("dve", Rec("tensor_tensor", out=t3[:], in0=t1[:], in1=lam_i[:], op=ALU.mult), sp_ + ["tr1a2"], sp_ + ["tr1a2"])
                      P.op("dve", Rec("tensor_tensor", out=cii[:], in0=cii[:], in1=t3[:], op=ALU.subtract), sp_ + ["tr1a2"], sp_)
                      P.op("dve", Rec("tensor_tensor", out=cii[:], in0=cii[:], in1=t2[:], op=ALU.mult), sp_ + ["tr1a1"], sp_)
                      bc = lambda a: a[:].unsqueeze(2).broadcast_to([128, 32, 16])
                      cmul("dve", Bbr[:], Bbi[:], bc(crr), bc(cii), Br[:], Bi[:], PCr[:], PCi[:], sp_ + ["ssmB", "PC"], ["ssmB", "PC"])
                      P.op("dve", Rec("tensor_tensor", out=t1[:], in0=mag[:], in1=mag[:], op=ALU.mult), sp_, sp_)
                      P.op("dve", Rec("reciprocal", out=t1[:], in_=t1[:]), sp_, sp_)
                      P.op("dve", Rec("tensor_tensor", out=iar[:], in0=abr[:], in1=t1[:], op=ALU.mult), sp_, sp_)
                      P.op("dve", Rec("scalar_tensor_tensor", out=iai[:], in0=abi[:], scalar=-1.0, in1=t1[:], op0=ALU.mult, op1=ALU.mult), sp_, sp_)
                      pr = ["PC", "ssmp"]
                      P.op("pool", Rec("memset", PCr[:, :, 7:8], 1.0), pr, pr)
                      P.op("pool", Rec("memset", PCi[:, :, 7:8], 0.0), pr, pr)
                      for kk in range(8, 16):
                          cmul("dve", PCr[:, :, kk], PCi[:, :, kk], PCr[:, :, kk - 1], PCi[:, :, kk - 1], abr[:], abi[:], t2[:], t3[:], pr + ["tr1a1", "tr1a2"], pr + ["tr1a1", "tr1a2"])
                      for kk in range(6, -1, -1):
                          cmul("dve", PCr[:, :, kk], PCi[:, :, kk], PCr[:, :, kk + 1], PCi[:, :, kk + 1], iar[:], iai[:], t2[:], t3[:], pr + ["tr1a1", "tr1a2"], pr + ["tr1a1", "tr1a2"])
                      for i in range(8):
                          P.op("pool", Rec("tensor_copy", out=PBr[:, :, i], in_=PCr[:, :, 14 - i]), pr, ["PB"])
                          P.op("pool", Rec("tensor_copy", out=PBi[:, :, i], in_=PCi[:, :, 14 - i]), pr, ["PB"])
                      cmul("dve", hend[:, 0, :], hend[:, 1, :], PCr[:, :, 11], PCi[:, :, 11], h0r[:], h0i[:], t2[:], t3[:], pr + ["h0", "hend", "tr1a1", "tr1a2"], ["hend", "tr1a1", "tr1a2"])

                      ckpt(6)
                      yield
                      for sl in range(4):
                          slab, rslab = w_next("w_in", l, 1280 + sl * 256)
                          for i in range(8):
                              nrow = NCH1 if i < 4 else NCHK
                              for k in range(16):
                                  lhs = sap(Hb[:, k, i:i + 1], [[8, nrow]])
                                  P.op("pe", Rec("matmul", ps[i % 2][0:nrow, 0:256], lhsT=lhs, rhs=slab[:, k, 0:256], start=(k == 0), stop=(k == 15)),
                                       [rslab, "Hb%d" % k], [RP[i % 2]])
                              P.op("act", Rec("activation", out=Vu[0:nrow, :, i, :], in_=ps[i % 2][0:nrow, 0:256].rearrange("p (g c) -> p g c", g=16), func=AF.Copy), [RP[i % 2]], ["Vu"])
                          yield
                          for gq in range(4):
                              ub = psb(4)
                              for gg in range(4):
                                  g = gq * 4 + gg
                                  P.op("pe", Rec("transpose", ub[:, gg * 128:gg * 128 + NCH1], Vu[0:NCH1, g, :, :], identb[0:NCH1, 0:NCH1]),
                                       ["Vu", "identb"], [RP[4]])
                              for gg in range(4):
                                  g = gq * 4 + gg
                                  P.op("dve", Rec("tensor_copy", out=Ug[g][:], in_=ub[:, gg * 128:gg * 128 + NCH1]), [RP[4]], ["Ug%d" % g])
                          yield
                          j0 = sl * NPB
                          bcp = lambda a: a[:, j0:j0 + NPB].unsqueeze(2).broadcast_to([128, NPB, 65])
                          posb = posf.unsqueeze(1).broadcast_to([128, NPB, 65])
                          P.op("dve", Rec("tensor_tensor", out=ang[:], in0=bcp(phi), in1=posb, op=ALU.mult), ["ssmp", "cs", "trb"], ["trb"])
                          sincos("dve", ang[:], u2[:].bitcast(I32), u1[:], u2[:], Tc[:], Ts[:], ["trb"], "tr2")
                          P.op("dve", Rec("tensor_tensor", out=d0[:], in0=bcp(rho), in1=cs[:, 965:1030].unsqueeze(1).broadcast_to([128, NPB, 65]), op=ALU.mult), ["ssmp", "cs"], ["d0"])
                          for jj in range(NPB):
                              j = j0 + jj
                              rsp = ["ssmB", "PB", "PC", "ssmC", "ssmp"]
                              pb_r = PBr[:, j, :].unsqueeze(2).broadcast_to([128, 8, 16]); pb_i = PBi[:, j, :].unsqueeze(2).broadcast_to([128, 8, 16])
                              bb_r = Bbr[:, j, :].unsqueeze(1).broadcast_to([128, 8, 16]); bb_i = Bbi[:, j, :].unsqueeze(1).broadcast_to([128, 8, 16])
                              v3 = lambda a, n: a[:, 0:n * 16].rearrange("p (a b) -> p a b", b=16)
                              P.op("pool", Rec("tensor_tensor", out=v3(tA, 8), in0=pb_i, in1=bb_i, op=ALU.mult), rsp + ["tA"], ["tA"])
                              P.op("pool", Rec("tensor_tensor", out=v3(tB, 8), in0=pb_r, in1=bb_r, op=ALU.mult), rsp + ["tB"], ["tB"])
                              P.op("pool", Rec("tensor_tensor", out=v3(MBt[0], 8), in0=v3(tB, 8), in1=v3(tA, 8), op=ALU.subtract), ["tA", "tB"], ["MBt"])
                              P.op("pool", Rec("tensor_tensor", out=v3(tA, 8), in0=pb_r, in1=bb_i, op=ALU.mult), rsp + ["tA"], ["tA"])
                              P.op("pool", Rec("tensor_tensor", out=v3(tB, 8), in0=pb_i, in1=bb_r, op=ALU.mult), rsp + ["tB"], ["tB"])
                              P.op("pool", Rec("tensor_tensor", out=v3(MBt[1], 8), in0=v3(tA, 8), in1=v3(tB, 8), op=ALU.add), ["tA", "tB"], ["MBt"])
                              pc_r = PCr[:, j, :].unsqueeze(2).broadcast_to([128, 16, 16]); pc_i = PCi[:, j, :].unsqueeze(2).broadcast_to([128, 16, 16])
                              cc_r = Cr[:, j, :].unsqueeze(1).broadcast_to([128, 16, 16]); cc_i = Ci[:, j, :].unsqueeze(1).broadcast_to([128, 16, 16])
                              P.op("dve", Rec("tensor_tensor", out=v3(tA, 16), in0=pc_i, in1=cc_i, op=ALU.mult), rsp + ["tA"], ["tA"])
                              P.op("dve", Rec("tensor_tensor", out=v3(tB, 16), in0=pc_r, in1=cc_r, op=ALU.mult), rsp + ["tB"], ["tB"])
                              MCm = MCB[jj]
                              P.op("dve", Rec("tensor_tensor", out=v3(MCm[0], 16), in0=v3(tB, 16), in1=v3(tA, 16), op=ALU.subtract), ["tA", "tB"], ["MCm"])
                              P.op("dve", Rec("tensor_tensor", out=v3(tA, 16), in0=pc_r, in1=cc_i, op=ALU.mult), rsp + ["tA"], ["tA"])
                              P.op("dve", Rec("tensor_tensor", out=v3(tB, 16), in0=pc_i, in1=cc_r, op=ALU.mult), rsp + ["tB"], ["tB"])
                              P.op("dve", Rec("scalar_tensor_tensor", out=v3(MCm[1], 16), in0=v3(tA, 16), scalar=-1.0, in1=v3(tB, 16), op0=ALU.mult, op1=ALU.subtract), ["tA", "tB"], ["MCm"])
                              tb = psb(4)
                              for ri in range(2):
                                  P.op("pe", Rec("transpose", tb[:, ri * 128:(ri + 1) * 128], MBt[ri][:], identb[:]), ["MBt", "identb"], [RP[4]])
                              for gl in range(2):
                                  for ri in range(2):
                                      P.op("act", Rec("activation", out=MBp[gl * 2 + ri][:, gl * 64:gl * 64 + 64], in_=tb[:, ri * 128 + gl * 64:ri * 128 + gl * 64 + 64], func=AF.Copy),
                                           [RP[4]], ["MBp"])
                              for gl in range(2):
                                  sl_ = slice(gl * 64, gl * 64 + 64)
                                  P.op("pe", Rec("matmul", ps[2][:, gl * 128:(gl + 1) * 128], lhsT=MBt[0][sl_, :], rhs=MCm[0][sl_, 0:128], start=True, stop=False), ["MBt", "MCm"], [RP[2]])
                                  P.op("pe", Rec("matmul", ps[2][:, gl * 128:(gl + 1) * 128], lhsT=MBt[1][sl_, :], rhs=MCm[1][sl_, 0:128], start=False, stop=True), ["MBt", "MCm"], [RP[2]])
                                  P.op("dve", Rec("tensor_tensor", out=TgS[jj * 2 + gl][:], in0=ps[2][:, gl * 128:(gl + 1) * 128], in1=tmask[:], op=ALU.mult), [RP[2], "tmask"], ["TgS"])
                              for ri in range(2):
                                  for gl in range(2):
                                      g = jj * 2 + gl
                                      P.op("pe", Rec("matmul", ps[5][:, ri * 128:ri * 128 + NCH1], lhsT=MBp[gl * 2 + ri][:], rhs=Ug[g][:], start=(gl == 0), stop=(gl == 1)),
                                           ["MBp", "Ug%d" % g], [RP[5]])
                              xr = ps[5][:, 0:NCHK]; xi = ps[5][:, 128:128 + NCHK]
                              tcj = Tc[:, jj, 1:65]; tsj = Ts[:, jj, 1:65]
                              rdm = [RP[5], "tr2c", "tr2s"]
                              P.op("dve", Rec("tensor_tensor", out=Dr[:, jj, 1:65], in0=xr, in1=tcj, op=ALU.mult), rdm, ["Dm"])
                              P.op("dve", Rec("tensor_tensor", out=u1[:, jj, 1:65], in0=xi, in1=tsj, op=ALU.mult), rdm + ["tr2a1"], ["tr2a1"])
                              P.op("dve", Rec("tensor_tensor", out=Di[:, jj, 1:65], in0=xi, in1=tcj, op=ALU.mult), rdm, ["Dm"])
                              P.op("dve", Rec("tensor_tensor", out=u2[:, jj, 1:65], in0=xr, in1=tsj, op=ALU.mult), rdm + ["tr2a2"], ["tr2a2"])
                              P.op("act", Rec("activation", out=Xs[:, 0, jj:jj + 1], in_=ps[5][:, NCHK:NCHK + 1], func=AF.Copy), [RP[5]], ["Xs"])
                              P.op("act", Rec("activation", out=Xs[:, 1, jj:jj + 1], in_=ps[5][:, 128 + NCHK:128 + NCHK + 1], func=AF.Copy), [RP[5]], ["Xs"])
                              yield
                          yield
                          dm = ["Dm", "tr2a1", "tr2a2"]
                          P.op("dve", Rec("tensor_tensor", out=Dr[:, :, 1:65], in0=Dr[:, :, 1:65], in1=u1[:, :, 1:65], op=ALU.add), dm, ["Dm"])
                          P.op("dve", Rec("tensor_tensor", out=Di[:, :, 1:65], in0=Di[:, :, 1:65], in1=u2[:, :, 1:65], op=ALU.subtract), dm, ["Dm"])
                          P.op("dve", Rec("tensor_copy", out=Dr[:, :, 0], in_=Hr[:, l, j0:j0 + NPB]), ["H", "Dm"], ["Dm"])
                          P.op("dve", Rec("tensor_copy", out=Di[:, :, 0], in_=Hi[:, l, j0:j0 + NPB]), ["H", "Dm"], ["Dm"])
                          fl = lambda a: a[:].rearrange("p a b -> p (a b)")
                          P.op("dve", Rec("tensor_tensor_scan", out=fl(Dr), data0=fl(d0), data1=fl(Dr), initial=0.0, op0=ALU.mult, op1=ALU.add), ["Dm", "d0"], ["Dm"])
                          P.op("dve", Rec("tensor_tensor_scan", out=fl(Di), data0=fl(d0), data1=fl(Di), initial=0.0, op0=ALU.mult, op1=ALU.add), ["Dm", "d0"], ["Dm"])
                          md = ["Dm", "tr2c", "tr2s", "tr2a1", "tr2a2", "trb"]
                          P.op("dve", Rec("tensor_tensor", out=u1[:], in0=Dr[:], in1=Tc[:], op=ALU.mult), md, ["tr2a1"])
                          P.op("dve", Rec("tensor_tensor", out=u2[:], in0=Di[:], in1=Ts[:], op=ALU.mult), md, ["tr2a2"])
                          P.op("dve", Rec("tensor_tensor", out=u1[:], in0=u1[:], in1=u2[:], op=ALU.subtract), md, ["tr2a1"])
                          P.op("dve", Rec("tensor_tensor", out=u2[:], in0=Dr[:], in1=Ts[:], op=ALU.mult), md, ["tr2a2"])
                          P.op("dve", Rec("tensor_tensor", out=ang[:], in0=Di[:], in1=Tc[:], op=ALU.mult), md, ["trb"])
                          P.op("dve", Rec("tensor_tensor", out=u2[:], in0=u2[:], in1=ang[:], op=ALU.add), md, ["tr2a2"])
                          P.op("act", Rec("activation", out=Sbr[:, :, 0:64], in_=u1[:, :, 0:64], func=AF.Copy), ["tr2a1"], ["Sb"])
                          P.op("act", Rec("activation", out=Sbi[:, :, 0:64], in_=u2[:, :, 0:64], func=AF.Copy), ["tr2a2"], ["Sb"])
                          P.op("act", Rec("activation", out=Sbr[:, :, 64], in_=h0r[:, j0:j0 + NPB], func=AF.Copy), ["h0"], ["Sb"])
                          P.op("act", Rec("activation", out=Sbi[:, :, 64], in_=h0i[:, j0:j0 + NPB], func=AF.Copy), ["h0"], ["Sb"])
                          P.op("dve", Rec("tensor_copy", out=Hr[:, l, j0:j0 + NPB], in_=u1[:, :, 64]), ["tr2a1"], ["H"])
                          P.op("dve", Rec("tensor_copy", out=Hi[:, l, j0:j0 + NPB], in_=u2[:, :, 64]), ["tr2a2"], ["H"])
                          cmul("dve", tA[:, 0:NPB], tA[:, NPB:2 * NPB], PCr[:, j0:j0 + NPB, 3], PCi[:, j0:j0 + NPB, 3], Xs[:, 0, :], Xs[:, 1, :], tB[:, 0:NPB], tB[:, NPB:2 * NPB],
                               ["PC", "Xs", "tA", "tB"], ["tA", "tB"])
                          P.op("dve", Rec("tensor_tensor", out=hend[:, 0, j0:j0 + NPB], in0=hend[:, 0, j0:j0 + NPB], in1=tA[:, 0:NPB], op=ALU.add), ["tA", "hend"], ["hend"])
                          P.op("dve", Rec("tensor_tensor", out=hend[:, 1, j0:j0 + NPB], in0=hend[:, 1, j0:j0 + NPB], in1=tA[:, NPB:2 * NPB], op=ALU.add), ["tA", "hend"], ["hend"])
                          yield
                          for gq in range(4):
                              for gg in range(4):
                                  g = gq * 4 + gg
                                  jj = g // 2; gl = g % 2
                                  sl_ = slice(gl * 64, gl * 64 + 64)
                                  oc = slice(gg * 128, (gg + 1) * 128)
                                  P.op("pe", Rec("matmul", ps[2][0:NCH1, oc], lhsT=Ug[g][:], rhs=TgS[g][:], start=True, stop=False), ["Ug%d" % g, "TgS"], [RP[2]])
                                  P.op("pe", Rec("matmul", ps[2][0:NCH1, oc], lhsT=Sbr[sl_, jj, :], rhs=MCB[jj][0][sl_, 128:256], start=False, stop=False), ["Sb", "MCm"], [RP[2]])
                                  P.op("pe", Rec("matmul", ps[2][0:NCH1, oc], lhsT=Sbi[sl_, jj, :], rhs=MCB[jj][1][sl_, 128:256], start=False, stop=True), ["Sb", "MCm"], [RP[2]])
                              yield
                              ch0 = sl * 256 + gq * 64
                              P.op("dve", Rec("tensor_tensor", out=vd[0:NCH1], in0=Vu[0:NCH1, gq * 4:(gq + 1) * 4, :, :], in1=sap(Drow[0:NCH1, ch0:ch0 + 1], [[16, 4], [0, 8], [1, 16]]), op=ALU.mult),
                                   ["Vu", "Drow"], ["vd"])
                              P.op("dve", Rec("tensor_tensor", out=ypre[0:NCH1], in0=ps[2][0:NCH1, :].rearrange("p (g i c) -> p g i c", g=4, i=8),
                                                                    in1=vd[0:NCH1], op=ALU.add), [RP[2], "vd"], ["ypre"])
                              half = gq % 2
                              P.op("act", Rec("activation", out=zcm[0:NCH1, :, half * 64:(half + 1) * 64].rearrange("p i (g c) -> p g i c", g=4), in_=ypre[0:NCH1], func=AF.Gelu), ["ypre"], ["zcm"])
                              if half == 1:
                                  mz = sl * 2 + gq // 2
                                  zb = psb(5)
                                  for i in range(8):
                                      P.op("pe", Rec("transpose", zb[:, i * 128:i * 128 + NCH1], zcm[0:NCH1, i, :], identb[0:NCH1, 0:NCH1]), ["zcm", "identb"], [RP[5]])
                                  zv = zb[:, 0:1024].rearrange("p (i n) -> p i n", i=8)
                                  P.op("dve", Rec("tensor_copy", out=sap(Zf[:, mz, 0:1], [[1, 4], [8, NCH1]]), in_=zv[:, 0:4, 0:NCH1]), [RP[5]], ["fb%d" % mz])
                                  P.op("dve", Rec("tensor_copy", out=sap(Zf[:, mz, 4:5], [[1, 4], [8, NCHK]]), in_=zv[:, 4:8, 0:NCHK]), [RP[5]], ["fb%d" % mz])
                  gens = [gen_attn(), gen_ssm()]
                  while gens:
                      for g_ in list(gens):
                          try:
                              next(g_)
                          except StopIteration:
                              gens.remove(g_)
                  ckpt(7)
                  for ri, (dst_p, dst_s, Hx) in enumerate([(ohrp, ohrs, Hr), (ohip, ohis, Hi)]):
                      P.dma("sp", Rec("dma_start", out=dap(dst_p, l * 4096, [[1, 128], [128, 32]]), in_=Hx[:, l, :]), ["H"], ["ohp%d" % ri])
                      if HS[0]:
                          P.dma("sp", Rec("dma_start", out=dap(dst_s, (l * 4 + sidx) * 4096, [[1, 128], [128, 32]]), in_=hend[:, ri, :]), ["hend"], ["ohs%d" % ri])
                  for j in range(4):
                      slab, rslab = w_next("w_glu", l, j * 256)
                      for mm in range(2):
                          m = j * 2 + mm
                          def ev_g(pm, psm, rpm, rps, m=m):
                              P.op("act", Rec("activation", out=Pf[0][:, 0:256], in_=pm[:, 0:256], func=AF.Sigmoid), rpm, ["Pf0"])
                              P.op("act", Rec("activation", out=Pf[1][:, 0:256], in_=pm[:, 256:512], func=AF.Sigmoid), rpm, ["Pf1"])
                              P.op("dve", Rec("tensor_tensor", out=Hb[:, 8 + m, 0:256], in0=Pf[0][:, 0:256], in1=Zf[:, m, 0:256], op=ALU.mult), ["Pf0", "fb%d" % m], ["Hb%d" % (8 + m)])
                              P.op("dve", Rec("tensor_tensor", out=Hb[:, 8 + m, 256:512], in0=Pf[1][:, 0:256], in1=Zf[:, m, 256:512], op=ALU.mult), ["Pf1", "fb%d" % m], ["Hb%d" % (8 + m)])
                              if psm is not None:
                                  P.op("act", Rec("activation", out=sm[:, 0:NS], in_=psm, func=AF.Sigmoid), rps, ["sm0", "sm1"])
                              P.op("dve", Rec("tensor_tensor", out=Hb[:, 8 + m, NP:NT], in0=sm[:, 0:NS], in1=Zf[:, m, NP:NT], op=ALU.mult), ["sm0", "sm1", "fb%d" % m], ["Hb%d" % (8 + m)])
                          proj_chunk(slab, rslab, 8, mm * 128, lambda k: Zf[:, k, :], lambda k: ["fb%d" % k], ev_g)
                  ckpt(8)
                  rmsnorm([Ao[:, k, :] for k in range(8)], ["Ao%d" % k for k in range(8)], lambda k, l=l: gains[:, l, 32 + k:33 + k],
                          [Ao[:, k, :] for k in range(8)], ["Ao%d" % k for k in range(8)], 1024)
                  rmsnorm([Hb[:, 8 + k, :] for k in range(8)], ["Hb%d" % (8 + k) for k in range(8)], lambda k, l=l: gains[:, l, 40 + k:41 + k],
                          [Hb[:, 8 + k, :] for k in range(8)], ["Hb%d" % (8 + k) for k in range(8)], 1024)
                  mix_rhs = lambda k: (Ao[:, k, :] if k < 8 else Hb[:, k, :])
                  mix_regs = lambda k: ["Ao%d" % k] if k < 8 else ["Hb%d" % k]
                  for j in range(8):
                      slab, rslab = w_next("w_out", l, j * 256)
                      for mm in range(2):
                          m = j * 2 + mm
                          def ev_o(pm, psm, rpm, rps, m=m):
                              P.op("dve", Rec("tensor_tensor", out=X[:, m, 0:NP], in0=pm, in1=X[:, m, 0:NP], op=ALU.add), rpm + ["X%d" % m], ["X%d" % m])
                              if psm is not None:
                                  P.op("dve", Rec("tensor_tensor", out=X[:, m, NP:NT], in0=psm, in1=X[:, m, NP:NT], op=ALU.add), rps + ["X%d" % m], ["X%d" % m])
                          proj_chunk(slab, rslab, 16, mm * 128, mix_rhs, mix_regs, ev_o)
                  ckpt(9)
                  rmsnorm([X[:, k, :] for k in range(16)], ["X%d" % k for k in range(16)], lambda k, l=l: gains[:, l, 16 + k:17 + k],
                          [Hb[:, k, :] for k in range(16)], ["Hb%d" % k for k in range(16)], D)
                  for fp in range(0 if int(_os.environ.get("SKIPFFN", "0")) else 4):
                      c0 = fp * 1536
                      nsl = 6 if fp < 3 else 4
                      for j in range(nsl):
                          slg, rslg = w_next("w_gate", l, c0 + j * 256)
                          slu, rslu = w_next("w_up", l, c0 + j * 256)
                          for mm in range(2):
                              ma = j * 2 + mm
                              def ev_gate(pm, psm, rpm, rps, ma=ma):
                                  P.op("act", Rec("activation", out=Pf[0][:, 0:256], in_=pm[:, 0:256], func=AF.Silu), rpm, ["Pf0"])
                                  P.op("act", Rec("activation", out=Pf[1][:, 0:256], in_=pm[:, 256:512], func=AF.Silu), rpm, ["Pf1"])
                                  if psm is not None:
                                      P.op("act", Rec("activation", out=sm[:, 0:NS], in_=psm, func=AF.Silu), rps, ["sm0", "sm1"])
                              def ev_up(pm, psm, rpm, rps, ma=ma):
                                  P.op("dve", Rec("tensor_tensor", out=ACT_[:, ma, 0:256], in0=pm[:, 0:256], in1=Pf[0][:, 0:256], op=ALU.mult), rpm + ["Pf0"], ["fb%d" % ma])
                                  P.op("dve", Rec("tensor_tensor", out=ACT_[:, ma, 256:512], in0=pm[:, 256:512], in1=Pf[1][:, 0:256], op=ALU.mult), rpm + ["Pf1"], ["fb%d" % ma])
                                  if psm is not None:
                                      P.op("dve", Rec("tensor_tensor", out=ACT_[:, ma, NP:NT], in0=psm, in1=sm[:, 0:NS], op=ALU.mult), rps + ["sm0", "sm1"], ["fb%d" % ma])
                              proj_chunk(slg, rslg, 16, mm * 128, hb_rhs, hb_regs, ev_gate)
                              proj_chunk(slu, rslu, 16, mm * 128, hb_rhs, hb_regs, ev_up)
                      nk = nsl * 2
                      for j in range(8):
                          slab, rslab = w_next("w_down", l, j * 256)
                          for mm in range(2):
                              m = j * 2 + mm
                              def ev_d(pm, psm, rpm, rps, m=m):
                                  P.op("dve", Rec("tensor_tensor", out=X[:, m, 0:NP], in0=pm, in1=X[:, m, 0:NP], op=ALU.add), rpm + ["X%d" % m], ["X%d" % m])
                                  if psm is not None:
                                      P.op("dve", Rec("tensor_tensor", out=X[:, m, NP:NT], in0=psm, in1=X[:, m, NP:NT], op=ALU.add), rps + ["X%d" % m], ["X%d" % m])
                              proj_chunk(slab, rslab, nk, mm * 128, lambda k: ACT_[:, k, :], lambda k: ["fb%d" % k], ev_d)

              ckpt(10)
              for k in range(16):
                  q = sqs[k % 2]
                  P.op("act", Rec("activation", out=q[:], in_=X[:, k, :], func=AF.Square), ["X%d" % k], ["sq%d" % (k % 2)])
                  P.op("pe", Rec("matmul", ps[3][:, 0:NP], lhsT=onesb[:], rhs=q[:, 0:NP], start=(k == 0), stop=(k == 15)), ["sq%d" % (k % 2), "onesb"], [RP[3]])
                  if HS[0]:
                      P.op("pe", Rec("matmul", ps[2][:, 480:480 + NS], lhsT=onesb[:], rhs=q[:, NP:NT], start=(k == 0), stop=(k == 15)), ["sq%d" % (k % 2), "onesb"], [RP[2]])
              P.op("act", Rec("activation", out=rstd[:, 0:NP], in_=ps[3][:, 0:NP], func=AF.Sqrt, scale=1.0 / D, bias=epsb[:, 0:1]), [RP[3], "epsb"], ["rstd"])
              if HS[0]:
                  P.op("act", Rec("activation", out=rstd[:, NP:NT], in_=ps[2][:, 480:480 + NS], func=AF.Sqrt, scale=1.0 / D, bias=epsb[:, 0:1]), [RP[2], "epsb"], ["rstd"])
              P.op("dve", Rec("reciprocal", out=rstd[:], in_=rstd[:]), ["rstd"], ["rstd"])
              for k in range(16):
                  P.op("dve", Rec("scalar_tensor_tensor", out=X[:, k, :], in0=X[:, k, :], scalar=gfin[:, k:k + 1], in1=rstd[:], op0=ALU.mult, op1=ALU.mult),
                       ["X%d" % k, "rstd", "gfin"], ["X%d" % k])
              for blk in range(NB + (1 if HS[0] else 0)):
                  if blk < NB:
                      cols = slice(blk * 128, (blk + 1) * 128); nr = 128
                  else:
                      cols = slice(NP, NT); nr = NS
                  for kq in range(4):
                      for kk in range(4):
                          k = kq * 4 + kk
                          P.op("pe", Rec("transpose", ps[4][0:nr, kk * 128:(kk + 1) * 128], X[:, k, cols], identf), ["X%d" % k, "cs"], [RP[4]])
                      P.op("dve", Rec("tensor_copy", out=stage[0:nr, kq * 512:(kq + 1) * 512], in_=ps[4][0:nr, :]), [RP[4]], STG)
                  if blk < NB:
                      r0 = pi * NP + blk * 128
                      P.dma("sp", Rec("dma_start", out=dap(yp, r0 * D, [[D, 128], [1, D]]), in_=stage[:]), STG, ["yp"])
                  else:
                      P.dma("sp", Rec("dma_start", out=dap(ys, sidx * 4 * D, [[D, 4], [1, D]]), in_=stage[0:4, :]), STG, ["ys"])

        except _Stop:
            pass
        if dbg:
            P.dma("sp", Rec("dma_start", out=dap(dbg_t, 0, [[16 * NT, 128], [1, 8 * NT]]), in_=Ao[:].rearrange("p a b -> p (a b)")), ["Ao%d" % k for k in range(8)], ["dbg"])
            P.dma("sp", Rec("dma_start", out=dap(dbg_t, 8 * NT, [[16 * NT, 128], [1, 8 * NT]]), in_=Hb[:, 8:16, :].rearrange("p a b -> p (a b)")), ["Hb%d" % k for k in range(8, 16)], ["dbg"])
        P.op("sp", Rec("nop", ), ["yp", "ys", "okp", "ovp", "oks", "ovs", "ohp0", "ohp1", "ohs0", "ohs1"] + (["dbg"] if dbg else []), [])
        P.emit()
    return nc


def make_consts():
    c = np.zeros((128, 1100), np.float32)
    c[:, 0:128] = np.eye(128, dtype=np.float32)
    i = np.arange(128)[:, None]
    j = np.arange(128)[None, :]
    c[:, 128:256] = np.where(j > i, 0.0, NEG)
    c[:, 256:384] = np.where(j <= i, 0.0, NEG)
    c[:, 384:512] = NEG
    c[:, 512:640] = c[:, 256:384]
    js = np.arange(132)[None, :]
    c[:, 640:772] = np.where((js > i) & (js <= i + 128), 0.0, NEG)
    r = np.arange(128)[:, None] // 16
    cc = np.arange(128)[None, :] // 16
    c[:, 772:900] = (cc >= r).astype(np.float32)
    c[:, 900:965] = np.arange(65, dtype=np.float32)[None, :]
    c[:, 965:1030] = 1.0
    c[:, 965] = 0.0
    return c


_NC_CACHE = {}


def kernel(**inputs):
    inp = {k: np.ascontiguousarray(np.asarray(v)) for k, v in inputs.items()}
    npass = SEQ // NP
    key = (npass, DEPTH)
    if key not in _NC_CACHE:
        _NC_CACHE[key] = build(npass, DEPTH)
    nc = _NC_CACHE[key]
    cst = make_consts()
    wnames = ["norm_mix", "w_in", "attn_sink", "ssm_a_re", "ssm_a_im", "ssm_log_dt", "ssm_b_re", "ssm_b_im",
              "ssm_c_re", "ssm_c_im", "ssm_d", "w_glu", "norm_attn_out", "norm_ssm_out", "w_out", "norm_ffn",
              "w_gate", "w_up", "w_down", "norm_final"]
    in_maps = []
    for c in range(8):
        m = {n: inp[n] for n in wnames}
        m["cst"] = cst
        m["xp"] = inp["x_prompt"][c % 2]
        m["xs"] = inp["x_sample"][4 * c:4 * c + 4].reshape(16, D)
        m["ck"] = inp["cache_k"][:, 4 * c:4 * c + 4].reshape(DEPTH, 4, 128, 128)
        m["cv"] = inp["cache_v"][:, 4 * c:4 * c + 4].reshape(DEPTH, 4, 128, 128)
        m["hr0"] = inp["state_ssm_re"][:, 4 * c:4 * c + 4]
        m["hi0"] = inp["state_ssm_im"][:, 4 * c:4 * c + 4]
        in_maps.append({k: np.ascontiguousarray(v, dtype=np.float32) for k, v in m.items()})
    res = run_bass_kernel_spmd(nc, in_maps, core_ids=list(range(8))).results
    f = np.float32
    y_prompt = np.stack([res[0]["yp"], res[1]["yp"]]).astype(f)
    y_sample = np.concatenate([res[c]["ys"].reshape(4, 4, D) for c in range(8)], 0).astype(f)
    k_p = np.stack([res[0]["okp"], res[1]["okp"]], 1).reshape(DEPTH, 2, 128, 2, 64).astype(f)
    v_p = np.stack([res[0]["ovp"], res[1]["ovp"]], 1).reshape(DEPTH, 2, 128, 2, 64).astype(f)
    hr_p = np.stack([res[0]["ohrp"], res[1]["ohrp"]], 1).astype(f)
    hi_p = np.stack([res[0]["ohip"], res[1]["ohip"]], 1).astype(f)
    k_s = np.concatenate([res[c]["oks"] for c in range(8)], 1).reshape(DEPTH, 32, 128, 2, 64).astype(f)
    v_s = np.concatenate([res[c]["ovs"] for c in range(8)], 1).reshape(DEPTH, 32, 128, 2, 64).astype(f)
    hr_s = np.concatenate([res[c]["ohrs"] for c in range(8)], 1).astype(f)
    hi_s = np.concatenate([res[c]["ohis"] for c in range(8)], 1).astype(f)
    return (y_prompt, y_sample, k_p, v_p, hr_p, hi_p, k_s, v_s, hr_s, hi_s)
```

```python
import contextlib
import math
import numpy as np
import concourse.bass as bass
import concourse.mybir as mybir
from concourse.bass import AP
from concourse.bass_utils import run_bass_kernel_spmd

F32 = mybir.dt.float32
BF16 = mybir.dt.bfloat16
I32 = mybir.dt.int32
AF = mybir.ActivationFunctionType
ALU = mybir.AluOpType
AX = mybir.AxisListType

D = 2048
DEPTH = 4
SEQ = 4096
DIN = 2304
DFF = 5632
NP = 512
NS = 4
NT = NP + NS
NB = NP // 128
NCHK = NP // 8
NCH1 = NCHK + 1
EPS = 1e-5
NEG = -30000.0
TWO_PI = 2.0 * math.pi


class Reg:
    __slots__ = ("name", "w", "rs")

    def __init__(self, name):
        self.name = name
        self.w = None
        self.rs = []


class Op:
    __slots__ = ("eng", "fn", "deps", "marked", "count", "is_dma", "sem", "semval")

    def __init__(self, eng, fn, is_dma=False):
        self.eng = eng
        self.fn = fn
        self.deps = []
        self.marked = False
        self.count = 0
        self.is_dma = is_dma
        self.sem = None
        self.semval = 0


ENGS = ("pe", "act", "dve", "pool", "sp")
N_DMA_SEMS = 32


class Prog:
    def __init__(self, nc):
        self.nc = nc
        self.ops = {e: [] for e in ENGS}
        self.dma_last = [None] * N_DMA_SEMS
        self.dma_cum = [0] * N_DMA_SEMS
        self.dma_rr = 0
        self.regs = {}

    def R(self, name):
        r = self.regs.get(name)
        if r is None:
            r = self.regs[name] = Reg(name)
        return r

    def _deps(self, op, reads, writes):
        deps = []
        for r in reads:
            if r.w is not None:
                deps.append(r.w)
        for w in writes:
            if w.w is not None:
                deps.append(w.w)
            deps.extend(w.rs)
        for r in reads:
            r.rs.append(op)
        for w in writes:
            w.w = op
            w.rs = []
        seen = set()
        out = []
        for d in deps:
            if id(d) in seen or d is op:
                continue
            seen.add(id(d))
            if op.eng == "pe" and d.eng == "pe" and not d.is_dma and not op.is_dma:
                continue
            out.append(d)
            d.marked = True
        op.deps = out

    def op(self, eng, fn, reads=(), writes=()):
        o = Op(eng, fn)
        writes = list(writes) + [x for x in reads if isinstance(x, str) and x.startswith("ps")]
        reads = [x for x in reads if not (isinstance(x, str) and x.startswith("ps"))]
        self._deps(o, [self.R(x) if isinstance(x, str) else x for x in reads],
                   [self.R(x) if isinstance(x, str) else x for x in writes])
        self.ops[eng].append(o)
        return o

    def dma(self, queue, fn, reads=(), writes=()):
        o = Op(queue, fn, is_dma=True)
        s = self.dma_rr
        self.dma_rr = (self.dma_rr + 1) % N_DMA_SEMS
        o.sem = s
        self.dma_cum[s] += 16
        o.semval = self.dma_cum[s]
        self._deps(o, [self.R(x) if isinstance(x, str) else x for x in reads],
                   [self.R(x) if isinstance(x, str) else x for x in writes])
        if self.dma_last[s] is not None:
            o.deps.append(self.dma_last[s])
        self.dma_last[s] = o
        self.ops[queue].append(o)
        return o

    def emit(self):
        nc = self.nc
        for e in ENGS:
            c = 0
            for o in self.ops[e]:
                if o.is_dma:
                    continue
                if o.marked:
                    c += 1
                o.count = c
        with contextlib.ExitStack() as st:
            esem = {e: st.enter_context(nc.semaphore("es_" + e)) for e in ENGS}
            dsem = [st.enter_context(nc.semaphore("ds_%d" % i)) for i in range(N_DMA_SEMS)]
            block = st.enter_context(nc.Block())
            handles = {"pe": block.tensor, "act": block.scalar, "dve": block.vector,
                       "pool": block.gpsimd, "sp": block.sync}
            for e in ENGS:
                ops = self.ops[e]

                def body(eh, ops=ops, e=e):
                    waited = {}
                    for o in ops:
                        for d in o.deps:
                            if d.is_dma:
                                key, sem, val = ("d", d.sem), dsem[d.sem], d.semval
                            else:
                                key, sem, val = ("e", d.eng), esem[d.eng], d.count
                            if waited.get(key, 0) >= val:
                                continue
                            waited[key] = val
                            eh.wait_ge(sem, val)
                        inst = o.fn(eh)
                        if o.is_dma:
                            inst.then_inc(dsem[o.sem], 16)
                        elif o.marked:
                            inst.then_inc(esem[e], 1)

                handles[e](body)


def Rec(name, *a, **k):
    return lambda e: getattr(e, name)(*a, **k)


def sap(base, dims):
    return AP(tensor=base.tensor, offset=base.offset, ap=[list(base.ap[0])] + [[int(a), int(b)] for a, b in dims])


def dap(t, offset, dims):
    return AP(tensor=t, offset=int(offset), ap=[[int(a), int(b)] for a, b in dims])


class _Stop(Exception):
    pass


def build(npass=8, depth=DEPTH, dbg=None, stop=None):
    nc = bass.Bass("TRN2", target_bir_lowering=False)
    P = Prog(nc)
    dt_in = lambda n, s: nc.dram_tensor(n, list(s), F32, kind="ExternalInput")
    dt_out = lambda n, s: nc.dram_tensor(n, list(s), F32, kind="ExternalOutput")
    xp = dt_in("xp", [SEQ, D])
    xs = dt_in("xs", [16, D])
    ck = dt_in("ck", [DEPTH, 4, 128, 128])
    cv = dt_in("cv", [DEPTH, 4, 128, 128])
    hr0 = dt_in("hr0", [DEPTH, 4, 64, 64])
    hi0 = dt_in("hi0", [DEPTH, 4, 64, 64])
    W = {}
    for n, s in [("norm_mix", [DEPTH, D]), ("w_in", [DEPTH, D, DIN]), ("attn_sink", [DEPTH, 16]),
                 ("ssm_a_re", [DEPTH, 64, 64]), ("ssm_a_im", [DEPTH, 64, 64]), ("ssm_log_dt", [DEPTH, 64]),
                 ("ssm_b_re", [DEPTH, 64, 64, 16]), ("ssm_b_im", [DEPTH, 64, 64, 16]),
                 ("ssm_c_re", [DEPTH, 64, 16, 64]), ("ssm_c_im", [DEPTH, 64, 16, 64]),
                 ("ssm_d", [DEPTH, 1024]), ("w_glu", [DEPTH, 1024, 1024]),
                 ("norm_attn_out", [DEPTH, 1024]), ("norm_ssm_out", [DEPTH, 1024]),
                 ("w_out", [DEPTH, D, D]), ("norm_ffn", [DEPTH, D]),
                 ("w_gate", [DEPTH, D, DFF]), ("w_up", [DEPTH, D, DFF]), ("w_down", [DEPTH, DFF, D]),
                 ("norm_final", [D])]:
        W[n] = dt_in(n, s)
    cst = dt_in("cst", [128, 1100])
    yp = dt_out("yp", [SEQ, D])
    ys = dt_out("ys", [16, D])
    okp = dt_out("okp", [DEPTH, 128, 128])
    ovp = dt_out("ovp", [DEPTH, 128, 128])
    ohrp = dt_out("ohrp", [DEPTH, 64, 64])
    ohip = dt_out("ohip", [DEPTH, 64, 64])
    oks = dt_out("oks", [DEPTH, 4, 128, 128])
    ovs = dt_out("ovs", [DEPTH, 4, 128, 128])
    ohrs = dt_out("ohrs", [DEPTH, 4, 64, 64])
    ohis = dt_out("ohis", [DEPTH, 4, 64, 64])
    scrT = nc.dram_tensor("scrT", [DEPTH, 128, 1088], F32, kind="Internal")
    scrM = nc.dram_tensor("scrM", [DEPTH * 32, 128, 512], BF16, kind="Internal")
    scrC = nc.dram_tensor("scrC", [DEPTH * 4, 128, 4096], BF16, kind="Internal")
    scrG = nc.dram_tensor("scrG", [DEPTH * 4, 128, 2048], BF16, kind="Internal")
    dbg_t = nc.dram_tensor("dbg", [128, 16 * NT], BF16, kind="ExternalOutput") if dbg else None

    st = contextlib.ExitStack()
    with st:
        st.enter_context(nc.allow_non_contiguous_dma(reason="small strided parameter loads"))
        sb = lambda n, s, d=F32: st.enter_context(nc.sbuf_tensor(n, list(s), d))
        X = sb("X", [128, 16, NT])
        Hb = sb("Hb", [128, 16, NT], BF16)
        Ao = sb("Ao", [128, 8, NT], BF16)
        FB = sb("FB", [128, 6 * NT])
        ACT_ = FB[:].bitcast(BF16).rearrange("p (a b) -> p a b", a=12)
        Zf = ACT_
        stage = FB[:, 0:D]
        rstd = sb("rstd", [128, NT])
        sqs = [sb("sq%d" % i, [128, NT], BF16) for i in range(2)]
        NSLOT = 4
        ring = [sb("ring%d" % i, [128, 16, 256], BF16) for i in range(NSLOT)]
        cs = sb("cs", [128, 1100])
        identb = sb("identb", [128, 128], BF16)
        onesb = sb("onesb", [128, 128], BF16)
        maskb = sb("maskb", [128, 256], BF16)
        maskfb = sb("maskfb", [128, 256], BF16)
        masksb = sb("masksb", [128, 132], BF16)
        gains = sb("gains", [128, DEPTH, 64])
        gfin = sb("gfin", [128, 16])
        sinkb = sb("sinkb", [128, DEPTH * 16])
        Kb = [sb("Kb%d" % i, [128, 128 + NP], BF16) for i in range(2)]
        KS = [sb("KS%d" % i, [128, 132], BF16) for i in range(2)]
        Vb = sb("Vb", [128, NB + 1, 4, 128], BF16)
        VSc = sb("VSc", [128, 4, 128], BF16)
        VSn = sb("VSn", [4, 4, 128], BF16)
        Khalo = sb("Khalo", [128, DEPTH, 2, 128], BF16)
        Vhalo = sb("Vhalo", [128, DEPTH, 4, 128], BF16)
        kvst = sb("kvst", [128, 256])
        kvss = kvst[0:4, :]
        cpy = kvst
        cks = sb("cks", [128, 128], BF16)
        Pf = [sb("Pf%d" % i, [128, 256]) for i in range(2)]
        Pn = [sb("Pn%d" % i, [128, 256], BF16) for i in range(2)]
        PT = [sb("PT%d" % i, [128, 2, 128], BF16) for i in range(2)]
        sm = sb("sm", [128, 16])
        S1 = lambda n: sb(n, [128, 32])
        lam_r, lam_i, dtt, mag, th, abr, abi, t1, t2, t3, t4, crr, cii, rho, phi, iar, iai = [
            S1(n) for n in "lam_r lam_i dtt mag th abr abi t1 t2 t3 t4 crr cii rho phi iar iai".split()]
        ti = sb("ti", [128, 32], I32)
        Br = sb("Br", [128, 32, 16]); Bi = sb("Bi", [128, 32, 16])
        Bbr = Br; Bbi = Bi
        Cn = sb("Cn", [16, 16, 64])
        Cr = sb("Cr", [128, 32, 16]); Ci = sb("Ci", [128, 32, 16])
        PCr = sb("PCr", [128, 32, 16]); PCi = sb("PCi", [128, 32, 16])
        PBr = sb("PBr", [128, 32, 8]); PBi = sb("PBi", [128, 32, 8])
        Drow = sb("Drow", [128, 1024])
        Hr = sb("Hr", [128, DEPTH, 32]); Hi = sb("Hi", [128, DEPTH, 32])
        h0r = sb("h0r", [128, 32]); h0i = sb("h0i", [128, 32])
        Vu = sb("Vu", [128, 16, 8, 16], BF16)
        Ug = [sb("Ug%d" % i, [128, NCH1], BF16) for i in range(16)]
        MBt = [sb("MBt%d" % i, [128, 128], BF16) for i in range(2)]
        MBpT = sb("MBpT", [128, 2, 4, 128], BF16)
        TgSt = sb("TgSt", [128, 16, 128], BF16)
        MCBt = sb("MCBt", [128, 8, 2, 256], BF16)
        TgS = [TgSt[:, i, :] for i in range(16)]
        MCB = [[MCBt[:, i, r, :] for r in range(2)] for i in range(8)]
        tA = sb("tA", [128, 256]); tB = sb("tB", [128, 256])
        tmask = sb("tmask", [128, 128])
        NPB = 8
        Tc = sb("Tc", [128, NPB, 65]); Ts = sb("Ts", [128, NPB, 65])
        Dr = sb("Dr", [128, NPB, 65]); Di = sb("Di", [128, NPB, 65])
        d0 = sb("d0", [128, NPB, 65])
        ang = sb("ang", [128, NPB, 65])
        u1 = sb("u1", [128, NPB, 65]); u2 = sb("u2", [128, NPB, 65])
        Sbr = sb("Sbr", [128, NPB, 65], BF16); Sbi = sb("Sbi", [128, NPB, 65], BF16)
        Xs = sb("Xs", [128, 2, NPB])
        hend = sb("hend", [128, 2, 32])
        vd = sb("vd", [128, 4, 8, 16])
        ypre = sb("ypre", [128, 4, 8, 16])
        zcm = sb("zcm", [128, 8, 128], BF16)
        ps = [st.enter_context(nc.psum_tensor("ps%d" % i, [128, 512], F32)) for i in range(8)]
        RP = ["ps%d" % i for i in range(8)]
        STG = ["fb%d" % i for i in range(8)]

        def psb(b):
            return ps[b][:].bitcast(BF16)

        P.dma("sp", Rec("dma_start", out=cs[:], in_=cst.ap()), [], ["cs"])
        P.op("dve", Rec("tensor_copy", out=identb[:], in_=cs[:, 0:128]), ["cs"], ["identb"])
        P.op("dve", Rec("tensor_copy", out=maskb[:], in_=cs[:, 128:384]), ["cs"], ["maskb"])
        P.op("dve", Rec("tensor_copy", out=maskfb[:], in_=cs[:, 384:640]), ["cs"], ["maskfb"])
        P.op("dve", Rec("tensor_copy", out=masksb[:], in_=cs[:, 640:772]), ["cs"], ["masksb"])
        P.op("dve", Rec("tensor_copy", out=tmask[:], in_=cs[:, 772:900]), ["cs"], ["tmask"])
        P.op("pool", Rec("memset", onesb[:], 1.0), [], ["onesb"])
        identf = cs[:, 0:128]
        posf = cs[:, 900:965]
        for l in range(depth):
            for nm, off, nk in [("norm_mix", 0, 16), ("norm_ffn", 16, 16), ("norm_attn_out", 32, 8), ("norm_ssm_out", 40, 8)]:
                src = dap(W[nm], l * nk * 128, [[1, 128], [128, nk]])
                P.dma("sp", Rec("dma_start", out=gains[:, l, off:off + nk], in_=src), [], ["gains"])
        P.dma("sp", Rec("dma_start", out=gfin[:], in_=dap(W["norm_final"], 0, [[1, 128], [128, 16]])), [], ["gfin"])
        P.dma("sp", Rec("dma_start", out=sinkb[:], in_=dap(W["attn_sink"], 0, [[0, 128], [1, DEPTH * 16]])), [], ["sinkb"])
        P.op("pool", Rec("memset", Hr[:], 0.0), [], ["H"])
        P.op("pool", Rec("memset", Hi[:], 0.0), [], ["H"])
        P.op("pool", Rec("memset", Khalo[:], 0.0), [], ["Khalo"])
        P.op("pool", Rec("memset", Vhalo[:], 0.0), [], ["Vhalo"])
        P.op("pool", Rec("memset", Vb[:], 0.0), [], ["Vb"])
        P.op("pool", Rec("memset", VSc[:], 0.0), [], ["VSc"])
        P.op("pool", Rec("memset", VSn[:], 0.0), [], ["VSn"])
        P.op("pool", Rec("memset", Vu[:], 0.0), [], ["Vu"])
        P.op("pool", Rec("memset", MBpT[:], 0.0), [], ["MBp0", "MBp1"])

        sched = []
        for ps_i in range(npass):
            for l in range(depth):
                sched.append(("w_in", l, 0, 16, 1024, 256))
                for j in range(4):
                    sched.append(("w_in", l, 0, 16, j * 256, 256))
                for j in range(4):
                    sched.append(("w_in", l, 0, 16, 1280 + j * 256, 256))
                for j in range(4):
                    sched.append(("w_glu", l, 0, 8, j * 256, 256))
                for j in range(8):
                    sched.append(("w_out", l, 0, 16, j * 256, 256))
                for fp in range(4):
                    c0 = fp * 1536
                    nsl = 6 if fp < 3 else 4
                    for j in range(nsl):
                        sched.append(("w_gate", l, 0, 16, c0 + j * 256, 256))
                        sched.append(("w_up", l, 0, 16, c0 + j * 256, 256))
                    nk = nsl * 2
                    for j in range(8):
                        sched.append(("w_down", l, c0, nk, j * 256, 256))
        wcur = [0, 0]
        NCOLS = {"w_in": DIN, "w_glu": 1024, "w_out": D, "w_gate": DFF, "w_up": DFF, "w_down": D}
        NROWS = {"w_in": D, "w_glu": 1024, "w_out": D, "w_gate": D, "w_up": D, "w_down": DFF}

        def w_issue(upto):
            while wcur[1] < min(upto, len(sched)):
                i = wcur[1]
                nm, l, r0, nk, c0, ncl = sched[i]
                slot = i % NSLOT
                if nm == "w_krep":
                    for kvh in range(2):
                        for r_ in range(2):
                            src = dap(W["w_in"], l * D * DIN + 1024 + kvh * 64, [[DIN, 128], [128 * DIN, 16], [1, 64]])
                            c_ = kvh * 128 + r_ * 64
                            P.dma("pool", Rec("dma_start", out=ring[slot][:, :, c_:c_ + 64], in_=src),
                                  [], ["ring%d" % slot])
                    wcur[1] += 1
                    continue
                ncols = NCOLS[nm]
                src = dap(W[nm], l * NROWS[nm] * ncols + r0 * ncols + c0, [[ncols, 128], [128 * ncols, nk], [1, ncl]])
                P.dma("pool", Rec("dma_start", out=ring[slot][:, 0:nk, 0:ncl], in_=src),
                      [], ["ring%d" % slot])
                wcur[1] += 1

        def w_next(nm, l, c0):
            i = wcur[0]
            assert sched[i][0] == nm and sched[i][1] == l and sched[i][4] == c0, (sched[i], nm, l, c0)
            w_issue(i + NSLOT - 1)
            wcur[0] += 1
            return ring[i % NSLOT], "ring%d" % (i % NSLOT)

        pcnt = [0]
        HS = [True]

        def proj_chunk(slab, rslab, nk, mcol, rhs_fn, rhs_regs, evac_fn, lhs_rep=False):
            b = pcnt[0] % 2
            sbk = 2 + pcnt[0] % 2
            so = ((pcnt[0] // 2) % 8) * 4
            pcnt[0] += 1
            for k in range(nk):
                lhs = slab[:, k, mcol:mcol + 128]
                r = rhs_fn(k)
                P.op("pe", Rec("matmul", ps[b][:, 0:NP], lhsT=lhs, rhs=r[:, 0:NP], start=(k == 0), stop=(k == nk - 1)),
                     [rslab] + rhs_regs(k), [RP[b]])
                if HS[0]:
                    P.op("pe", Rec("matmul", ps[sbk][:, so:so + NS], lhsT=lhs, rhs=r[:, NP:NT], start=(k == 0), stop=(k == nk - 1)),
                         [rslab] + rhs_regs(k), [RP[sbk]])
            evac_fn(ps[b][:, 0:NP], ps[sbk][:, so:so + NS] if HS[0] else None, [RP[b]], [RP[sbk]])

        def rmsnorm(srcs, src_regs, gain_fn, dsts, dst_regs, dim):
            nk = len(srcs)
            for k in range(nk):
                q = sqs[k % 2]
                P.op("act", Rec("activation", out=q[:], in_=srcs[k], func=AF.Square), [src_regs[k]], ["sq%d" % (k % 2)])
                P.op("pe", Rec("matmul", ps[3][:, 0:NP], lhsT=onesb[:], rhs=q[:, 0:NP], start=(k == 0), stop=(k == nk - 1)),
                     ["sq%d" % (k % 2), "onesb"], [RP[3]])
                if HS[0]:
                    P.op("pe", Rec("matmul", ps[2][:, 480:480 + NS], lhsT=onesb[:], rhs=q[:, NP:NT], start=(k == 0), stop=(k == nk - 1)),
                         ["sq%d" % (k % 2), "onesb"], [RP[2]])
            P.op("act", Rec("activation", out=rstd[:, 0:NP], in_=ps[3][:, 0:NP], func=AF.Sqrt, scale=1.0 / dim, bias=epsb[:, 0:1]), [RP[3], "epsb"], ["rstd"])
            if HS[0]:
                P.op("act", Rec("activation", out=rstd[:, NP:NT], in_=ps[2][:, 480:480 + NS], func=AF.Sqrt, scale=1.0 / dim, bias=epsb[:, 0:1]), [RP[2], "epsb"], ["rstd"])
            P.op("dve", Rec("reciprocal", out=rstd[:], in_=rstd[:]), ["rstd"], ["rstd"])
            for k in range(nk):
                P.op("dve", Rec("scalar_tensor_tensor", out=dsts[k], in0=srcs[k], scalar=gain_fn(k), in1=rstd[:], op0=ALU.mult, op1=ALU.mult),
                     [src_regs[k], "rstd", "gains", "gfin"], [dst_regs[k]])

        epsb = sb("epsb", [128, 1])
        P.op("pool", Rec("memset", epsb[:], EPS), [], ["epsb"])
        halfpi = sb("halfpi", [128, 1])
        P.op("pool", Rec("memset", halfpi[:], math.pi / 2), [], ["halfpi"])

        def sincos(eng_v, angle, itmp, a1, a2, out_c, out_s, regs_in, rtag):
            P.op("dve", Rec("tensor_scalar", out=a1, in0=angle, scalar1=1.0 / TWO_PI, scalar2=None, op0=ALU.mult), regs_in, [rtag + "a1"])
            P.op("dve", Rec("tensor_copy", out=itmp, in_=a1), [rtag + "a1"], [rtag + "a2"])
            P.op("dve", Rec("tensor_copy", out=a1, in_=itmp), [rtag + "a2"], [rtag + "a1"])
            P.op("dve", Rec("scalar_tensor_tensor", out=angle, in0=a1, scalar=-TWO_PI, in1=angle, op0=ALU.mult, op1=ALU.add), [rtag + "a1"] + regs_in, regs_in)
            P.op("act", Rec("activation", out=a1, in_=angle, func=AF.Sin, scale=0.5), regs_in, [rtag + "a1"])
            P.op("act", Rec("activation", out=a2, in_=angle, func=AF.Sin, scale=0.5, bias=halfpi[:, 0:1]), regs_in + ["halfpi"], [rtag + "a2"])
            P.op("dve", Rec("scalar_tensor_tensor", out=out_s, in0=a1, scalar=2.0, in1=a2, op0=ALU.mult, op1=ALU.mult), [rtag + "a1", rtag + "a2"], [rtag + "s"])
            P.op("dve", Rec("tensor_tensor", out=a2, in0=a1, in1=a1, op=ALU.mult), [rtag + "a1"], [rtag + "a2"])
            P.op("dve", Rec("tensor_scalar", out=out_c, in0=a2, scalar1=-2.0, scalar2=1.0, op0=ALU.mult, op1=ALU.add), [rtag + "a2"], [rtag + "c"])

        def cmul(eng, o_r, o_i, a_r, a_i, b_r, b_i, tmp1, tmp2, rin, rout):
            P.op(eng, Rec("tensor_tensor", out=tmp1, in0=a_i, in1=b_i, op=ALU.mult), rin, rout)
            P.op(eng, Rec("tensor_tensor", out=tmp2, in0=a_i, in1=b_r, op=ALU.mult), rin, rout)
            P.op(eng, Rec("tensor_tensor", out=o_r, in0=a_r, in1=b_r, op=ALU.mult), rin, rout)
            P.op(eng, Rec("tensor_tensor", out=o_i, in0=a_r, in1=b_i, op=ALU.mult), rin, rout)
            P.op(eng, Rec("tensor_tensor", out=o_r, in0=o_r, in1=tmp1, op=ALU.subtract), rin, rout)
            P.op(eng, Rec("tensor_tensor", out=o_i, in0=o_i, in1=tmp2, op=ALU.add), rin, rout)

        def ckpt(n):
            if stop is not None and n >= stop:
                raise _Stop()

        try:
          for pi in range(npass):
              sidx = pi % 4
              HS[0] = pi < 4
              for blk in range(NB):
                  r0 = pi * NP + blk * 128
                  P.dma("sp", Rec("dma_start", out=stage[:], in_=dap(xp, r0 * D, [[D, 128], [1, D]])), [], STG)
                  for kq in range(4):
                      for kk in range(4):
                          k = kq * 4 + kk
                          P.op("pe", Rec("transpose", ps[4][:, kk * 128:(kk + 1) * 128], stage[:, k * 128:(k + 1) * 128], identf),
                               STG + ["cs"], [RP[4]])
                      P.op("dve", Rec("tensor_copy", out=X[:, kq * 4:kq * 4 + 4, blk * 128:(blk + 1) * 128],
                                                                      in_=ps[4][:].rearrange("p (a b) -> p a b", a=4)),
                           [RP[4]], ["X%d" % k for k in range(kq * 4, kq * 4 + 4)])
              P.dma("sp", Rec("dma_start", out=stage[0:4, :], in_=dap(xs, sidx * 4 * D, [[D, 4], [1, D]])), [], STG)
              for k in range(16 if HS[0] else 0):
                  P.op("pe", Rec("transpose", ps[4][:, k * 4:(k + 1) * 4], stage[0:4, k * 128:(k + 1) * 128], identf[0:4, 0:4]),
                       STG + ["cs"], [RP[4]])
              if HS[0]:
                  P.op("dve", Rec("tensor_copy", out=X[:, :, NP:NT], in_=ps[4][:, 0:64].rearrange("p (a b) -> p a b", a=16)),
                       [RP[4]], ["X%d" % k for k in range(16)])

              ckpt(1)
              for l in range(depth):
                  rmsnorm([X[:, k, :] for k in range(16)], ["X%d" % k for k in range(16)], lambda k, l=l: gains[:, l, k:k + 1],
                          [Hb[:, k, :] for k in range(16)], ["Hb%d" % k for k in range(16)], D)
                  hb_rhs = lambda k: Hb[:, k, :]
                  hb_regs = lambda k: ["Hb%d" % k]
                  ckpt(2)
                  for kvh in range(2):
                      P.op("pool", Rec("tensor_copy", out=Kb[kvh][:, 0:128], in_=Khalo[:, l, kvh, :]), ["Khalo"], ["Kb%d" % kvh])
                  P.op("pool", Rec("tensor_copy", out=Vb[:, 0, :, :], in_=Vhalo[:, l, :, :]), ["Vhalo"], ["Vb"])
                  ckpt(2.1)
                  slab, rslab = w_next("w_in", l, 1024)

                  def ev_k(pm, psm, rpm, rps):
                      for kvh in range(2):
                          hs = slice(kvh * 64, kvh * 64 + 64)
                          P.op("act", Rec("activation", out=Kb[kvh][hs, 128:128 + NP], in_=pm[hs, :], func=AF.Copy), rpm, ["Kb%d" % kvh])
                          if psm is not None:
                              P.op("act", Rec("activation", out=KS[kvh][hs, 128:132], in_=psm[hs, :], func=AF.Copy), rps, ["KS%d" % kvh])
                  proj_chunk(slab, rslab, 16, 0, hb_rhs, hb_regs, ev_k)
                  for kvh in range(2):
                      hs = slice(kvh * 64, kvh * 64 + 64)
                      ho = slice((1 - kvh) * 64, (1 - kvh) * 64 + 64)
                      P.dma("sp", Rec("dma_start", out=Kb[kvh][ho, 128:128 + NP], in_=Kb[kvh][hs, 128:128 + NP]), ["Kb%d" % kvh], ["Kb%d" % kvh])
                  ckpt(2.2)
                  import os as _os
                  for blk in range(NB + (1 if HS[0] else 0)):
                      if blk < NB:
                          cols = slice(blk * 128, (blk + 1) * 128); mrows = 128
                      else:
                          cols = slice(NP, NT); mrows = NS
                      for k in range(16):
                          P.op("pe", Rec("matmul", ps[5][0:mrows, 0:256], lhsT=Hb[:, k, cols], rhs=slab[:, k, 0:256], start=(k == 0), stop=(k == 15)),
                               [rslab, "Hb%d" % k], [RP[5]])
                      if blk < NB:
                          for kvh in range(2 - 2 * int(_os.environ.get("SKIP_A", "0"))):
                              for odd in range(2):
                                  P.op("dve", Rec("tensor_copy", out=Vb[:, blk + 1, kvh * 2 + odd, odd * 64:odd * 64 + 64], in_=ps[5][:, 128 + kvh * 64:128 + kvh * 64 + 64]),
                                       [RP[5]], ["Vb"])
                          if blk == NB - 1 and not int(_os.environ.get("SKIP_B", "0")):
                              P.op("act", Rec("activation", out=kvst[:], in_=ps[5][:, 0:256], func=AF.Copy), [RP[5]], ["kvst"])
                              P.dma("sp", Rec("dma_start", out=okp.ap()[l], in_=kvst[:, 0:128]), ["kvst"], ["okp"])
                              P.dma("sp", Rec("dma_start", out=ovp.ap()[l], in_=kvst[:, 128:256]), ["kvst"], ["ovp"])
                      else:
                          for kvh in range(2):
                              for odd in range(2):
                                  P.op("dve", Rec("tensor_copy", out=VSn[0:4, kvh * 2 + odd, odd * 64:odd * 64 + 64], in_=ps[5][0:4, 128 + kvh * 64:128 + kvh * 64 + 64]),
                                       [RP[5]], ["VSn"])
                          P.op("act", Rec("activation", out=kvss[:], in_=ps[5][0:4, 0:256], func=AF.Copy), [RP[5]], ["kvst"])
                          P.dma("sp", Rec("dma_start", out=oks.ap()[l, sidx, 124:128, :], in_=kvss[:, 0:128]), ["kvst"], ["oks"])
                          P.dma("sp", Rec("dma_start", out=ovs.ap()[l, sidx, 124:128, :], in_=kvss[:, 128:256]), ["kvst"], ["ovs"])
                          P.dma("sp", Rec("dma_start", out=cpy[0:124, 0:128], in_=ck.ap()[l, sidx, 4:128, :]), [], ["kvst"])
                          P.dma("sp", Rec("dma_start", out=cpy[0:124, 128:256], in_=cv.ap()[l, sidx, 4:128, :]), [], ["kvst"])
                          P.dma("sp", Rec("dma_start", out=oks.ap()[l, sidx, 0:124, :], in_=cpy[0:124, 0:128]), ["kvst"], ["oks"])
                          P.dma("sp", Rec("dma_start", out=ovs.ap()[l, sidx, 0:124, :], in_=cpy[0:124, 128:256]), ["kvst"], ["ovs"])
                  ckpt(2.3)
                  for kvh in range(2):
                      P.op("pool", Rec("tensor_copy", out=Khalo[:, l, kvh, :], in_=Kb[kvh][:, NP:NP + 128]), ["Kb%d" % kvh], ["Khalo"])
                  P.op("pool", Rec("tensor_copy", out=Vhalo[:, l, :, :], in_=Vb[:, NB, :, :]), ["Vb"], ["Vhalo"])
                  ckpt(2.4)
                  if HS[0]:
                      P.dma("pool", Rec("dma_start", out=cks[:, 0:128], in_=ck.ap()[l, sidx]), [], ["cks"])
                      P.op("pe", Rec("matmul", ps[5][:, 256:384], lhsT=cks[:, 0:128], rhs=identb[:], start=True, stop=True), ["cks", "identb"], [RP[5]])
                  for kvh in range(2 if HS[0] else 0):
                      hs = slice(kvh * 64, kvh * 64 + 64)
                      ho = slice((1 - kvh) * 64, (1 - kvh) * 64 + 64)
                      P.op("act", Rec("activation", out=KS[kvh][hs, 0:128], in_=ps[5][hs, 256:384], func=AF.Copy), [RP[5]], ["KS%d" % kvh])
                      P.dma("sp", Rec("dma_start", out=KS[kvh][ho, :], in_=KS[kvh][hs, :]), ["KS%d" % kvh], ["KS%d" % kvh])
                  for kvh in range(2 if HS[0] else 0):
                      for odd in range(2):
                          P.dma("pool", Rec("dma_start", out=VSc[:, kvh * 2 + odd, odd * 64:odd * 64 + 64], in_=cv.ap()[l, sidx, :, kvh * 64:kvh * 64 + 64]), [], ["VSc"])
                  ckpt(3)
                  for j in range(4):
                      slab, rslab = w_next("w_in", l, j * 256)
                      for mm in range(2):
                          m = j * 2 + mm
                          def ev_q(pm, psm, rpm, rps, m=m):
                              P.op("act", Rec("activation", out=Ao[:, m, 0:NP], in_=pm, func=AF.Copy, scale=0.125), rpm, ["Ao%d" % m])
                              if psm is not None:
                                  P.op("act", Rec("activation", out=Ao[:, m, NP:NT], in_=psm, func=AF.Copy, scale=0.125), rps, ["Ao%d" % m])
                          proj_chunk(slab, rslab, 16, mm * 128, hb_rhs, hb_regs, ev_q)
                  ckpt(4)
                  ucnt = [0]

                  def attn_unit(m, nq, qcols, keyfn, nkeys, mk, segs, l=l):
                      for odd in range(2):
                          h = 2 * m + odd
                          kvh = h // 8
                          hp = odd * 64
                          u = ucnt[0] % 2
                          ucnt[0] += 1
                          sbk = 6 + u
                          P.op("pe", Rec("matmul", ps[sbk][0:nq, 0:nkeys], lhsT=Ao[hp:hp + 64, m, qcols], rhs=keyfn(kvh)[hp:hp + 64, :], start=True, stop=False),
                               ["Ao%d" % m, "Kb%d" % kvh, "KS%d" % kvh], [RP[sbk]])
                          P.op("pe", Rec("matmul", ps[sbk][0:nq, 0:nkeys], lhsT=identb[0:nq, 0:nq], rhs=mk[0:nq, 0:nkeys], start=False, stop=True),
                               ["identb", "maskb", "maskfb", "masksb"], [RP[sbk]])
                          c0 = u * 8
                          scol = sinkb[0:nq, l * 16 + h:l * 16 + h + 1]
                          P.op("dve", Rec("reduce_max", out=sm[0:nq, c0:c0 + 1], in_=ps[sbk][0:nq, 0:nkeys], axis=AX.X), [RP[sbk]], ["sm%d" % u])
                          P.op("dve", Rec("tensor_scalar", out=sm[0:nq, c0 + 1:c0 + 2], in0=sm[0:nq, c0:c0 + 1], scalar1=scol, scalar2=-1.0, op0=ALU.max, op1=ALU.mult),
                               ["sm%d" % u, "sinkb"], ["sm%d" % u])
                          P.op("act", Rec("activation", out=Pf[u][0:nq, 0:nkeys], in_=ps[sbk][0:nq, 0:nkeys], func=AF.Exp, bias=sm[0:nq, c0 + 1:c0 + 2], accum_out=sm[0:nq, c0 + 2:c0 + 3]),
                               [RP[sbk], "sm%d" % u], ["Pf%d" % u, "sm%d" % u])
                          P.op("act", Rec("activation", out=sm[0:nq, c0 + 3:c0 + 4], in_=sm[0:nq, c0 + 1:c0 + 2], func=AF.Exp, bias=scol),
                               ["sm%d" % u, "sinkb"], ["sm%d" % u])
                          P.op("dve", Rec("tensor_tensor", out=sm[0:nq, c0 + 4:c0 + 5], in0=sm[0:nq, c0 + 2:c0 + 3], in1=sm[0:nq, c0 + 3:c0 + 4], op=ALU.add), ["sm%d" % u], ["sm%d" % u])
                          P.op("dve", Rec("reciprocal", out=sm[0:nq, c0 + 5:c0 + 6], in_=sm[0:nq, c0 + 4:c0 + 5]), ["sm%d" % u], ["sm%d" % u])
                          P.op("dve", Rec("tensor_scalar", out=Pn[u][0:nq, 0:nkeys], in0=Pf[u][0:nq, 0:nkeys], scalar1=sm[0:nq, c0 + 5:c0 + 6], scalar2=None, op0=ALU.mult),
                               ["Pf%d" % u, "sm%d" % u], ["Pn%d" % u])
                          ptb = psb(4 + u)
                          ko = 0
                          for si, (nk_, vfn) in enumerate(segs):
                              P.op("pe", Rec("transpose", ptb[0:nk_, si * 128:si * 128 + nq], Pn[u][0:nq, ko:ko + nk_], identb[0:nq, 0:nq]),
                                   ["Pn%d" % u, "identb"], [RP[4 + u]])
                              P.op("act", Rec("activation", out=PT[u][0:nk_, si, 0:nq], in_=ptb[0:nk_, si * 128:si * 128 + nq], func=AF.Copy),
                                   [RP[4 + u]], ["PT%d" % u])
                              ko += nk_
                          for si, (nk_, vfn) in enumerate(segs):
                              first = (odd == 0 and si == 0)
                              last = (odd == 1 and si == len(segs) - 1)
                              P.op("pe", Rec("matmul", ps[3][:, 0:nq], lhsT=vfn(kvh * 2 + odd)[0:nk_, :], rhs=PT[u][0:nk_, si, 0:nq], start=first, stop=last),
                                   ["PT%d" % u, "Vb", "VSc", "VSn"], [RP[3]])
                      P.op("dve", Rec("tensor_copy", out=Ao[:, m, qcols], in_=ps[3][:, 0:nq]), [RP[3]], ["Ao%d" % m])

                  for nb in range(NB):
                      for m in range(8):
                          mk = maskfb if (pi == 0 and nb == 0) else maskb
                          attn_unit(m, 128, slice(nb * 128, (nb + 1) * 128),
                                    lambda kvh, nb=nb: Kb[kvh][:, nb * 128:nb * 128 + 256], 256, mk,
                                    [(128, lambda v, nb=nb: Vb[:, nb, v, :]), (128, lambda v, nb=nb: Vb[:, nb + 1, v, :])])
                  for m in range(8 if HS[0] else 0):
                      attn_unit(m, NS, slice(NP, NT), lambda kvh: KS[kvh][:, 0:132], 132, masksb,
                                [(128, lambda v: VSc[:, v, :]), (NS, lambda v: VSn[:, v, :])])

                  ckpt(5)
                  if pi == 0:
                      for nm, dst in [("ssm_a_re", lam_r), ("ssm_a_im", lam_i)]:
                          P.dma("sp", Rec("dma_start", out=dst[:], in_=dap(W[nm], l * 4096, [[1, 128], [128, 32]])), [], ["ssmp"])
                      for gl in range(2):
                          P.dma("sp", Rec("dma_start", out=dtt[gl * 64:(gl + 1) * 64, :], in_=dap(W["ssm_log_dt"], l * 64 + gl, [[0, 64], [2, 32]])), [], ["ssmp"])
                      for nm, dst in [("ssm_b_re", Br), ("ssm_b_im", Bi)]:
                          P.dma("sp", Rec("dma_start", out=dst[:], in_=dap(W[nm], l * 65536, [[16, 128], [2048, 32], [1, 16]])), [], ["ssmB"])
                      for nm, dst, rc in [("ssm_c_re", Cr, "Cr"), ("ssm_c_im", Ci, "Ci")]:
                          for jq in range(4):
                              P.dma("sp", Rec("dma_start", out=Cn[:], in_=dap(W[nm], l * 65536 + jq * 16384, [[64, 16], [1024, 16], [1, 64]])), [], ["Cn"])
                              for jj in range(8):
                                  P.op("pe", Rec("transpose", ps[4][:, jj * 16:(jj + 1) * 16], Cn[:, 2 * jj:2 * jj + 2, :], identf[0:16, 0:16]), ["Cn", "cs"], [RP[4]])
                              P.op("dve", Rec("tensor_copy", out=dst[:, jq * 8:(jq + 1) * 8, :], in_=ps[4][:, 0:128].rearrange("p (a b) -> p a b", a=8)), [RP[4]], ["ssmC"])
                  P.dma("sp", Rec("dma_start", out=Drow[:], in_=dap(W["ssm_d"], l * 1024, [[0, 128], [1, 1024]])), [], ["Drow"])
                  for src_t, dst in [(hr0, h0r), (hi0, h0i)]:
                      P.dma("sp", Rec("dma_start", out=dst[:], in_=dap(src_t, (l * 4 + sidx) * 4096, [[1, 128], [128, 32]])), [], ["h0"])
                  if pi == 0:
                      sp_ = ["ssmp"]
                      P.op("act", Rec("activation", out=dtt[:], in_=dtt[:], func=AF.Exp), sp_, sp_)
                      P.op("dve", Rec("tensor_tensor", out=t1[:], in0=lam_r[:], in1=dtt[:], op=ALU.mult), sp_, sp_)
                      P.op("act", Rec("activation", out=mag[:], in_=t1[:], func=AF.Exp), sp_, sp_)
                      P.op("act", Rec("activation", out=rho[:], in_=t1[:], func=AF.Exp, scale=8.0), sp_, sp_)
                      P.op("dve", Rec("tensor_tensor", out=th[:], in0=lam_i[:], in1=dtt[:], op=ALU.mult), sp_, sp_)
                      P.op("dve", Rec("tensor_scalar", out=phi[:], in0=th[:], scalar1=8.0, scalar2=None, op0=ALU.mult), sp_, sp_)
                      sincos("dve", th[:], t3[:].bitcast(I32), t2[:], t3[:], abr[:], abi[:], sp_, "tr1")
                      P.op("dve", Rec("tensor_tensor", out=abr[:], in0=abr[:], in1=mag[:], op=ALU.mult), sp_ + ["tr1c"], sp_)
                      P.op("dve", Rec("tensor_tensor", out=abi[:], in0=abi[:], in1=mag[:], op=ALU.mult), sp_ + ["tr1s"], sp_)
                      P.op("dve", Rec("tensor_scalar", out=t2[:], in0=phi[:], scalar1=1.0 / TWO_PI, scalar2=None, op0=ALU.mult), sp_ + ["tr1a1"], sp_ + ["tr1a1"])
                      P.op("dve", Rec("tensor_copy", out=ti[:], in_=t2[:]), sp_ + ["tr1a1", "tr1a2"], sp_ + ["tr1a2"])
                      P.op("dve", Rec("tensor_copy", out=t2[:], in_=ti[:]), sp_ + ["tr1a1", "tr1a2"], sp_ + ["tr1a1"])
                      P.op("dve", Rec("scalar_tensor_tensor", out=phi[:], in0=t2[:], scalar=-TWO_PI, in1=phi[:], op0=ALU.mult, op1=ALU.add), sp_ + ["tr1a1"], sp_)
                      P.op("dve", Rec("tensor_scalar", out=t1[:], in0=abr[:], scalar1=-1.0, scalar2=None, op0=ALU.add), sp_, sp_)
                      P.op("dve", Rec("tensor_tensor", out=t2[:], in0=lam_r[:], in1=lam_r[:], op=ALU.mult), sp_ + ["tr1a1"], sp_ + ["tr1a1"])
                      P.op("dve", Rec("tensor_tensor", out=t3[:], in0=lam_i[:], in1=lam_i[:], op=ALU.mult), sp_ + ["tr1a2"], sp_ + ["tr1a2"])
                      P.op("dve", Rec("tensor_tensor", out=t2[:], in0=t2[:], in1=t3[:], op=ALU.add), sp_ + ["tr1a1", "tr1a2"], sp_ + ["tr1a1"])
                      P.op("dve", Rec("reciprocal", out=t2[:], in_=t2[:]), sp_ + ["tr1a1"], sp_ + ["tr1a1"])
                      P.op("dve", Rec("tensor_tensor", out=crr[:], in0=t1[:], in1=lam_r[:], op=ALU.mult), sp_, sp_)
                      P.op("dve", Rec("tensor_tensor", out=t3[:], in0=abi[:], in1=lam_i[:], op=ALU.mult), sp_ + ["tr1a2"], sp_ + ["tr1a2"])
                      P.op("dve", Rec("tensor_tensor", out=crr[:], in0=crr[:], in1=t3[:], op=ALU.add), sp_ + ["tr1a2"], sp_)
                      P.op("dve", Rec("tensor_tensor", out=crr[:], in0=crr[:], in1=t2[:], op=ALU.mult), sp_ + ["tr1a1"], sp_)
                      P.op("dve", Rec("tensor_tensor", out=cii[:], in0=abi[:], in1=lam_r[:], op=ALU.mult), sp_, sp_)
                      P.op("dve", Rec("tensor_tensor", out=t3[:], in0=t1[:], in1=lam_i[:], op=ALU.mult), sp_ + ["tr1a2"], sp_ + ["tr1a2"])
                      P.op("dve", Rec("tensor_tensor", out=cii[:], in0=cii[:], in1=t3[:], op=ALU.subtract), sp_ + ["tr1a2"], sp_)
                      P.op("dve", Rec("tensor_tensor", out=cii[:], in0=cii[:], in1=t2[:], op=ALU.mult), sp_ + ["tr1a1"], sp_)
                      bc = lambda a: a[:].unsqueeze(2).broadcast_to([128, 32, 16])
                      cmul("dve", Bbr[:], Bbi[:], bc(crr), bc(cii), Br[:], Bi[:], PCr[:], PCi[:], sp_ + ["ssmB", "PC"], ["ssmB", "PC"])
                      P.op("dve", Rec("tensor_tensor", out=t1[:], in0=mag[:], in1=mag[:], op=ALU.mult), sp_, sp_)
                      P.op("dve", Rec("reciprocal", out=t1[:], in_=t1[:]), sp_, sp_)
                      P.op("dve", Rec("tensor_tensor", out=iar[:], in0=abr[:], in1=t1[:], op=ALU.mult), sp_, sp_)
                      P.op("dve", Rec("scalar_tensor_tensor", out=iai[:], in0=abi[:], scalar=-1.0, in1=t1[:], op0=ALU.mult, op1=ALU.mult), sp_, sp_)
                      pr = ["PC", "ssmp"]
                      P.op("pool", Rec("memset", PCr[:, :, 7:8], 1.0), pr, pr)
                      P.op("pool", Rec("memset", PCi[:, :, 7:8], 0.0), pr, pr)
                      for kk in range(8, 16):
                          cmul("dve", PCr[:, :, kk], PCi[:, :, kk], PCr[:, :, kk - 1], PCi[:, :, kk - 1], abr[:], abi[:], t2[:], t3[:], pr + ["tr1a1", "tr1a2"], pr + ["tr1a1", "tr1a2"])
                      for kk in range(6, -1, -1):
                          cmul("dve", PCr[:, :, kk], PCi[:, :, kk], PCr[:, :, kk + 1], PCi[:, :, kk + 1], iar[:], iai[:], t2[:], t3[:], pr + ["tr1a1", "tr1a2"], pr + ["tr1a1", "tr1a2"])
                      for i in range(8):
                          P.op("pool", Rec("tensor_copy", out=PBr[:, :, i], in_=PCr[:, :, 14 - i]), pr, ["PB"])
                          P.op("pool", Rec("tensor_copy", out=PBi[:, :, i], in_=PCi[:, :, 14 - i]), pr, ["PB"])
                  if pi == 0:
                      for t_, o_, n_ in [(PCr, 0, 512), (PCi, 512, 512)]:
                          P.dma("sp", Rec("dma_start", out=dap(scrT, l * 128 * 1088 + o_, [[1088, 128], [1, n_]]), in_=t_[:].rearrange("p a b -> p (a b)")), ["PC"], ["scrT%d" % l])
                      for t_, o_ in [(rho, 1024), (phi, 1056)]:
                          P.dma("sp", Rec("dma_start", out=dap(scrT, l * 128 * 1088 + o_, [[1088, 128], [1, 32]]), in_=t_[:]), ["ssmp"], ["scrT%d" % l])
                  else:
                      for t_, o_, n_ in [(PCr, 0, 512), (PCi, 512, 512)]:
                          P.dma("sp", Rec("dma_start", out=t_[:].rearrange("p a b -> p (a b)"), in_=dap(scrT, l * 128 * 1088 + o_, [[1088, 128], [1, n_]])), ["scrT%d" % l], ["PC"])
                      for t_, o_ in [(rho, 1024), (phi, 1056)]:
                          P.dma("sp", Rec("dma_start", out=t_[:], in_=dap(scrT, l * 128 * 1088 + o_, [[1088, 128], [1, 32]])), ["scrT%d" % l], ["ssmp"])
                  cmul("dve", hend[:, 0, :], hend[:, 1, :], PCr[:, :, 11], PCi[:, :, 11], h0r[:], h0i[:], t2[:], t3[:], pr + ["h0", "hend", "tr1a1", "tr1a2"], ["hend", "tr1a1", "tr1a2"])

                  ckpt(6)
                  for sl in range(4):
                      slab, rslab = w_next("w_in", l, 1280 + sl * 256)
                      for i in range(8):
                          nrow = NCH1 if i < 4 else NCHK
                          for k in range(16):
                              lhs = sap(Hb[:, k, i:i + 1], [[8, nrow]])
                              P.op("pe", Rec("matmul", ps[i % 2][0:nrow, 0:256], lhsT=lhs, rhs=slab[:, k, 0:256], start=(k == 0), stop=(k == 15)),
                                   [rslab, "Hb%d" % k], [RP[i % 2]])
                          P.op("act", Rec("activation", out=Vu[0:nrow, :, i, :], in_=ps[i % 2][0:nrow, 0:256].rearrange("p (g c) -> p g c", g=16), func=AF.Copy), [RP[i % 2]], ["Vu"])
                      for gq in range(4):
                          ub = psb(4)
                          for gg in range(4):
                              g = gq * 4 + gg
                              P.op("pe", Rec("transpose", ub[:, gg * 128:gg * 128 + NCH1], Vu[0:NCH1, g, :, :], identb[0:NCH1, 0:NCH1]),
                                   ["Vu", "identb"], [RP[4]])
                          for gg in range(4):
                              g = gq * 4 + gg
                              P.op("dve", Rec("tensor_copy", out=Ug[g][:], in_=ub[:, gg * 128:gg * 128 + NCH1]), [RP[4]], ["Ug%d" % g])
                      j0 = sl * NPB
                      bcp = lambda a: a[:, j0:j0 + NPB].unsqueeze(2).broadcast_to([128, NPB, 65])
                      posb = posf.unsqueeze(1).broadcast_to([128, NPB, 65])
                      P.op("dve", Rec("tensor_tensor", out=ang[:], in0=bcp(phi), in1=posb, op=ALU.mult), ["ssmp", "cs", "trb"], ["trb"])
                      sincos("dve", ang[:], u2[:].bitcast(I32), u1[:], u2[:], Tc[:], Ts[:], ["trb"], "tr2")
                      P.op("dve", Rec("tensor_tensor", out=d0[:], in0=bcp(rho), in1=cs[:, 965:1030].unsqueeze(1).broadcast_to([128, NPB, 65]), op=ALU.mult), ["ssmp", "cs"], ["d0"])
                      creg = "scrC_%d_%d" % (l, sl)
                      csrc = dap(scrC, (l * 4 + sl) * 128 * 4096, [[4096, 128], [1, 4096]])
                      gsrc = dap(scrG, (l * 4 + sl) * 128 * 2048, [[2048, 128], [1, 2048]])
                      if pi > 0:
                          P.dma("sp", Rec("dma_start", out=MCBt[:].rearrange("p a b c -> p (a b c)"), in_=csrc), [creg], ["MCm"])
                          P.dma("sp", Rec("dma_start", out=TgSt[:].rearrange("p a b -> p (a b)"), in_=gsrc), [creg], ["TgS"])
                      for jj in range(NPB):
                          j = j0 + jj
                          MBp = [MBpT[:, jj % 2, q_, :] for q_ in range(4)]
                          if pi == 0:
                              rsp = ["ssmB", "PB", "PC", "ssmC", "ssmp"]
                              pb_r = PBr[:, j, :].unsqueeze(2).broadcast_to([128, 8, 16]); pb_i = PBi[:, j, :].unsqueeze(2).broadcast_to([128, 8, 16])
                              bb_r = Bbr[:, j, :].unsqueeze(1).broadcast_to([128, 8, 16]); bb_i = Bbi[:, j, :].unsqueeze(1).broadcast_to([128, 8, 16])
                              v3 = lambda a, n: a[:, 0:n * 16].rearrange("p (a b) -> p a b", b=16)
                              P.op("pool", Rec("tensor_tensor", out=v3(tA, 8), in0=pb_i, in1=bb_i, op=ALU.mult), rsp + ["tA"], ["tA"])
                              P.op("pool", Rec("tensor_tensor", out=v3(tB, 8), in0=pb_r, in1=bb_r, op=ALU.mult), rsp + ["tB"], ["tB"])
                              P.op("pool", Rec("tensor_tensor", out=v3(MBt[0], 8), in0=v3(tB, 8), in1=v3(tA, 8), op=ALU.subtract), ["tA", "tB"], ["MBt"])
                              P.op("pool", Rec("tensor_tensor", out=v3(tA, 8), in0=pb_r, in1=bb_i, op=ALU.mult), rsp + ["tA"], ["tA"])
                              P.op("pool", Rec("tensor_tensor", out=v3(tB, 8), in0=pb_i, in1=bb_r, op=ALU.mult), rsp + ["tB"], ["tB"])
                              P.op("pool", Rec("tensor_tensor", out=v3(MBt[1], 8), in0=v3(tA, 8), in1=v3(tB, 8), op=ALU.add), ["tA", "tB"], ["MBt"])
                              pc_r = PCr[:, j, :].unsqueeze(2).broadcast_to([128, 16, 16]); pc_i = PCi[:, j, :].unsqueeze(2).broadcast_to([128, 16, 16])
                              cc_r = Cr[:, j, :].unsqueeze(1).broadcast_to([128, 16, 16]); cc_i = Ci[:, j, :].unsqueeze(1).broadcast_to([128, 16, 16])
                              P.op("dve", Rec("tensor_tensor", out=v3(tA, 16), in0=pc_i, in1=cc_i, op=ALU.mult), rsp + ["tA"], ["tA"])
                              P.op("dve", Rec("tensor_tensor", out=v3(tB, 16), in0=pc_r, in1=cc_r, op=ALU.mult), rsp + ["tB"], ["tB"])
                              MCm = MCB[jj]
                              P.op("dve", Rec("tensor_tensor", out=v3(MCm[0], 16), in0=v3(tB, 16), in1=v3(tA, 16), op=ALU.subtract), ["tA", "tB"], ["MCm"])
                              P.op("dve", Rec("tensor_tensor", out=v3(tA, 16), in0=pc_r, in1=cc_i, op=ALU.mult), rsp + ["tA"], ["tA"])
                              P.op("dve", Rec("tensor_tensor", out=v3(tB, 16), in0=pc_i, in1=cc_r, op=ALU.mult), rsp + ["tB"], ["tB"])
                              P.op("dve", Rec("scalar_tensor_tensor", out=v3(MCm[1], 16), in0=v3(tA, 16), scalar=-1.0, in1=v3(tB, 16), op0=ALU.mult, op1=ALU.subtract), ["tA", "tB"], ["MCm"])
                              tb = psb(5)
                              for ri in range(2):
                                  P.op("pe", Rec("transpose", tb[:, ri * 128:(ri + 1) * 128], MBt[ri][:], identb[:]), ["MBt", "identb"], [RP[5]])
                              for gl in range(2):
                                  for ri in range(2):
                                      P.op("act", Rec("activation", out=MBp[gl * 2 + ri][:, gl * 64:gl * 64 + 64], in_=tb[:, ri * 128 + gl * 64:ri * 128 + gl * 64 + 64], func=AF.Copy),
                                           [RP[5]], ["MBp%d" % (jj % 2)])
                              for gl in range(2):
                                  sl_ = slice(gl * 64, gl * 64 + 64)
                                  P.op("pe", Rec("matmul", ps[3][:, gl * 128:(gl + 1) * 128], lhsT=MBt[0][sl_, :], rhs=MCm[0][sl_, 0:128], start=True, stop=False), ["MBt", "MCm"], [RP[3]])
                                  P.op("pe", Rec("matmul", ps[3][:, gl * 128:(gl + 1) * 128], lhsT=MBt[1][sl_, :], rhs=MCm[1][sl_, 0:128], start=False, stop=True), ["MBt", "MCm"], [RP[3]])
                                  P.op("dve", Rec("tensor_tensor", out=TgS[jj * 2 + gl][:], in0=ps[3][:, gl * 128:(gl + 1) * 128], in1=tmask[:], op=ALU.mult), [RP[3], "tmask"], ["TgS"])
                          mreg = "scrM_%d_%d" % (l, j)
                          msrc = dap(scrM, (l * 32 + j) * 128 * 512, [[512, 128], [1, 512]])
                          if pi == 0:
                              P.dma("sp", Rec("dma_start", out=msrc, in_=MBpT[:, jj % 2].rearrange("p a b -> p (a b)")), ["MBp%d" % (jj % 2)], [mreg])
                          else:
                              P.dma("sp", Rec("dma_start", out=MBpT[:, jj % 2].rearrange("p a b -> p (a b)"), in_=msrc), [mreg], ["MBp%d" % (jj % 2)])
                          for ri in range(2):
                              for gl in range(2):
                                  g = jj * 2 + gl
                                  P.op("pe", Rec("matmul", ps[6][:, ri * 128:ri * 128 + NCH1], lhsT=MBp[gl * 2 + ri][:], rhs=Ug[g][:], start=(gl == 0), stop=(gl == 1)),
                                       ["MBp%d" % (jj % 2), "Ug%d" % g], [RP[6]])
                          xr = ps[6][:, 0:NCHK]; xi = ps[6][:, 128:128 + NCHK]
                          tcj = Tc[:, jj, 1:65]; tsj = Ts[:, jj, 1:65]
                          rdm = [RP[6], "tr2c", "tr2s"]
                          P.op("dve", Rec("tensor_tensor", out=Dr[:, jj, 1:65], in0=xr, in1=tcj, op=ALU.mult), rdm, ["Dm"])
                          P.op("dve", Rec("tensor_tensor", out=u1[:, jj, 1:65], in0=xi, in1=tsj, op=ALU.mult), rdm + ["tr2a1"], ["tr2a1"])
                          P.op("dve", Rec("tensor_tensor", out=Di[:, jj, 1:65], in0=xi, in1=tcj, op=ALU.mult), rdm, ["Dm"])
                          P.op("dve", Rec("tensor_tensor", out=u2[:, jj, 1:65], in0=xr, in1=tsj, op=ALU.mult), rdm + ["tr2a2"], ["tr2a2"])
                          P.op("act", Rec("activation", out=Xs[:, 0, jj:jj + 1], in_=ps[6][:, NCHK:NCHK + 1], func=AF.Copy), [RP[6]], ["Xs"])
                          P.op("act", Rec("activation", out=Xs[:, 1, jj:jj + 1], in_=ps[6][:, 128 + NCHK:128 + NCHK + 1], func=AF.Copy), [RP[6]], ["Xs"])
                      if pi == 0:
                          P.dma("sp", Rec("dma_start", out=csrc, in_=MCBt[:].rearrange("p a b c -> p (a b c)")), ["MCm"], [creg])
                          P.dma("sp", Rec("dma_start", out=gsrc, in_=TgSt[:].rearrange("p a b -> p (a b)")), ["TgS"], [creg])
                      dm = ["Dm", "tr2a1", "tr2a2"]
                      P.op("dve", Rec("tensor_tensor", out=Dr[:, :, 1:65], in0=Dr[:, :, 1:65], in1=u1[:, :, 1:65], op=ALU.add), dm, ["Dm"])
                      P.op("dve", Rec("tensor_tensor", out=Di[:, :, 1:65], in0=Di[:, :, 1:65], in1=u2[:, :, 1:65], op=ALU.subtract), dm, ["Dm"])
                      P.op("dve", Rec("tensor_copy", out=Dr[:, :, 0], in_=Hr[:, l, j0:j0 + NPB]), ["H", "Dm"], ["Dm"])
                      P.op("dve", Rec("tensor_copy", out=Di[:, :, 0], in_=Hi[:, l, j0:j0 + NPB]), ["H", "Dm"], ["Dm"])
                      fl = lambda a: a[:].rearrange("p a b -> p (a b)")
                      P.op("dve", Rec("tensor_tensor_scan", out=fl(Dr), data0=fl(d0), data1=fl(Dr), initial=0.0, op0=ALU.mult, op1=ALU.add), ["Dm", "d0"], ["Dm"])
                      P.op("dve", Rec("tensor_tensor_scan", out=fl(Di), data0=fl(d0), data1=fl(Di), initial=0.0, op0=ALU.mult, op1=ALU.add), ["Dm", "d0"], ["Dm"])
                      md = ["Dm", "tr2c", "tr2s", "tr2a1", "tr2a2", "trb"]
                      P.op("dve", Rec("tensor_tensor", out=u1[:], in0=Dr[:], in1=Tc[:], op=ALU.mult), md, ["tr2a1"])
                      P.op("dve", Rec("tensor_tensor", out=u2[:], in0=Di[:], in1=Ts[:], op=ALU.mult), md, ["tr2a2"])
                      P.op("dve", Rec("tensor_tensor", out=u1[:], in0=u1[:], in1=u2[:], op=ALU.subtract), md, ["tr2a1"])
                      P.op("dve", Rec("tensor_tensor", out=u2[:], in0=Dr[:], in1=Ts[:], op=ALU.mult), md, ["tr2a2"])
                      P.op("dve", Rec("tensor_tensor", out=ang[:], in0=Di[:], in1=Tc[:], op=ALU.mult), md, ["trb"])
                      P.op("dve", Rec("tensor_tensor", out=u2[:], in0=u2[:], in1=ang[:], op=ALU.add), md, ["tr2a2"])
                      P.op("act", Rec("activation", out=Sbr[:, :, 0:64], in_=u1[:, :, 0:64], func=AF.Copy), ["tr2a1"], ["Sb"])
                      P.op("act", Rec("activation", out=Sbi[:, :, 0:64], in_=u2[:, :, 0:64], func=AF.Copy), ["tr2a2"], ["Sb"])
                      P.op("act", Rec("activation", out=Sbr[:, :, 64], in_=h0r[:, j0:j0 + NPB], func=AF.Copy), ["h0"], ["Sb"])
                      P.op("act", Rec("activation", out=Sbi[:, :, 64], in_=h0i[:, j0:j0 + NPB], func=AF.Copy), ["h0"], ["Sb"])
                      P.op("dve", Rec("tensor_copy", out=Hr[:, l, j0:j0 + NPB], in_=u1[:, :, 64]), ["tr2a1"], ["H"])
                      P.op("dve", Rec("tensor_copy", out=Hi[:, l, j0:j0 + NPB], in_=u2[:, :, 64]), ["tr2a2"], ["H"])
                      cmul("dve", tA[:, 0:NPB], tA[:, NPB:2 * NPB], PCr[:, j0:j0 + NPB, 3], PCi[:, j0:j0 + NPB, 3], Xs[:, 0, :], Xs[:, 1, :], tB[:, 0:NPB], tB[:, NPB:2 * NPB],
                           ["PC", "Xs", "tA", "tB"], ["tA", "tB"])
                      P.op("dve", Rec("tensor_tensor", out=hend[:, 0, j0:j0 + NPB], in0=hend[:, 0, j0:j0 + NPB], in1=tA[:, 0:NPB], op=ALU.add), ["tA", "hend"], ["hend"])
                      P.op("dve", Rec("tensor_tensor", out=hend[:, 1, j0:j0 + NPB], in0=hend[:, 1, j0:j0 + NPB], in1=tA[:, NPB:2 * NPB], op=ALU.add), ["tA", "hend"], ["hend"])
                      for gq in range(4):
                          for gg in range(4):
                              g = gq * 4 + gg
                              jj = g // 2; gl = g % 2
                              sl_ = slice(gl * 64, gl * 64 + 64)
                              oc = slice(gg * 128, (gg + 1) * 128)
                              P.op("pe", Rec("matmul", ps[7][0:NCH1, oc], lhsT=Ug[g][:], rhs=TgS[g][:], start=True, stop=False), ["Ug%d" % g, "TgS"], [RP[7]])
                              P.op("pe", Rec("matmul", ps[7][0:NCH1, oc], lhsT=Sbr[sl_, jj, :], rhs=MCB[jj][0][sl_, 128:256], start=False, stop=False), ["Sb", "MCm"], [RP[7]])
                              P.op("pe", Rec("matmul", ps[7][0:NCH1, oc], lhsT=Sbi[sl_, jj, :], rhs=MCB[jj][1][sl_, 128:256], start=False, stop=True), ["Sb", "MCm"], [RP[7]])
                          ch0 = sl * 256 + gq * 64
                          P.op("dve", Rec("tensor_tensor", out=vd[0:NCH1], in0=Vu[0:NCH1, gq * 4:(gq + 1) * 4, :, :], in1=sap(Drow[0:NCH1, ch0:ch0 + 1], [[16, 4], [0, 8], [1, 16]]), op=ALU.mult),
                               ["Vu", "Drow"], ["vd"])
                          P.op("dve", Rec("tensor_tensor", out=ypre[0:NCH1], in0=ps[7][0:NCH1, :].rearrange("p (g i c) -> p g i c", g=4, i=8),
                                                                in1=vd[0:NCH1], op=ALU.add), [RP[7], "vd"], ["ypre"])
                          half = gq % 2
                          P.op("act", Rec("activation", out=zcm[0:NCH1, :, half * 64:(half + 1) * 64].rearrange("p i (g c) -> p g i c", g=4), in_=ypre[0:NCH1], func=AF.Gelu), ["ypre"], ["zcm"])
                          if half == 1:
                              mz = sl * 2 + gq // 2
                              zb = psb(5)
                              for i in range(8):
                                  P.op("pe", Rec("transpose", zb[:, i * 128:i * 128 + NCH1], zcm[0:NCH1, i, :], identb[0:NCH1, 0:NCH1]), ["zcm", "identb"], [RP[5]])
                              zv = zb[:, 0:1024].rearrange("p (i n) -> p i n", i=8)
                              P.op("dve", Rec("tensor_copy", out=sap(Zf[:, mz, 0:1], [[1, 4], [8, NCH1]]), in_=zv[:, 0:4, 0:NCH1]), [RP[5]], ["fb%d" % mz])
                              P.op("dve", Rec("tensor_copy", out=sap(Zf[:, mz, 4:5], [[1, 4], [8, NCHK]]), in_=zv[:, 4:8, 0:NCHK]), [RP[5]], ["fb%d" % mz])
                  ckpt(7)
                  for ri, (dst_p, dst_s, Hx) in enumerate([(ohrp, ohrs, Hr), (ohip, ohis, Hi)]):
                      P.dma("sp", Rec("dma_start", out=dap(dst_p, l * 4096, [[1, 128], [128, 32]]), in_=Hx[:, l, :]), ["H"], ["ohp%d" % ri])
                      if HS[0]:
                          P.dma("sp", Rec("dma_start", out=dap(dst_s, (l * 4 + sidx) * 4096, [[1, 128], [128, 32]]), in_=hend[:, ri, :]), ["hend"], ["ohs%d" % ri])
                  for j in range(4):
                      slab, rslab = w_next("w_glu", l, j * 256)
                      for mm in range(2):
                          m = j * 2 + mm
                          def ev_g(pm, psm, rpm, rps, m=m):
                              P.op("act", Rec("activation", out=Pf[0][:, 0:256], in_=pm[:, 0:256], func=AF.Sigmoid), rpm, ["Pf0"])
                              P.op("act", Rec("activation", out=Pf[1][:, 0:256], in_=pm[:, 256:512], func=AF.Sigmoid), rpm, ["Pf1"])
                              P.op("dve", Rec("tensor_tensor", out=Hb[:, 8 + m, 0:256], in0=Pf[0][:, 0:256], in1=Zf[:, m, 0:256], op=ALU.mult), ["Pf0", "fb%d" % m], ["Hb%d" % (8 + m)])
                              P.op("dve", Rec("tensor_tensor", out=Hb[:, 8 + m, 256:512], in0=Pf[1][:, 0:256], in1=Zf[:, m, 256:512], op=ALU.mult), ["Pf1", "fb%d" % m], ["Hb%d" % (8 + m)])
                              if psm is not None:
                                  P.op("act", Rec("activation", out=sm[:, 0:NS], in_=psm, func=AF.Sigmoid), rps, ["sm0", "sm1"])
                              P.op("dve", Rec("tensor_tensor", out=Hb[:, 8 + m, NP:NT], in0=sm[:, 0:NS], in1=Zf[:, m, NP:NT], op=ALU.mult), ["sm0", "sm1", "fb%d" % m], ["Hb%d" % (8 + m)])
                          proj_chunk(slab, rslab, 8, mm * 128, lambda k: Zf[:, k, :], lambda k: ["fb%d" % k], ev_g)
                  ckpt(8)
                  rmsnorm([Ao[:, k, :] for k in range(8)], ["Ao%d" % k for k in range(8)], lambda k, l=l: gains[:, l, 32 + k:33 + k],
                          [Ao[:, k, :] for k in range(8)], ["Ao%d" % k for k in range(8)], 1024)
                  rmsnorm([Hb[:, 8 + k, :] for k in range(8)], ["Hb%d" % (8 + k) for k in range(8)], lambda k, l=l: gains[:, l, 40 + k:41 + k],
                          [Hb[:, 8 + k, :] for k in range(8)], ["Hb%d" % (8 + k) for k in range(8)], 1024)
                  mix_rhs = lambda k: (Ao[:, k, :] if k < 8 else Hb[:, k, :])
                  mix_regs = lambda k: ["Ao%d" % k] if k < 8 else ["Hb%d" % k]
                  for j in range(8):
                      slab, rslab = w_next("w_out", l, j * 256)
                      for mm in range(2):
                          m = j * 2 + mm
                          def ev_o(pm, psm, rpm, rps, m=m):
                              P.op("dve", Rec("tensor_tensor", out=X[:, m, 0:NP], in0=pm, in1=X[:, m, 0:NP], op=ALU.add), rpm + ["X%d" % m], ["X%d" % m])
                              if psm is not None:
                                  P.op("dve", Rec("tensor_tensor", out=X[:, m, NP:NT], in0=psm, in1=X[:, m, NP:NT], op=ALU.add), rps + ["X%d" % m], ["X%d" % m])
                          proj_chunk(slab, rslab, 16, mm * 128, mix_rhs, mix_regs, ev_o)
                  ckpt(9)
                  rmsnorm([X[:, k, :] for k in range(16)], ["X%d" % k for k in range(16)], lambda k, l=l: gains[:, l, 16 + k:17 + k],
                          [Hb[:, k, :] for k in range(16)], ["Hb%d" % k for k in range(16)], D)
                  for fp in range(0 if int(_os.environ.get("SKIPFFN", "0")) else 4):
                      c0 = fp * 1536
                      nsl = 6 if fp < 3 else 4
                      for j in range(nsl):
                          slg, rslg = w_next("w_gate", l, c0 + j * 256)
                          slu, rslu = w_next("w_up", l, c0 + j * 256)
                          for mm in range(2):
                              ma = j * 2 + mm
                              def ev_gate(pm, psm, rpm, rps, ma=ma):
                                  P.op("act", Rec("activation", out=Pf[0][:, 0:256], in_=pm[:, 0:256], func=AF.Silu), rpm, ["Pf0"])
                                  P.op("act", Rec("activation", out=Pf[1][:, 0:256], in_=pm[:, 256:512], func=AF.Silu), rpm, ["Pf1"])
                                  if psm is not None:
                                      P.op("act", Rec("activation", out=sm[:, 0:NS], in_=psm, func=AF.Silu), rps, ["sm0", "sm1"])
                              def ev_up(pm, psm, rpm, rps, ma=ma):
                                  P.op("dve", Rec("tensor_tensor", out=ACT_[:, ma, 0:256], in0=pm[:, 0:256], in1=Pf[0][:, 0:256], op=ALU.mult), rpm + ["Pf0"], ["fb%d" % ma])
                                  P.op("dve", Rec("tensor_tensor", out=ACT_[:, ma, 256:512], in0=pm[:, 256:512], in1=Pf[1][:, 0:256], op=ALU.mult), rpm + ["Pf1"], ["fb%d" % ma])
                                  if psm is not None:
                                      P.op("dve", Rec("tensor_tensor", out=ACT_[:, ma, NP:NT], in0=psm, in1=sm[:, 0:NS], op=ALU.mult), rps + ["sm0", "sm1"], ["fb%d" % ma])
                              proj_chunk(slg, rslg, 16, mm * 128, hb_rhs, hb_regs, ev_gate)
                              proj_chunk(slu, rslu, 16, mm * 128, hb_rhs, hb_regs, ev_up)
                      nk = nsl * 2
                      for j in range(8):
                          slab, rslab = w_next("w_down", l, j * 256)
                          for mm in range(2):
                              m = j * 2 + mm
                              def ev_d(pm, psm, rpm, rps, m=m):
                                  P.op("dve", Rec("tensor_tensor", out=X[:, m, 0:NP], in0=pm, in1=X[:, m, 0:NP], op=ALU.add), rpm + ["X%d" % m], ["X%d" % m])
                                  if psm is not None:
                                      P.op("dve", Rec("tensor_tensor", out=X[:, m, NP:NT], in0=psm, in1=X[:, m, NP:NT], op=ALU.add), rps + ["X%d" % m], ["X%d" % m])
                              proj_chunk(slab, rslab, nk, mm * 128, lambda k: ACT_[:, k, :], lambda k: ["fb%d" % k], ev_d)

              ckpt(10)
              for k in range(16):
                  q = sqs[k % 2]
                  P.op("act", Rec("activation", out=q[:], in_=X[:, k, :], func=AF.Square), ["X%d" % k], ["sq%d" % (k % 2)])
                  P.op("pe", Rec("matmul", ps[3][:, 0:NP], lhsT=onesb[:], rhs=q[:, 0:NP], start=(k == 0), stop=(k == 15)), ["sq%d" % (k % 2), "onesb"], [RP[3]])
                  if HS[0]:
                      P.op("pe", Rec("matmul", ps[2][:, 480:480 + NS], lhsT=onesb[:], rhs=q[:, NP:NT], start=(k == 0), stop=(k == 15)), ["sq%d" % (k % 2), "onesb"], [RP[2]])
              P.op("act", Rec("activation", out=rstd[:, 0:NP], in_=ps[3][:, 0:NP], func=AF.Sqrt, scale=1.0 / D, bias=epsb[:, 0:1]), [RP[3], "epsb"], ["rstd"])
              if HS[0]:
                  P.op("act", Rec("activation", out=rstd[:, NP:NT], in_=ps[2][:, 480:480 + NS], func=AF.Sqrt, scale=1.0 / D, bias=epsb[:, 0:1]), [RP[2], "epsb"], ["rstd"])
              P.op("dve", Rec("reciprocal", out=rstd[:], in_=rstd[:]), ["rstd"], ["rstd"])
              for k in range(16):
                  P.op("dve", Rec("scalar_tensor_tensor", out=X[:, k, :], in0=X[:, k, :], scalar=gfin[:, k:k + 1], in1=rstd[:], op0=ALU.mult, op1=ALU.mult),
                       ["X%d" % k, "rstd", "gfin"], ["X%d" % k])
              for blk in range(NB + (1 if HS[0] else 0)):
                  if blk < NB:
                      cols = slice(blk * 128, (blk + 1) * 128); nr = 128
                  else:
                      cols = slice(NP, NT); nr = NS
                  for kq in range(4):
                      for kk in range(4):
                          k = kq * 4 + kk
                          P.op("pe", Rec("transpose", ps[4][0:nr, kk * 128:(kk + 1) * 128], X[:, k, cols], identf), ["X%d" % k, "cs"], [RP[4]])
                      P.op("dve", Rec("tensor_copy", out=stage[0:nr, kq * 512:(kq + 1) * 512], in_=ps[4][0:nr, :]), [RP[4]], STG)
                  if blk < NB:
                      r0 = pi * NP + blk * 128
                      P.dma("sp", Rec("dma_start", out=dap(yp, r0 * D, [[D, 128], [1, D]]), in_=stage[:]), STG, ["yp"])
                  else:
                      P.dma("sp", Rec("dma_start", out=dap(ys, sidx * 4 * D, [[D, 4], [1, D]]), in_=stage[0:4, :]), STG, ["ys"])

        except _Stop:
            pass
        if dbg:
            P.dma("sp", Rec("dma_start", out=dap(dbg_t, 0, [[16 * NT, 128], [1, 8 * NT]]), in_=Ao[:].rearrange("p a b -> p (a b)")), ["Ao%d" % k for k in range(8)], ["dbg"])
            P.dma("sp", Rec("dma_start", out=dap(dbg_t, 8 * NT, [[16 * NT, 128], [1, 8 * NT]]), in_=Hb[:, 8:16, :].rearrange("p a b -> p (a b)")), ["Hb%d" % k for k in range(8, 16)], ["dbg"])
        P.op("sp", Rec("nop", ), ["yp", "ys", "okp", "ovp", "oks", "ovs", "ohp0", "ohp1", "ohs0", "ohs1"] + (["dbg"] if dbg else []), [])
        P.emit()
    return nc


def make_consts():
    c = np.zeros((128, 1100), np.float32)
    c[:, 0:128] = np.eye(128, dtype=np.float32)
    i = np.arange(128)[:, None]
    j = np.arange(128)[None, :]
    c[:, 128:256] = np.where(j > i, 0.0, NEG)
    c[:, 256:384] = np.where(j <= i, 0.0, NEG)
    c[:, 384:512] = NEG
    c[:, 512:640] = c[:, 256:384]
    js = np.arange(132)[None, :]
    c[:, 640:772] = np.where((js > i) & (js <= i + 128), 0.0, NEG)
    r = np.arange(128)[:, None] // 16
    cc = np.arange(128)[None, :] // 16
    c[:, 772:900] = (cc >= r).astype(np.float32)
    c[:, 900:965] = np.arange(65, dtype=np.float32)[None, :]
    c[:, 965:1030] = 1.0
    c[:, 965] = 0.0
    return c


_NC_CACHE = {}


def kernel(**inputs):
    inp = {k: np.ascontiguousarray(np.asarray(v)) for k, v in inputs.items()}
    npass = SEQ // NP
    key = (npass, DEPTH)
    if key not in _NC_CACHE:
        _NC_CACHE[key] = build(npass, DEPTH)
    nc = _NC_CACHE[key]
    cst = make_consts()
    wnames = ["norm_mix", "w_in", "attn_sink", "ssm_a_re", "ssm_a_im", "ssm_log_dt", "ssm_b_re", "ssm_b_im",
              "ssm_c_re", "ssm_c_im", "ssm_d", "w_glu", "norm_attn_out", "norm_ssm_out", "w_out", "norm_ffn",
              "w_gate", "w_up", "w_down", "norm_final"]
    in_maps = []
    for c in range(8):
        m = {n: inp[n] for n in wnames}
        m["cst"] = cst
        m["xp"] = inp["x_prompt"][c % 2]
        m["xs"] = inp["x_sample"][4 * c:4 * c + 4].reshape(16, D)
        m["ck"] = inp["cache_k"][:, 4 * c:4 * c + 4].reshape(DEPTH, 4, 128, 128)
        m["cv"] = inp["cache_v"][:, 4 * c:4 * c + 4].reshape(DEPTH, 4, 128, 128)
        m["hr0"] = inp["state_ssm_re"][:, 4 * c:4 * c + 4]
        m["hi0"] = inp["state_ssm_im"][:, 4 * c:4 * c + 4]
        in_maps.append({k: np.ascontiguousarray(v, dtype=np.float32) for k, v in m.items()})
    res = run_bass_kernel_spmd(nc, in_maps, core_ids=list(range(8))).results
    f = np.float32
    y_prompt = np.stack([res[0]["yp"], res[1]["yp"]]).astype(f)
    y_sample = np.concatenate([res[c]["ys"].reshape(4, 4, D) for c in range(8)], 0).astype(f)
    k_p = np.stack([res[0]["okp"], res[1]["okp"]], 1).reshape(DEPTH, 2, 128, 2, 64).astype(f)
    v_p = np.stack([res[0]["ovp"], res[1]["ovp"]], 1).reshape(DEPTH, 2, 128, 2, 64).astype(f)
    hr_p = np.stack([res[0]["ohrp"], res[1]["ohrp"]], 1).astype(f)
    hi_p = np.stack([res[0]["ohip"], res[1]["ohip"]], 1).astype(f)
    k_s = np.concatenate([res[c]["oks"] for c in range(8)], 1).reshape(DEPTH, 32, 128, 2, 64).astype(f)
    v_s = np.concatenate([res[c]["ovs"] for c in range(8)], 1).reshape(DEPTH, 32, 128, 2, 64).astype(f)
    hr_s = np.concatenate([res[c]["ohrs"] for c in range(8)], 1).astype(f)
    hi_s = np.concatenate([res[c]["ohis"] for c in range(8)], 1).astype(f)
    return (y_prompt, y_sample, k_p, v_p, hr_p, hi_p, k_s, v_s, hr_s, hi_s)
```

```python
import contextlib
import math
import numpy as np
import concourse.bass as bass
import concourse.mybir as mybir
from concourse.bass import AP
from concourse.bass_utils import run_bass_kernel_spmd

F32 = mybir.dt.float32
BF16 = mybir.dt.bfloat16
I32 = mybir.dt.int32
AF = mybir.ActivationFunctionType
ALU = mybir.AluOpType
AX = mybir.AxisListType

D = 2048
DEPTH = 4
SEQ = 4096
DIN = 2304
DFF = 5632
NP = 512
NS = 4
NT = NP + NS
NB = NP // 128
NCHK = NP // 8
NCH1 = NCHK + 1
EPS = 1e-5
NEG = -30000.0
TWO_PI = 2.0 * math.pi


class Reg:
    __slots__ = ("name", "w", "rs")

    def __init__(self, name):
        self.name = name
        self.w = None
        self.rs = []


class Op:
    __slots__ = ("eng", "fn", "deps", "marked", "count", "is_dma", "sem", "semval")

    def __init__(self, eng, fn, is_dma=False):
        self.eng = eng
        self.fn = fn
        self.deps = []
        self.marked = False
        self.count = 0
        self.is_dma = is_dma
        self.sem = None
        self.semval = 0


ENGS = ("pe", "act", "dve", "pool", "sp")
N_DMA_SEMS = 32


class Prog:
    def __init__(self, nc):
        self.nc = nc
        self.ops = {e: [] for e in ENGS}
        self.dma_last = [None] * N_DMA_SEMS
        self.dma_cum = [0] * N_DMA_SEMS
        self.dma_rr = 0
        self.regs = {}

    def R(self, name):
        r = self.regs.get(name)
        if r is None:
            r = self.regs[name] = Reg(name)
        return r

    def _deps(self, op, reads, writes):
        deps = []
        for r in reads:
            if r.w is not None:
                deps.append(r.w)
        for w in writes:
            if w.w is not None:
                deps.append(w.w)
            deps.extend(w.rs)
        for r in reads:
            r.rs.append(op)
        for w in writes:
            w.w = op
            w.rs = []
        seen = set()
        out = []
        for d in deps:
            if id(d) in seen or d is op:
                continue
            seen.add(id(d))
            if op.eng == "pe" and d.eng == "pe" and not d.is_dma and not op.is_dma:
                continue
            out.append(d)
            d.marked = True
        op.deps = out

    def op(self, eng, fn, reads=(), writes=()):
        o = Op(eng, fn)
        writes = list(writes) + [x for x in reads if isinstance(x, str) and x.startswith("ps")]
        reads = [x for x in reads if not (isinstance(x, str) and x.startswith("ps"))]
        self._deps(o, [self.R(x) if isinstance(x, str) else x for x in reads],
                   [self.R(x) if isinstance(x, str) else x for x in writes])
        self.ops[eng].append(o)
        return o

    def dma(self, queue, fn, reads=(), writes=()):
        o = Op(queue, fn, is_dma=True)
        s = self.dma_rr
        self.dma_rr = (self.dma_rr + 1) % N_DMA_SEMS
        o.sem = s
        self.dma_cum[s] += 16
        o.semval = self.dma_cum[s]
        self._deps(o, [self.R(x) if isinstance(x, str) else x for x in reads],
                   [self.R(x) if isinstance(x, str) else x for x in writes])
        if self.dma_last[s] is not None:
            o.deps.append(self.dma_last[s])
        self.dma_last[s] = o
        self.ops[queue].append(o)
        return o

    def emit(self):
        nc = self.nc
        for e in ENGS:
            c = 0
            for o in self.ops[e]:
                if o.is_dma:
                    continue
                if o.marked:
                    c += 1
                o.count = c
        with contextlib.ExitStack() as st:
            esem = {e: st.enter_context(nc.semaphore("es_" + e)) for e in ENGS}
            dsem = [st.enter_context(nc.semaphore("ds_%d" % i)) for i in range(N_DMA_SEMS)]
            block = st.enter_context(nc.Block())
            handles = {"pe": block.tensor, "act": block.scalar, "dve": block.vector,
                       "pool": block.gpsimd, "sp": block.sync}
            for e in ENGS:
                ops = self.ops[e]

                def body(eh, ops=ops, e=e):
                    waited = {}
                    for o in ops:
                        for d in o.deps:
                            if d.is_dma:
                                key, sem, val = ("d", d.sem), dsem[d.sem], d.semval
                            else:
                                key, sem, val = ("e", d.eng), esem[d.eng], d.count
                            if waited.get(key, 0) >= val:
                                continue
                            waited[key] = val
                            eh.wait_ge(sem, val)
                        inst = o.fn(eh)
                        if o.is_dma:
                            inst.then_inc(dsem[o.sem], 16)
                        elif o.marked:
                            inst.then_inc(esem[e], 1)

                handles[e](body)


def Rec(name, *a, **k):
    return lambda e: getattr(e, name)(*a, **k)


def sap(base, dims):
    return AP(tensor=base.tensor, offset=base.offset, ap=[list(base.ap[0])] + [[int(a), int(b)] for a, b in dims])


def dap(t, offset, dims):
    return AP(tensor=t, offset=int(offset), ap=[[int(a), int(b)] for a, b in dims])


class _Stop(Exception):
    pass


def build(npass=8, depth=DEPTH, dbg=None, stop=None):
    nc = bass.Bass("TRN2", target_bir_lowering=False)
    P = Prog(nc)
    dt_in = lambda n, s: nc.dram_tensor(n, list(s), F32, kind="ExternalInput")
    dt_out = lambda n, s: nc.dram_tensor(n, list(s), F32, kind="ExternalOutput")
    xp = dt_in("xp", [SEQ, D])
    xs = dt_in("xs", [16, D])
    ck = dt_in("ck", [DEPTH, 4, 128, 128])
    cv = dt_in("cv", [DEPTH, 4, 128, 128])
    hr0 = dt_in("hr0", [DEPTH, 4, 64, 64])
    hi0 = dt_in("hi0", [DEPTH, 4, 64, 64])
    W = {}
    for n, s in [("norm_mix", [DEPTH, D]), ("w_in", [DEPTH, D, DIN]), ("attn_sink", [DEPTH, 16]),
                 ("ssm_a_re", [DEPTH, 64, 64]), ("ssm_a_im", [DEPTH, 64, 64]), ("ssm_log_dt", [DEPTH, 64]),
                 ("ssm_b_re", [DEPTH, 64, 64, 16]), ("ssm_b_im", [DEPTH, 64, 64, 16]),
                 ("ssm_c_re", [DEPTH, 64, 16, 64]), ("ssm_c_im", [DEPTH, 64, 16, 64]),
                 ("ssm_d", [DEPTH, 1024]), ("w_glu", [DEPTH, 1024, 1024]),
                 ("norm_attn_out", [DEPTH, 1024]), ("norm_ssm_out", [DEPTH, 1024]),
                 ("w_out", [DEPTH, D, D]), ("norm_ffn", [DEPTH, D]),
                 ("w_gate", [DEPTH, D, DFF]), ("w_up", [DEPTH, D, DFF]), ("w_down", [DEPTH, DFF, D]),
                 ("norm_final", [D])]:
        W[n] = dt_in(n, s)
    cst = dt_in("cst", [128, 1032])
    yp = dt_out("yp", [SEQ, D])
    ys = dt_out("ys", [16, D])
    okp = dt_out("okp", [DEPTH, 128, 128])
    ovp = dt_out("ovp", [DEPTH, 128, 128])
    ohrp = dt_out("ohrp", [DEPTH, 64, 64])
    ohip = dt_out("ohip", [DEPTH, 64, 64])
    oks = dt_out("oks", [DEPTH, 4, 128, 128])
    ovs = dt_out("ovs", [DEPTH, 4, 128, 128])
    ohrs = dt_out("ohrs", [DEPTH, 4, 64, 64])
    ohis = dt_out("ohis", [DEPTH, 4, 64, 64])
    scrT = nc.dram_tensor("scrT", [DEPTH, 128, 1088], F32, kind="Internal")
    scrM = nc.dram_tensor("scrM", [DEPTH * 32, 128, 512], BF16, kind="Internal")
    scrC = nc.dram_tensor("scrC", [DEPTH * 4, 128, 4096], BF16, kind="Internal")
    scrG = nc.dram_tensor("scrG", [DEPTH * 4, 128, 2048], BF16, kind="Internal")
    dbg_t = nc.dram_tensor("dbg", [128, 16 * NT], BF16, kind="ExternalOutput") if dbg else None

    st = contextlib.ExitStack()
    with st:
        st.enter_context(nc.allow_non_contiguous_dma(reason="small strided parameter loads"))
        sb = lambda n, s, d=F32: st.enter_context(nc.sbuf_tensor(n, list(s), d))
        X = sb("X", [128, 16, NT])
        Hb = sb("Hb", [128, 16, NT], BF16)
        Ao = sb("Ao", [128, 8, NT], BF16)
        FB = sb("FB", [128, 6 * NT])
        ACT_ = FB[:].bitcast(BF16).rearrange("p (a b) -> p a b", a=12)
        Zf = ACT_
        stage = FB[:, 0:D]
        rstd = sb("rstd", [128, NT])
        sqs = [sb("sq%d" % i, [128, NT], BF16) for i in range(2)]
        NSLOT = 4
        ring = [sb("ring%d" % i, [128, 16, 256], BF16) for i in range(NSLOT)]
        cs = sb("cs", [128, 1032])
        identb = sb("identb", [128, 128], BF16)
        onesb = sb("onesb", [128, 128], BF16)
        maskb = sb("maskb", [128, 256], BF16)
        maskfb = sb("maskfb", [128, 256], BF16)
        masksb = sb("masksb", [128, 132], BF16)
        gains = sb("gains", [128, DEPTH, 64])
        gfin = sb("gfin", [128, 16])
        sinkb = sb("sinkb", [128, DEPTH * 16])
        Kb = [sb("Kb%d" % i, [128, 128 + NP], BF16) for i in range(2)]
        KS = [sb("KS%d" % i, [128, 132], BF16) for i in range(2)]
        Vb = sb("Vb", [128, NB + 1, 4, 128], BF16)
        VSc = sb("VSc", [128, 4, 128], BF16)
        VSn = sb("VSn", [4, 4, 128], BF16)
        Khalo = sb("Khalo", [128, DEPTH, 2, 128], BF16)
        Vhalo = sb("Vhalo", [128, DEPTH, 4, 128], BF16)
        kvst = sb("kvst", [128, 256])
        kvss = kvst[0:4, :]
        cpy = kvst
        cks = sb("cks", [128, 128], BF16)
        PFB = sb("PFB", [128, 512])
        Pf = [PFB[:, 0:256], PFB[:, 256:512]]
        Pe = [PFB[:].bitcast(BF16)[:, i * 256:(i + 1) * 256] for i in range(3)]
        Pn = [sb("Pn%d" % i, [128, 256], BF16) for i in range(3)]
        PT = [sb("PT%d" % i, [128, 2, 128], BF16) for i in range(3)]
        sm = sb("sm", [128, 24])
        S1 = lambda n: sb(n, [128, 32])
        lam_r, lam_i, dtt, mag, th, abr, abi, t1, t2, t3, t4, crr, cii, rho, phi, iar, iai = [
            S1(n) for n in "lam_r lam_i dtt mag th abr abi t1 t2 t3 t4 crr cii rho phi iar iai".split()]
        ti = sb("ti", [128, 32], I32)
        Br = sb("Br", [128, 32, 16]); Bi = sb("Bi", [128, 32, 16])
        Bbr = Br; Bbi = Bi
        Cn = sb("Cn", [16, 16, 64])
        Cr = sb("Cr", [128, 32, 16]); Ci = sb("Ci", [128, 32, 16])
        PCr = sb("PCr", [128, 32, 16]); PCi = sb("PCi", [128, 32, 16])
        PBr = sb("PBr", [128, 32, 8]); PBi = sb("PBi", [128, 32, 8])
        Drow = sb("Drow", [128, 1024])
        Hr = sb("Hr", [128, DEPTH, 32]); Hi = sb("Hi", [128, DEPTH, 32])
        h0r = sb("h0r", [128, 32]); h0i = sb("h0i", [128, 32])
        Vu = sb("Vu", [128, 16, 8, 16], BF16)
        Ug = [sb("Ug%d" % i, [128, NCH1], BF16) for i in range(16)]
        MBt = [sb("MBt%d" % i, [128, 128], BF16) for i in range(2)]
        MBpT = sb("MBpT", [128, 2, 4, 128], BF16)
        TgSt = sb("TgSt", [128, 16, 128], BF16)
        MCBt = sb("MCBt", [128, 8, 2, 256], BF16)
        TgS = [TgSt[:, i, :] for i in range(16)]
        MCB = [[MCBt[:, i, r, :] for r in range(2)] for i in range(8)]
        tA = sb("tA", [128, 256]); tB = sb("tB", [128, 256])
        tmask = sb("tmask", [128, 128])
        NPB = 8
        Tc = sb("Tc", [128, NPB, 65]); Ts = sb("Ts", [128, NPB, 65])
        Dr = sb("Dr", [128, NPB, 65]); Di = sb("Di", [128, NPB, 65])
        d0 = sb("d0", [128, NPB, 65])
        ang = sb("ang", [128, NPB, 65])
        u1 = sb("u1", [128, NPB, 65]); u2 = sb("u2", [128, NPB, 65])
        Sbr = sb("Sbr", [128, NPB, 65], BF16); Sbi = sb("Sbi", [128, NPB, 65], BF16)
        Xs = sb("Xs", [128, 2, NPB])
        hend = sb("hend", [128, 2, 32])
        vd = sb("vd", [128, 4, 8, 16])
        ypre = sb("ypre", [128, 4, 8, 16])
        zcm = sb("zcm", [128, 8, 128], BF16)
        ps = [st.enter_context(nc.psum_tensor("ps%d" % i, [128, 512], F32)) for i in range(8)]
        RP = ["ps%d" % i for i in range(8)]
        STG = ["fb%d" % i for i in range(8)]

        def psb(b):
            return ps[b][:].bitcast(BF16)

        P.dma("sp", Rec("dma_start", out=cs[:], in_=cst.ap()), [], ["cs"])
        P.op("dve", Rec("tensor_copy", out=identb[:], in_=cs[:, 0:128]), ["cs"], ["identb"])
        P.op("dve", Rec("tensor_copy", out=maskb[:], in_=cs[:, 128:384]), ["cs"], ["maskb"])
        P.op("dve", Rec("tensor_copy", out=maskfb[:], in_=cs[:, 384:640]), ["cs"], ["maskfb"])
        P.op("dve", Rec("tensor_copy", out=masksb[:], in_=cs[:, 640:772]), ["cs"], ["masksb"])
        P.op("dve", Rec("tensor_copy", out=tmask[:], in_=cs[:, 772:900]), ["cs"], ["tmask"])
        P.op("pool", Rec("memset", onesb[:], 1.0), [], ["onesb"])
        identf = cs[:, 0:128]
        posf = cs[:, 900:965]
        for l in range(depth):
            for nm, off, nk in [("norm_mix", 0, 16), ("norm_ffn", 16, 16), ("norm_attn_out", 32, 8), ("norm_ssm_out", 40, 8)]:
                src = dap(W[nm], l * nk * 128, [[1, 128], [128, nk]])
                P.dma("sp", Rec("dma_start", out=gains[:, l, off:off + nk], in_=src), [], ["gains"])
        P.dma("sp", Rec("dma_start", out=gfin[:], in_=dap(W["norm_final"], 0, [[1, 128], [128, 16]])), [], ["gfin"])
        P.dma("sp", Rec("dma_start", out=sinkb[:], in_=dap(W["attn_sink"], 0, [[0, 128], [1, DEPTH * 16]])), [], ["sinkb"])
        P.op("pool", Rec("memset", Hr[:], 0.0), [], ["H"])
        P.op("pool", Rec("memset", Hi[:], 0.0), [], ["H"])
        P.op("pool", Rec("memset", Khalo[:], 0.0), [], ["Khalo"])
        P.op("pool", Rec("memset", Vhalo[:], 0.0), [], ["Vhalo"])
        P.op("pool", Rec("memset", Vb[:], 0.0), [], ["Vb"])
        P.op("pool", Rec("memset", VSc[:], 0.0), [], ["VSc"])
        P.op("pool", Rec("memset", VSn[:], 0.0), [], ["VSn"])
        P.op("pool", Rec("memset", Vu[:], 0.0), [], ["Vu"])
        P.op("pool", Rec("memset", MBpT[:], 0.0), [], ["MBp0", "MBp1"])

        sched = []
        for ps_i in range(npass):
            for l in range(depth):
                sched.append(("w_in", l, 0, 16, 1024, 256))
                for j in range(4):
                    sched.append(("w_in", l, 0, 16, j * 256, 256))
                for j in range(4):
                    sched.append(("w_in", l, 0, 16, 1280 + j * 256, 256))
                for j in range(4):
                    sched.append(("w_glu", l, 0, 8, j * 256, 256))
                for j in range(8):
                    sched.append(("w_out", l, 0, 16, j * 256, 256))
                for fp in range(4):
                    c0 = fp * 1536
                    nsl = 6 if fp < 3 else 4
                    for j in range(nsl):
                        sched.append(("w_gate", l, 0, 16, c0 + j * 256, 256))
                        sched.append(("w_up", l, 0, 16, c0 + j * 256, 256))
                    nk = nsl * 2
                    for j in range(8):
                        sched.append(("w_down", l, c0, nk, j * 256, 256))
        wcur = [0, 0]
        NCOLS = {"w_in": DIN, "w_glu": 1024, "w_out": D, "w_gate": DFF, "w_up": DFF, "w_down": D}
        NROWS = {"w_in": D, "w_glu": 1024, "w_out": D, "w_gate": D, "w_up": D, "w_down": DFF}

        def w_issue(upto):
            while wcur[1] < min(upto, len(sched)):
                i = wcur[1]
                nm, l, r0, nk, c0, ncl = sched[i]
                slot = i % NSLOT
                if nm == "w_krep":
                    for kvh in range(2):
                        for r_ in range(2):
                            src = dap(W["w_in"], l * D * DIN + 1024 + kvh * 64, [[DIN, 128], [128 * DIN, 16], [1, 64]])
                            c_ = kvh * 128 + r_ * 64
                            P.dma("pool", Rec("dma_start", out=ring[slot][:, :, c_:c_ + 64], in_=src),
                                  [], ["ring%d" % slot])
                    wcur[1] += 1
                    continue
                ncols = NCOLS[nm]
                src = dap(W[nm], l * NROWS[nm] * ncols + r0 * ncols + c0, [[ncols, 128], [128 * ncols, nk], [1, ncl]])
                P.dma("pool", Rec("dma_start", out=ring[slot][:, 0:nk, 0:ncl], in_=src),
                      [], ["ring%d" % slot])
                wcur[1] += 1

        def w_next(nm, l, c0):
            i = wcur[0]
            assert sched[i][0] == nm and sched[i][1] == l and sched[i][4] == c0, (sched[i], nm, l, c0)
            w_issue(i + NSLOT - 1)
            wcur[0] += 1
            return ring[i % NSLOT], "ring%d" % (i % NSLOT)

        pcnt = [0]
        HS = [True]

        def proj_chunk(slab, rslab, nk, mcol, rhs_fn, rhs_regs, evac_fn, lhs_rep=False):
            b = pcnt[0] % 2
            sbk = 2 + pcnt[0] % 2
            so = ((pcnt[0] // 2) % 8) * 4
            pcnt[0] += 1
            for k in range(nk):
                lhs = slab[:, k, mcol:mcol + 128]
                r = rhs_fn(k)
                P.op("pe", Rec("matmul", ps[b][:, 0:NP], lhsT=lhs, rhs=r[:, 0:NP], start=(k == 0), stop=(k == nk - 1)),
                     [rslab] + rhs_regs(k), [RP[b]])
                if HS[0]:
                    P.op("pe", Rec("matmul", ps[sbk][:, so:so + NS], lhsT=lhs, rhs=r[:, NP:NT], start=(k == 0), stop=(k == nk - 1)),
                         [rslab] + rhs_regs(k), [RP[sbk]])
            evac_fn(ps[b][:, 0:NP], ps[sbk][:, so:so + NS] if HS[0] else None, [RP[b]], [RP[sbk]])

        def rmsnorm(srcs, src_regs, gain_fn, dsts, dst_regs, dim):
            nk = len(srcs)
            for k in range(nk):
                q = sqs[k % 2]
                P.op("act", Rec("activation", out=q[:], in_=srcs[k], func=AF.Square), [src_regs[k]], ["sq%d" % (k % 2)])
                P.op("pe", Rec("matmul", ps[3][:, 0:NP], lhsT=onesb[:], rhs=q[:, 0:NP], start=(k == 0), stop=(k == nk - 1)),
                     ["sq%d" % (k % 2), "onesb"], [RP[3]])
                if HS[0]:
                    P.op("pe", Rec("matmul", ps[2][:, 480:480 + NS], lhsT=onesb[:], rhs=q[:, NP:NT], start=(k == 0), stop=(k == nk - 1)),
                         ["sq%d" % (k % 2), "onesb"], [RP[2]])
            P.op("act", Rec("activation", out=rstd[:, 0:NP], in_=ps[3][:, 0:NP], func=AF.Sqrt, scale=1.0 / dim, bias=epsb[:, 0:1]), [RP[3], "epsb"], ["rstd"])
            if HS[0]:
                P.op("act", Rec("activation", out=rstd[:, NP:NT], in_=ps[2][:, 480:480 + NS], func=AF.Sqrt, scale=1.0 / dim, bias=epsb[:, 0:1]), [RP[2], "epsb"], ["rstd"])
            P.op("dve", Rec("reciprocal", out=rstd[:], in_=rstd[:]), ["rstd"], ["rstd"])
            for k in range(nk):
                P.op("dve", Rec("scalar_tensor_tensor", out=dsts[k], in0=srcs[k], scalar=gain_fn(k), in1=rstd[:], op0=ALU.mult, op1=ALU.mult),
                     [src_regs[k], "rstd", "gains", "gfin"], [dst_regs[k]])

        epsb = sb("epsb", [128, 1])
        P.op("pool", Rec("memset", epsb[:], EPS), [], ["epsb"])
        halfpi = sb("halfpi", [128, 1])
        P.op("pool", Rec("memset", halfpi[:], math.pi / 2), [], ["halfpi"])

        def sincos(eng_v, angle, itmp, a1, a2, out_c, out_s, regs_in, rtag):
            P.op("dve", Rec("tensor_scalar", out=a1, in0=angle, scalar1=1.0 / TWO_PI, scalar2=None, op0=ALU.mult), regs_in, [rtag + "a1"])
            P.op("dve", Rec("tensor_copy", out=itmp, in_=a1), [rtag + "a1"], [rtag + "a2"])
            P.op("dve", Rec("tensor_copy", out=a1, in_=itmp), [rtag + "a2"], [rtag + "a1"])
            P.op("dve", Rec("scalar_tensor_tensor", out=angle, in0=a1, scalar=-TWO_PI, in1=angle, op0=ALU.mult, op1=ALU.add), [rtag + "a1"] + regs_in, regs_in)
            P.op("act", Rec("activation", out=a1, in_=angle, func=AF.Sin, scale=0.5), regs_in, [rtag + "a1"])
            P.op("act", Rec("activation", out=a2, in_=angle, func=AF.Sin, scale=0.5, bias=halfpi[:, 0:1]), regs_in + ["halfpi"], [rtag + "a2"])
            P.op("dve", Rec("scalar_tensor_tensor", out=out_s, in0=a1, scalar=2.0, in1=a2, op0=ALU.mult, op1=ALU.mult), [rtag + "a1", rtag + "a2"], [rtag + "s"])
            P.op("dve", Rec("tensor_tensor", out=a2, in0=a1, in1=a1, op=ALU.mult), [rtag + "a1"], [rtag + "a2"])
            P.op("dve", Rec("tensor_scalar", out=out_c, in0=a2, scalar1=-2.0, scalar2=1.0, op0=ALU.mult, op1=ALU.add), [rtag + "a2"], [rtag + "c"])

        def cmul(eng, o_r, o_i, a_r, a_i, b_r, b_i, tmp1, tmp2, rin, rout):
            P.op(eng, Rec("tensor_tensor", out=tmp1, in0=a_i, in1=b_i, op=ALU.mult), rin, rout)
            P.op(eng, Rec("tensor_tensor", out=tmp2, in0=a_i, in1=b_r, op=ALU.mult), rin, rout)
            P.op(eng, Rec("tensor_tensor", out=o_r, in0=a_r, in1=b_r, op=ALU.mult), rin, rout)
            P.op(eng, Rec("tensor_tensor", out=o_i, in0=a_r, in1=b_i, op=ALU.mult), rin, rout)
            P.op(eng, Rec("tensor_tensor", out=o_r, in0=o_r, in1=tmp1, op=ALU.subtract), rin, rout)
            P.op(eng, Rec("tensor_tensor", out=o_i, in0=o_i, in1=tmp2, op=ALU.add), rin, rout)

        def ckpt(n):
            if stop is not None and n >= stop:
                raise _Stop()

        try:
          for pi in range(npass):
              sidx = pi % 4
              HS[0] = pi < 4
              for blk in range(NB):
                  r0 = pi * NP + blk * 128
                  P.dma("sp", Rec("dma_start", out=stage[:], in_=dap(xp, r0 * D, [[D, 128], [1, D]])), [], STG)
                  for kq in range(4):
                      for kk in range(4):
                          k = kq * 4 + kk
                          P.op("pe", Rec("transpose", ps[4][:, kk * 128:(kk + 1) * 128], stage[:, k * 128:(k + 1) * 128], identf),
                               STG + ["cs"], [RP[4]])
                      P.op("dve", Rec("tensor_copy", out=X[:, kq * 4:kq * 4 + 4, blk * 128:(blk + 1) * 128],
                                                                      in_=ps[4][:].rearrange("p (a b) -> p a b", a=4)),
                           [RP[4]], ["X%d" % k for k in range(kq * 4, kq * 4 + 4)])
              P.dma("sp", Rec("dma_start", out=stage[0:4, :], in_=dap(xs, sidx * 4 * D, [[D, 4], [1, D]])), [], STG)
              for k in range(16 if HS[0] else 0):
                  P.op("pe", Rec("transpose", ps[4][:, k * 4:(k + 1) * 4], stage[0:4, k * 128:(k + 1) * 128], identf[0:4, 0:4]),
                       STG + ["cs"], [RP[4]])
              if HS[0]:
                  P.op("dve", Rec("tensor_copy", out=X[:, :, NP:NT], in_=ps[4][:, 0:64].rearrange("p (a b) -> p a b", a=16)),
                       [RP[4]], ["X%d" % k for k in range(16)])

              ckpt(1)
              for l in range(depth):
                  rmsnorm([X[:, k, :] for k in range(16)], ["X%d" % k for k in range(16)], lambda k, l=l: gains[:, l, k:k + 1],
                          [Hb[:, k, :] for k in range(16)], ["Hb%d" % k for k in range(16)], D)
                  hb_rhs = lambda k: Hb[:, k, :]
                  hb_regs = lambda k: ["Hb%d" % k]
                  ckpt(2)
                  for kvh in range(2):
                      P.op("pool", Rec("tensor_copy", out=Kb[kvh][:, 0:128], in_=Khalo[:, l, kvh, :]), ["Khalo"], ["Kb%d" % kvh])
                  P.op("pool", Rec("tensor_copy", out=Vb[:, 0, :, :], in_=Vhalo[:, l, :, :]), ["Vhalo"], ["Vb"])
                  ckpt(2.1)
                  slab, rslab = w_next("w_in", l, 1024)

                  def ev_k(pm, psm, rpm, rps):
                      for kvh in range(2):
                          hs = slice(kvh * 64, kvh * 64 + 64)
                          P.op("act", Rec("activation", out=Kb[kvh][hs, 128:128 + NP], in_=pm[hs, :], func=AF.Copy), rpm, ["Kb%d" % kvh])
                          if psm is not None:
                              P.op("act", Rec("activation", out=KS[kvh][hs, 128:132], in_=psm[hs, :], func=AF.Copy), rps, ["KS%d" % kvh])
                  proj_chunk(slab, rslab, 16, 0, hb_rhs, hb_regs, ev_k)
                  for kvh in range(2):
                      hs = slice(kvh * 64, kvh * 64 + 64)
                      ho = slice((1 - kvh) * 64, (1 - kvh) * 64 + 64)
                      P.dma("sp", Rec("dma_start", out=Kb[kvh][ho, 128:128 + NP], in_=Kb[kvh][hs, 128:128 + NP]), ["Kb%d" % kvh], ["Kb%d" % kvh])
                  ckpt(2.2)
                  import os as _os
                  for blk in range(NB + (1 if HS[0] else 0)):
                      if blk < NB:
                          cols = slice(blk * 128, (blk + 1) * 128); mrows = 128
                      else:
                          cols = slice(NP, NT); mrows = NS
                      for k in range(16):
                          P.op("pe", Rec("matmul", ps[5][0:mrows, 0:256], lhsT=Hb[:, k, cols], rhs=slab[:, k, 0:256], start=(k == 0), stop=(k == 15)),
                               [rslab, "Hb%d" % k], [RP[5]])
                      if blk < NB:
                          for kvh in range(2 - 2 * int(_os.environ.get("SKIP_A", "0"))):
                              for odd in range(2):
                                  P.op("dve", Rec("tensor_copy", out=Vb[:, blk + 1, kvh * 2 + odd, odd * 64:odd * 64 + 64], in_=ps[5][:, 128 + kvh * 64:128 + kvh * 64 + 64]),
                                       [RP[5]], ["Vb"])
                          if blk == NB - 1 and not int(_os.environ.get("SKIP_B", "0")):
                              P.op("act", Rec("activation", out=kvst[:], in_=ps[5][:, 0:256], func=AF.Copy), [RP[5]], ["kvst"])
                              P.dma("sp", Rec("dma_start", out=okp.ap()[l], in_=kvst[:, 0:128]), ["kvst"], ["okp"])
                              P.dma("sp", Rec("dma_start", out=ovp.ap()[l], in_=kvst[:, 128:256]), ["kvst"], ["ovp"])
                      else:
                          for kvh in range(2):
                              for odd in range(2):
                                  P.op("dve", Rec("tensor_copy", out=VSn[0:4, kvh * 2 + odd, odd * 64:odd * 64 + 64], in_=ps[5][0:4, 128 + kvh * 64:128 + kvh * 64 + 64]),
                                       [RP[5]], ["VSn"])
                          P.op("act", Rec("activation", out=kvss[:], in_=ps[5][0:4, 0:256], func=AF.Copy), [RP[5]], ["kvst"])
                          P.dma("sp", Rec("dma_start", out=oks.ap()[l, sidx, 124:128, :], in_=kvss[:, 0:128]), ["kvst"], ["oks"])
                          P.dma("sp", Rec("dma_start", out=ovs.ap()[l, sidx, 124:128, :], in_=kvss[:, 128:256]), ["kvst"], ["ovs"])
                          P.dma("sp", Rec("dma_start", out=cpy[0:124, 0:128], in_=ck.ap()[l, sidx, 4:128, :]), [], ["kvst"])
                          P.dma("sp", Rec("dma_start", out=cpy[0:124, 128:256], in_=cv.ap()[l, sidx, 4:128, :]), [], ["kvst"])
                          P.dma("sp", Rec("dma_start", out=oks.ap()[l, sidx, 0:124, :], in_=cpy[0:124, 0:128]), ["kvst"], ["oks"])
                          P.dma("sp", Rec("dma_start", out=ovs.ap()[l, sidx, 0:124, :], in_=cpy[0:124, 128:256]), ["kvst"], ["ovs"])
                  ckpt(2.3)
                  for kvh in range(2):
                      P.op("pool", Rec("tensor_copy", out=Khalo[:, l, kvh, :], in_=Kb[kvh][:, NP:NP + 128]), ["Kb%d" % kvh], ["Khalo"])
                  P.op("pool", Rec("tensor_copy", out=Vhalo[:, l, :, :], in_=Vb[:, NB, :, :]), ["Vb"], ["Vhalo"])
                  ckpt(2.4)
                  if HS[0]:
                      P.dma("pool", Rec("dma_start", out=cks[:, 0:128], in_=ck.ap()[l, sidx]), [], ["cks"])
                      P.op("pe", Rec("matmul", ps[5][:, 256:384], lhsT=cks[:, 0:128], rhs=identb[:], start=True, stop=True), ["cks", "identb"], [RP[5]])
                  for kvh in range(2 if HS[0] else 0):
                      hs = slice(kvh * 64, kvh * 64 + 64)
                      ho = slice((1 - kvh) * 64, (1 - kvh) * 64 + 64)
                      P.op("act", Rec("activation", out=KS[kvh][hs, 0:128], in_=ps[5][hs, 256:384], func=AF.Copy), [RP[5]], ["KS%d" % kvh])
                      P.dma("sp", Rec("dma_start", out=KS[kvh][ho, :], in_=KS[kvh][hs, :]), ["KS%d" % kvh], ["KS%d" % kvh])
                  for kvh in range(2 if HS[0] else 0):
                      for odd in range(2):
                          P.dma("pool", Rec("dma_start", out=VSc[:, kvh * 2 + odd, odd * 64:odd * 64 + 64], in_=cv.ap()[l, sidx, :, kvh * 64:kvh * 64 + 64]), [], ["VSc"])
                  ckpt(3)
                  for j in range(4):
                      slab, rslab = w_next("w_in", l, j * 256)
                      for mm in range(2):
                          m = j * 2 + mm
                          def ev_q(pm, psm, rpm, rps, m=m):
                              P.op("act", Rec("activation", out=Ao[:, m, 0:NP], in_=pm, func=AF.Copy, scale=0.125), rpm, ["Ao%d" % m])
                              if psm is not None:
                                  P.op("act", Rec("activation", out=Ao[:, m, NP:NT], in_=psm, func=AF.Copy, scale=0.125), rps, ["Ao%d" % m])
                          proj_chunk(slab, rslab, 16, mm * 128, hb_rhs, hb_regs, ev_q)
                  ckpt(4)
                  units = []
                  for nb in range(NB):
                      for m in range(8):
                          mk = maskfb if (pi == 0 and nb == 0) else maskb
                          for odd in range(2):
                              units.append((m, odd, 128, slice(nb * 128, (nb + 1) * 128), (lambda kvh, nb=nb: Kb[kvh][:, nb * 128:nb * 128 + 256]), 256, mk,
                                            [(128, (lambda v, nb=nb: Vb[:, nb, v, :])), (128, (lambda v, nb=nb: Vb[:, nb + 1, v, :]))]))
                  for m in range(8 if HS[0] else 0):
                      for odd in range(2):
                          units.append((m, odd, NS, slice(NP, NT), (lambda kvh: KS[kvh][:, 0:132]), 132, masksb,
                                        [(128, (lambda v: VSc[:, v, :])), (NS, (lambda v: VSn[:, v, :]))]))

                  def stA(i, l=l):
                      m, odd, nq, qcols, keyfn, nkeys, mk, segs = units[i]
                      h = 2 * m + odd; kvh = h // 8; hp = odd * 64; u = i % 3; sbk = 5 + u; c0 = u * 8
                      scol = sinkb[0:nq, l * 16 + h:l * 16 + h + 1]
                      P.op("pe", Rec("matmul", ps[sbk][0:nq, 0:nkeys], lhsT=Ao[hp:hp + 64, m, qcols], rhs=keyfn(kvh)[hp:hp + 64, :], start=True, stop=False),
                           ["Ao%d" % m, "Kb%d" % kvh, "KS%d" % kvh], [RP[sbk]])
                      P.op("pe", Rec("matmul", ps[sbk][0:nq, 0:nkeys], lhsT=identb[0:nq, 0:nq], rhs=mk[0:nq, 0:nkeys], start=False, stop=True),
                           ["identb", "maskb", "maskfb", "masksb"], [RP[sbk]])
                      P.op("dve", Rec("reduce_max", out=sm[0:nq, c0:c0 + 1], in_=ps[sbk][0:nq, 0:nkeys], axis=AX.X), [RP[sbk]], ["sm%d" % u])
                      P.op("dve", Rec("tensor_scalar", out=sm[0:nq, c0 + 1:c0 + 2], in0=sm[0:nq, c0:c0 + 1], scalar1=scol, scalar2=-1.0, op0=ALU.max, op1=ALU.mult),
                           ["sm%d" % u, "sinkb"], ["sm%d" % u])
                      P.op("act", Rec("activation", out=Pe[u][0:nq, 0:nkeys], in_=ps[sbk][0:nq, 0:nkeys], func=AF.Exp, bias=sm[0:nq, c0 + 1:c0 + 2], accum_out=sm[0:nq, c0 + 2:c0 + 3]),
                           [RP[sbk], "sm%d" % u], ["Pf%d" % u, "sm%d" % u])
                      P.op("act", Rec("activation", out=sm[0:nq, c0 + 3:c0 + 4], in_=sm[0:nq, c0 + 1:c0 + 2], func=AF.Exp, bias=scol),
                           ["sm%d" % u, "sinkb"], ["sm%d" % u])

                  def stB1(i):
                      m, odd, nq, qcols, keyfn, nkeys, mk, segs = units[i]
                      u = i % 3; sbk = 5 + u; c0 = u * 8
                      P.op("dve", Rec("tensor_tensor", out=sm[0:nq, c0 + 4:c0 + 5], in0=sm[0:nq, c0 + 2:c0 + 3], in1=sm[0:nq, c0 + 3:c0 + 4], op=ALU.add), ["sm%d" % u], ["sm%d" % u])
                      P.op("dve", Rec("reciprocal", out=sm[0:nq, c0 + 5:c0 + 6], in_=sm[0:nq, c0 + 4:c0 + 5]), ["sm%d" % u], ["sm%d" % u])
                      P.op("dve", Rec("tensor_scalar", out=Pn[u][0:nq, 0:nkeys], in0=Pe[u][0:nq, 0:nkeys], scalar1=sm[0:nq, c0 + 5:c0 + 6], scalar2=None, op0=ALU.mult),
                           ["Pf%d" % u, "sm%d" % u], ["Pn%d" % u])
                      ptb = psb(sbk)
                      ko = 0
                      for si, (nk_, vfn) in enumerate(segs):
                          P.op("pe", Rec("transpose", ptb[0:nk_, si * 128:si * 128 + nq], Pn[u][0:nq, ko:ko + nk_], identb[0:nq, 0:nq]),
                               ["Pn%d" % u, "identb"], [RP[sbk]])
                          ko += nk_

                  def stB2(i):
                      m, odd, nq, qcols, keyfn, nkeys, mk, segs = units[i]
                      h = 2 * m + odd; kvh = h // 8; u = i % 3; sbk = 5 + u
                      ptb = psb(sbk)
                      for si, (nk_, vfn) in enumerate(segs):
                          P.op("act", Rec("activation", out=PT[u][0:nk_, si, 0:nq], in_=ptb[0:nk_, si * 128:si * 128 + nq], func=AF.Copy),
                               [RP[sbk]], ["PT%d" % u])
                      for si, (nk_, vfn) in enumerate(segs):
                          first = (odd == 0 and si == 0)
                          last = (odd == 1 and si == len(segs) - 1)
                          P.op("pe", Rec("matmul", ps[3][:, 0:nq], lhsT=vfn(kvh * 2 + odd)[0:nk_, :], rhs=PT[u][0:nk_, si, 0:nq], start=first, stop=last),
                               ["PT%d" % u, "Vb", "VSc", "VSn"], [RP[3]])
                      if odd == 1:
                          P.op("dve", Rec("tensor_copy", out=Ao[:, m, qcols], in_=ps[3][:, 0:nq]), [RP[3]], ["Ao%d" % m])

                  NU = len(units)
                  for t in range(NU + 2):
                      if 0 <= t - 1 < NU:
                          stB1(t - 1)
                      if t < NU:
                          stA(t)
                      if 0 <= t - 2 < NU:
                          stB2(t - 2)

                  ckpt(5)
                  if pi == 0:
                      for nm, dst in [("ssm_a_re", lam_r), ("ssm_a_im", lam_i)]:
                          P.dma("sp", Rec("dma_start", out=dst[:], in_=dap(W[nm], l * 4096, [[1, 128], [128, 32]])), [], ["ssmp"])
                      for gl in range(2):
                          P.dma("sp", Rec("dma_start", out=dtt[gl * 64:(gl + 1) * 64, :], in_=dap(W["ssm_log_dt"], l * 64 + gl, [[0, 64], [2, 32]])), [], ["ssmp"])
                      for nm, dst in [("ssm_b_re", Br), ("ssm_b_im", Bi)]:
                          P.dma("sp", Rec("dma_start", out=dst[:], in_=dap(W[nm], l * 65536, [[16, 128], [2048, 32], [1, 16]])), [], ["ssmB"])
                      for nm, dst, rc in [("ssm_c_re", Cr, "Cr"), ("ssm_c_im", Ci, "Ci")]:
                          for jq in range(4):
                              P.dma("sp", Rec("dma_start", out=Cn[:], in_=dap(W[nm], l * 65536 + jq * 16384, [[64, 16], [1024, 16], [1, 64]])), [], ["Cn"])
                              for jj in range(8):
                                  P.op("pe", Rec("transpose", ps[4][:, jj * 16:(jj + 1) * 16], Cn[:, 2 * jj:2 * jj + 2, :], identf[0:16, 0:16]), ["Cn", "cs"], [RP[4]])
                              P.op("dve", Rec("tensor_copy", out=dst[:, jq * 8:(jq + 1) * 8, :], in_=ps[4][:, 0:128].rearrange("p (a b) -> p a b", a=8)), [RP[4]], ["ssmC"])
                  P.dma("sp", Rec("dma_start", out=Drow[:], in_=dap(W["ssm_d"], l * 1024, [[0, 128], [1, 1024]])), [], ["Drow"])
                  for src_t, dst in [(hr0, h0r), (hi0, h0i)]:
                      P.dma("sp", Rec("dma_start", out=dst[:], in_=dap(src_t, (l * 4 + sidx) * 4096, [[1, 128], [128, 32]])), [], ["h0"])
                  if pi == 0:
                      sp_ = ["ssmp"]
                      P.op("act", Rec("activation", out=dtt[:], in_=dtt[:], func=AF.Exp), sp_, sp_)
                      P.op("dve", Rec("tensor_tensor", out=t1[:], in0=lam_r[:], in1=dtt[:], op=ALU.mult), sp_, sp_)
                      P.op("act", Rec("activation", out=mag[:], in_=t1[:], func=AF.Exp), sp_, sp_)
                      P.op("act", Rec("activation", out=rho[:], in_=t1[:], func=AF.Exp, scale=8.0), sp_, sp_)
                      P.op("dve", Rec("tensor_tensor", out=th[:], in0=lam_i[:], in1=dtt[:], op=ALU.mult), sp_, sp_)
                      P.op("dve", Rec("tensor_scalar", out=phi[:], in0=th[:], scalar1=8.0, scalar2=None, op0=ALU.mult), sp_, sp_)
                      sincos("dve", th[:], t3[:].bitcast(I32), t2[:], t3[:], abr[:], abi[:], sp_, "tr1")
                      P.op("dve", Rec("tensor_tensor", out=abr[:], in0=abr[:], in1=mag[:], op=ALU.mult), sp_ + ["tr1c"], sp_)
                      P.op("dve", Rec("tensor_tensor", out=abi[:], in0=abi[:], in1=mag[:], op=ALU.mult), sp_ + ["tr1s"], sp_)
                      P.op("dve", Rec("tensor_scalar", out=t2[:], in0=phi[:], scalar1=1.0 / TWO_PI, scalar2=None, op0=ALU.mult), sp_ + ["tr1a1"], sp_ + ["tr1a1"])
                      P.op("dve", Rec("tensor_copy", out=ti[:], in_=t2[:]), sp_ + ["tr1a1", "tr1a2"], sp_ + ["tr1a2"])
                      P.op("dve", Rec("tensor_copy", out=t2[:], in_=ti[:]), sp_ + ["tr1a1", "tr1a2"], sp_ + ["tr1a1"])
                      P.op("dve", Rec("scalar_tensor_tensor", out=phi[:], in0=t2[:], scalar=-TWO_PI, in1=phi[:], op0=ALU.mult, op1=ALU.add), sp_ + ["tr1a1"], sp_)
                      P.op("dve", Rec("tensor_scalar", out=t1[:], in0=abr[:], scalar1=-1.0, scalar2=None, op0=ALU.add), sp_, sp_)
                      P.op("dve", Rec("tensor_tensor", out=t2[:], in0=lam_r[:], in1=lam_r[:], op=ALU.mult), sp_ + ["tr1a1"], sp_ + ["tr1a1"])
                      P.op("dve", Rec("tensor_tensor", out=t3[:], in0=lam_i[:], in1=lam_i[:], op=ALU.mult), sp_ + ["tr1a2"], sp_ + ["tr1a2"])
                      P.op("dve", Rec("tensor_tensor", out=t2[:], in0=t2[:], in1=t3[:], op=ALU.add), sp_ + ["tr1a1", "tr1a2"], sp_ + ["tr1a1"])
                      P.op("dve", Rec("reciprocal", out=t2[:], in_=t2[:]), sp_ + ["tr1a1"], sp_ + ["tr1a1"])
                      P.op("dve", Rec("tensor_tensor", out=crr[:], in0=t1[:], in1=lam_r[:], op=ALU.mult), sp_, sp_)
                      P.op("dve", Rec("tensor_tensor", out=t3[:], in0=abi[:], in1=lam_i[:], op=ALU.mult), sp_ + ["tr1a2"], sp_ + ["tr1a2"])
                      P.op("dve", Rec("tensor_tensor", out=crr[:], in0=crr[:], in1=t3[:], op=ALU.add), sp_ + ["tr1a2"], sp_)
                      P.op("dve", Rec("tensor_tensor", out=crr[:], in0=crr[:], in1=t2[:], op=ALU.mult), sp_ + ["tr1a1"], sp_)
                      P.op("dve", Rec("tensor_tensor", out=cii[:], in0=abi[:], in1=lam_r[:], op=ALU.mult), sp_, sp_)
                      P.op("dve", Rec("tensor_tensor", out=t3[:], in0=t1[:], in1=lam_i[:], op=ALU.mult), sp_ + ["tr1a2"], sp_ + ["tr1a2"])
                      P.op("dve", Rec("tensor_tensor", out=cii[:], in0=cii[:], in1=t3[:], op=ALU.subtract), sp_ + ["tr1a2"], sp_)
                      P.op("dve", Rec("tensor_tensor", out=cii[:], in0=cii[:], in1=t2[:], op=ALU.mult), sp_ + ["tr1a1"], sp_)
                      bc = lambda a: a[:].unsqueeze(2).broadcast_to([128, 32, 16])
                      cmul("dve", Bbr[:], Bbi[:], bc(crr), bc(cii), Br[:], Bi[:], PCr[:], PCi[:], sp_ + ["ssmB", "PC"], ["ssmB", "PC"])
                      P.op("dve", Rec("tensor_tensor", out=t1[:], in0=mag[:], in1=mag[:], op=ALU.mult), sp_, sp_)
                      P.op("dve", Rec("reciprocal", out=t1[:], in_=t1[:]), sp_, sp_)
                      P.op("dve", Rec("tensor_tensor", out=iar[:], in0=abr[:], in1=t1[:], op=ALU.mult), sp_, sp_)
                      P.op("dve", Rec("scalar_tensor_tensor", out=iai[:], in0=abi[:], scalar=-1.0, in1=t1[:], op0=ALU.mult, op1=ALU.mult), sp_, sp_)
                      pr = ["PC", "ssmp"]
                      P.op("pool", Rec("memset", PCr[:, :, 7:8], 1.0), pr, pr)
                      P.op("pool", Rec("memset", PCi[:, :, 7:8], 0.0), pr, pr)
                      for kk in range(8, 16):
                          cmul("dve", PCr[:, :, kk], PCi[:, :, kk], PCr[:, :, kk - 1], PCi[:, :, kk - 1], abr[:], abi[:], t2[:], t3[:], pr + ["tr1a1", "tr1a2"], pr + ["tr1a1", "tr1a2"])
                      for kk in range(6, -1, -1):
                          cmul("dve", PCr[:, :, kk], PCi[:, :, kk], PCr[:, :, kk + 1], PCi[:, :, kk + 1], iar[:], iai[:], t2[:], t3[:], pr + ["tr1a1", "tr1a2"], pr + ["tr1a1", "tr1a2"])
                      for i in range(8):
                          P.op("pool", Rec("tensor_copy", out=PBr[:, :, i], in_=PCr[:, :, 14 - i]), pr, ["PB"])
                          P.op("pool", Rec("tensor_copy", out=PBi[:, :, i], in_=PCi[:, :, 14 - i]), pr, ["PB"])
                  if pi == 0:
                      for t_, o_, n_ in [(PCr, 0, 512), (PCi, 512, 512)]:
                          P.dma("sp", Rec("dma_start", out=dap(scrT, l * 128 * 1088 + o_, [[1088, 128], [1, n_]]), in_=t_[:].rearrange("p a b -> p (a b)")), ["PC"], ["scrT%d" % l])
                      for t_, o_ in [(rho, 1024), (phi, 1056)]:
                          P.dma("sp", Rec("dma_start", out=dap(scrT, l * 128 * 1088 + o_, [[1088, 128], [1, 32]]), in_=t_[:]), ["ssmp"], ["scrT%d" % l])
                  else:
                      for t_, o_, n_ in [(PCr, 0, 512), (PCi, 512, 512)]:
                          P.dma("sp", Rec("dma_start", out=t_[:].rearrange("p a b -> p (a b)"), in_=dap(scrT, l * 128 * 1088 + o_, [[1088, 128], [1, n_]])), ["scrT%d" % l], ["PC"])
                      for t_, o_ in [(rho, 1024), (phi, 1056)]:
                          P.dma("sp", Rec("dma_start", out=t_[:], in_=dap(scrT, l * 128 * 1088 + o_, [[1088, 128], [1, 32]])), ["scrT%d" % l], ["ssmp"])
                  cmul("dve", hend[:, 0, :], hend[:, 1, :], PCr[:, :, 11], PCi[:, :, 11], h0r[:], h0i[:], t2[:], t3[:], pr + ["h0", "hend", "tr1a1", "tr1a2"], ["hend", "tr1a1", "tr1a2"])

                  ckpt(6)
                  for sl in range(4):
                      slab, rslab = w_next("w_in", l, 1280 + sl * 256)
                      for i in range(8):
                          nrow = NCH1 if i < 4 else NCHK
                          for k in range(16):
                              lhs = sap(Hb[:, k, i:i + 1], [[8, nrow]])
                              P.op("pe", Rec("matmul", ps[i % 2][0:nrow, 0:256], lhsT=lhs, rhs=slab[:, k, 0:256], start=(k == 0), stop=(k == 15)),
                                   [rslab, "Hb%d" % k], [RP[i % 2]])
                          P.op("act", Rec("activation", out=Vu[0:nrow, :, i, :], in_=ps[i % 2][0:nrow, 0:256].rearrange("p (g c) -> p g c", g=16), func=AF.Copy), [RP[i % 2]], ["Vu"])
                      for gq in range(4):
                          ub = psb(4)
                          for gg in range(4):
                              g = gq * 4 + gg
                              P.op("pe", Rec("transpose", ub[:, gg * 128:gg * 128 + NCH1], Vu[0:NCH1, g, :, :], identb[0:NCH1, 0:NCH1]),
                                   ["Vu", "identb"], [RP[4]])
                          for gg in range(4):
                              g = gq * 4 + gg
                              P.op("dve", Rec("tensor_copy", out=Ug[g][:], in_=ub[:, gg * 128:gg * 128 + NCH1]), [RP[4]], ["Ug%d" % g])
                      j0 = sl * NPB
                      bcp = lambda a: a[:, j0:j0 + NPB].unsqueeze(2).broadcast_to([128, NPB, 65])
                      posb = posf.unsqueeze(1).broadcast_to([128, NPB, 65])
                      P.op("dve", Rec("tensor_tensor", out=ang[:], in0=bcp(phi), in1=posb, op=ALU.mult), ["ssmp", "cs", "trb"], ["trb"])
                      sincos("dve", ang[:], u2[:].bitcast(I32), u1[:], u2[:], Tc[:], Ts[:], ["trb"], "tr2")
                      P.op("dve", Rec("tensor_tensor", out=d0[:], in0=bcp(rho), in1=cs[:, 965:1030].unsqueeze(1).broadcast_to([128, NPB, 65]), op=ALU.mult), ["ssmp", "cs"], ["d0"])
                      creg = "scrC_%d_%d" % (l, sl)
                      csrc = dap(scrC, (l * 4 + sl) * 128 * 4096, [[4096, 128], [1, 4096]])
                      gsrc = dap(scrG, (l * 4 + sl) * 128 * 2048, [[2048, 128], [1, 2048]])
                      if pi > 0:
                          P.dma("sp", Rec("dma_start", out=MCBt[:].rearrange("p a b c -> p (a b c)"), in_=csrc), [creg], ["MCm"])
                          P.dma("sp", Rec("dma_start", out=TgSt[:].rearrange("p a b -> p (a b)"), in_=gsrc), [creg], ["TgS"])
                      for jj in range(NPB):
                          j = j0 + jj
                          MBp = [MBpT[:, jj % 2, q_, :] for q_ in range(4)]
                          if pi == 0:
                              rsp = ["ssmB", "PB", "PC", "ssmC", "ssmp"]
                              pb_r = PBr[:, j, :].unsqueeze(2).broadcast_to([128, 8, 16]); pb_i = PBi[:, j, :].unsqueeze(2).broadcast_to([128, 8, 16])
                              bb_r = Bbr[:, j, :].unsqueeze(1).broadcast_to([128, 8, 16]); bb_i = Bbi[:, j, :].unsqueeze(1).broadcast_to([128, 8, 16])
                              v3 = lambda a, n: a[:, 0:n * 16].rearrange("p (a b) -> p a b", b=16)
                              P.op("pool", Rec("tensor_tensor", out=v3(tA, 8), in0=pb_i, in1=bb_i, op=ALU.mult), rsp + ["tA"], ["tA"])
                              P.op("pool", Rec("tensor_tensor", out=v3(tB, 8), in0=pb_r, in1=bb_r, op=ALU.mult), rsp + ["tB"], ["tB"])
                              P.op("pool", Rec("tensor_tensor", out=v3(MBt[0], 8), in0=v3(tB, 8), in1=v3(tA, 8), op=ALU.subtract), ["tA", "tB"], ["MBt"])
                              P.op("pool", Rec("tensor_tensor", out=v3(tA, 8), in0=pb_r, in1=bb_i, op=ALU.mult), rsp + ["tA"], ["tA"])
                              P.op("pool", Rec("tensor_tensor", out=v3(tB, 8), in0=pb_i, in1=bb_r, op=ALU.mult), rsp + ["tB"], ["tB"])
                              P.op("pool", Rec("tensor_tensor", out=v3(MBt[1], 8), in0=v3(tA, 8), in1=v3(tB, 8), op=ALU.add), ["tA", "tB"], ["MBt"])
                              pc_r = PCr[:, j, :].unsqueeze(2).broadcast_to([128, 16, 16]); pc_i = PCi[:, j, :].unsqueeze(2).broadcast_to([128, 16, 16])
                              cc_r = Cr[:, j, :].unsqueeze(1).broadcast_to([128, 16, 16]); cc_i = Ci[:, j, :].unsqueeze(1).broadcast_to([128, 16, 16])
                              P.op("dve", Rec("tensor_tensor", out=v3(tA, 16), in0=pc_i, in1=cc_i, op=ALU.mult), rsp + ["tA"], ["tA"])
                              P.op("dve", Rec("tensor_tensor", out=v3(tB, 16), in0=pc_r, in1=cc_r, op=ALU.mult), rsp + ["tB"], ["tB"])
                              MCm = MCB[jj]
                              P.op("dve", Rec("tensor_tensor", out=v3(MCm[0], 16), in0=v3(tB, 16), in1=v3(tA, 16), op=ALU.subtract), ["tA", "tB"], ["MCm"])
                              P.op("dve", Rec("tensor_tensor", out=v3(tA, 16), in0=pc_r, in1=cc_i, op=ALU.mult), rsp + ["tA"], ["tA"])
                              P.op("dve", Rec("tensor_tensor", out=v3(tB, 16), in0=pc_i, in1=cc_r, op=ALU.mult), rsp + ["tB"], ["tB"])
                              P.op("dve", Rec("scalar_tensor_tensor", out=v3(MCm[1], 16), in0=v3(tA, 16), scalar=-1.0, in1=v3(tB, 16), op0=ALU.mult, op1=ALU.subtract), ["tA", "tB"], ["MCm"])
                              tb = psb(5)
                              for ri in range(2):
                                  P.op("pe", Rec("transpose", tb[:, ri * 128:(ri + 1) * 128], MBt[ri][:], identb[:]), ["MBt", "identb"], [RP[5]])
                              for gl in range(2):
                                  for ri in range(2):
                                      P.op("act", Rec("activation", out=MBp[gl * 2 + ri][:, gl * 64:gl * 64 + 64], in_=tb[:, ri * 128 + gl * 64:ri * 128 + gl * 64 + 64], func=AF.Copy),
                                           [RP[5]], ["MBp%d" % (jj % 2)])
                              for gl in range(2):
                                  sl_ = slice(gl * 64, gl * 64 + 64)
                                  P.op("pe", Rec("matmul", ps[3][:, gl * 128:(gl + 1) * 128], lhsT=MBt[0][sl_, :], rhs=MCm[0][sl_, 0:128], start=True, stop=False), ["MBt", "MCm"], [RP[3]])
                                  P.op("pe", Rec("matmul", ps[3][:, gl * 128:(gl + 1) * 128], lhsT=MBt[1][sl_, :], rhs=MCm[1][sl_, 0:128], start=False, stop=True), ["MBt", "MCm"], [RP[3]])
                                  P.op("dve", Rec("tensor_tensor", out=TgS[jj * 2 + gl][:], in0=ps[3][:, gl * 128:(gl + 1) * 128], in1=tmask[:], op=ALU.mult), [RP[3], "tmask"], ["TgS"])
                          mreg = "scrM_%d_%d" % (l, j)
                          msrc = dap(scrM, (l * 32 + j) * 128 * 512, [[512, 128], [1, 512]])
                          if pi == 0:
                              P.dma("sp", Rec("dma_start", out=msrc, in_=MBpT[:, jj % 2].rearrange("p a b -> p (a b)")), ["MBp%d" % (jj % 2)], [mreg])
                          else:
                              P.dma("sp", Rec("dma_start", out=MBpT[:, jj % 2].rearrange("p a b -> p (a b)"), in_=msrc), [mreg], ["MBp%d" % (jj % 2)])
                          for ri in range(2):
                              for gl in range(2):
                                  g = jj * 2 + gl
                                  P.op("pe", Rec("matmul", ps[6][:, ri * 128:ri * 128 + NCH1], lhsT=MBp[gl * 2 + ri][:], rhs=Ug[g][:], start=(gl == 0), stop=(gl == 1)),
                                       ["MBp%d" % (jj % 2), "Ug%d" % g], [RP[6]])
                          xr = ps[6][:, 0:NCHK]; xi = ps[6][:, 128:128 + NCHK]
                          tcj = Tc[:, jj, 1:65]; tsj = Ts[:, jj, 1:65]
                          rdm = [RP[6], "tr2c", "tr2s"]
                          P.op("dve", Rec("tensor_tensor", out=Dr[:, jj, 1:65], in0=xr, in1=tcj, op=ALU.mult), rdm, ["Dm"])
                          P.op("dve", Rec("tensor_tensor", out=u1[:, jj, 1:65], in0=xi, in1=tsj, op=ALU.mult), rdm + ["tr2a1"], ["tr2a1"])
                          P.op("dve", Rec("tensor_tensor", out=Di[:, jj, 1:65], in0=xi, in1=tcj, op=ALU.mult), rdm, ["Dm"])
                          P.op("dve", Rec("tensor_tensor", out=u2[:, jj, 1:65], in0=xr, in1=tsj, op=ALU.mult), rdm + ["tr2a2"], ["tr2a2"])
                          P.op("act", Rec("activation", out=Xs[:, 0, jj:jj + 1], in_=ps[6][:, NCHK:NCHK + 1], func=AF.Copy), [RP[6]], ["Xs"])
                          P.op("act", Rec("activation", out=Xs[:, 1, jj:jj + 1], in_=ps[6][:, 128 + NCHK:128 + NCHK + 1], func=AF.Copy), [RP[6]], ["Xs"])
                      if pi == 0:
                          P.dma("sp", Rec("dma_start", out=csrc, in_=MCBt[:].rearrange("p a b c -> p (a b c)")), ["MCm"], [creg])
                          P.dma("sp", Rec("dma_start", out=gsrc, in_=TgSt[:].rearrange("p a b -> p (a b)")), ["TgS"], [creg])
                      dm = ["Dm", "tr2a1", "tr2a2"]
                      P.op("dve", Rec("tensor_tensor", out=Dr[:, :, 1:65], in0=Dr[:, :, 1:65], in1=u1[:, :, 1:65], op=ALU.add), dm, ["Dm"])
                      P.op("dve", Rec("tensor_tensor", out=Di[:, :, 1:65], in0=Di[:, :, 1:65], in1=u2[:, :, 1:65], op=ALU.subtract), dm, ["Dm"])
                      P.op("dve", Rec("tensor_copy", out=Dr[:, :, 0], in_=Hr[:, l, j0:j0 + NPB]), ["H", "Dm"], ["Dm"])
                      P.op("dve", Rec("tensor_copy", out=Di[:, :, 0], in_=Hi[:, l, j0:j0 + NPB]), ["H", "Dm"], ["Dm"])
                      fl = lambda a: a[:].rearrange("p a b -> p (a b)")
                      P.op("dve", Rec("tensor_tensor_scan", out=fl(Dr), data0=fl(d0), data1=fl(Dr), initial=0.0, op0=ALU.mult, op1=ALU.add), ["Dm", "d0"], ["Dm"])
                      P.op("dve", Rec("tensor_tensor_scan", out=fl(Di), data0=fl(d0), data1=fl(Di), initial=0.0, op0=ALU.mult, op1=ALU.add), ["Dm", "d0"], ["Dm"])
                      md = ["Dm", "tr2c", "tr2s", "tr2a1", "tr2a2", "trb"]
                      P.op("dve", Rec("tensor_tensor", out=u1[:], in0=Dr[:], in1=Tc[:], op=ALU.mult), md, ["tr2a1"])
                      P.op("dve", Rec("tensor_tensor", out=u2[:], in0=Di[:], in1=Ts[:], op=ALU.mult), md, ["tr2a2"])
                      P.op("dve", Rec("tensor_tensor", out=u1[:], in0=u1[:], in1=u2[:], op=ALU.subtract), md, ["tr2a1"])
                      P.op("dve", Rec("tensor_tensor", out=u2[:], in0=Dr[:], in1=Ts[:], op=ALU.mult), md, ["tr2a2"])
                      P.op("dve", Rec("tensor_tensor", out=ang[:], in0=Di[:], in1=Tc[:], op=ALU.mult), md, ["trb"])
                      P.op("dve", Rec("tensor_tensor", out=u2[:], in0=u2[:], in1=ang[:], op=ALU.add), md, ["tr2a2"])
                      P.op("act", Rec("activation", out=Sbr[:, :, 0:64], in_=u1[:, :, 0:64], func=AF.Copy), ["tr2a1"], ["Sb"])
                      P.op("act", Rec("activation", out=Sbi[:, :, 0:64], in_=u2[:, :, 0:64], func=AF.Copy), ["tr2a2"], ["Sb"])
                      P.op("act", Rec("activation", out=Sbr[:, :, 64], in_=h0r[:, j0:j0 + NPB], func=AF.Copy), ["h0"], ["Sb"])
                      P.op("act", Rec("activation", out=Sbi[:, :, 64], in_=h0i[:, j0:j0 + NPB], func=AF.Copy), ["h0"], ["Sb"])
                      P.op("dve", Rec("tensor_copy", out=Hr[:, l, j0:j0 + NPB], in_=u1[:, :, 64]), ["tr2a1"], ["H"])
                      P.op("dve", Rec("tensor_copy", out=Hi[:, l, j0:j0 + NPB], in_=u2[:, :, 64]), ["tr2a2"], ["H"])
                      cmul("dve", tA[:, 0:NPB], tA[:, NPB:2 * NPB], PCr[:, j0:j0 + NPB, 3], PCi[:, j0:j0 + NPB, 3], Xs[:, 0, :], Xs[:, 1, :], tB[:, 0:NPB], tB[:, NPB:2 * NPB],
                           ["PC", "Xs", "tA", "tB"], ["tA", "tB"])
                      P.op("dve", Rec("tensor_tensor", out=hend[:, 0, j0:j0 + NPB], in0=hend[:, 0, j0:j0 + NPB], in1=tA[:, 0:NPB], op=ALU.add), ["tA", "hend"], ["hend"])
                      P.op("dve", Rec("tensor_tensor", out=hend[:, 1, j0:j0 + NPB], in0=hend[:, 1, j0:j0 + NPB], in1=tA[:, NPB:2 * NPB], op=ALU.add), ["tA", "hend"], ["hend"])
                      for gq in range(4):
                          for gg in range(4):
                              g = gq * 4 + gg
                              jj = g // 2; gl = g % 2
                              sl_ = slice(gl * 64, gl * 64 + 64)
                              oc = slice(gg * 128, (gg + 1) * 128)
                              P.op("pe", Rec("matmul", ps[7][0:NCH1, oc], lhsT=Ug[g][:], rhs=TgS[g][:], start=True, stop=False), ["Ug%d" % g, "TgS"], [RP[7]])
                              P.op("pe", Rec("matmul", ps[7][0:NCH1, oc], lhsT=Sbr[sl_, jj, :], rhs=MCB[jj][0][sl_, 128:256], start=False, stop=False), ["Sb", "MCm"], [RP[7]])
                              P.op("pe", Rec("matmul", ps[7][0:NCH1, oc], lhsT=Sbi[sl_, jj, :], rhs=MCB[jj][1][sl_, 128:256], start=False, stop=True), ["Sb", "MCm"], [RP[7]])
                          ch0 = sl * 256 + gq * 64
                          P.op("dve", Rec("tensor_tensor", out=vd[0:NCH1], in0=Vu[0:NCH1, gq * 4:(gq + 1) * 4, :, :], in1=sap(Drow[0:NCH1, ch0:ch0 + 1], [[16, 4], [0, 8], [1, 16]]), op=ALU.mult),
                               ["Vu", "Drow"], ["vd"])
                          P.op("dve", Rec("tensor_tensor", out=ypre[0:NCH1], in0=ps[7][0:NCH1, :].rearrange("p (g i c) -> p g i c", g=4, i=8),
                                                                in1=vd[0:NCH1], op=ALU.add), [RP[7], "vd"], ["ypre"])
                          half = gq % 2
                          P.op("act", Rec("activation", out=zcm[0:NCH1, :, half * 64:(half + 1) * 64].rearrange("p i (g c) -> p g i c", g=4), in_=ypre[0:NCH1], func=AF.Gelu), ["ypre"], ["zcm"])
                          if half == 1:
                              mz = sl * 2 + gq // 2
                              zb = psb(5)
                              for i in range(8):
                                  P.op("pe", Rec("transpose", zb[:, i * 128:i * 128 + NCH1], zcm[0:NCH1, i, :], identb[0:NCH1, 0:NCH1]), ["zcm", "identb"], [RP[5]])
                              zv = zb[:, 0:1024].rearrange("p (i n) -> p i n", i=8)
                              P.op("dve", Rec("tensor_copy", out=sap(Zf[:, mz, 0:1], [[1, 4], [8, NCH1]]), in_=zv[:, 0:4, 0:NCH1]), [RP[5]], ["fb%d" % mz])
                              P.op("dve", Rec("tensor_copy", out=sap(Zf[:, mz, 4:5], [[1, 4], [8, NCHK]]), in_=zv[:, 4:8, 0:NCHK]), [RP[5]], ["fb%d" % mz])
                  ckpt(7)
                  for ri, (dst_p, dst_s, Hx) in enumerate([(ohrp, ohrs, Hr), (ohip, ohis, Hi)]):
                      P.dma("sp", Rec("dma_start", out=dap(dst_p, l * 4096, [[1, 128], [128, 32]]), in_=Hx[:, l, :]), ["H"], ["ohp%d" % ri])
                      if HS[0]:
                          P.dma("sp", Rec("dma_start", out=dap(dst_s, (l * 4 + sidx) * 4096, [[1, 128], [128, 32]]), in_=hend[:, ri, :]), ["hend"], ["ohs%d" % ri])
                  for j in range(4):
                      slab, rslab = w_next("w_glu", l, j * 256)
                      for mm in range(2):
                          m = j * 2 + mm
                          def ev_g(pm, psm, rpm, rps, m=m):
                              P.op("act", Rec("activation", out=Pf[0][:, 0:256], in_=pm[:, 0:256], func=AF.Sigmoid), rpm, ["Pf0", "Pf1"])
                              P.op("act", Rec("activation", out=Pf[1][:, 0:256], in_=pm[:, 256:512], func=AF.Sigmoid), rpm, ["Pf2"])
                              P.op("dve", Rec("tensor_tensor", out=Hb[:, 8 + m, 0:256], in0=Pf[0][:, 0:256], in1=Zf[:, m, 0:256], op=ALU.mult), ["Pf0", "Pf1", "fb%d" % m], ["Hb%d" % (8 + m)])
                              P.op("dve", Rec("tensor_tensor", out=Hb[:, 8 + m, 256:512], in0=Pf[1][:, 0:256], in1=Zf[:, m, 256:512], op=ALU.mult), ["Pf2", "fb%d" % m], ["Hb%d" % (8 + m)])
                              if psm is not None:
                                  P.op("act", Rec("activation", out=sm[:, 0:NS], in_=psm, func=AF.Sigmoid), rps, ["sm0", "sm1"])
                              P.op("dve", Rec("tensor_tensor", out=Hb[:, 8 + m, NP:NT], in0=sm[:, 0:NS], in1=Zf[:, m, NP:NT], op=ALU.mult), ["sm0", "sm1", "fb%d" % m], ["Hb%d" % (8 + m)])
                          proj_chunk(slab, rslab, 8, mm * 128, lambda k: Zf[:, k, :], lambda k: ["fb%d" % k], ev_g)
                  ckpt(8)
                  rmsnorm([Ao[:, k, :] for k in range(8)], ["Ao%d" % k for k in range(8)], lambda k, l=l: gains[:, l, 32 + k:33 + k],
                          [Ao[:, k, :] for k in range(8)], ["Ao%d" % k for k in range(8)], 1024)
                  rmsnorm([Hb[:, 8 + k, :] for k in range(8)], ["Hb%d" % (8 + k) for k in range(8)], lambda k, l=l: gains[:, l, 40 + k:41 + k],
                          [Hb[:, 8 + k, :] for k in range(8)], ["Hb%d" % (8 + k) for k in range(8)], 1024)
                  mix_rhs = lambda k: (Ao[:, k, :] if k < 8 else Hb[:, k, :])
                  mix_regs = lambda k: ["Ao%d" % k] if k < 8 else ["Hb%d" % k]
                  for j in range(8):
                      slab, rslab = w_next("w_out", l, j * 256)
                      for mm in range(2):
                          m = j * 2 + mm
                          def ev_o(pm, psm, rpm, rps, m=m):
                              P.op("dve", Rec("tensor_tensor", out=X[:, m, 0:NP], in0=pm, in1=X[:, m, 0:NP], op=ALU.add), rpm + ["X%d" % m], ["X%d" % m])
                              if psm is not None:
                                  P.op("dve", Rec("tensor_tensor", out=X[:, m, NP:NT], in0=psm, in1=X[:, m, NP:NT], op=ALU.add), rps + ["X%d" % m], ["X%d" % m])
                          proj_chunk(slab, rslab, 16, mm * 128, mix_rhs, mix_regs, ev_o)
                  ckpt(9)
                  rmsnorm([X[:, k, :] for k in range(16)], ["X%d" % k for k in range(16)], lambda k, l=l: gains[:, l, 16 + k:17 + k],
                          [Hb[:, k, :] for k in range(16)], ["Hb%d" % k for k in range(16)], D)
                  for fp in range(0 if int(_os.environ.get("SKIPFFN", "0")) else 4):
                      c0 = fp * 1536
                      nsl = 6 if fp < 3 else 4
                      for j in range(nsl):
                          slg, rslg = w_next("w_gate", l, c0 + j * 256)
                          slu, rslu = w_next("w_up", l, c0 + j * 256)
                          for mm in range(2):
                              ma = j * 2 + mm
                              def ev_gate(pm, psm, rpm, rps, ma=ma):
                                  P.op("act", Rec("activation", out=Pf[0][:, 0:256], in_=pm[:, 0:256], func=AF.Silu), rpm, ["Pf0", "Pf1"])
                                  P.op("act", Rec("activation", out=Pf[1][:, 0:256], in_=pm[:, 256:512], func=AF.Silu), rpm, ["Pf2"])
                                  if psm is not None:
                                      P.op("act", Rec("activation", out=sm[:, 0:NS], in_=psm, func=AF.Silu), rps, ["sm0", "sm1"])
                              def ev_up(pm, psm, rpm, rps, ma=ma):
                                  P.op("dve", Rec("tensor_tensor", out=ACT_[:, ma, 0:256], in0=pm[:, 0:256], in1=Pf[0][:, 0:256], op=ALU.mult), rpm + ["Pf0", "Pf1"], ["fb%d" % ma])
                                  P.op("dve", Rec("tensor_tensor", out=ACT_[:, ma, 256:512], in0=pm[:, 256:512], in1=Pf[1][:, 0:256], op=ALU.mult), rpm + ["Pf2"], ["fb%d" % ma])
                                  if psm is not None:
                                      P.op("dve", Rec("tensor_tensor", out=ACT_[:, ma, NP:NT], in0=psm, in1=sm[:, 0:NS], op=ALU.mult), rps + ["sm0", "sm1"], ["fb%d" % ma])
                              proj_chunk(slg, rslg, 16, mm * 128, hb_rhs, hb_regs, ev_gate)
                              proj_chunk(slu, rslu, 16, mm * 128, hb_rhs, hb_regs, ev_up)
                      nk = nsl * 2
                      for j in range(8):
                          slab, rslab = w_next("w_down", l, j * 256)
                          for mm in range(2):
                              m = j * 2 + mm
                              def ev_d(pm, psm, rpm, rps, m=m):
                                  P.op("dve", Rec("tensor_tensor", out=X[:, m, 0:NP], in0=pm, in1=X[:, m, 0:NP], op=ALU.add), rpm + ["X%d" % m], ["X%d" % m])
                                  if psm is not None:
                                      P.op("dve", Rec("tensor_tensor", out=X[:, m, NP:NT], in0=psm, in1=X[:, m, NP:NT], op=ALU.add), rps + ["X%d" % m], ["X%d" % m])
                              proj_chunk(slab, rslab, nk, mm * 128, lambda k: ACT_[:, k, :], lambda k: ["fb%d" % k], ev_d)

              ckpt(10)
              for k in range(16):
                  q = sqs[k % 2]
                  P.op("act", Rec("activation", out=q[:], in_=X[:, k, :], func=AF.Square), ["X%d" % k], ["sq%d" % (k % 2)])
                  P.op("pe", Rec("matmul", ps[3][:, 0:NP], lhsT=onesb[:], rhs=q[:, 0:NP], start=(k == 0), stop=(k == 15)), ["sq%d" % (k % 2), "onesb"], [RP[3]])
                  if HS[0]:
                      P.op("pe", Rec("matmul", ps[2][:, 480:480 + NS], lhsT=onesb[:], rhs=q[:, NP:NT], start=(k == 0), stop=(k == 15)), ["sq%d" % (k % 2), "onesb"], [RP[2]])
              P.op("act", Rec("activation", out=rstd[:, 0:NP], in_=ps[3][:, 0:NP], func=AF.Sqrt, scale=1.0 / D, bias=epsb[:, 0:1]), [RP[3], "epsb"], ["rstd"])
              if HS[0]:
                  P.op("act", Rec("activation", out=rstd[:, NP:NT], in_=ps[2][:, 480:480 + NS], func=AF.Sqrt, scale=1.0 / D, bias=epsb[:, 0:1]), [RP[2], "epsb"], ["rstd"])
              P.op("dve", Rec("reciprocal", out=rstd[:], in_=rstd[:]), ["rstd"], ["rstd"])
              for k in range(16):
                  P.op("dve", Rec("scalar_tensor_tensor", out=X[:, k, :], in0=X[:, k, :], scalar=gfin[:, k:k + 1], in1=rstd[:], op0=ALU.mult, op1=ALU.mult),
                       ["X%d" % k, "rstd", "gfin"], ["X%d" % k])
              for blk in range(NB + (1 if HS[0] else 0)):
                  if blk < NB:
                      cols = slice(blk * 128, (blk + 1) * 128); nr = 128
                  else:
                      cols = slice(NP, NT); nr = NS
                  for kq in range(4):
                      for kk in range(4):
                          k = kq * 4 + kk
                          P.op("pe", Rec("transpose", ps[4][0:nr, kk * 128:(kk + 1) * 128], X[:, k, cols], identf), ["X%d" % k, "cs"], [RP[4]])
                      P.op("dve", Rec("tensor_copy", out=stage[0:nr, kq * 512:(kq + 1) * 512], in_=ps[4][0:nr, :]), [RP[4]], STG)
                  if blk < NB:
                      r0 = pi * NP + blk * 128
                      P.dma("sp", Rec("dma_start", out=dap(yp, r0 * D, [[D, 128], [1, D]]), in_=stage[:]), STG, ["yp"])
                  else:
                      P.dma("sp", Rec("dma_start", out=dap(ys, sidx * 4 * D, [[D, 4], [1, D]]), in_=stage[0:4, :]), STG, ["ys"])

        except _Stop:
            pass
        if dbg:
            P.dma("sp", Rec("dma_start", out=dap(dbg_t, 0, [[16 * NT, 128], [1, 8 * NT]]), in_=Ao[:].rearrange("p a b -> p (a b)")), ["Ao%d" % k for k in range(8)], ["dbg"])
            P.dma("sp", Rec("dma_start", out=dap(dbg_t, 8 * NT, [[16 * NT, 128], [1, 8 * NT]]), in_=Hb[:, 8:16, :].rearrange("p a b -> p (a b)")), ["Hb%d" % k for k in range(8, 16)], ["dbg"])
        P.op("sp", Rec("nop", ), ["yp", "ys", "okp", "ovp", "oks", "ovs", "ohp0", "ohp1", "ohs0", "ohs1"] + (["dbg"] if dbg else []), [])
        P.emit()
    return nc


def make_consts():
    c = np.zeros((128, 1032), np.float32)
    c[:, 0:128] = np.eye(128, dtype=np.float32)
    i = np.arange(128)[:, None]
    j = np.arange(128)[None, :]
    c[:, 128:256] = np.where(j > i, 0.0, NEG)
    c[:, 256:384] = np.where(j <= i, 0.0, NEG)
    c[:, 384:512] = NEG
    c[:, 512:640] = c[:, 256:384]
    js = np.arange(132)[None, :]
    c[:, 640:772] = np.where((js > i) & (js <= i + 128), 0.0, NEG)
    r = np.arange(128)[:, None] // 16
    cc = np.arange(128)[None, :] // 16
    c[:, 772:900] = (cc >= r).astype(np.float32)
    c[:, 900:965] = np.arange(65, dtype=np.float32)[None, :]
    c[:, 965:1030] = 1.0
    c[:, 965] = 0.0
    return c


_NC_CACHE = {}


def kernel(**inputs):
    inp = {k: np.ascontiguousarray(np.asarray(v)) for k, v in inputs.items()}
    npass = SEQ // NP
    key = (npass, DEPTH)
    if key not in _NC_CACHE:
        _NC_CACHE[key] = build(npass, DEPTH)
    nc = _NC_CACHE[key]
    cst = make_consts()
    wnames = ["norm_mix", "w_in", "attn_sink", "ssm_a_re", "ssm_a_im", "ssm_log_dt", "ssm_b_re", "ssm_b_im",
              "ssm_c_re", "ssm_c_im", "ssm_d", "w_glu", "norm_attn_out", "norm_ssm_out", "w_out", "norm_ffn",
              "w_gate", "w_up", "w_down", "norm_final"]
    in_maps = []
    for c in range(8):
        m = {n: inp[n] for n in wnames}
        m["cst"] = cst
        m["xp"] = inp["x_prompt"][c % 2]
        m["xs"] = inp["x_sample"][4 * c:4 * c + 4].reshape(16, D)
        m["ck"] = inp["cache_k"][:, 4 * c:4 * c + 4].reshape(DEPTH, 4, 128, 128)
        m["cv"] = inp["cache_v"][:, 4 * c:4 * c + 4].reshape(DEPTH, 4, 128, 128)
        m["hr0"] = inp["state_ssm_re"][:, 4 * c:4 * c + 4]
        m["hi0"] = inp["state_ssm_im"][:, 4 * c:4 * c + 4]
        in_maps.append({k: np.ascontiguousarray(v, dtype=np.float32) for k, v in m.items()})
    res = run_bass_kernel_spmd(nc, in_maps, core_ids=list(range(8))).results
    f = np.float32
    y_prompt = np.stack([res[0]["yp"], res[1]["yp"]]).astype(f)
    y_sample = np.concatenate([res[c]["ys"].reshape(4, 4, D) for c in range(8)], 0).astype(f)
    k_p = np.stack([res[0]["okp"], res[1]["okp"]], 1).reshape(DEPTH, 2, 128, 2, 64).astype(f)
    v_p = np.stack([res[0]["ovp"], res[1]["ovp"]], 1).reshape(DEPTH, 2, 128, 2, 64).astype(f)
    hr_p = np.stack([res[0]["ohrp"], res[1]["ohrp"]], 1).astype(f)
    hi_p = np.stack([res[0]["ohip"], res[1]["ohip"]], 1).astype(f)
    k_s = np.concatenate([res[c]["oks"] for c in range(8)], 1).reshape(DEPTH, 32, 128, 2, 64).astype(f)
    v_s = np.concatenate([res[c]["ovs"] for c in range(8)], 1).reshape(DEPTH, 32, 128, 2, 64).astype(f)
    hr_s = np.concatenate([res[c]["ohrs"] for c in range(8)], 1).astype(f)
    hi_s = np.concatenate([res[c]["ohis"] for c in range(8)], 1).astype(f)
    return (y_prompt, y_sample, k_p, v_p, hr_p, hi_p, k_s, v_s, hr_s, hi_s)
```

```python
import contextlib
import math
import numpy as np
import concourse.bass as bass
import concourse.mybir as mybir
from concourse.bass import AP
from concourse.bass_utils import run_bass_kernel_spmd

F32 = mybir.dt.float32
BF16 = mybir.dt.bfloat16
I32 = mybir.dt.int32
AF = mybir.ActivationFunctionType
ALU = mybir.AluOpType
AX = mybir.AxisListType

D = 2048
DEPTH = 4
SEQ = 4096
DIN = 2304
DFF = 5632
NP = 512
NS = 4
NT = NP + NS
NB = NP // 128
NCHK = NP // 8
NCH1 = NCHK + 1
EPS = 1e-5
NEG = -30000.0
TWO_PI = 2.0 * math.pi


class Reg:
    __slots__ = ("name", "w", "rs")

    def __init__(self, name):
        self.name = name
        self.w = None
        self.rs = []


class Op:
    __slots__ = ("eng", "fn", "deps", "marked", "count", "is_dma", "sem", "semval")

    def __init__(self, eng, fn, is_dma=False):
        self.eng = eng
        self.fn = fn
        self.deps = []
        self.marked = False
        self.count = 0
        self.is_dma = is_dma
        self.sem = None
        self.semval = 0


ENGS = ("pe", "act", "dve", "pool", "sp")
N_DMA_SEMS = 32


class Prog:
    def __init__(self, nc):
        self.nc = nc
        self.ops = {e: [] for e in ENGS}
        self.dma_last = [None] * N_DMA_SEMS
        self.dma_cum = [0] * N_DMA_SEMS
        self.dma_rr = 0
        self.regs = {}

    def R(self, name):
        r = self.regs.get(name)
        if r is None:
            r = self.regs[name] = Reg(name)
        return r

    def _deps(self, op, reads, writes):
        deps = []
        for r in reads:
            if r.w is not None:
                deps.append(r.w)
        for w in writes:
            if w.w is not None:
                deps.append(w.w)
            deps.extend(w.rs)
        for r in reads:
            r.rs.append(op)
        for w in writes:
            w.w = op
            w.rs = []
        seen = set()
        out = []
        for d in deps:
            if id(d) in seen or d is op:
                continue
            seen.add(id(d))
            if op.eng == "pe" and d.eng == "pe" and not d.is_dma and not op.is_dma:
                continue
            out.append(d)
            d.marked = True
        op.deps = out

    def op(self, eng, fn, reads=(), writes=()):
        o = Op(eng, fn)
        writes = list(writes) + [x for x in reads if isinstance(x, str) and x.startswith("ps")]
        reads = [x for x in reads if not (isinstance(x, str) and x.startswith("ps"))]
        self._deps(o, [self.R(x) if isinstance(x, str) else x for x in reads],
                   [self.R(x) if isinstance(x, str) else x for x in writes])
        self.ops[eng].append(o)
        return o

    def dma(self, queue, fn, reads=(), writes=()):
        o = Op(queue, fn, is_dma=True)
        s = self.dma_rr
        self.dma_rr = (self.dma_rr + 1) % N_DMA_SEMS
        o.sem = s
        self.dma_cum[s] += 16
        o.semval = self.dma_cum[s]
        self._deps(o, [self.R(x) if isinstance(x, str) else x for x in reads],
                   [self.R(x) if isinstance(x, str) else x for x in writes])
        if self.dma_last[s] is not None:
            o.deps.append(self.dma_last[s])
        self.dma_last[s] = o
        self.ops[queue].append(o)
        return o

    def emit(self):
        nc = self.nc
        for e in ENGS:
            c = 0
            for o in self.ops[e]:
                if o.is_dma:
                    continue
                if o.marked:
                    c += 1
                o.count = c
        with contextlib.ExitStack() as st:
            esem = {e: st.enter_context(nc.semaphore("es_" + e)) for e in ENGS}
            dsem = [st.enter_context(nc.semaphore("ds_%d" % i)) for i in range(N_DMA_SEMS)]
            block = st.enter_context(nc.Block())
            handles = {"pe": block.tensor, "act": block.scalar, "dve": block.vector,
                       "pool": block.gpsimd, "sp": block.sync}
            for e in ENGS:
                ops = self.ops[e]

                def body(eh, ops=ops, e=e):
                    waited = {}
                    for o in ops:
                        for d in o.deps:
                            if d.is_dma:
                                key, sem, val = ("d", d.sem), dsem[d.sem], d.semval
                            else:
                                key, sem, val = ("e", d.eng), esem[d.eng], d.count
                            if waited.get(key, 0) >= val:
                                continue
                            waited[key] = val
                            eh.wait_ge(sem, val)
                        inst = o.fn(eh)
                        if o.is_dma:
                            inst.then_inc(dsem[o.sem], 16)
                        elif o.marked:
                            inst.then_inc(esem[e], 1)

                handles[e](body)


def Rec(name, *a, **k):
    return lambda e: getattr(e, name)(*a, **k)


def sap(base, dims):
    return AP(tensor=base.tensor, offset=base.offset, ap=[list(base.ap[0])] + [[int(a), int(b)] for a, b in dims])


def dap(t, offset, dims):
    return AP(tensor=t, offset=int(offset), ap=[[int(a), int(b)] for a, b in dims])


class _Stop(Exception):
    pass


def build(npass=8, depth=DEPTH, dbg=None, stop=None):
    nc = bass.Bass("TRN2", target_bir_lowering=False)
    P = Prog(nc)
    dt_in = lambda n, s: nc.dram_tensor(n, list(s), F32, kind="ExternalInput")
    dt_out = lambda n, s: nc.dram_tensor(n, list(s), F32, kind="ExternalOutput")
    xp = dt_in("xp", [SEQ, D])
    xs = dt_in("xs", [16, D])
    ck = dt_in("ck", [DEPTH, 4, 128, 128])
    cv = dt_in("cv", [DEPTH, 4, 128, 128])
    hr0 = dt_in("hr0", [DEPTH, 4, 64, 64])
    hi0 = dt_in("hi0", [DEPTH, 4, 64, 64])
    W = {}
    for n, s in [("norm_mix", [DEPTH, D]), ("w_in", [DEPTH, D, DIN]), ("attn_sink", [DEPTH, 16]),
                 ("ssm_a_re", [DEPTH, 64, 64]), ("ssm_a_im", [DEPTH, 64, 64]), ("ssm_log_dt", [DEPTH, 64]),
                 ("ssm_b_re", [DEPTH, 64, 64, 16]), ("ssm_b_im", [DEPTH, 64, 64, 16]),
                 ("ssm_c_re", [DEPTH, 64, 16, 64]), ("ssm_c_im", [DEPTH, 64, 16, 64]),
                 ("ssm_d", [DEPTH, 1024]), ("w_glu", [DEPTH, 1024, 1024]),
                 ("norm_attn_out", [DEPTH, 1024]), ("norm_ssm_out", [DEPTH, 1024]),
                 ("w_out", [DEPTH, D, D]), ("norm_ffn", [DEPTH, D]),
                 ("w_gate", [DEPTH, D, DFF]), ("w_up", [DEPTH, D, DFF]), ("w_down", [DEPTH, DFF, D]),
                 ("norm_final", [D])]:
        W[n] = dt_in(n, s)
    cst = dt_in("cst", [128, 1032])
    yp = dt_out("yp", [SEQ, D])
    ys = dt_out("ys", [16, D])
    okp = dt_out("okp", [DEPTH, 128, 128])
    ovp = dt_out("ovp", [DEPTH, 128, 128])
    ohrp = dt_out("ohrp", [DEPTH, 64, 64])
    ohip = dt_out("ohip", [DEPTH, 64, 64])
    oks = dt_out("oks", [DEPTH, 4, 128, 128])
    ovs = dt_out("ovs", [DEPTH, 4, 128, 128])
    ohrs = dt_out("ohrs", [DEPTH, 4, 64, 64])
    ohis = dt_out("ohis", [DEPTH, 4, 64, 64])
    scrT = nc.dram_tensor("scrT", [DEPTH, 128, 1088], F32, kind="Internal")
    scrM = nc.dram_tensor("scrM", [DEPTH * 32, 128, 512], BF16, kind="Internal")
    scrC = nc.dram_tensor("scrC", [DEPTH * 4, 128, 4096], BF16, kind="Internal")
    scrG = nc.dram_tensor("scrG", [DEPTH * 4, 128, 2048], BF16, kind="Internal")
    dbg_t = nc.dram_tensor("dbg", [128, 16 * NT], BF16, kind="ExternalOutput") if dbg else None

    st = contextlib.ExitStack()
    with st:
        st.enter_context(nc.allow_non_contiguous_dma(reason="small strided parameter loads"))
        sb = lambda n, s, d=F32: st.enter_context(nc.sbuf_tensor(n, list(s), d))
        X = sb("X", [128, 16, NT])
        Hb = sb("Hb", [128, 16, NT], BF16)
        Ao = sb("Ao", [128, 8, NT], BF16)
        FB = sb("FB", [128, 6 * NT])
        ACT_ = FB[:].bitcast(BF16).rearrange("p (a b) -> p a b", a=12)
        Zf = ACT_
        stage = FB[:, 0:D]
        rstd = sb("rstd", [128, NT])
        sqs = [sb("sq%d" % i, [128, NT], BF16) for i in range(2)]
        NSLOT = 4
        ring = [sb("ring%d" % i, [128, 16, 256], BF16) for i in range(NSLOT)]
        cs = sb("cs", [128, 1032])
        identb = sb("identb", [128, 128], BF16)
        onesb = sb("onesb", [128, 128], BF16)
        maskb = sb("maskb", [128, 256], BF16)
        maskfb = sb("maskfb", [128, 256], BF16)
        masksb = sb("masksb", [128, 132], BF16)
        gains = sb("gains", [128, DEPTH, 64])
        gfin = sb("gfin", [128, 16])
        sinkb = sb("sinkb", [128, DEPTH * 16])
        Kb = [sb("Kb%d" % i, [128, 128 + NP], BF16) for i in range(2)]
        KS = [sb("KS%d" % i, [128, 132], BF16) for i in range(2)]
        Vb = sb("Vb", [128, NB + 1, 4, 128], BF16)
        VSc = sb("VSc", [128, 4, 128], BF16)
        VSn = sb("VSn", [4, 4, 128], BF16)
        Khalo = sb("Khalo", [128, DEPTH, 2, 128], BF16)
        Vhalo = sb("Vhalo", [128, DEPTH, 4, 128], BF16)
        kvst = sb("kvst", [128, 256])
        kvss = kvst[0:4, :]
        cpy = kvst
        cks = sb("cks", [128, 128], BF16)
        PFB = sb("PFB", [128, 512])
        Pf = [PFB[:, 0:256], PFB[:, 256:512]]
        Pe = [PFB[:].bitcast(BF16)[:, i * 256:(i + 1) * 256] for i in range(3)]
        Pn = [sb("Pn%d" % i, [128, 256], BF16) for i in range(3)]
        PT = [sb("PT%d" % i, [128, 2, 128], BF16) for i in range(3)]
        sm = sb("sm", [128, 24])
        S1 = lambda n: sb(n, [128, 32])
        lam_r, lam_i, dtt, mag, th, abr, abi, t1, t2, t3, t4, crr, cii, rho, phi, iar, iai = [
            S1(n) for n in "lam_r lam_i dtt mag th abr abi t1 t2 t3 t4 crr cii rho phi iar iai".split()]
        ti = sb("ti", [128, 32], I32)
        Br = sb("Br", [128, 32, 16]); Bi = sb("Bi", [128, 32, 16])
        Bbr = Br; Bbi = Bi
        Cn = sb("Cn", [16, 16, 64])
        Cr = sb("Cr", [128, 32, 16]); Ci = sb("Ci", [128, 32, 16])
        PCr = sb("PCr", [128, 32, 16]); PCi = sb("PCi", [128, 32, 16])
        PBr = sb("PBr", [128, 32, 8]); PBi = sb("PBi", [128, 32, 8])
        Drow = sb("Drow", [128, 1024])
        Hr = sb("Hr", [128, DEPTH, 32]); Hi = sb("Hi", [128, DEPTH, 32])
        h0r = sb("h0r", [128, 32]); h0i = sb("h0i", [128, 32])
        Vu = sb("Vu", [128, 16, 8, 16], BF16)
        Ug = [sb("Ug%d" % i, [128, NCH1], BF16) for i in range(16)]
        MBt = [sb("MBt%d" % i, [128, 128], BF16) for i in range(2)]
        MBpT = sb("MBpT", [128, 2, 4, 128], BF16)
        TgSt = sb("TgSt", [128, 16, 128], BF16)
        MCBt = sb("MCBt", [128, 8, 2, 256], BF16)
        TgS = [TgSt[:, i, :] for i in range(16)]
        MCB = [[MCBt[:, i, r, :] for r in range(2)] for i in range(8)]
        tA = sb("tA", [128, 256]); tB = sb("tB", [128, 256])
        tmask = sb("tmask", [128, 128])
        NPB = 8
        Tc = sb("Tc", [128, NPB, 65]); Ts = sb("Ts", [128, NPB, 65])
        Dr = sb("Dr", [128, NPB, 65]); Di = sb("Di", [128, NPB, 65])
        d0 = sb("d0", [128, NPB, 65])
        ang = sb("ang", [128, NPB, 65])
        u1 = sb("u1", [128, NPB, 65]); u2 = sb("u2", [128, NPB, 65])
        Sbr = sb("Sbr", [128, NPB, 65], BF16); Sbi = sb("Sbi", [128, NPB, 65], BF16)
        Xs = sb("Xs", [128, 2, NPB])
        hend = sb("hend", [128, 2, 32])
        vd = sb("vd", [128, 4, 8, 16])
        ypre = sb("ypre", [128, 4, 8, 16])
        zcm = sb("zcm", [128, 8, 128], BF16)
        ps = [st.enter_context(nc.psum_tensor("ps%d" % i, [128, 512], F32)) for i in range(8)]
        RP = ["ps%d" % i for i in range(8)]
        STG = ["fb%d" % i for i in range(8)]

        def psb(b):
            return ps[b][:].bitcast(BF16)

        P.dma("sp", Rec("dma_start", out=cs[:], in_=cst.ap()), [], ["cs"])
        P.op("dve", Rec("tensor_copy", out=identb[:], in_=cs[:, 0:128]), ["cs"], ["identb"])
        P.op("dve", Rec("tensor_copy", out=maskb[:], in_=cs[:, 128:384]), ["cs"], ["maskb"])
        P.op("dve", Rec("tensor_copy", out=maskfb[:], in_=cs[:, 384:640]), ["cs"], ["maskfb"])
        P.op("dve", Rec("tensor_copy", out=masksb[:], in_=cs[:, 640:772]), ["cs"], ["masksb"])
        P.op("dve", Rec("tensor_copy", out=tmask[:], in_=cs[:, 772:900]), ["cs"], ["tmask"])
        P.op("pool", Rec("memset", onesb[:], 1.0), [], ["onesb"])
        identf = cs[:, 0:128]
        posf = cs[:, 900:965]
        for l in range(depth):
            for nm, off, nk in [("norm_mix", 0, 16), ("norm_ffn", 16, 16), ("norm_attn_out", 32, 8), ("norm_ssm_out", 40, 8)]:
                src = dap(W[nm], l * nk * 128, [[1, 128], [128, nk]])
                P.dma("sp", Rec("dma_start", out=gains[:, l, off:off + nk], in_=src), [], ["gains"])
        P.dma("sp", Rec("dma_start", out=gfin[:], in_=dap(W["norm_final"], 0, [[1, 128], [128, 16]])), [], ["gfin"])
        P.dma("sp", Rec("dma_start", out=sinkb[:], in_=dap(W["attn_sink"], 0, [[0, 128], [1, DEPTH * 16]])), [], ["sinkb"])
        P.op("pool", Rec("memset", Hr[:], 0.0), [], ["H"])
        P.op("pool", Rec("memset", Hi[:], 0.0), [], ["H"])
        P.op("pool", Rec("memset", Khalo[:], 0.0), [], ["Khalo"])
        P.op("pool", Rec("memset", Vhalo[:], 0.0), [], ["Vhalo"])
        P.op("pool", Rec("memset", Vb[:], 0.0), [], ["Vb"])
        P.op("pool", Rec("memset", VSc[:], 0.0), [], ["VSc"])
        P.op("pool", Rec("memset", VSn[:], 0.0), [], ["VSn"])
        P.op("pool", Rec("memset", Vu[:], 0.0), [], ["Vu"])
        P.op("pool", Rec("memset", MBpT[:], 0.0), [], ["MBp0", "MBp1"])

        sched = []
        for ps_i in range(npass):
            for l in range(depth):
                sched.append(("w_in", l, 0, 16, 1024, 256))
                for j in range(4):
                    sched.append(("w_in", l, 0, 16, j * 256, 256))
                for j in range(4):
                    sched.append(("w_in", l, 0, 16, 1280 + j * 256, 256))
                for j in range(4):
                    sched.append(("w_glu", l, 0, 8, j * 256, 256))
                for j in range(8):
                    sched.append(("w_out", l, 0, 16, j * 256, 256))
                for fp in range(4):
                    c0 = fp * 1536
                    nsl = 6 if fp < 3 else 4
                    for j in range(nsl):
                        sched.append(("w_gate", l, 0, 16, c0 + j * 256, 256))
                        sched.append(("w_up", l, 0, 16, c0 + j * 256, 256))
                    nk = nsl * 2
                    for j in range(8):
                        sched.append(("w_down", l, c0, nk, j * 256, 256))
        wcur = [0, 0]
        NCOLS = {"w_in": DIN, "w_glu": 1024, "w_out": D, "w_gate": DFF, "w_up": DFF, "w_down": D}
        NROWS = {"w_in": D, "w_glu": 1024, "w_out": D, "w_gate": D, "w_up": D, "w_down": DFF}

        def w_issue(upto):
            while wcur[1] < min(upto, len(sched)):
                i = wcur[1]
                nm, l, r0, nk, c0, ncl = sched[i]
                slot = i % NSLOT
                if nm == "w_krep":
                    for kvh in range(2):
                        for r_ in range(2):
                            src = dap(W["w_in"], l * D * DIN + 1024 + kvh * 64, [[DIN, 128], [128 * DIN, 16], [1, 64]])
                            c_ = kvh * 128 + r_ * 64
                            P.dma("pool", Rec("dma_start", out=ring[slot][:, :, c_:c_ + 64], in_=src),
                                  [], ["ring%d" % slot])
                    wcur[1] += 1
                    continue
                ncols = NCOLS[nm]
                src = dap(W[nm], l * NROWS[nm] * ncols + r0 * ncols + c0, [[ncols, 128], [128 * ncols, nk], [1, ncl]])
                P.dma("pool", Rec("dma_start", out=ring[slot][:, 0:nk, 0:ncl], in_=src),
                      [], ["ring%d" % slot])
                wcur[1] += 1

        def w_next(nm, l, c0):
            i = wcur[0]
            assert sched[i][0] == nm and sched[i][1] == l and sched[i][4] == c0, (sched[i], nm, l, c0)
            w_issue(i + NSLOT - 1)
            wcur[0] += 1
            return ring[i % NSLOT], "ring%d" % (i % NSLOT)

        pcnt = [0]
        HS = [True]

        def proj_chunk(slab, rslab, nk, mcol, rhs_fn, rhs_regs, evac_fn, lhs_rep=False):
            b = pcnt[0] % 2
            sbk = 2 + pcnt[0] % 2
            so = ((pcnt[0] // 2) % 8) * 4
            pcnt[0] += 1
            for k in range(nk):
                lhs = slab[:, k, mcol:mcol + 128]
                r = rhs_fn(k)
                P.op("pe", Rec("matmul", ps[b][:, 0:NP], lhsT=lhs, rhs=r[:, 0:NP], start=(k == 0), stop=(k == nk - 1)),
                     [rslab] + rhs_regs(k), [RP[b]])
                if HS[0]:
                    P.op("pe", Rec("matmul", ps[sbk][:, so:so + NS], lhsT=lhs, rhs=r[:, NP:NT], start=(k == 0), stop=(k == nk - 1)),
                         [rslab] + rhs_regs(k), [RP[sbk]])
            evac_fn(ps[b][:, 0:NP], ps[sbk][:, so:so + NS] if HS[0] else None, [RP[b]], [RP[sbk]])

        def rmsnorm(srcs, src_regs, gain_fn, dsts, dst_regs, dim):
            nk = len(srcs)
            for k in range(nk):
                q = sqs[k % 2]
                P.op("act", Rec("activation", out=q[:], in_=srcs[k], func=AF.Square), [src_regs[k]], ["sq%d" % (k % 2)])
                P.op("pe", Rec("matmul", ps[3][:, 0:NP], lhsT=onesb[:], rhs=q[:, 0:NP], start=(k == 0), stop=(k == nk - 1)),
                     ["sq%d" % (k % 2), "onesb"], [RP[3]])
                if HS[0]:
                    P.op("pe", Rec("matmul", ps[2][:, 480:480 + NS], lhsT=onesb[:], rhs=q[:, NP:NT], start=(k == 0), stop=(k == nk - 1)),
                         ["sq%d" % (k % 2), "onesb"], [RP[2]])
            P.op("act", Rec("activation", out=rstd[:, 0:NP], in_=ps[3][:, 0:NP], func=AF.Sqrt, scale=1.0 / dim, bias=epsb[:, 0:1]), [RP[3], "epsb"], ["rstd"])
            if HS[0]:
                P.op("act", Rec("activation", out=rstd[:, NP:NT], in_=ps[2][:, 480:480 + NS], func=AF.Sqrt, scale=1.0 / dim, bias=epsb[:, 0:1]), [RP[2], "epsb"], ["rstd"])
            P.op("dve", Rec("reciprocal", out=rstd[:], in_=rstd[:]), ["rstd"], ["rstd"])
            for k in range(nk):
                P.op("dve", Rec("scalar_tensor_tensor", out=dsts[k], in0=srcs[k], scalar=gain_fn(k), in1=rstd[:], op0=ALU.mult, op1=ALU.mult),
                     [src_regs[k], "rstd", "gains", "gfin"], [dst_regs[k]])

        epsb = sb("epsb", [128, 1])
        P.op("pool", Rec("memset", epsb[:], EPS), [], ["epsb"])
        halfpi = sb("halfpi", [128, 1])
        P.op("pool", Rec("memset", halfpi[:], math.pi / 2), [], ["halfpi"])

        def sincos(eng_v, angle, itmp, a1, a2, out_c, out_s, regs_in, rtag):
            P.op("dve", Rec("tensor_scalar", out=a1, in0=angle, scalar1=1.0 / TWO_PI, scalar2=None, op0=ALU.mult), regs_in, [rtag + "a1"])
            P.op("dve", Rec("tensor_copy", out=itmp, in_=a1), [rtag + "a1"], [rtag + "a2"])
            P.op("dve", Rec("tensor_copy", out=a1, in_=itmp), [rtag + "a2"], [rtag + "a1"])
            P.op("dve", Rec("scalar_tensor_tensor", out=angle, in0=a1, scalar=-TWO_PI, in1=angle, op0=ALU.mult, op1=ALU.add), [rtag + "a1"] + regs_in, regs_in)
            P.op("act", Rec("activation", out=a1, in_=angle, func=AF.Sin, scale=0.5), regs_in, [rtag + "a1"])
            P.op("act", Rec("activation", out=a2, in_=angle, func=AF.Sin, scale=0.5, bias=halfpi[:, 0:1]), regs_in + ["halfpi"], [rtag + "a2"])
            P.op("dve", Rec("scalar_tensor_tensor", out=out_s, in0=a1, scalar=2.0, in1=a2, op0=ALU.mult, op1=ALU.mult), [rtag + "a1", rtag + "a2"], [rtag + "s"])
            P.op("dve", Rec("tensor_tensor", out=a2, in0=a1, in1=a1, op=ALU.mult), [rtag + "a1"], [rtag + "a2"])
            P.op("dve", Rec("tensor_scalar", out=out_c, in0=a2, scalar1=-2.0, scalar2=1.0, op0=ALU.mult, op1=ALU.add), [rtag + "a2"], [rtag + "c"])

        def cmul(eng, o_r, o_i, a_r, a_i, b_r, b_i, tmp1, tmp2, rin, rout):
            P.op(eng, Rec("tensor_tensor", out=tmp1, in0=a_i, in1=b_i, op=ALU.mult), rin, rout)
            P.op(eng, Rec("tensor_tensor", out=tmp2, in0=a_i, in1=b_r, op=ALU.mult), rin, rout)
            P.op(eng, Rec("tensor_tensor", out=o_r, in0=a_r, in1=b_r, op=ALU.mult), rin, rout)
            P.op(eng, Rec("tensor_tensor", out=o_i, in0=a_r, in1=b_i, op=ALU.mult), rin, rout)
            P.op(eng, Rec("tensor_tensor", out=o_r, in0=o_r, in1=tmp1, op=ALU.subtract), rin, rout)
            P.op(eng, Rec("tensor_tensor", out=o_i, in0=o_i, in1=tmp2, op=ALU.add), rin, rout)

        def ckpt(n):
            if stop is not None and n >= stop:
                raise _Stop()

        try:
          for pi in range(npass):
              sidx = pi % 4
              HS[0] = pi < 4
              for blk in range(NB):
                  r0 = pi * NP + blk * 128
                  P.dma("sp", Rec("dma_start", out=stage[:], in_=dap(xp, r0 * D, [[D, 128], [1, D]])), [], STG)
                  for kq in range(4):
                      for kk in range(4):
                          k = kq * 4 + kk
                          P.op("pe", Rec("transpose", ps[4][:, kk * 128:(kk + 1) * 128], stage[:, k * 128:(k + 1) * 128], identf),
                               STG + ["cs"], [RP[4]])
                      P.op("dve", Rec("tensor_copy", out=X[:, kq * 4:kq * 4 + 4, blk * 128:(blk + 1) * 128],
                                                                      in_=ps[4][:].rearrange("p (a b) -> p a b", a=4)),
                           [RP[4]], ["X%d" % k for k in range(kq * 4, kq * 4 + 4)])
              P.dma("sp", Rec("dma_start", out=stage[0:4, :], in_=dap(xs, sidx * 4 * D, [[D, 4], [1, D]])), [], STG)
              for k in range(16 if HS[0] else 0):
                  P.op("pe", Rec("transpose", ps[4][:, k * 4:(k + 1) * 4], stage[0:4, k * 128:(k + 1) * 128], identf[0:4, 0:4]),
                       STG + ["cs"], [RP[4]])
              if HS[0]:
                  P.op("dve", Rec("tensor_copy", out=X[:, :, NP:NT], in_=ps[4][:, 0:64].rearrange("p (a b) -> p a b", a=16)),
                       [RP[4]], ["X%d" % k for k in range(16)])

              ckpt(1)
              for l in range(depth):
                  rmsnorm([X[:, k, :] for k in range(16)], ["X%d" % k for k in range(16)], lambda k, l=l: gains[:, l, k:k + 1],
                          [Hb[:, k, :] for k in range(16)], ["Hb%d" % k for k in range(16)], D)
                  hb_rhs = lambda k: Hb[:, k, :]
                  hb_regs = lambda k: ["Hb%d" % k]
                  ckpt(2)
                  for kvh in range(2):
                      P.op("pool", Rec("tensor_copy", out=Kb[kvh][:, 0:128], in_=Khalo[:, l, kvh, :]), ["Khalo"], ["Kb%d" % kvh])
                  P.op("pool", Rec("tensor_copy", out=Vb[:, 0, :, :], in_=Vhalo[:, l, :, :]), ["Vhalo"], ["Vb"])
                  ckpt(2.1)
                  slab, rslab = w_next("w_in", l, 1024)

                  def ev_k(pm, psm, rpm, rps):
                      for kvh in range(2):
                          hs = slice(kvh * 64, kvh * 64 + 64)
                          P.op("act", Rec("activation", out=Kb[kvh][hs, 128:128 + NP], in_=pm[hs, :], func=AF.Copy), rpm, ["Kb%d" % kvh])
                          if psm is not None:
                              P.op("act", Rec("activation", out=KS[kvh][hs, 128:132], in_=psm[hs, :], func=AF.Copy), rps, ["KS%d" % kvh])
                  proj_chunk(slab, rslab, 16, 0, hb_rhs, hb_regs, ev_k)
                  for kvh in range(2):
                      hs = slice(kvh * 64, kvh * 64 + 64)
                      ho = slice((1 - kvh) * 64, (1 - kvh) * 64 + 64)
                      P.dma("sp", Rec("dma_start", out=Kb[kvh][ho, 128:128 + NP], in_=Kb[kvh][hs, 128:128 + NP]), ["Kb%d" % kvh], ["Kb%d" % kvh])
                  ckpt(2.2)
                  import os as _os
                  for blk in range(NB + (1 if HS[0] else 0)):
                      if blk < NB:
                          cols = slice(blk * 128, (blk + 1) * 128); mrows = 128
                      else:
                          cols = slice(NP, NT); mrows = NS
                      for k in range(16):
                          P.op("pe", Rec("matmul", ps[5][0:mrows, 0:256], lhsT=Hb[:, k, cols], rhs=slab[:, k, 0:256], start=(k == 0), stop=(k == 15)),
                               [rslab, "Hb%d" % k], [RP[5]])
                      if blk < NB:
                          for kvh in range(2 - 2 * int(_os.environ.get("SKIP_A", "0"))):
                              for odd in range(2):
                                  P.op("dve", Rec("tensor_copy", out=Vb[:, blk + 1, kvh * 2 + odd, odd * 64:odd * 64 + 64], in_=ps[5][:, 128 + kvh * 64:128 + kvh * 64 + 64]),
                                       [RP[5]], ["Vb"])
                          if blk == NB - 1 and not int(_os.environ.get("SKIP_B", "0")):
                              P.op("act", Rec("activation", out=kvst[:], in_=ps[5][:, 0:256], func=AF.Copy), [RP[5]], ["kvst"])
                              P.dma("sp", Rec("dma_start", out=okp.ap()[l], in_=kvst[:, 0:128]), ["kvst"], ["okp"])
                              P.dma("sp", Rec("dma_start", out=ovp.ap()[l], in_=kvst[:, 128:256]), ["kvst"], ["ovp"])
                      else:
                          for kvh in range(2):
                              for odd in range(2):
                                  P.op("dve", Rec("tensor_copy", out=VSn[0:4, kvh * 2 + odd, odd * 64:odd * 64 + 64], in_=ps[5][0:4, 128 + kvh * 64:128 + kvh * 64 + 64]),
                                       [RP[5]], ["VSn"])
                          P.op("act", Rec("activation", out=kvss[:], in_=ps[5][0:4, 0:256], func=AF.Copy), [RP[5]], ["kvst"])
                          P.dma("sp", Rec("dma_start", out=oks.ap()[l, sidx, 124:128, :], in_=kvss[:, 0:128]), ["kvst"], ["oks"])
                          P.dma("sp", Rec("dma_start", out=ovs.ap()[l, sidx, 124:128, :], in_=kvss[:, 128:256]), ["kvst"], ["ovs"])
                          P.dma("sp", Rec("dma_start", out=cpy[0:124, 0:128], in_=ck.ap()[l, sidx, 4:128, :]), [], ["kvst"])
                          P.dma("sp", Rec("dma_start", out=cpy[0:124, 128:256], in_=cv.ap()[l, sidx, 4:128, :]), [], ["kvst"])
                          P.dma("sp", Rec("dma_start", out=oks.ap()[l, sidx, 0:124, :], in_=cpy[0:124, 0:128]), ["kvst"], ["oks"])
                          P.dma("sp", Rec("dma_start", out=ovs.ap()[l, sidx, 0:124, :], in_=cpy[0:124, 128:256]), ["kvst"], ["ovs"])
                  ckpt(2.3)
                  for kvh in range(2):
                      P.op("pool", Rec("tensor_copy", out=Khalo[:, l, kvh, :], in_=Kb[kvh][:, NP:NP + 128]), ["Kb%d" % kvh], ["Khalo"])
                  P.op("pool", Rec("tensor_copy", out=Vhalo[:, l, :, :], in_=Vb[:, NB, :, :]), ["Vb"], ["Vhalo"])
                  ckpt(2.4)
                  if HS[0]:
                      P.dma("pool", Rec("dma_start", out=cks[:, 0:128], in_=ck.ap()[l, sidx]), [], ["cks"])
                      P.op("pe", Rec("matmul", ps[5][:, 256:384], lhsT=cks[:, 0:128], rhs=identb[:], start=True, stop=True), ["cks", "identb"], [RP[5]])
                  for kvh in range(2 if HS[0] else 0):
                      hs = slice(kvh * 64, kvh * 64 + 64)
                      ho = slice((1 - kvh) * 64, (1 - kvh) * 64 + 64)
                      P.op("act", Rec("activation", out=KS[kvh][hs, 0:128], in_=ps[5][hs, 256:384], func=AF.Copy), [RP[5]], ["KS%d" % kvh])
                      P.dma("sp", Rec("dma_start", out=KS[kvh][ho, :], in_=KS[kvh][hs, :]), ["KS%d" % kvh], ["KS%d" % kvh])
                  for kvh in range(2 if HS[0] else 0):
                      for odd in range(2):
                          P.dma("pool", Rec("dma_start", out=VSc[:, kvh * 2 + odd, odd * 64:odd * 64 + 64], in_=cv.ap()[l, sidx, :, kvh * 64:kvh * 64 + 64]), [], ["VSc"])
                  ckpt(3)
                  for j in range(4):
                      slab, rslab = w_next("w_in", l, j * 256)
                      for mm in range(2):
                          m = j * 2 + mm
                          def ev_q(pm, psm, rpm, rps, m=m):
                              P.op("act", Rec("activation", out=Ao[:, m, 0:NP], in_=pm, func=AF.Copy, scale=0.125), rpm, ["Ao%d" % m])
                              if psm is not None:
                                  P.op("act", Rec("activation", out=Ao[:, m, NP:NT], in_=psm, func=AF.Copy, scale=0.125), rps, ["Ao%d" % m])
                          proj_chunk(slab, rslab, 16, mm * 128, hb_rhs, hb_regs, ev_q)
                  ckpt(4)
                  units = []
                  for nb in range(NB):
                      for m in range(8):
                          mk = maskfb if (pi == 0 and nb == 0) else maskb
                          for odd in range(2):
                              units.append((m, odd, 128, slice(nb * 128, (nb + 1) * 128), (lambda kvh, nb=nb: Kb[kvh][:, nb * 128:nb * 128 + 256]), 256, mk,
                                            [(128, (lambda v, nb=nb: Vb[:, nb, v, :])), (128, (lambda v, nb=nb: Vb[:, nb + 1, v, :]))]))
                  for m in range(8 if HS[0] else 0):
                      for odd in range(2):
                          units.append((m, odd, NS, slice(NP, NT), (lambda kvh: KS[kvh][:, 0:132]), 132, masksb,
                                        [(128, (lambda v: VSc[:, v, :])), (NS, (lambda v: VSn[:, v, :]))]))

                  def U_(i):
                      m, odd, nq, qcols, keyfn, nkeys, mk, segs = units[i]
                      h = 2 * m + odd
                      return m, odd, nq, qcols, keyfn, nkeys, mk, segs, h, h // 8, odd * 64, i % 3, 5 + i % 3, (i % 3) * 8

                  def sS(i):
                      m, odd, nq, qcols, keyfn, nkeys, mk, segs, h, kvh, hp, u, sbk, c0 = U_(i)
                      P.op("pe", Rec("matmul", ps[sbk][0:nq, 0:nkeys], lhsT=Ao[hp:hp + 64, m, qcols], rhs=keyfn(kvh)[hp:hp + 64, :], start=True, stop=False),
                           ["Ao%d" % m, "Kb%d" % kvh, "KS%d" % kvh], [RP[sbk]])
                      P.op("pe", Rec("matmul", ps[sbk][0:nq, 0:nkeys], lhsT=identb[0:nq, 0:nq], rhs=mk[0:nq, 0:nkeys], start=False, stop=True),
                           ["identb", "maskb", "maskfb", "masksb"], [RP[sbk]])

                  def sPre1(i):
                      m, odd, nq, qcols, keyfn, nkeys, mk, segs, h, kvh, hp, u, sbk, c0 = U_(i)
                      P.op("dve", Rec("reduce_max", out=sm[0:nq, c0:c0 + 1], in_=ps[sbk][0:nq, 0:nkeys], axis=AX.X), [RP[sbk]], ["sm%d" % u])

                  def sPre2(i, l=l):
                      m, odd, nq, qcols, keyfn, nkeys, mk, segs, h, kvh, hp, u, sbk, c0 = U_(i)
                      scol = sinkb[0:nq, l * 16 + h:l * 16 + h + 1]
                      P.op("dve", Rec("tensor_scalar", out=sm[0:nq, c0 + 1:c0 + 2], in0=sm[0:nq, c0:c0 + 1], scalar1=scol, scalar2=-1.0, op0=ALU.max, op1=ALU.mult),
                           ["sm%d" % u, "sinkb"], ["sm%d" % u])

                  def sExp(i, l=l):
                      m, odd, nq, qcols, keyfn, nkeys, mk, segs, h, kvh, hp, u, sbk, c0 = U_(i)
                      scol = sinkb[0:nq, l * 16 + h:l * 16 + h + 1]
                      P.op("act", Rec("activation", out=Pe[u][0:nq, 0:nkeys], in_=ps[sbk][0:nq, 0:nkeys], func=AF.Exp, bias=sm[0:nq, c0 + 1:c0 + 2], accum_out=sm[0:nq, c0 + 2:c0 + 3]),
                           [RP[sbk], "sm%d" % u], ["Pf%d" % u, "sm%d" % u])
                      P.op("act", Rec("activation", out=sm[0:nq, c0 + 3:c0 + 4], in_=sm[0:nq, c0 + 1:c0 + 2], func=AF.Exp, bias=scol),
                           ["sm%d" % u, "sinkb"], ["sm%d" % u])

                  def sPost1(i):
                      m, odd, nq, qcols, keyfn, nkeys, mk, segs, h, kvh, hp, u, sbk, c0 = U_(i)
                      P.op("dve", Rec("tensor_tensor", out=sm[0:nq, c0 + 4:c0 + 5], in0=sm[0:nq, c0 + 2:c0 + 3], in1=sm[0:nq, c0 + 3:c0 + 4], op=ALU.add), ["sm%d" % u], ["sm%d" % u])

                  def sPost2(i):
                      m, odd, nq, qcols, keyfn, nkeys, mk, segs, h, kvh, hp, u, sbk, c0 = U_(i)
                      P.op("dve", Rec("reciprocal", out=sm[0:nq, c0 + 5:c0 + 6], in_=sm[0:nq, c0 + 4:c0 + 5]), ["sm%d" % u], ["sm%d" % u])

                  def sPost3(i):
                      m, odd, nq, qcols, keyfn, nkeys, mk, segs, h, kvh, hp, u, sbk, c0 = U_(i)
                      P.op("dve", Rec("tensor_scalar", out=Pn[u][0:nq, 0:nkeys], in0=Pe[u][0:nq, 0:nkeys], scalar1=sm[0:nq, c0 + 5:c0 + 6], scalar2=None, op0=ALU.mult),
                           ["Pf%d" % u, "sm%d" % u], ["Pn%d" % u])

                  def sT(i):
                      m, odd, nq, qcols, keyfn, nkeys, mk, segs, h, kvh, hp, u, sbk, c0 = U_(i)
                      ptb = psb(i % 2)
                      ko = 0
                      for si, (nk_, vfn) in enumerate(segs):
                          P.op("pe", Rec("transpose", ptb[0:nk_, si * 128:si * 128 + nq], Pn[u][0:nq, ko:ko + nk_], identb[0:nq, 0:nq]),
                               ["Pn%d" % u, "identb"], [RP[i % 2]])
                          ko += nk_

                  def sEv(i):
                      m, odd, nq, qcols, keyfn, nkeys, mk, segs, h, kvh, hp, u, sbk, c0 = U_(i)
                      ptb = psb(i % 2)
                      for si, (nk_, vfn) in enumerate(segs):
                          P.op("act", Rec("activation", out=PT[u][0:nk_, si, 0:nq], in_=ptb[0:nk_, si * 128:si * 128 + nq], func=AF.Copy),
                               [RP[i % 2]], ["PT%d" % u])

                  def sPV(i):
                      m, odd, nq, qcols, keyfn, nkeys, mk, segs, h, kvh, hp, u, sbk, c0 = U_(i)
                      for si, (nk_, vfn) in enumerate(segs):
                          first = (odd == 0 and si == 0)
                          last = (odd == 1 and si == len(segs) - 1)
                          P.op("pe", Rec("matmul", ps[3][:, 0:nq], lhsT=vfn(kvh * 2 + odd)[0:nk_, :], rhs=PT[u][0:nk_, si, 0:nq], start=first, stop=last),
                               ["PT%d" % u, "Vb", "VSc", "VSn"], [RP[3]])

                  def sO(i):
                      m, odd, nq, qcols, keyfn, nkeys, mk, segs, h, kvh, hp, u, sbk, c0 = U_(i)
                      if odd == 1:
                          P.op("dve", Rec("tensor_copy", out=Ao[:, m, qcols], in_=ps[3][:, 0:nq]), [RP[3]], ["Ao%d" % m])

                  NU = len(units)
                  ok_ = lambda i: 0 <= i < NU
                  for t in range(NU + 8):
                      if ok_(t): sS(t)
                      if ok_(t - 1): sPre1(t - 1)
                      if ok_(t - 3): sPost1(t - 3)
                      if ok_(t - 1): sPre2(t - 1)
                      if ok_(t - 3): sPost2(t - 3)
                      if ok_(t - 7): sO(t - 7)
                      if ok_(t - 3): sPost3(t - 3)
                      if ok_(t - 2): sExp(t - 2)
                      if ok_(t - 4): sT(t - 4)
                      if ok_(t - 5): sEv(t - 5)
                      if ok_(t - 6): sPV(t - 6)

                  ckpt(5)
                  if pi == 0:
                      for nm, dst in [("ssm_a_re", lam_r), ("ssm_a_im", lam_i)]:
                          P.dma("sp", Rec("dma_start", out=dst[:], in_=dap(W[nm], l * 4096, [[1, 128], [128, 32]])), [], ["ssmp"])
                      for gl in range(2):
                          P.dma("sp", Rec("dma_start", out=dtt[gl * 64:(gl + 1) * 64, :], in_=dap(W["ssm_log_dt"], l * 64 + gl, [[0, 64], [2, 32]])), [], ["ssmp"])
                      for nm, dst in [("ssm_b_re", Br), ("ssm_b_im", Bi)]:
                          P.dma("sp", Rec("dma_start", out=dst[:], in_=dap(W[nm], l * 65536, [[16, 128], [2048, 32], [1, 16]])), [], ["ssmB"])
                      for nm, dst, rc in [("ssm_c_re", Cr, "Cr"), ("ssm_c_im", Ci, "Ci")]:
                          for jq in range(4):
                              P.dma("sp", Rec("dma_start", out=Cn[:], in_=dap(W[nm], l * 65536 + jq * 16384, [[64, 16], [1024, 16], [1, 64]])), [], ["Cn"])
                              for jj in range(8):
                                  P.op("pe", Rec("transpose", ps[4][:, jj * 16:(jj + 1) * 16], Cn[:, 2 * jj:2 * jj + 2, :], identf[0:16, 0:16]), ["Cn", "cs"], [RP[4]])
                              P.op("dve", Rec("tensor_copy", out=dst[:, jq * 8:(jq + 1) * 8, :], in_=ps[4][:, 0:128].rearrange("p (a b) -> p a b", a=8)), [RP[4]], ["ssmC"])
                  P.dma("sp", Rec("dma_start", out=Drow[:], in_=dap(W["ssm_d"], l * 1024, [[0, 128], [1, 1024]])), [], ["Drow"])
                  for src_t, dst in [(hr0, h0r), (hi0, h0i)]:
                      P.dma("sp", Rec("dma_start", out=dst[:], in_=dap(src_t, (l * 4 + sidx) * 4096, [[1, 128], [128, 32]])), [], ["h0"])
                  if pi == 0:
                      sp_ = ["ssmp"]
                      P.op("act", Rec("activation", out=dtt[:], in_=dtt[:], func=AF.Exp), sp_, sp_)
                      P.op("dve", Rec("tensor_tensor", out=t1[:], in0=lam_r[:], in1=dtt[:], op=ALU.mult), sp_, sp_)
                      P.op("act", Rec("activation", out=mag[:], in_=t1[:], func=AF.Exp), sp_, sp_)
                      P.op("act", Rec("activation", out=rho[:], in_=t1[:], func=AF.Exp, scale=8.0), sp_, sp_)
                      P.op("dve", Rec("tensor_tensor", out=th[:], in0=lam_i[:], in1=dtt[:], op=ALU.mult), sp_, sp_)
                      P.op("dve", Rec("tensor_scalar", out=phi[:], in0=th[:], scalar1=8.0, scalar2=None, op0=ALU.mult), sp_, sp_)
                      sincos("dve", th[:], t3[:].bitcast(I32), t2[:], t3[:], abr[:], abi[:], sp_, "tr1")
                      P.op("dve", Rec("tensor_tensor", out=abr[:], in0=abr[:], in1=mag[:], op=ALU.mult), sp_ + ["tr1c"], sp_)
                      P.op("dve", Rec("tensor_tensor", out=abi[:], in0=abi[:], in1=mag[:], op=ALU.mult), sp_ + ["tr1s"], sp_)
                      P.op("dve", Rec("tensor_scalar", out=t2[:], in0=phi[:], scalar1=1.0 / TWO_PI, scalar2=None, op0=ALU.mult), sp_ + ["tr1a1"], sp_ + ["tr1a1"])
                      P.op("dve", Rec("tensor_copy", out=ti[:], in_=t2[:]), sp_ + ["tr1a1", "tr1a2"], sp_ + ["tr1a2"])
                      P.op("dve", Rec("tensor_copy", out=t2[:], in_=ti[:]), sp_ + ["tr1a1", "tr1a2"], sp_ + ["tr1a1"])
                      P.op("dve", Rec("scalar_tensor_tensor", out=phi[:], in0=t2[:], scalar=-TWO_PI, in1=phi[:], op0=ALU.mult, op1=ALU.add), sp_ + ["tr1a1"], sp_)
                      P.op("dve", Rec("tensor_scalar", out=t1[:], in0=abr[:], scalar1=-1.0, scalar2=None, op0=ALU.add), sp_, sp_)
                      P.op("dve", Rec("tensor_tensor", out=t2[:], in0=lam_r[:], in1=lam_r[:], op=ALU.mult), sp_ + ["tr1a1"], sp_ + ["tr1a1"])
                      P.op("dve", Rec("tensor_tensor", out=t3[:], in0=lam_i[:], in1=lam_i[:], op=ALU.mult), sp_ + ["tr1a2"], sp_ + ["tr1a2"])
                      P.op("dve", Rec("tensor_tensor", out=t2[:], in0=t2[:], in1=t3[:], op=ALU.add), sp_ + ["tr1a1", "tr1a2"], sp_ + ["tr1a1"])
                      P.op("dve", Rec("reciprocal", out=t2[:], in_=t2[:]), sp_ + ["tr1a1"], sp_ + ["tr1a1"])
                      P.op("dve", Rec("tensor_tensor", out=crr[:], in0=t1[:], in1=lam_r[:], op=ALU.mult), sp_, sp_)
                      P.op("dve", Rec("tensor_tensor", out=t3[:], in0=abi[:], in1=lam_i[:], op=ALU.mult), sp_ + ["tr1a2"], sp_ + ["tr1a2"])
                      P.op("dve", Rec("tensor_tensor", out=crr[:], in0=crr[:], in1=t3[:], op=ALU.add), sp_ + ["tr1a2"], sp_)
                      P.op("dve", Rec("tensor_tensor", out=crr[:], in0=crr[:], in1=t2[:], op=ALU.mult), sp_ + ["tr1a1"], sp_)
                      P.op("dve", Rec("tensor_tensor", out=cii[:], in0=abi[:], in1=lam_r[:], op=ALU.mult), sp_, sp_)
                      P.op("dve", Rec("tensor_tensor", out=t3[:], in0=t1[:], in1=lam_i[:], op=ALU.mult), sp_ + ["tr1a2"], sp_ + ["tr1a2"])
                      P.op("dve", Rec("tensor_tensor", out=cii[:], in0=cii[:], in1=t3[:], op=ALU.subtract), sp_ + ["tr1a2"], sp_)
                      P.op("dve", Rec("tensor_tensor", out=cii[:], in0=cii[:], in1=t2[:], op=ALU.mult), sp_ + ["tr1a1"], sp_)
                      bc = lambda a: a[:].unsqueeze(2).broadcast_to([128, 32, 16])
                      cmul("dve", Bbr[:], Bbi[:], bc(crr), bc(cii), Br[:], Bi[:], PCr[:], PCi[:], sp_ + ["ssmB", "PC"], ["ssmB", "PC"])
                      P.op("dve", Rec("tensor_tensor", out=t1[:], in0=mag[:], in1=mag[:], op=ALU.mult), sp_, sp_)
                      P.op("dve", Rec("reciprocal", out=t1[:], in_=t1[:]), sp_, sp_)
                      P.op("dve", Rec("tensor_tensor", out=iar[:], in0=abr[:], in1=t1[:], op=ALU.mult), sp_, sp_)
                      P.op("dve", Rec("scalar_tensor_tensor", out=iai[:], in0=abi[:], scalar=-1.0, in1=t1[:], op0=ALU.mult, op1=ALU.mult), sp_, sp_)
                      pr = ["PC", "ssmp"]
                      P.op("pool", Rec("memset", PCr[:, :, 7:8], 1.0), pr, pr)
                      P.op("pool", Rec("memset", PCi[:, :, 7:8], 0.0), pr, pr)
                      for kk in range(8, 16):
                          cmul("dve", PCr[:, :, kk], PCi[:, :, kk], PCr[:, :, kk - 1], PCi[:, :, kk - 1], abr[:], abi[:], t2[:], t3[:], pr + ["tr1a1", "tr1a2"], pr + ["tr1a1", "tr1a2"])
                      for kk in range(6, -1, -1):
                          cmul("dve", PCr[:, :, kk], PCi[:, :, kk], PCr[:, :, kk + 1], PCi[:, :, kk + 1], iar[:], iai[:], t2[:], t3[:], pr + ["tr1a1", "tr1a2"], pr + ["tr1a1", "tr1a2"])
                      for i in range(8):
                          P.op("pool", Rec("tensor_copy", out=PBr[:, :, i], in_=PCr[:, :, 14 - i]), pr, ["PB"])
                          P.op("pool", Rec("tensor_copy", out=PBi[:, :, i], in_=PCi[:, :, 14 - i]), pr, ["PB"])
                  if pi == 0:
                      for t_, o_, n_ in [(PCr, 0, 512), (PCi, 512, 512)]:
                          P.dma("sp", Rec("dma_start", out=dap(scrT, l * 128 * 1088 + o_, [[1088, 128], [1, n_]]), in_=t_[:].rearrange("p a b -> p (a b)")), ["PC"], ["scrT%d" % l])
                      for t_, o_ in [(rho, 1024), (phi, 1056)]:
                          P.dma("sp", Rec("dma_start", out=dap(scrT, l * 128 * 1088 + o_, [[1088, 128], [1, 32]]), in_=t_[:]), ["ssmp"], ["scrT%d" % l])
                  else:
                      for t_, o_, n_ in [(PCr, 0, 512), (PCi, 512, 512)]:
                          P.dma("sp", Rec("dma_start", out=t_[:].rearrange("p a b -> p (a b)"), in_=dap(scrT, l * 128 * 1088 + o_, [[1088, 128], [1, n_]])), ["scrT%d" % l], ["PC"])
                      for t_, o_ in [(rho, 1024), (phi, 1056)]:
                          P.dma("sp", Rec("dma_start", out=t_[:], in_=dap(scrT, l * 128 * 1088 + o_, [[1088, 128], [1, 32]])), ["scrT%d" % l], ["ssmp"])
                  cmul("dve", hend[:, 0, :], hend[:, 1, :], PCr[:, :, 11], PCi[:, :, 11], h0r[:], h0i[:], t2[:], t3[:], pr + ["h0", "hend", "tr1a1", "tr1a2"], ["hend", "tr1a1", "tr1a2"])

                  ckpt(6)
                  for sl in range(4):
                      slab, rslab = w_next("w_in", l, 1280 + sl * 256)
                      for i in range(8):
                          nrow = NCH1 if i < 4 else NCHK
                          for k in range(16):
                              lhs = sap(Hb[:, k, i:i + 1], [[8, nrow]])
                              P.op("pe", Rec("matmul", ps[i % 2][0:nrow, 0:256], lhsT=lhs, rhs=slab[:, k, 0:256], start=(k == 0), stop=(k == 15)),
                                   [rslab, "Hb%d" % k], [RP[i % 2]])
                          P.op("act", Rec("activation", out=Vu[0:nrow, :, i, :], in_=ps[i % 2][0:nrow, 0:256].rearrange("p (g c) -> p g c", g=16), func=AF.Copy), [RP[i % 2]], ["Vu"])
                      for gq in range(4):
                          ub = psb(4)
                          for gg in range(4):
                              g = gq * 4 + gg
                              P.op("pe", Rec("transpose", ub[:, gg * 128:gg * 128 + NCH1], Vu[0:NCH1, g, :, :], identb[0:NCH1, 0:NCH1]),
                                   ["Vu", "identb"], [RP[4]])
                          for gg in range(4):
                              g = gq * 4 + gg
                              P.op("dve", Rec("tensor_copy", out=Ug[g][:], in_=ub[:, gg * 128:gg * 128 + NCH1]), [RP[4]], ["Ug%d" % g])
                      j0 = sl * NPB
                      bcp = lambda a: a[:, j0:j0 + NPB].unsqueeze(2).broadcast_to([128, NPB, 65])
                      posb = posf.unsqueeze(1).broadcast_to([128, NPB, 65])
                      P.op("dve", Rec("tensor_tensor", out=ang[:], in0=bcp(phi), in1=posb, op=ALU.mult), ["ssmp", "cs", "trb"], ["trb"])
                      sincos("dve", ang[:], u2[:].bitcast(I32), u1[:], u2[:], Tc[:], Ts[:], ["trb"], "tr2")
                      P.op("dve", Rec("tensor_tensor", out=d0[:], in0=bcp(rho), in1=cs[:, 965:1030].unsqueeze(1).broadcast_to([128, NPB, 65]), op=ALU.mult), ["ssmp", "cs"], ["d0"])
                      creg = "scrC_%d_%d" % (l, sl)
                      csrc = dap(scrC, (l * 4 + sl) * 128 * 4096, [[4096, 128], [1, 4096]])
                      gsrc = dap(scrG, (l * 4 + sl) * 128 * 2048, [[2048, 128], [1, 2048]])
                      if pi > 0:
                          P.dma("sp", Rec("dma_start", out=MCBt[:].rearrange("p a b c -> p (a b c)"), in_=csrc), [creg], ["MCm"])
                          P.dma("sp", Rec("dma_start", out=TgSt[:].rearrange("p a b -> p (a b)"), in_=gsrc), [creg], ["TgS"])
                      for jj in range(NPB):
                          j = j0 + jj
                          MBp = [MBpT[:, jj % 2, q_, :] for q_ in range(4)]
                          if pi == 0:
                              rsp = ["ssmB", "PB", "PC", "ssmC", "ssmp"]
                              pb_r = PBr[:, j, :].unsqueeze(2).broadcast_to([128, 8, 16]); pb_i = PBi[:, j, :].unsqueeze(2).broadcast_to([128, 8, 16])
                              bb_r = Bbr[:, j, :].unsqueeze(1).broadcast_to([128, 8, 16]); bb_i = Bbi[:, j, :].unsqueeze(1).broadcast_to([128, 8, 16])
                              v3 = lambda a, n: a[:, 0:n * 16].rearrange("p (a b) -> p a b", b=16)
                              P.op("pool", Rec("tensor_tensor", out=v3(tA, 8), in0=pb_i, in1=bb_i, op=ALU.mult), rsp + ["tA"], ["tA"])
                              P.op("pool", Rec("tensor_tensor", out=v3(tB, 8), in0=pb_r, in1=bb_r, op=ALU.mult), rsp + ["tB"], ["tB"])
                              P.op("pool", Rec("tensor_tensor", out=v3(MBt[0], 8), in0=v3(tB, 8), in1=v3(tA, 8), op=ALU.subtract), ["tA", "tB"], ["MBt"])
                              P.op("pool", Rec("tensor_tensor", out=v3(tA, 8), in0=pb_r, in1=bb_i, op=ALU.mult), rsp + ["tA"], ["tA"])
                              P.op("pool", Rec("tensor_tensor", out=v3(tB, 8), in0=pb_i, in1=bb_r, op=ALU.mult), rsp + ["tB"], ["tB"])
                              P.op("pool", Rec("tensor_tensor", out=v3(MBt[1], 8), in0=v3(tA, 8), in1=v3(tB, 8), op=ALU.add), ["tA", "tB"], ["MBt"])
                              pc_r = PCr[:, j, :].unsqueeze(2).broadcast_to([128, 16, 16]); pc_i = PCi[:, j, :].unsqueeze(2).broadcast_to([128, 16, 16])
                              cc_r = Cr[:, j, :].unsqueeze(1).broadcast_to([128, 16, 16]); cc_i = Ci[:, j, :].unsqueeze(1).broadcast_to([128, 16, 16])
                              P.op("dve", Rec("tensor_tensor", out=v3(tA, 16), in0=pc_i, in1=cc_i, op=ALU.mult), rsp + ["tA"], ["tA"])
                              P.op("dve", Rec("tensor_tensor", out=v3(tB, 16), in0=pc_r, in1=cc_r, op=ALU.mult), rsp + ["tB"], ["tB"])
                              MCm = MCB[jj]
                              P.op("dve", Rec("tensor_tensor", out=v3(MCm[0], 16), in0=v3(tB, 16), in1=v3(tA, 16), op=ALU.subtract), ["tA", "tB"], ["MCm"])
                              P.op("dve", Rec("tensor_tensor", out=v3(tA, 16), in0=pc_r, in1=cc_i, op=ALU.mult), rsp + ["tA"], ["tA"])
                              P.op("dve", Rec("tensor_tensor", out=v3(tB, 16), in0=pc_i, in1=cc_r, op=ALU.mult), rsp + ["tB"], ["tB"])
                              P.op("dve", Rec("scalar_tensor_tensor", out=v3(MCm[1], 16), in0=v3(tA, 16), scalar=-1.0, in1=v3(tB, 16), op0=ALU.mult, op1=ALU.subtract), ["tA", "tB"], ["MCm"])
                              tb = psb(5)
                              for ri in range(2):
                                  P.op("pe", Rec("transpose", tb[:, ri * 128:(ri + 1) * 128], MBt[ri][:], identb[:]), ["MBt", "identb"], [RP[5]])
                              for gl in range(2):
                                  for ri in range(2):
                                      P.op("act", Rec("activation", out=MBp[gl * 2 + ri][:, gl * 64:gl * 64 + 64], in_=tb[:, ri * 128 + gl * 64:ri * 128 + gl * 64 + 64], func=AF.Copy),
                                           [RP[5]], ["MBp%d" % (jj % 2)])
                              for gl in range(2):
                                  sl_ = slice(gl * 64, gl * 64 + 64)
                                  P.op("pe", Rec("matmul", ps[3][:, gl * 128:(gl + 1) * 128], lhsT=MBt[0][sl_, :], rhs=MCm[0][sl_, 0:128], start=True, stop=False), ["MBt", "MCm"], [RP[3]])
                                  P.op("pe", Rec("matmul", ps[3][:, gl * 128:(gl + 1) * 128], lhsT=MBt[1][sl_, :], rhs=MCm[1][sl_, 0:128], start=False, stop=True), ["MBt", "MCm"], [RP[3]])
                                  P.op("dve", Rec("tensor_tensor", out=TgS[jj * 2 + gl][:], in0=ps[3][:, gl * 128:(gl + 1) * 128], in1=tmask[:], op=ALU.mult), [RP[3], "tmask"], ["TgS"])
                          mreg = "scrM_%d_%d" % (l, j)
                          msrc = dap(scrM, (l * 32 + j) * 128 * 512, [[512, 128], [1, 512]])
                          if pi == 0:
                              P.dma("sp", Rec("dma_start", out=msrc, in_=MBpT[:, jj % 2].rearrange("p a b -> p (a b)")), ["MBp%d" % (jj % 2)], [mreg])
                          else:
                              P.dma("sp", Rec("dma_start", out=MBpT[:, jj % 2].rearrange("p a b -> p (a b)"), in_=msrc), [mreg], ["MBp%d" % (jj % 2)])
                          for ri in range(2):
                              for gl in range(2):
                                  g = jj * 2 + gl
                                  P.op("pe", Rec("matmul", ps[6][:, ri * 128:ri * 128 + NCH1], lhsT=MBp[gl * 2 + ri][:], rhs=Ug[g][:], start=(gl == 0), stop=(gl == 1)),
                                       ["MBp%d" % (jj % 2), "Ug%d" % g], [RP[6]])
                          xr = ps[6][:, 0:NCHK]; xi = ps[6][:, 128:128 + NCHK]
                          tcj = Tc[:, jj, 1:65]; tsj = Ts[:, jj, 1:65]
                          rdm = [RP[6], "tr2c", "tr2s"]
                          P.op("dve", Rec("tensor_tensor", out=Dr[:, jj, 1:65], in0=xr, in1=tcj, op=ALU.mult), rdm, ["Dm"])
                          P.op("dve", Rec("tensor_tensor", out=u1[:, jj, 1:65], in0=xi, in1=tsj, op=ALU.mult), rdm + ["tr2a1"], ["tr2a1"])
                          P.op("dve", Rec("tensor_tensor", out=Di[:, jj, 1:65], in0=xi, in1=tcj, op=ALU.mult), rdm, ["Dm"])
                          P.op("dve", Rec("tensor_tensor", out=u2[:, jj, 1:65], in0=xr, in1=tsj, op=ALU.mult), rdm + ["tr2a2"], ["tr2a2"])
                          P.op("act", Rec("activation", out=Xs[:, 0, jj:jj + 1], in_=ps[6][:, NCHK:NCHK + 1], func=AF.Copy), [RP[6]], ["Xs"])
                          P.op("act", Rec("activation", out=Xs[:, 1, jj:jj + 1], in_=ps[6][:, 128 + NCHK:128 + NCHK + 1], func=AF.Copy), [RP[6]], ["Xs"])
                      if pi == 0:
                          P.dma("sp", Rec("dma_start", out=csrc, in_=MCBt[:].rearrange("p a b c -> p (a b c)")), ["MCm"], [creg])
                          P.dma("sp", Rec("dma_start", out=gsrc, in_=TgSt[:].rearrange("p a b -> p (a b)")), ["TgS"], [creg])
                      dm = ["Dm", "tr2a1", "tr2a2"]
                      P.op("dve", Rec("tensor_tensor", out=Dr[:, :, 1:65], in0=Dr[:, :, 1:65], in1=u1[:, :, 1:65], op=ALU.add), dm, ["Dm"])
                      P.op("dve", Rec("tensor_tensor", out=Di[:, :, 1:65], in0=Di[:, :, 1:65], in1=u2[:, :, 1:65], op=ALU.subtract), dm, ["Dm"])
                      P.op("dve", Rec("tensor_copy", out=Dr[:, :, 0], in_=Hr[:, l, j0:j0 + NPB]), ["H", "Dm"], ["Dm"])
                      P.op("dve", Rec("tensor_copy", out=Di[:, :, 0], in_=Hi[:, l, j0:j0 + NPB]), ["H", "Dm"], ["Dm"])
                      fl = lambda a: a[:].rearrange("p a b -> p (a b)")
                      P.op("dve", Rec("tensor_tensor_scan", out=fl(Dr), data0=fl(d0), data1=fl(Dr), initial=0.0, op0=ALU.mult, op1=ALU.add), ["Dm", "d0"], ["Dm"])
                      P.op("dve", Rec("tensor_tensor_scan", out=fl(Di), data0=fl(d0), data1=fl(Di), initial=0.0, op0=ALU.mult, op1=ALU.add), ["Dm", "d0"], ["Dm"])
                      md = ["Dm", "tr2c", "tr2s", "tr2a1", "tr2a2", "trb"]
                      P.op("dve", Rec("tensor_tensor", out=u1[:], in0=Dr[:], in1=Tc[:], op=ALU.mult), md, ["tr2a1"])
                      P.op("dve", Rec("tensor_tensor", out=u2[:], in0=Di[:], in1=Ts[:], op=ALU.mult), md, ["tr2a2"])
                      P.op("dve", Rec("tensor_tensor", out=u1[:], in0=u1[:], in1=u2[:], op=ALU.subtract), md, ["tr2a1"])
                      P.op("dve", Rec("tensor_tensor", out=u2[:], in0=Dr[:], in1=Ts[:], op=ALU.mult), md, ["tr2a2"])
                      P.op("dve", Rec("tensor_tensor", out=ang[:], in0=Di[:], in1=Tc[:], op=ALU.mult), md, ["trb"])
                      P.op("dve", Rec("tensor_tensor", out=u2[:], in0=u2[:], in1=ang[:], op=ALU.add), md, ["tr2a2"])
                      P.op("act", Rec("activation", out=Sbr[:, :, 0:64], in_=u1[:, :, 0:64], func=AF.Copy), ["tr2a1"], ["Sb"])
                      P.op("act", Rec("activation", out=Sbi[:, :, 0:64], in_=u2[:, :, 0:64], func=AF.Copy), ["tr2a2"], ["Sb"])
                      P.op("act", Rec("activation", out=Sbr[:, :, 64], in_=h0r[:, j0:j0 + NPB], func=AF.Copy), ["h0"], ["Sb"])
                      P.op("act", Rec("activation", out=Sbi[:, :, 64], in_=h0i[:, j0:j0 + NPB], func=AF.Copy), ["h0"], ["Sb"])
                      P.op("dve", Rec("tensor_copy", out=Hr[:, l, j0:j0 + NPB], in_=u1[:, :, 64]), ["tr2a1"], ["H"])
                      P.op("dve", Rec("tensor_copy", out=Hi[:, l, j0:j0 + NPB], in_=u2[:, :, 64]), ["tr2a2"], ["H"])
                      cmul("dve", tA[:, 0:NPB], tA[:, NPB:2 * NPB], PCr[:, j0:j0 + NPB, 3], PCi[:, j0:j0 + NPB, 3], Xs[:, 0, :], Xs[:, 1, :], tB[:, 0:NPB], tB[:, NPB:2 * NPB],
                           ["PC", "Xs", "tA", "tB"], ["tA", "tB"])
                      P.op("dve", Rec("tensor_tensor", out=hend[:, 0, j0:j0 + NPB], in0=hend[:, 0, j0:j0 + NPB], in1=tA[:, 0:NPB], op=ALU.add), ["tA", "hend"], ["hend"])
                      P.op("dve", Rec("tensor_tensor", out=hend[:, 1, j0:j0 + NPB], in0=hend[:, 1, j0:j0 + NPB], in1=tA[:, NPB:2 * NPB], op=ALU.add), ["tA", "hend"], ["hend"])
                      for gq in range(4):
                          for gg in range(4):
                              g = gq * 4 + gg
                              jj = g // 2; gl = g % 2
                              sl_ = slice(gl * 64, gl * 64 + 64)
                              oc = slice(gg * 128, (gg + 1) * 128)
                              P.op("pe", Rec("matmul", ps[7][0:NCH1, oc], lhsT=Ug[g][:], rhs=TgS[g][:], start=True, stop=False), ["Ug%d" % g, "TgS"], [RP[7]])
                              P.op("pe", Rec("matmul", ps[7][0:NCH1, oc], lhsT=Sbr[sl_, jj, :], rhs=MCB[jj][0][sl_, 128:256], start=False, stop=False), ["Sb", "MCm"], [RP[7]])
                              P.op("pe", Rec("matmul", ps[7][0:NCH1, oc], lhsT=Sbi[sl_, jj, :], rhs=MCB[jj][1][sl_, 128:256], start=False, stop=True), ["Sb", "MCm"], [RP[7]])
                          ch0 = sl * 256 + gq * 64
                          P.op("dve", Rec("tensor_tensor", out=vd[0:NCH1], in0=Vu[0:NCH1, gq * 4:(gq + 1) * 4, :, :], in1=sap(Drow[0:NCH1, ch0:ch0 + 1], [[16, 4], [0, 8], [1, 16]]), op=ALU.mult),
                               ["Vu", "Drow"], ["vd"])
                          P.op("dve", Rec("tensor_tensor", out=ypre[0:NCH1], in0=ps[7][0:NCH1, :].rearrange("p (g i c) -> p g i c", g=4, i=8),
                                                                in1=vd[0:NCH1], op=ALU.add), [RP[7], "vd"], ["ypre"])
                          half = gq % 2
                          P.op("act", Rec("activation", out=zcm[0:NCH1, :, half * 64:(half + 1) * 64].rearrange("p i (g c) -> p g i c", g=4), in_=ypre[0:NCH1], func=AF.Gelu), ["ypre"], ["zcm"])
                          if half == 1:
                              mz = sl * 2 + gq // 2
                              zb = psb(5)
                              for i in range(8):
                                  P.op("pe", Rec("transpose", zb[:, i * 128:i * 128 + NCH1], zcm[0:NCH1, i, :], identb[0:NCH1, 0:NCH1]), ["zcm", "identb"], [RP[5]])
                              zv = zb[:, 0:1024].rearrange("p (i n) -> p i n", i=8)
                              P.op("dve", Rec("tensor_copy", out=sap(Zf[:, mz, 0:1], [[1, 4], [8, NCH1]]), in_=zv[:, 0:4, 0:NCH1]), [RP[5]], ["fb%d" % mz])
                              P.op("dve", Rec("tensor_copy", out=sap(Zf[:, mz, 4:5], [[1, 4], [8, NCHK]]), in_=zv[:, 4:8, 0:NCHK]), [RP[5]], ["fb%d" % mz])
                  ckpt(7)
                  for ri, (dst_p, dst_s, Hx) in enumerate([(ohrp, ohrs, Hr), (ohip, ohis, Hi)]):
                      P.dma("sp", Rec("dma_start", out=dap(dst_p, l * 4096, [[1, 128], [128, 32]]), in_=Hx[:, l, :]), ["H"], ["ohp%d" % ri])
                      if HS[0]:
                          P.dma("sp", Rec("dma_start", out=dap(dst_s, (l * 4 + sidx) * 4096, [[1, 128], [128, 32]]), in_=hend[:, ri, :]), ["hend"], ["ohs%d" % ri])
                  for j in range(4):
                      slab, rslab = w_next("w_glu", l, j * 256)
                      for mm in range(2):
                          m = j * 2 + mm
                          def ev_g(pm, psm, rpm, rps, m=m):
                              P.op("act", Rec("activation", out=Pf[0][:, 0:256], in_=pm[:, 0:256], func=AF.Sigmoid), rpm, ["Pf0", "Pf1"])
                              P.op("act", Rec("activation", out=Pf[1][:, 0:256], in_=pm[:, 256:512], func=AF.Sigmoid), rpm, ["Pf2"])
                              P.op("dve", Rec("tensor_tensor", out=Hb[:, 8 + m, 0:256], in0=Pf[0][:, 0:256], in1=Zf[:, m, 0:256], op=ALU.mult), ["Pf0", "Pf1", "fb%d" % m], ["Hb%d" % (8 + m)])
                              P.op("dve", Rec("tensor_tensor", out=Hb[:, 8 + m, 256:512], in0=Pf[1][:, 0:256], in1=Zf[:, m, 256:512], op=ALU.mult), ["Pf2", "fb%d" % m], ["Hb%d" % (8 + m)])
                              if psm is not None:
                                  P.op("act", Rec("activation", out=sm[:, 0:NS], in_=psm, func=AF.Sigmoid), rps, ["sm0", "sm1"])
                              P.op("dve", Rec("tensor_tensor", out=Hb[:, 8 + m, NP:NT], in0=sm[:, 0:NS], in1=Zf[:, m, NP:NT], op=ALU.mult), ["sm0", "sm1", "fb%d" % m], ["Hb%d" % (8 + m)])
                          proj_chunk(slab, rslab, 8, mm * 128, lambda k: Zf[:, k, :], lambda k: ["fb%d" % k], ev_g)
                  ckpt(8)
                  rmsnorm([Ao[:, k, :] for k in range(8)], ["Ao%d" % k for k in range(8)], lambda k, l=l: gains[:, l, 32 + k:33 + k],
                          [Ao[:, k, :] for k in range(8)], ["Ao%d" % k for k in range(8)], 1024)
                  rmsnorm([Hb[:, 8 + k, :] for k in range(8)], ["Hb%d" % (8 + k) for k in range(8)], lambda k, l=l: gains[:, l, 40 + k:41 + k],
                          [Hb[:, 8 + k, :] for k in range(8)], ["Hb%d" % (8 + k) for k in range(8)], 1024)
                  mix_rhs = lambda k: (Ao[:, k, :] if k < 8 else Hb[:, k, :])
                  mix_regs = lambda k: ["Ao%d" % k] if k < 8 else ["Hb%d" % k]
                  for j in range(8):
                      slab, rslab = w_next("w_out", l, j * 256)
                      for mm in range(2):
                          m = j * 2 + mm
                          def ev_o(pm, psm, rpm, rps, m=m):
                              P.op("dve", Rec("tensor_tensor", out=X[:, m, 0:NP], in0=pm, in1=X[:, m, 0:NP], op=ALU.add), rpm + ["X%d" % m], ["X%d" % m])
                              if psm is not None:
                                  P.op("dve", Rec("tensor_tensor", out=X[:, m, NP:NT], in0=psm, in1=X[:, m, NP:NT], op=ALU.add), rps + ["X%d" % m], ["X%d" % m])
                          proj_chunk(slab, rslab, 16, mm * 128, mix_rhs, mix_regs, ev_o)
                  ckpt(9)
                  rmsnorm([X[:, k, :] for k in range(16)], ["X%d" % k for k in range(16)], lambda k, l=l: gains[:, l, 16 + k:17 + k],
                          [Hb[:, k, :] for k in range(16)], ["Hb%d" % k for k in range(16)], D)
                  for fp in range(0 if int(_os.environ.get("SKIPFFN", "0")) else 4):
                      c0 = fp * 1536
                      nsl = 6 if fp < 3 else 4
                      for j in range(nsl):
                          slg, rslg = w_next("w_gate", l, c0 + j * 256)
                          slu, rslu = w_next("w_up", l, c0 + j * 256)
                          for mm in range(2):
                              ma = j * 2 + mm
                              def ev_gate(pm, psm, rpm, rps, ma=ma):
                                  P.op("act", Rec("activation", out=Pf[0][:, 0:256], in_=pm[:, 0:256], func=AF.Silu), rpm, ["Pf0", "Pf1"])
                                  P.op("act", Rec("activation", out=Pf[1][:, 0:256], in_=pm[:, 256:512], func=AF.Silu), rpm, ["Pf2"])
                                  if psm is not None:
                                      P.op("act", Rec("activation", out=sm[:, 0:NS], in_=psm, func=AF.Silu), rps, ["sm0", "sm1"])
                              def ev_up(pm, psm, rpm, rps, ma=ma):
                                  P.op("dve", Rec("tensor_tensor", out=ACT_[:, ma, 0:256], in0=pm[:, 0:256], in1=Pf[0][:, 0:256], op=ALU.mult), rpm + ["Pf0", "Pf1"], ["fb%d" % ma])
                                  P.op("dve", Rec("tensor_tensor", out=ACT_[:, ma, 256:512], in0=pm[:, 256:512], in1=Pf[1][:, 0:256], op=ALU.mult), rpm + ["Pf2"], ["fb%d" % ma])
                                  if psm is not None:
                                      P.op("dve", Rec("tensor_tensor", out=ACT_[:, ma, NP:NT], in0=psm, in1=sm[:, 0:NS], op=ALU.mult), rps + ["sm0", "sm1"], ["fb%d" % ma])
                              proj_chunk(slg, rslg, 16, mm * 128, hb_rhs, hb_regs, ev_gate)
                              proj_chunk(slu, rslu, 16, mm * 128, hb_rhs, hb_regs, ev_up)
                      nk = nsl * 2
                      for j in range(8):
                          slab, rslab = w_next("w_down", l, j * 256)
                          for mm in range(2):
                              m = j * 2 + mm
                              def ev_d(pm, psm, rpm, rps, m=m):
                                  P.op("dve", Rec("tensor_tensor", out=X[:, m, 0:NP], in0=pm, in1=X[:, m, 0:NP], op=ALU.add), rpm + ["X%d" % m], ["X%d" % m])
                                  if psm is not None:
                                      P.op("dve", Rec("tensor_tensor", out=X[:, m, NP:NT], in0=psm, in1=X[:, m, NP:NT], op=ALU.add), rps + ["X%d" % m], ["X%d" % m])
                              proj_chunk(slab, rslab, nk, mm * 128, lambda k: ACT_[:, k, :], lambda k: ["fb%d" % k], ev_d)

              ckpt(10)
              for k in range(16):
                  q = sqs[k % 2]
                  P.op("act", Rec("activation", out=q[:], in_=X[:, k, :], func=AF.Square), ["X%d" % k], ["sq%d" % (k % 2)])
                  P.op("pe", Rec("matmul", ps[3][:, 0:NP], lhsT=onesb[:], rhs=q[:, 0:NP], start=(k == 0), stop=(k == 15)), ["sq%d" % (k % 2), "onesb"], [RP[3]])
                  if HS[0]:
                      P.op("pe", Rec("matmul", ps[2][:, 480:480 + NS], lhsT=onesb[:], rhs=q[:, NP:NT], start=(k == 0), stop=(k == 15)), ["sq%d" % (k % 2), "onesb"], [RP[2]])
              P.op("act", Rec("activation", out=rstd[:, 0:NP], in_=ps[3][:, 0:NP], func=AF.Sqrt, scale=1.0 / D, bias=epsb[:, 0:1]), [RP[3], "epsb"], ["rstd"])
              if HS[0]:
                  P.op("act", Rec("activation", out=rstd[:, NP:NT], in_=ps[2][:, 480:480 + NS], func=AF.Sqrt, scale=1.0 / D, bias=epsb[:, 0:1]), [RP[2], "epsb"], ["rstd"])
              P.op("dve", Rec("reciprocal", out=rstd[:], in_=rstd[:]), ["rstd"], ["rstd"])
              for k in range(16):
                  P.op("dve", Rec("scalar_tensor_tensor", out=X[:, k, :], in0=X[:, k, :], scalar=gfin[:, k:k + 1], in1=rstd[:], op0=ALU.mult, op1=ALU.mult),
                       ["X%d" % k, "rstd", "gfin"], ["X%d" % k])
              for blk in range(NB + (1 if HS[0] else 0)):
                  if blk < NB:
                      cols = slice(blk * 128, (blk + 1) * 128); nr = 128
                  else:
                      cols = slice(NP, NT); nr = NS
                  for kq in range(4):
                      for kk in range(4):
                          k = kq * 4 + kk
                          P.op("pe", Rec("transpose", ps[4][0:nr, kk * 128:(kk + 1) * 128], X[:, k, cols], identf), ["X%d" % k, "cs"], [RP[4]])
                      P.op("dve", Rec("tensor_copy", out=stage[0:nr, kq * 512:(kq + 1) * 512], in_=ps[4][0:nr, :]), [RP[4]], STG)
                  if blk < NB:
                      r0 = pi * NP + blk * 128
                      P.dma("sp", Rec("dma_start", out=dap(yp, r0 * D, [[D, 128], [1, D]]), in_=stage[:]), STG, ["yp"])
                  else:
                      P.dma("sp", Rec("dma_start", out=dap(ys, sidx * 4 * D, [[D, 4], [1, D]]), in_=stage[0:4, :]), STG, ["ys"])

        except _Stop:
            pass
        if dbg:
            P.dma("sp", Rec("dma_start", out=dap(dbg_t, 0, [[16 * NT, 128], [1, 8 * NT]]), in_=Ao[:].rearrange("p a b -> p (a b)")), ["Ao%d" % k for k in range(8)], ["dbg"])
            P.dma("sp", Rec("dma_start", out=dap(dbg_t, 8 * NT, [[16 * NT, 128], [1, 8 * NT]]), in_=Hb[:, 8:16, :].rearrange("p a b -> p (a b)")), ["Hb%d" % k for k in range(8, 16)], ["dbg"])
        P.op("sp", Rec("nop", ), ["yp", "ys", "okp", "ovp", "oks", "ovs", "ohp0", "ohp1", "ohs0", "ohs1"] + (["dbg"] if dbg else []), [])
        P.emit()
    return nc


def make_consts():
    c = np.zeros((128, 1032), np.float32)
    c[:, 0:128] = np.eye(128, dtype=np.float32)
    i = np.arange(128)[:, None]
    j = np.arange(128)[None, :]
    c[:, 128:256] = np.where(j > i, 0.0, NEG)
    c[:, 256:384] = np.where(j <= i, 0.0, NEG)
    c[:, 384:512] = NEG
    c[:, 512:640] = c[:, 256:384]
    js = np.arange(132)[None, :]
    c[:, 640:772] = np.where((js > i) & (js <= i + 128), 0.0, NEG)
    r = np.arange(128)[:, None] // 16
    cc = np.arange(128)[None, :] // 16
    c[:, 772:900] = (cc >= r).astype(np.float32)
    c[:, 900:965] = np.arange(65, dtype=np.float32)[None, :]
    c[:, 965:1030] = 1.0
    c[:, 965] = 0.0
    return c


_NC_CACHE = {}


def kernel(**inputs):
    inp = {k: np.ascontiguousarray(np.asarray(v)) for k, v in inputs.items()}
    npass = SEQ // NP
    key = (npass, DEPTH)
    if key not in _NC_CACHE:
        _NC_CACHE[key] = build(npass, DEPTH)
    nc = _NC_CACHE[key]
    cst = make_consts()
    wnames = ["norm_mix", "w_in", "attn_sink", "ssm_a_re", "ssm_a_im", "ssm_log_dt", "ssm_b_re", "ssm_b_im",
              "ssm_c_re", "ssm_c_im", "ssm_d", "w_glu", "norm_attn_out", "norm_ssm_out", "w_out", "norm_ffn",
              "w_gate", "w_up", "w_down", "norm_final"]
    in_maps = []
    for c in range(8):
        m = {n: inp[n] for n in wnames}
        m["cst"] = cst
        m["xp"] = inp["x_prompt"][c % 2]
        m["xs"] = inp["x_sample"][4 * c:4 * c + 4].reshape(16, D)
        m["ck"] = inp["cache_k"][:, 4 * c:4 * c + 4].reshape(DEPTH, 4, 128, 128)
        m["cv"] = inp["cache_v"][:, 4 * c:4 * c + 4].reshape(DEPTH, 4, 128, 128)
        m["hr0"] = inp["state_ssm_re"][:, 4 * c:4 * c + 4]
        m["hi0"] = inp["state_ssm_im"][:, 4 * c:4 * c + 4]
        in_maps.append({k: np.ascontiguousarray(v, dtype=np.float32) for k, v in m.items()})
    res = run_bass_kernel_spmd(nc, in_maps, core_ids=list(range(8))).results
    f = np.float32
    y_prompt = np.stack([res[0]["yp"], res[1]["yp"]]).astype(f)
    y_sample = np.concatenate([res[c]["ys"].reshape(4, 4, D) for c in range(8)], 0).astype(f)
    k_p = np.stack([res[0]["okp"], res[1]["okp"]], 1).reshape(DEPTH, 2, 128, 2, 64).astype(f)
    v_p = np.stack([res[0]["ovp"], res[1]["ovp"]], 1).reshape(DEPTH, 2, 128, 2, 64).astype(f)
    hr_p = np.stack([res[0]["ohrp"], res[1]["ohrp"]], 1).astype(f)
    hi_p = np.stack([res[0]["ohip"], res[1]["ohip"]], 1).astype(f)
    k_s = np.concatenate([res[c]["oks"] for c in range(8)], 1).reshape(DEPTH, 32, 128, 2, 64).astype(f)
    v_s = np.concatenate([res[c]["ovs"] for c in range(8)], 1).reshape(DEPTH, 32, 128, 2, 64).astype(f)
    hr_s = np.concatenate([res[c]["ohrs"] for c in range(8)], 1).astype(f)
    hi_s = np.concatenate([res[c]["ohis"] for c in range(8)], 1).astype(f)
    return (y_prompt, y_sample, k_p, v_p, hr_p, hi_p, k_s, v_s, hr_s, hi_s)
```

```python
import contextlib
import math
import numpy as np
import concourse.bass as bass
import concourse.mybir as mybir
from concourse.bass import AP
from concourse.bass_utils import run_bass_kernel_spmd

F32 = mybir.dt.float32
BF16 = mybir.dt.bfloat16
I32 = mybir.dt.int32
AF = mybir.ActivationFunctionType
ALU = mybir.AluOpType
AX = mybir.AxisListType

D = 2048
DEPTH = 4
SEQ = 4096
DIN = 2304
DFF = 5632
NP = 512
NS = 4
NT = NP + NS
NB = NP // 128
NCHK = NP // 8
NCH1 = NCHK + 1
EPS = 1e-5
NEG = -30000.0
TWO_PI = 2.0 * math.pi


class Reg:
    __slots__ = ("name", "w", "rs")

    def __init__(self, name):
        self.name = name
        self.w = None
        self.rs = []


class Op:
    __slots__ = ("eng", "fn", "deps", "marked", "count", "is_dma", "sem", "semval")

    def __init__(self, eng, fn, is_dma=False):
        self.eng = eng
        self.fn = fn
        self.deps = []
        self.marked = False
        self.count = 0
        self.is_dma = is_dma
        self.sem = None
        self.semval = 0


ENGS = ("pe", "act", "dve", "pool", "sp")
N_DMA_SEMS = 32


class Prog:
    def __init__(self, nc):
        self.nc = nc
        self.ops = {e: [] for e in ENGS}
        self.dma_last = [None] * N_DMA_SEMS
        self.dma_cum = [0] * N_DMA_SEMS
        self.dma_rr = 0
        self.regs = {}

    def R(self, name):
        r = self.regs.get(name)
        if r is None:
            r = self.regs[name] = Reg(name)
        return r

    def _deps(self, op, reads, writes):
        deps = []
        for r in reads:
            if r.w is not None:
                deps.append(r.w)
        for w in writes:
            if w.w is not None:
                deps.append(w.w)
            deps.extend(w.rs)
        for r in reads:
            r.rs.append(op)
        for w in writes:
            w.w = op
            w.rs = []
        seen = set()
        out = []
        for d in deps:
            if id(d) in seen or d is op:
                continue
            seen.add(id(d))
            if op.eng == "pe" and d.eng == "pe" and not d.is_dma and not op.is_dma:
                continue
            out.append(d)
            d.marked = True
        op.deps = out

    def op(self, eng, fn, reads=(), writes=()):
        o = Op(eng, fn)
        writes = list(writes) + [x for x in reads if isinstance(x, str) and x.startswith("ps")]
        reads = [x for x in reads if not (isinstance(x, str) and x.startswith("ps"))]
        self._deps(o, [self.R(x) if isinstance(x, str) else x for x in reads],
                   [self.R(x) if isinstance(x, str) else x for x in writes])
        self.ops[eng].append(o)
        return o

    def dma(self, queue, fn, reads=(), writes=()):
        o = Op(queue, fn, is_dma=True)
        s = self.dma_rr
        self.dma_rr = (self.dma_rr + 1) % N_DMA_SEMS
        o.sem = s
        self.dma_cum[s] += 16
        o.semval = self.dma_cum[s]
        self._deps(o, [self.R(x) if isinstance(x, str) else x for x in reads],
                   [self.R(x) if isinstance(x, str) else x for x in writes])
        if self.dma_last[s] is not None:
            o.deps.append(self.dma_last[s])
        self.dma_last[s] = o
        self.ops[queue].append(o)
        return o

    def emit(self):
        nc = self.nc
        for e in ENGS:
            c = 0
            for o in self.ops[e]:
                if o.is_dma:
                    continue
                if o.marked:
                    c += 1
                o.count = c
        with contextlib.ExitStack() as st:
            esem = {e: st.enter_context(nc.semaphore("es_" + e)) for e in ENGS}
            dsem = [st.enter_context(nc.semaphore("ds_%d" % i)) for i in range(N_DMA_SEMS)]
            block = st.enter_context(nc.Block())
            handles = {"pe": block.tensor, "act": block.scalar, "dve": block.vector,
                       "pool": block.gpsimd, "sp": block.sync}
            for e in ENGS:
                ops = self.ops[e]

                def body(eh, ops=ops, e=e):
                    waited = {}
                    for o in ops:
                        for d in o.deps:
                            if d.is_dma:
                                key, sem, val = ("d", d.sem), dsem[d.sem], d.semval
                            else:
                                key, sem, val = ("e", d.eng), esem[d.eng], d.count
                            if waited.get(key, 0) >= val:
                                continue
                            waited[key] = val
                            eh.wait_ge(sem, val)
                        inst = o.fn(eh)
                        if o.is_dma:
                            inst.then_inc(dsem[o.sem], 16)
                        elif o.marked:
                            inst.then_inc(esem[e], 1)

                handles[e](body)


def Rec(name, *a, **k):
    return lambda e: getattr(e, name)(*a, **k)


def sap(base, dims):
    return AP(tensor=base.tensor, offset=base.offset, ap=[list(base.ap[0])] + [[int(a), int(b)] for a, b in dims])


def dap(t, offset, dims):
    return AP(tensor=t, offset=int(offset), ap=[[int(a), int(b)] for a, b in dims])


class _Stop(Exception):
    pass


def build(npass=8, depth=DEPTH, dbg=None, stop=None):
    nc = bass.Bass("TRN2", target_bir_lowering=False)
    P = Prog(nc)
    dt_in = lambda n, s: nc.dram_tensor(n, list(s), F32, kind="ExternalInput")
    dt_out = lambda n, s: nc.dram_tensor(n, list(s), F32, kind="ExternalOutput")
    xp = dt_in("xp", [SEQ, D])
    xs = dt_in("xs", [16, D])
    ck = dt_in("ck", [DEPTH, 4, 128, 128])
    cv = dt_in("cv", [DEPTH, 4, 128, 128])
    hr0 = dt_in("hr0", [DEPTH, 4, 64, 64])
    hi0 = dt_in("hi0", [DEPTH, 4, 64, 64])
    W = {}
    for n, s in [("norm_mix", [DEPTH, D]), ("w_in", [DEPTH, D, DIN]), ("attn_sink", [DEPTH, 16]),
                 ("ssm_a_re", [DEPTH, 64, 64]), ("ssm_a_im", [DEPTH, 64, 64]), ("ssm_log_dt", [DEPTH, 64]),
                 ("ssm_b_re", [DEPTH, 64, 64, 16]), ("ssm_b_im", [DEPTH, 64, 64, 16]),
                 ("ssm_c_re", [DEPTH, 64, 16, 64]), ("ssm_c_im", [DEPTH, 64, 16, 64]),
                 ("ssm_d", [DEPTH, 1024]), ("w_glu", [DEPTH, 1024, 1024]),
                 ("norm_attn_out", [DEPTH, 1024]), ("norm_ssm_out", [DEPTH, 1024]),
                 ("w_out", [DEPTH, D, D]), ("norm_ffn", [DEPTH, D]),
                 ("w_gate", [DEPTH, D, DFF]), ("w_up", [DEPTH, D, DFF]), ("w_down", [DEPTH, DFF, D]),
                 ("norm_final", [D])]:
        W[n] = dt_in(n, s)
    cst = dt_in("cst", [128, 1032])
    yp = dt_out("yp", [SEQ, D])
    ys = dt_out("ys", [16, D])
    okp = dt_out("okp", [DEPTH, 128, 128])
    ovp = dt_out("ovp", [DEPTH, 128, 128])
    ohrp = dt_out("ohrp", [DEPTH, 64, 64])
    ohip = dt_out("ohip", [DEPTH, 64, 64])
    oks = dt_out("oks", [DEPTH, 4, 128, 128])
    ovs = dt_out("ovs", [DEPTH, 4, 128, 128])
    ohrs = dt_out("ohrs", [DEPTH, 4, 64, 64])
    ohis = dt_out("ohis", [DEPTH, 4, 64, 64])
    scrT = nc.dram_tensor("scrT", [DEPTH, 128, 1088], F32, kind="Internal")
    scrR = nc.dram_tensor("scrR", [DEPTH * 4 * 3, 128, 520], F32, kind="Internal")
    scrM = nc.dram_tensor("scrM", [DEPTH * 32, 128, 512], BF16, kind="Internal")
    scrC = nc.dram_tensor("scrC", [DEPTH * 4, 128, 4096], BF16, kind="Internal")
    scrG = nc.dram_tensor("scrG", [DEPTH * 4, 128, 2048], BF16, kind="Internal")
    dbg_t = nc.dram_tensor("dbg", [128, 16 * NT], BF16, kind="ExternalOutput") if dbg else None

    st = contextlib.ExitStack()
    with st:
        st.enter_context(nc.allow_non_contiguous_dma(reason="small strided parameter loads"))
        sb = lambda n, s, d=F32: st.enter_context(nc.sbuf_tensor(n, list(s), d))
        X = sb("X", [128, 16, NT])
        Hb = sb("Hb", [128, 16, NT], BF16)
        Ao = sb("Ao", [128, 8, NT], BF16)
        FB = sb("FB", [128, 6 * NT])
        ACT_ = FB[:].bitcast(BF16).rearrange("p (a b) -> p a b", a=12)
        Zf = ACT_
        stage = FB[:, 0:D]
        rstd = sb("rstd", [128, NT])
        sqs = [sb("sq%d" % i, [128, NT], BF16) for i in range(2)]
        NSLOT = 4
        ring = [sb("ring%d" % i, [128, 16, 256], BF16) for i in range(NSLOT)]
        cs = sb("cs", [128, 1032])
        identb = sb("identb", [128, 128], BF16)
        onesb = sb("onesb", [128, 128], BF16)
        maskb = sb("maskb", [128, 256], BF16)
        maskfb = sb("maskfb", [128, 256], BF16)
        masksb = sb("masksb", [128, 132], BF16)
        gains = sb("gains", [128, DEPTH, 64])
        gfin = sb("gfin", [128, 16])
        sinkb = sb("sinkb", [128, DEPTH * 16])
        Kb = [sb("Kb%d" % i, [128, 128 + NP], BF16) for i in range(2)]
        KS = [sb("KS%d" % i, [128, 132], BF16) for i in range(2)]
        Vb = sb("Vb", [128, NB + 1, 4, 128], BF16)
        VSc = sb("VSc", [128, 4, 128], BF16)
        VSn = sb("VSn", [4, 4, 128], BF16)
        Khalo = sb("Khalo", [128, DEPTH, 2, 128], BF16)
        Vhalo = sb("Vhalo", [128, DEPTH, 4, 128], BF16)
        kvst = sb("kvst", [128, 256])
        kvss = kvst[0:4, :]
        cpy = kvst
        cks = sb("cks", [128, 128], BF16)
        PFB = sb("PFB", [128, 512])
        Pf = [PFB[:, 0:256], PFB[:, 256:512]]
        Pe = [PFB[:].bitcast(BF16)[:, i * 256:(i + 1) * 256] for i in range(3)]
        Pn = [sb("Pn%d" % i, [128, 256], BF16) for i in range(3)]
        PT = [sb("PT%d" % i, [128, 2, 128], BF16) for i in range(3)]
        sm = sb("sm", [128, 24])
        S1 = lambda n: sb(n, [128, 32])
        lam_r, lam_i, dtt, mag, th, abr, abi, t1, t2, t3, t4, crr, cii, rho, phi, iar, iai = [
            S1(n) for n in "lam_r lam_i dtt mag th abr abi t1 t2 t3 t4 crr cii rho phi iar iai".split()]
        ti = sb("ti", [128, 32], I32)
        Br = sb("Br", [128, 32, 16]); Bi = sb("Bi", [128, 32, 16])
        Bbr = Br; Bbi = Bi
        Cn = sb("Cn", [16, 16, 64])
        Cr = sb("Cr", [128, 32, 16]); Ci = sb("Ci", [128, 32, 16])
        PCr = sb("PCr", [128, 32, 16]); PCi = sb("PCi", [128, 32, 16])
        PBr = sb("PBr", [128, 32, 8]); PBi = sb("PBi", [128, 32, 8])
        Drow = sb("Drow", [128, 1024])
        Hr = sb("Hr", [128, DEPTH, 32]); Hi = sb("Hi", [128, DEPTH, 32])
        h0r = sb("h0r", [128, 32]); h0i = sb("h0i", [128, 32])
        Vu = sb("Vu", [128, 16, 8, 16], BF16)
        Ug = [sb("Ug%d" % i, [128, NCH1], BF16) for i in range(16)]
        MBt = [sb("MBt%d" % i, [128, 128], BF16) for i in range(2)]
        MBpT = sb("MBpT", [128, 2, 4, 128], BF16)
        TgSt = sb("TgSt", [128, 16, 128], BF16)
        MCBt = sb("MCBt", [128, 8, 2, 256], BF16)
        TgS = [TgSt[:, i, :] for i in range(16)]
        MCB = [[MCBt[:, i, r, :] for r in range(2)] for i in range(8)]
        tA = sb("tA", [128, 256]); tB = sb("tB", [128, 256])
        tmask = sb("tmask", [128, 128])
        NPB = 8
        Tc = sb("Tc", [128, NPB, 65]); Ts = sb("Ts", [128, NPB, 65])
        Dr = sb("Dr", [128, NPB, 65]); Di = sb("Di", [128, NPB, 65])
        d0 = sb("d0", [128, NPB, 65])
        ang = sb("ang", [128, NPB, 65])
        u1 = sb("u1", [128, NPB, 65]); u2 = sb("u2", [128, NPB, 65])
        Sbr = sb("Sbr", [128, NPB, 65], BF16); Sbi = sb("Sbi", [128, NPB, 65], BF16)
        Xs = sb("Xs", [128, 2, NPB])
        hend = sb("hend", [128, 2, 32])
        vd = sb("vd", [128, 4, 8, 16])
        ypre = sb("ypre", [128, 4, 8, 16])
        zcm = sb("zcm", [128, 8, 128], BF16)
        ps = [st.enter_context(nc.psum_tensor("ps%d" % i, [128, 512], F32)) for i in range(8)]
        RP = ["ps%d" % i for i in range(8)]
        STG = ["fb%d" % i for i in range(8)]

        def psb(b):
            return ps[b][:].bitcast(BF16)

        P.dma("sp", Rec("dma_start", out=cs[:], in_=cst.ap()), [], ["cs"])
        P.op("dve", Rec("tensor_copy", out=identb[:], in_=cs[:, 0:128]), ["cs"], ["identb"])
        P.op("dve", Rec("tensor_copy", out=maskb[:], in_=cs[:, 128:384]), ["cs"], ["maskb"])
        P.op("dve", Rec("tensor_copy", out=maskfb[:], in_=cs[:, 384:640]), ["cs"], ["maskfb"])
        P.op("dve", Rec("tensor_copy", out=masksb[:], in_=cs[:, 640:772]), ["cs"], ["masksb"])
        P.op("dve", Rec("tensor_copy", out=tmask[:], in_=cs[:, 772:900]), ["cs"], ["tmask"])
        P.op("pool", Rec("memset", onesb[:], 1.0), [], ["onesb"])
        identf = cs[:, 0:128]
        posf = cs[:, 900:965]
        for l in range(depth):
            for nm, off, nk in [("norm_mix", 0, 16), ("norm_ffn", 16, 16), ("norm_attn_out", 32, 8), ("norm_ssm_out", 40, 8)]:
                src = dap(W[nm], l * nk * 128, [[1, 128], [128, nk]])
                P.dma("sp", Rec("dma_start", out=gains[:, l, off:off + nk], in_=src), [], ["gains"])
        P.dma("sp", Rec("dma_start", out=gfin[:], in_=dap(W["norm_final"], 0, [[1, 128], [128, 16]])), [], ["gfin"])
        P.dma("sp", Rec("dma_start", out=sinkb[:], in_=dap(W["attn_sink"], 0, [[0, 128], [1, DEPTH * 16]])), [], ["sinkb"])
        P.op("pool", Rec("memset", Hr[:], 0.0), [], ["H"])
        P.op("pool", Rec("memset", Hi[:], 0.0), [], ["H"])
        P.op("pool", Rec("memset", Khalo[:], 0.0), [], ["Khalo"])
        P.op("pool", Rec("memset", Vhalo[:], 0.0), [], ["Vhalo"])
        P.op("pool", Rec("memset", Vb[:], 0.0), [], ["Vb"])
        P.op("pool", Rec("memset", VSc[:], 0.0), [], ["VSc"])
        P.op("pool", Rec("memset", VSn[:], 0.0), [], ["VSn"])
        P.op("pool", Rec("memset", Vu[:], 0.0), [], ["Vu"])
        P.op("pool", Rec("memset", MBpT[:], 0.0), [], ["MBp0", "MBp1"])

        sched = []
        for ps_i in range(npass):
            for l in range(depth):
                sched.append(("w_in", l, 0, 16, 1024, 256))
                for j in range(4):
                    sched.append(("w_in", l, 0, 16, j * 256, 256))
                for j in range(4):
                    sched.append(("w_in", l, 0, 16, 1280 + j * 256, 256))
                for j in range(4):
                    sched.append(("w_glu", l, 0, 8, j * 256, 256))
                for j in range(8):
                    sched.append(("w_out", l, 0, 16, j * 256, 256))
                for fp in range(4):
                    c0 = fp * 1536
                    nsl = 6 if fp < 3 else 4
                    for j in range(nsl):
                        sched.append(("w_gate", l, 0, 16, c0 + j * 256, 256))
                        sched.append(("w_up", l, 0, 16, c0 + j * 256, 256))
                    nk = nsl * 2
                    for j in range(8):
                        sched.append(("w_down", l, c0, nk, j * 256, 256))
        wcur = [0, 0]
        NCOLS = {"w_in": DIN, "w_glu": 1024, "w_out": D, "w_gate": DFF, "w_up": DFF, "w_down": D}
        NROWS = {"w_in": D, "w_glu": 1024, "w_out": D, "w_gate": D, "w_up": D, "w_down": DFF}

        def w_issue(upto):
            while wcur[1] < min(upto, len(sched)):
                i = wcur[1]
                nm, l, r0, nk, c0, ncl = sched[i]
                slot = i % NSLOT
                if nm == "w_krep":
                    for kvh in range(2):
                        for r_ in range(2):
                            src = dap(W["w_in"], l * D * DIN + 1024 + kvh * 64, [[DIN, 128], [128 * DIN, 16], [1, 64]])
                            c_ = kvh * 128 + r_ * 64
                            P.dma("pool", Rec("dma_start", out=ring[slot][:, :, c_:c_ + 64], in_=src),
                                  [], ["ring%d" % slot])
                    wcur[1] += 1
                    continue
                ncols = NCOLS[nm]
                src = dap(W[nm], l * NROWS[nm] * ncols + r0 * ncols + c0, [[ncols, 128], [128 * ncols, nk], [1, ncl]])
                P.dma("pool", Rec("dma_start", out=ring[slot][:, 0:nk, 0:ncl], in_=src),
                      [], ["ring%d" % slot])
                wcur[1] += 1

        def w_next(nm, l, c0):
            i = wcur[0]
            assert sched[i][0] == nm and sched[i][1] == l and sched[i][4] == c0, (sched[i], nm, l, c0)
            w_issue(i + NSLOT - 1)
            wcur[0] += 1
            return ring[i % NSLOT], "ring%d" % (i % NSLOT)

        pcnt = [0]
        HS = [True]

        def proj_chunk(slab, rslab, nk, mcol, rhs_fn, rhs_regs, evac_fn, lhs_rep=False):
            b = pcnt[0] % 2
            sbk = 2 + pcnt[0] % 2
            so = ((pcnt[0] // 2) % 8) * 4
            pcnt[0] += 1
            for k in range(nk):
                lhs = slab[:, k, mcol:mcol + 128]
                r = rhs_fn(k)
                P.op("pe", Rec("matmul", ps[b][:, 0:NP], lhsT=lhs, rhs=r[:, 0:NP], start=(k == 0), stop=(k == nk - 1)),
                     [rslab] + rhs_regs(k), [RP[b]])
                if HS[0]:
                    P.op("pe", Rec("matmul", ps[sbk][:, so:so + NS], lhsT=lhs, rhs=r[:, NP:NT], start=(k == 0), stop=(k == nk - 1)),
                         [rslab] + rhs_regs(k), [RP[sbk]])
            evac_fn(ps[b][:, 0:NP], ps[sbk][:, so:so + NS] if HS[0] else None, [RP[b]], [RP[sbk]])

        def rmsnorm(srcs, src_regs, gain_fn, dsts, dst_regs, dim):
            nk = len(srcs)
            for k in range(nk):
                q = sqs[k % 2]
                P.op("act", Rec("activation", out=q[:], in_=srcs[k], func=AF.Square), [src_regs[k]], ["sq%d" % (k % 2)])
                P.op("pe", Rec("matmul", ps[3][:, 0:NP], lhsT=onesb[:], rhs=q[:, 0:NP], start=(k == 0), stop=(k == nk - 1)),
                     ["sq%d" % (k % 2), "onesb"], [RP[3]])
                if HS[0]:
                    P.op("pe", Rec("matmul", ps[2][:, 480:480 + NS], lhsT=onesb[:], rhs=q[:, NP:NT], start=(k == 0), stop=(k == nk - 1)),
                         ["sq%d" % (k % 2), "onesb"], [RP[2]])
            P.op("act", Rec("activation", out=rstd[:, 0:NP], in_=ps[3][:, 0:NP], func=AF.Sqrt, scale=1.0 / dim, bias=epsb[:, 0:1]), [RP[3], "epsb"], ["rstd"])
            if HS[0]:
                P.op("act", Rec("activation", out=rstd[:, NP:NT], in_=ps[2][:, 480:480 + NS], func=AF.Sqrt, scale=1.0 / dim, bias=epsb[:, 0:1]), [RP[2], "epsb"], ["rstd"])
            P.op("dve", Rec("reciprocal", out=rstd[:], in_=rstd[:]), ["rstd"], ["rstd"])
            for k in range(nk):
                P.op("dve", Rec("scalar_tensor_tensor", out=dsts[k], in0=srcs[k], scalar=gain_fn(k), in1=rstd[:], op0=ALU.mult, op1=ALU.mult),
                     [src_regs[k], "rstd", "gains", "gfin"], [dst_regs[k]])

        epsb = sb("epsb", [128, 1])
        P.op("pool", Rec("memset", epsb[:], EPS), [], ["epsb"])
        halfpi = sb("halfpi", [128, 1])
        P.op("pool", Rec("memset", halfpi[:], math.pi / 2), [], ["halfpi"])

        def sincos(eng_v, angle, itmp, a1, a2, out_c, out_s, regs_in, rtag):
            P.op("dve", Rec("tensor_scalar", out=a1, in0=angle, scalar1=1.0 / TWO_PI, scalar2=None, op0=ALU.mult), regs_in, [rtag + "a1"])
            P.op("dve", Rec("tensor_copy", out=itmp, in_=a1), [rtag + "a1"], [rtag + "a2"])
            P.op("dve", Rec("tensor_copy", out=a1, in_=itmp), [rtag + "a2"], [rtag + "a1"])
            P.op("dve", Rec("scalar_tensor_tensor", out=angle, in0=a1, scalar=-TWO_PI, in1=angle, op0=ALU.mult, op1=ALU.add), [rtag + "a1"] + regs_in, regs_in)
            P.op("act", Rec("activation", out=a1, in_=angle, func=AF.Sin, scale=0.5), regs_in, [rtag + "a1"])
            P.op("act", Rec("activation", out=a2, in_=angle, func=AF.Sin, scale=0.5, bias=halfpi[:, 0:1]), regs_in + ["halfpi"], [rtag + "a2"])
            P.op("dve", Rec("scalar_tensor_tensor", out=out_s, in0=a1, scalar=2.0, in1=a2, op0=ALU.mult, op1=ALU.mult), [rtag + "a1", rtag + "a2"], [rtag + "s"])
            P.op("dve", Rec("tensor_tensor", out=a2, in0=a1, in1=a1, op=ALU.mult), [rtag + "a1"], [rtag + "a2"])
            P.op("dve", Rec("tensor_scalar", out=out_c, in0=a2, scalar1=-2.0, scalar2=1.0, op0=ALU.mult, op1=ALU.add), [rtag + "a2"], [rtag + "c"])

        def cmul(eng, o_r, o_i, a_r, a_i, b_r, b_i, tmp1, tmp2, rin, rout):
            P.op(eng, Rec("tensor_tensor", out=tmp1, in0=a_i, in1=b_i, op=ALU.mult), rin, rout)
            P.op(eng, Rec("tensor_tensor", out=tmp2, in0=a_i, in1=b_r, op=ALU.mult), rin, rout)
            P.op(eng, Rec("tensor_tensor", out=o_r, in0=a_r, in1=b_r, op=ALU.mult), rin, rout)
            P.op(eng, Rec("tensor_tensor", out=o_i, in0=a_r, in1=b_i, op=ALU.mult), rin, rout)
            P.op(eng, Rec("tensor_tensor", out=o_r, in0=o_r, in1=tmp1, op=ALU.subtract), rin, rout)
            P.op(eng, Rec("tensor_tensor", out=o_i, in0=o_i, in1=tmp2, op=ALU.add), rin, rout)

        def ckpt(n):
            if stop is not None and n >= stop:
                raise _Stop()

        try:
          for pi in range(npass):
              sidx = pi % 4
              HS[0] = pi < 4
              for blk in range(NB):
                  r0 = pi * NP + blk * 128
                  P.dma("sp", Rec("dma_start", out=stage[:], in_=dap(xp, r0 * D, [[D, 128], [1, D]])), [], STG)
                  for kq in range(4):
                      for kk in range(4):
                          k = kq * 4 + kk
                          P.op("pe", Rec("transpose", ps[4][:, kk * 128:(kk + 1) * 128], stage[:, k * 128:(k + 1) * 128], identf),
                               STG + ["cs"], [RP[4]])
                      P.op("dve", Rec("tensor_copy", out=X[:, kq * 4:kq * 4 + 4, blk * 128:(blk + 1) * 128],
                                                                      in_=ps[4][:].rearrange("p (a b) -> p a b", a=4)),
                           [RP[4]], ["X%d" % k for k in range(kq * 4, kq * 4 + 4)])
              P.dma("sp", Rec("dma_start", out=stage[0:4, :], in_=dap(xs, sidx * 4 * D, [[D, 4], [1, D]])), [], STG)
              for k in range(16 if HS[0] else 0):
                  P.op("pe", Rec("transpose", ps[4][:, k * 4:(k + 1) * 4], stage[0:4, k * 128:(k + 1) * 128], identf[0:4, 0:4]),
                       STG + ["cs"], [RP[4]])
              if HS[0]:
                  P.op("dve", Rec("tensor_copy", out=X[:, :, NP:NT], in_=ps[4][:, 0:64].rearrange("p (a b) -> p a b", a=16)),
                       [RP[4]], ["X%d" % k for k in range(16)])

              ckpt(1)
              for l in range(depth):
                  rmsnorm([X[:, k, :] for k in range(16)], ["X%d" % k for k in range(16)], lambda k, l=l: gains[:, l, k:k + 1],
                          [Hb[:, k, :] for k in range(16)], ["Hb%d" % k for k in range(16)], D)
                  hb_rhs = lambda k: Hb[:, k, :]
                  hb_regs = lambda k: ["Hb%d" % k]
                  ckpt(2)
                  for kvh in range(2):
                      P.op("pool", Rec("tensor_copy", out=Kb[kvh][:, 0:128], in_=Khalo[:, l, kvh, :]), ["Khalo"], ["Kb%d" % kvh])
                  P.op("pool", Rec("tensor_copy", out=Vb[:, 0, :, :], in_=Vhalo[:, l, :, :]), ["Vhalo"], ["Vb"])
                  ckpt(2.1)
                  slab, rslab = w_next("w_in", l, 1024)

                  def ev_k(pm, psm, rpm, rps):
                      for kvh in range(2):
                          hs = slice(kvh * 64, kvh * 64 + 64)
                          P.op("act", Rec("activation", out=Kb[kvh][hs, 128:128 + NP], in_=pm[hs, :], func=AF.Copy), rpm, ["Kb%d" % kvh])
                          if psm is not None:
                              P.op("act", Rec("activation", out=KS[kvh][hs, 128:132], in_=psm[hs, :], func=AF.Copy), rps, ["KS%d" % kvh])
                  proj_chunk(slab, rslab, 16, 0, hb_rhs, hb_regs, ev_k)
                  for kvh in range(2):
                      hs = slice(kvh * 64, kvh * 64 + 64)
                      ho = slice((1 - kvh) * 64, (1 - kvh) * 64 + 64)
                      P.dma("sp", Rec("dma_start", out=Kb[kvh][ho, 128:128 + NP], in_=Kb[kvh][hs, 128:128 + NP]), ["Kb%d" % kvh], ["Kb%d" % kvh])
                  ckpt(2.2)
                  import os as _os
                  for blk in range(NB + (1 if HS[0] else 0)):
                      if blk < NB:
                          cols = slice(blk * 128, (blk + 1) * 128); mrows = 128
                      else:
                          cols = slice(NP, NT); mrows = NS
                      for k in range(16):
                          P.op("pe", Rec("matmul", ps[5][0:mrows, 0:256], lhsT=Hb[:, k, cols], rhs=slab[:, k, 0:256], start=(k == 0), stop=(k == 15)),
                               [rslab, "Hb%d" % k], [RP[5]])
                      if blk < NB:
                          for kvh in range(2 - 2 * int(_os.environ.get("SKIP_A", "0"))):
                              for odd in range(2):
                                  P.op("dve", Rec("tensor_copy", out=Vb[:, blk + 1, kvh * 2 + odd, odd * 64:odd * 64 + 64], in_=ps[5][:, 128 + kvh * 64:128 + kvh * 64 + 64]),
                                       [RP[5]], ["Vb"])
                          if blk == NB - 1 and not int(_os.environ.get("SKIP_B", "0")):
                              P.op("act", Rec("activation", out=kvst[:], in_=ps[5][:, 0:256], func=AF.Copy), [RP[5]], ["kvst"])
                              P.dma("sp", Rec("dma_start", out=okp.ap()[l], in_=kvst[:, 0:128]), ["kvst"], ["okp"])
                              P.dma("sp", Rec("dma_start", out=ovp.ap()[l], in_=kvst[:, 128:256]), ["kvst"], ["ovp"])
                      else:
                          for kvh in range(2):
                              for odd in range(2):
                                  P.op("dve", Rec("tensor_copy", out=VSn[0:4, kvh * 2 + odd, odd * 64:odd * 64 + 64], in_=ps[5][0:4, 128 + kvh * 64:128 + kvh * 64 + 64]),
                                       [RP[5]], ["VSn"])
                          P.op("act", Rec("activation", out=kvss[:], in_=ps[5][0:4, 0:256], func=AF.Copy), [RP[5]], ["kvst"])
                          P.dma("sp", Rec("dma_start", out=oks.ap()[l, sidx, 124:128, :], in_=kvss[:, 0:128]), ["kvst"], ["oks"])
                          P.dma("sp", Rec("dma_start", out=ovs.ap()[l, sidx, 124:128, :], in_=kvss[:, 128:256]), ["kvst"], ["ovs"])
                          P.dma("sp", Rec("dma_start", out=cpy[0:124, 0:128], in_=ck.ap()[l, sidx, 4:128, :]), [], ["kvst"])
                          P.dma("sp", Rec("dma_start", out=cpy[0:124, 128:256], in_=cv.ap()[l, sidx, 4:128, :]), [], ["kvst"])
                          P.dma("sp", Rec("dma_start", out=oks.ap()[l, sidx, 0:124, :], in_=cpy[0:124, 0:128]), ["kvst"], ["oks"])
                          P.dma("sp", Rec("dma_start", out=ovs.ap()[l, sidx, 0:124, :], in_=cpy[0:124, 128:256]), ["kvst"], ["ovs"])
                  ckpt(2.3)
                  for kvh in range(2):
                      P.op("pool", Rec("tensor_copy", out=Khalo[:, l, kvh, :], in_=Kb[kvh][:, NP:NP + 128]), ["Kb%d" % kvh], ["Khalo"])
                  P.op("pool", Rec("tensor_copy", out=Vhalo[:, l, :, :], in_=Vb[:, NB, :, :]), ["Vb"], ["Vhalo"])
                  ckpt(2.4)
                  if HS[0]:
                      P.dma("pool", Rec("dma_start", out=cks[:, 0:128], in_=ck.ap()[l, sidx]), [], ["cks"])
                      P.op("pe", Rec("matmul", ps[5][:, 256:384], lhsT=cks[:, 0:128], rhs=identb[:], start=True, stop=True), ["cks", "identb"], [RP[5]])
                  for kvh in range(2 if HS[0] else 0):
                      hs = slice(kvh * 64, kvh * 64 + 64)
                      ho = slice((1 - kvh) * 64, (1 - kvh) * 64 + 64)
                      P.op("act", Rec("activation", out=KS[kvh][hs, 0:128], in_=ps[5][hs, 256:384], func=AF.Copy), [RP[5]], ["KS%d" % kvh])
                      P.dma("sp", Rec("dma_start", out=KS[kvh][ho, :], in_=KS[kvh][hs, :]), ["KS%d" % kvh], ["KS%d" % kvh])
                  for kvh in range(2 if HS[0] else 0):
                      for odd in range(2):
                          P.dma("pool", Rec("dma_start", out=VSc[:, kvh * 2 + odd, odd * 64:odd * 64 + 64], in_=cv.ap()[l, sidx, :, kvh * 64:kvh * 64 + 64]), [], ["VSc"])
                  ckpt(3)
                  for j in range(4):
                      slab, rslab = w_next("w_in", l, j * 256)
                      for mm in range(2):
                          m = j * 2 + mm
                          def ev_q(pm, psm, rpm, rps, m=m):
                              P.op("act", Rec("activation", out=Ao[:, m, 0:NP], in_=pm, func=AF.Copy, scale=0.125), rpm, ["Ao%d" % m])
                              if psm is not None:
                                  P.op("act", Rec("activation", out=Ao[:, m, NP:NT], in_=psm, func=AF.Copy, scale=0.125), rps, ["Ao%d" % m])
                          proj_chunk(slab, rslab, 16, mm * 128, hb_rhs, hb_regs, ev_q)
                  ckpt(4)
                  units = []
                  for nb in range(NB):
                      for m in range(8):
                          mk = maskfb if (pi == 0 and nb == 0) else maskb
                          for odd in range(2):
                              units.append((m, odd, 128, slice(nb * 128, (nb + 1) * 128), (lambda kvh, nb=nb: Kb[kvh][:, nb * 128:nb * 128 + 256]), 256, mk,
                                            [(128, (lambda v, nb=nb: Vb[:, nb, v, :])), (128, (lambda v, nb=nb: Vb[:, nb + 1, v, :]))]))
                  for m in range(8 if HS[0] else 0):
                      for odd in range(2):
                          units.append((m, odd, NS, slice(NP, NT), (lambda kvh: KS[kvh][:, 0:132]), 132, masksb,
                                        [(128, (lambda v: VSc[:, v, :])), (NS, (lambda v: VSn[:, v, :]))]))

                  def U_(i):
                      m, odd, nq, qcols, keyfn, nkeys, mk, segs = units[i]
                      h = 2 * m + odd
                      return m, odd, nq, qcols, keyfn, nkeys, mk, segs, h, h // 8, odd * 64, i % 3, 5 + i % 3, (i % 3) * 8

                  def sS(i):
                      m, odd, nq, qcols, keyfn, nkeys, mk, segs, h, kvh, hp, u, sbk, c0 = U_(i)
                      P.op("pe", Rec("matmul", ps[sbk][0:nq, 0:nkeys], lhsT=Ao[hp:hp + 64, m, qcols], rhs=keyfn(kvh)[hp:hp + 64, :], start=True, stop=False),
                           ["Ao%d" % m, "Kb%d" % kvh, "KS%d" % kvh], [RP[sbk]])
                      P.op("pe", Rec("matmul", ps[sbk][0:nq, 0:nkeys], lhsT=identb[0:nq, 0:nq], rhs=mk[0:nq, 0:nkeys], start=False, stop=True),
                           ["identb", "maskb", "maskfb", "masksb"], [RP[sbk]])

                  def sPre1(i):
                      m, odd, nq, qcols, keyfn, nkeys, mk, segs, h, kvh, hp, u, sbk, c0 = U_(i)
                      P.op("dve", Rec("reduce_max", out=sm[0:nq, c0:c0 + 1], in_=ps[sbk][0:nq, 0:nkeys], axis=AX.X), [RP[sbk]], ["sm%d" % u])

                  def sPre2(i, l=l):
                      m, odd, nq, qcols, keyfn, nkeys, mk, segs, h, kvh, hp, u, sbk, c0 = U_(i)
                      scol = sinkb[0:nq, l * 16 + h:l * 16 + h + 1]
                      P.op("dve", Rec("tensor_scalar", out=sm[0:nq, c0 + 1:c0 + 2], in0=sm[0:nq, c0:c0 + 1], scalar1=scol, scalar2=-1.0, op0=ALU.max, op1=ALU.mult),
                           ["sm%d" % u, "sinkb"], ["sm%d" % u])

                  def sExp(i, l=l):
                      m, odd, nq, qcols, keyfn, nkeys, mk, segs, h, kvh, hp, u, sbk, c0 = U_(i)
                      scol = sinkb[0:nq, l * 16 + h:l * 16 + h + 1]
                      P.op("act", Rec("activation", out=Pe[u][0:nq, 0:nkeys], in_=ps[sbk][0:nq, 0:nkeys], func=AF.Exp, bias=sm[0:nq, c0 + 1:c0 + 2], accum_out=sm[0:nq, c0 + 2:c0 + 3]),
                           [RP[sbk], "sm%d" % u], ["Pf%d" % u, "sm%d" % u])
                      P.op("act", Rec("activation", out=sm[0:nq, c0 + 3:c0 + 4], in_=sm[0:nq, c0 + 1:c0 + 2], func=AF.Exp, bias=scol),
                           ["sm%d" % u, "sinkb"], ["sm%d" % u])

                  def sPost1(i):
                      m, odd, nq, qcols, keyfn, nkeys, mk, segs, h, kvh, hp, u, sbk, c0 = U_(i)
                      P.op("dve", Rec("tensor_tensor", out=sm[0:nq, c0 + 4:c0 + 5], in0=sm[0:nq, c0 + 2:c0 + 3], in1=sm[0:nq, c0 + 3:c0 + 4], op=ALU.add), ["sm%d" % u], ["sm%d" % u])

                  def sPost2(i):
                      m, odd, nq, qcols, keyfn, nkeys, mk, segs, h, kvh, hp, u, sbk, c0 = U_(i)
                      P.op("dve", Rec("reciprocal", out=sm[0:nq, c0 + 5:c0 + 6], in_=sm[0:nq, c0 + 4:c0 + 5]), ["sm%d" % u], ["sm%d" % u])

                  def sPost3(i):
                      m, odd, nq, qcols, keyfn, nkeys, mk, segs, h, kvh, hp, u, sbk, c0 = U_(i)
                      P.op("dve", Rec("tensor_scalar", out=Pn[u][0:nq, 0:nkeys], in0=Pe[u][0:nq, 0:nkeys], scalar1=sm[0:nq, c0 + 5:c0 + 6], scalar2=None, op0=ALU.mult),
                           ["Pf%d" % u, "sm%d" % u], ["Pn%d" % u])

                  def sT(i):
                      m, odd, nq, qcols, keyfn, nkeys, mk, segs, h, kvh, hp, u, sbk, c0 = U_(i)
                      ptb = psb(i % 2)
                      ko = 0
                      for si, (nk_, vfn) in enumerate(segs):
                          P.op("pe", Rec("transpose", ptb[0:nk_, si * 128:si * 128 + nq], Pn[u][0:nq, ko:ko + nk_], identb[0:nq, 0:nq]),
                               ["Pn%d" % u, "identb"], [RP[i % 2]])
                          ko += nk_

                  def sEv(i):
                      m, odd, nq, qcols, keyfn, nkeys, mk, segs, h, kvh, hp, u, sbk, c0 = U_(i)
                      ptb = psb(i % 2)
                      for si, (nk_, vfn) in enumerate(segs):
                          P.op("act", Rec("activation", out=PT[u][0:nk_, si, 0:nq], in_=ptb[0:nk_, si * 128:si * 128 + nq], func=AF.Copy),
                               [RP[i % 2]], ["PT%d" % u])

                  def sPV(i):
                      m, odd, nq, qcols, keyfn, nkeys, mk, segs, h, kvh, hp, u, sbk, c0 = U_(i)
                      for si, (nk_, vfn) in enumerate(segs):
                          first = (odd == 0 and si == 0)
                          last = (odd == 1 and si == len(segs) - 1)
                          P.op("pe", Rec("matmul", ps[3][:, 0:nq], lhsT=vfn(kvh * 2 + odd)[0:nk_, :], rhs=PT[u][0:nk_, si, 0:nq], start=first, stop=last),
                               ["PT%d" % u, "Vb", "VSc", "VSn"], [RP[3]])

                  def sO(i):
                      m, odd, nq, qcols, keyfn, nkeys, mk, segs, h, kvh, hp, u, sbk, c0 = U_(i)
                      if odd == 1:
                          P.op("dve", Rec("tensor_copy", out=Ao[:, m, qcols], in_=ps[3][:, 0:nq]), [RP[3]], ["Ao%d" % m])

                  NU = len(units)
                  ok_ = lambda i: 0 <= i < NU
                  for t in range(NU + 8):
                      if ok_(t): sS(t)
                      if ok_(t - 1): sPre1(t - 1)
                      if ok_(t - 3): sPost1(t - 3)
                      if ok_(t - 1): sPre2(t - 1)
                      if ok_(t - 3): sPost2(t - 3)
                      if ok_(t - 7): sO(t - 7)
                      if ok_(t - 3): sPost3(t - 3)
                      if ok_(t - 2): sExp(t - 2)
                      if ok_(t - 4): sT(t - 4)
                      if ok_(t - 5): sEv(t - 5)
                      if ok_(t - 6): sPV(t - 6)

                  ckpt(5)
                  if pi == 0:
                      for nm, dst in [("ssm_a_re", lam_r), ("ssm_a_im", lam_i)]:
                          P.dma("sp", Rec("dma_start", out=dst[:], in_=dap(W[nm], l * 4096, [[1, 128], [128, 32]])), [], ["ssmp"])
                      for gl in range(2):
                          P.dma("sp", Rec("dma_start", out=dtt[gl * 64:(gl + 1) * 64, :], in_=dap(W["ssm_log_dt"], l * 64 + gl, [[0, 64], [2, 32]])), [], ["ssmp"])
                      for nm, dst in [("ssm_b_re", Br), ("ssm_b_im", Bi)]:
                          P.dma("sp", Rec("dma_start", out=dst[:], in_=dap(W[nm], l * 65536, [[16, 128], [2048, 32], [1, 16]])), [], ["ssmB"])
                      for nm, dst, rc in [("ssm_c_re", Cr, "Cr"), ("ssm_c_im", Ci, "Ci")]:
                          for jq in range(4):
                              P.dma("sp", Rec("dma_start", out=Cn[:], in_=dap(W[nm], l * 65536 + jq * 16384, [[64, 16], [1024, 16], [1, 64]])), [], ["Cn"])
                              for jj in range(8):
                                  P.op("pe", Rec("transpose", ps[4][:, jj * 16:(jj + 1) * 16], Cn[:, 2 * jj:2 * jj + 2, :], identf[0:16, 0:16]), ["Cn", "cs"], [RP[4]])
                              P.op("dve", Rec("tensor_copy", out=dst[:, jq * 8:(jq + 1) * 8, :], in_=ps[4][:, 0:128].rearrange("p (a b) -> p a b", a=8)), [RP[4]], ["ssmC"])
                  P.dma("sp", Rec("dma_start", out=Drow[:], in_=dap(W["ssm_d"], l * 1024, [[0, 128], [1, 1024]])), [], ["Drow"])
                  for src_t, dst in [(hr0, h0r), (hi0, h0i)]:
                      P.dma("sp", Rec("dma_start", out=dst[:], in_=dap(src_t, (l * 4 + sidx) * 4096, [[1, 128], [128, 32]])), [], ["h0"])
                  if pi == 0:
                      sp_ = ["ssmp"]
                      P.op("act", Rec("activation", out=dtt[:], in_=dtt[:], func=AF.Exp), sp_, sp_)
                      P.op("dve", Rec("tensor_tensor", out=t1[:], in0=lam_r[:], in1=dtt[:], op=ALU.mult), sp_, sp_)
                      P.op("act", Rec("activation", out=mag[:], in_=t1[:], func=AF.Exp), sp_, sp_)
                      P.op("act", Rec("activation", out=rho[:], in_=t1[:], func=AF.Exp, scale=8.0), sp_, sp_)
                      P.op("dve", Rec("tensor_tensor", out=th[:], in0=lam_i[:], in1=dtt[:], op=ALU.mult), sp_, sp_)
                      P.op("dve", Rec("tensor_scalar", out=phi[:], in0=th[:], scalar1=8.0, scalar2=None, op0=ALU.mult), sp_, sp_)
                      sincos("dve", th[:], t3[:].bitcast(I32), t2[:], t3[:], abr[:], abi[:], sp_, "tr1")
                      P.op("dve", Rec("tensor_tensor", out=abr[:], in0=abr[:], in1=mag[:], op=ALU.mult), sp_ + ["tr1c"], sp_)
                      P.op("dve", Rec("tensor_tensor", out=abi[:], in0=abi[:], in1=mag[:], op=ALU.mult), sp_ + ["tr1s"], sp_)
                      P.op("dve", Rec("tensor_scalar", out=t2[:], in0=phi[:], scalar1=1.0 / TWO_PI, scalar2=None, op0=ALU.mult), sp_ + ["tr1a1"], sp_ + ["tr1a1"])
                      P.op("dve", Rec("tensor_copy", out=ti[:], in_=t2[:]), sp_ + ["tr1a1", "tr1a2"], sp_ + ["tr1a2"])
                      P.op("dve", Rec("tensor_copy", out=t2[:], in_=ti[:]), sp_ + ["tr1a1", "tr1a2"], sp_ + ["tr1a1"])
                      P.op("dve", Rec("scalar_tensor_tensor", out=phi[:], in0=t2[:], scalar=-TWO_PI, in1=phi[:], op0=ALU.mult, op1=ALU.add), sp_ + ["tr1a1"], sp_)
                      P.op("dve", Rec("tensor_scalar", out=t1[:], in0=abr[:], scalar1=-1.0, scalar2=None, op0=ALU.add), sp_, sp_)
                      P.op("dve", Rec("tensor_tensor", out=t2[:], in0=lam_r[:], in1=lam_r[:], op=ALU.mult), sp_ + ["tr1a1"], sp_ + ["tr1a1"])
                      P.op("dve", Rec("tensor_tensor", out=t3[:], in0=lam_i[:], in1=lam_i[:], op=ALU.mult), sp_ + ["tr1a2"], sp_ + ["tr1a2"])
                      P.op("dve", Rec("tensor_tensor", out=t2[:], in0=t2[:], in1=t3[:], op=ALU.add), sp_ + ["tr1a1", "tr1a2"], sp_ + ["tr1a1"])
                      P.op("dve", Rec("reciprocal", out=t2[:], in_=t2[:]), sp_ + ["tr1a1"], sp_ + ["tr1a1"])
                      P.op("dve", Rec("tensor_tensor", out=crr[:], in0=t1[:], in1=lam_r[:], op=ALU.mult), sp_, sp_)
                      P.op("dve", Rec("tensor_tensor", out=t3[:], in0=abi[:], in1=lam_i[:], op=ALU.mult), sp_ + ["tr1a2"], sp_ + ["tr1a2"])
                      P.op("dve", Rec("tensor_tensor", out=crr[:], in0=crr[:], in1=t3[:], op=ALU.add), sp_ + ["tr1a2"], sp_)
                      P.op("dve", Rec("tensor_tensor", out=crr[:], in0=crr[:], in1=t2[:], op=ALU.mult), sp_ + ["tr1a1"], sp_)
                      P.op("dve", Rec("tensor_tensor", out=cii[:], in0=abi[:], in1=lam_r[:], op=ALU.mult), sp_, sp_)
                      P.op("dve", Rec("tensor_tensor", out=t3[:], in0=t1[:], in1=lam_i[:], op=ALU.mult), sp_ + ["tr1a2"], sp_ + ["tr1a2"])
                      P.op("dve", Rec("tensor_tensor", out=cii[:], in0=cii[:], in1=t3[:], op=ALU.subtract), sp_ + ["tr1a2"], sp_)
                      P.op("dve", Rec("tensor_tensor", out=cii[:], in0=cii[:], in1=t2[:], op=ALU.mult), sp_ + ["tr1a1"], sp_)
                      bc = lambda a: a[:].unsqueeze(2).broadcast_to([128, 32, 16])
                      cmul("dve", Bbr[:], Bbi[:], bc(crr), bc(cii), Br[:], Bi[:], PCr[:], PCi[:], sp_ + ["ssmB", "PC"], ["ssmB", "PC"])
                      P.op("dve", Rec("tensor_tensor", out=t1[:], in0=mag[:], in1=mag[:], op=ALU.mult), sp_, sp_)
                      P.op("dve", Rec("reciprocal", out=t1[:], in_=t1[:]), sp_, sp_)
                      P.op("dve", Rec("tensor_tensor", out=iar[:], in0=abr[:], in1=t1[:], op=ALU.mult), sp_, sp_)
                      P.op("dve", Rec("scalar_tensor_tensor", out=iai[:], in0=abi[:], scalar=-1.0, in1=t1[:], op0=ALU.mult, op1=ALU.mult), sp_, sp_)
                      pr = ["PC", "ssmp"]
                      P.op("pool", Rec("memset", PCr[:, :, 7:8], 1.0), pr, pr)
                      P.op("pool", Rec("memset", PCi[:, :, 7:8], 0.0), pr, pr)
                      for kk in range(8, 16):
                          cmul("dve", PCr[:, :, kk], PCi[:, :, kk], PCr[:, :, kk - 1], PCi[:, :, kk - 1], abr[:], abi[:], t2[:], t3[:], pr + ["tr1a1", "tr1a2"], pr + ["tr1a1", "tr1a2"])
                      for kk in range(6, -1, -1):
                          cmul("dve", PCr[:, :, kk], PCi[:, :, kk], PCr[:, :, kk + 1], PCi[:, :, kk + 1], iar[:], iai[:], t2[:], t3[:], pr + ["tr1a1", "tr1a2"], pr + ["tr1a1", "tr1a2"])
                      for i in range(8):
                          P.op("pool", Rec("tensor_copy", out=PBr[:, :, i], in_=PCr[:, :, 14 - i]), pr, ["PB"])
                          P.op("pool", Rec("tensor_copy", out=PBi[:, :, i], in_=PCi[:, :, 14 - i]), pr, ["PB"])
                  if pi == 0:
                      for t_, o_, n_ in [(PCr, 0, 512), (PCi, 512, 512)]:
                          P.dma("sp", Rec("dma_start", out=dap(scrT, l * 128 * 1088 + o_, [[1088, 128], [1, n_]]), in_=t_[:].rearrange("p a b -> p (a b)")), ["PC"], ["scrT%d" % l])
                      for t_, o_ in [(rho, 1024), (phi, 1056)]:
                          P.dma("sp", Rec("dma_start", out=dap(scrT, l * 128 * 1088 + o_, [[1088, 128], [1, 32]]), in_=t_[:]), ["ssmp"], ["scrT%d" % l])
                  else:
                      for t_, o_, n_ in [(PCr, 0, 512), (PCi, 512, 512)]:
                          P.dma("sp", Rec("dma_start", out=t_[:].rearrange("p a b -> p (a b)"), in_=dap(scrT, l * 128 * 1088 + o_, [[1088, 128], [1, n_]])), ["scrT%d" % l], ["PC"])
                      for t_, o_ in [(rho, 1024), (phi, 1056)]:
                          P.dma("sp", Rec("dma_start", out=t_[:], in_=dap(scrT, l * 128 * 1088 + o_, [[1088, 128], [1, 32]])), ["scrT%d" % l], ["ssmp"])
                  cmul("dve", hend[:, 0, :], hend[:, 1, :], PCr[:, :, 11], PCi[:, :, 11], h0r[:], h0i[:], t2[:], t3[:], pr + ["h0", "hend", "tr1a1", "tr1a2"], ["hend", "tr1a1", "tr1a2"])

                  ckpt(6)
                  for sl in range(4):
                      slab, rslab = w_next("w_in", l, 1280 + sl * 256)
                      for i in range(8):
                          nrow = NCH1 if i < 4 else NCHK
                          for k in range(16):
                              lhs = sap(Hb[:, k, i:i + 1], [[8, nrow]])
                              P.op("pe", Rec("matmul", ps[i % 2][0:nrow, 0:256], lhsT=lhs, rhs=slab[:, k, 0:256], start=(k == 0), stop=(k == 15)),
                                   [rslab, "Hb%d" % k], [RP[i % 2]])
                          P.op("act", Rec("activation", out=Vu[0:nrow, :, i, :], in_=ps[i % 2][0:nrow, 0:256].rearrange("p (g c) -> p g c", g=16), func=AF.Copy), [RP[i % 2]], ["Vu"])
                      for gq in range(4):
                          ub = psb(4)
                          for gg in range(4):
                              g = gq * 4 + gg
                              P.op("pe", Rec("transpose", ub[:, gg * 128:gg * 128 + NCH1], Vu[0:NCH1, g, :, :], identb[0:NCH1, 0:NCH1]),
                                   ["Vu", "identb"], [RP[4]])
                          for gg in range(4):
                              g = gq * 4 + gg
                              P.op("dve", Rec("tensor_copy", out=Ug[g][:], in_=ub[:, gg * 128:gg * 128 + NCH1]), [RP[4]], ["Ug%d" % g])
                      j0 = sl * NPB
                      bcp = lambda a: a[:, j0:j0 + NPB].unsqueeze(2).broadcast_to([128, NPB, 65])
                      posb = posf.unsqueeze(1).broadcast_to([128, NPB, 65])
                      treg = "scrR_%d_%d" % (l, sl)
                      tsrc = [dap(scrR, ((l * 4 + sl) * 3 + q_) * 128 * 520, [[520, 128], [1, 520]]) for q_ in range(3)]
                      flt = lambda a: a[:].rearrange("p a b -> p (a b)")
                      if pi == 0:
                          P.op("dve", Rec("tensor_tensor", out=ang[:], in0=bcp(phi), in1=posb, op=ALU.mult), ["ssmp", "cs", "trb"], ["trb"])
                          sincos("dve", ang[:], u2[:].bitcast(I32), u1[:], u2[:], Tc[:], Ts[:], ["trb"], "tr2")
                          P.op("dve", Rec("tensor_tensor", out=d0[:], in0=bcp(rho), in1=cs[:, 965:1030].unsqueeze(1).broadcast_to([128, NPB, 65]), op=ALU.mult), ["ssmp", "cs"], ["d0"])
                          P.dma("sp", Rec("dma_start", out=tsrc[0], in_=flt(Tc)), ["tr2c"], [treg])
                          P.dma("sp", Rec("dma_start", out=tsrc[1], in_=flt(Ts)), ["tr2s"], [treg])
                          P.dma("sp", Rec("dma_start", out=tsrc[2], in_=flt(d0)), ["d0"], [treg])
                      else:
                          P.dma("sp", Rec("dma_start", out=flt(Tc), in_=tsrc[0]), [treg], ["tr2c"])
                          P.dma("sp", Rec("dma_start", out=flt(Ts), in_=tsrc[1]), [treg], ["tr2s"])
                          P.dma("sp", Rec("dma_start", out=flt(d0), in_=tsrc[2]), [treg], ["d0"])
                      creg = "scrC_%d_%d" % (l, sl)
                      csrc = dap(scrC, (l * 4 + sl) * 128 * 4096, [[4096, 128], [1, 4096]])
                      gsrc = dap(scrG, (l * 4 + sl) * 128 * 2048, [[2048, 128], [1, 2048]])
                      if pi > 0:
                          P.dma("sp", Rec("dma_start", out=MCBt[:].rearrange("p a b c -> p (a b c)"), in_=csrc), [creg], ["MCm"])
                          P.dma("sp", Rec("dma_start", out=TgSt[:].rearrange("p a b -> p (a b)"), in_=gsrc), [creg], ["TgS"])
                      for jj in range(NPB):
                          j = j0 + jj
                          MBp = [MBpT[:, jj % 2, q_, :] for q_ in range(4)]
                          if pi == 0:
                              rsp = ["ssmB", "PB", "PC", "ssmC", "ssmp"]
                              pb_r = PBr[:, j, :].unsqueeze(2).broadcast_to([128, 8, 16]); pb_i = PBi[:, j, :].unsqueeze(2).broadcast_to([128, 8, 16])
                              bb_r = Bbr[:, j, :].unsqueeze(1).broadcast_to([128, 8, 16]); bb_i = Bbi[:, j, :].unsqueeze(1).broadcast_to([128, 8, 16])
                              v3 = lambda a, n: a[:, 0:n * 16].rearrange("p (a b) -> p a b", b=16)
                              P.op("pool", Rec("tensor_tensor", out=v3(tA, 8), in0=pb_i, in1=bb_i, op=ALU.mult), rsp + ["tA"], ["tA"])
                              P.op("pool", Rec("tensor_tensor", out=v3(tB, 8), in0=pb_r, in1=bb_r, op=ALU.mult), rsp + ["tB"], ["tB"])
                              P.op("pool", Rec("tensor_tensor", out=v3(MBt[0], 8), in0=v3(tB, 8), in1=v3(tA, 8), op=ALU.subtract), ["tA", "tB"], ["MBt"])
                              P.op("pool", Rec("tensor_tensor", out=v3(tA, 8), in0=pb_r, in1=bb_i, op=ALU.mult), rsp + ["tA"], ["tA"])
                              P.op("pool", Rec("tensor_tensor", out=v3(tB, 8), in0=pb_i, in1=bb_r, op=ALU.mult), rsp + ["tB"], ["tB"])
                              P.op("pool", Rec("tensor_tensor", out=v3(MBt[1], 8), in0=v3(tA, 8), in1=v3(tB, 8), op=ALU.add), ["tA", "tB"], ["MBt"])
                              pc_r = PCr[:, j, :].unsqueeze(2).broadcast_to([128, 16, 16]); pc_i = PCi[:, j, :].unsqueeze(2).broadcast_to([128, 16, 16])
                              cc_r = Cr[:, j, :].unsqueeze(1).broadcast_to([128, 16, 16]); cc_i = Ci[:, j, :].unsqueeze(1).broadcast_to([128, 16, 16])
                              P.op("dve", Rec("tensor_tensor", out=v3(tA, 16), in0=pc_i, in1=cc_i, op=ALU.mult), rsp + ["tA"], ["tA"])
                              P.op("dve", Rec("tensor_tensor", out=v3(tB, 16), in0=pc_r, in1=cc_r, op=ALU.mult), rsp + ["tB"], ["tB"])
                              MCm = MCB[jj]
                              P.op("dve", Rec("tensor_tensor", out=v3(MCm[0], 16), in0=v3(tB, 16), in1=v3(tA, 16), op=ALU.subtract), ["tA", "tB"], ["MCm"])
                              P.op("dve", Rec("tensor_tensor", out=v3(tA, 16), in0=pc_r, in1=cc_i, op=ALU.mult), rsp + ["tA"], ["tA"])
                              P.op("dve", Rec("tensor_tensor", out=v3(tB, 16), in0=pc_i, in1=cc_r, op=ALU.mult), rsp + ["tB"], ["tB"])
                              P.op("dve", Rec("scalar_tensor_tensor", out=v3(MCm[1], 16), in0=v3(tA, 16), scalar=-1.0, in1=v3(tB, 16), op0=ALU.mult, op1=ALU.subtract), ["tA", "tB"], ["MCm"])
                              tb = psb(5)
                              for ri in range(2):
                                  P.op("pe", Rec("transpose", tb[:, ri * 128:(ri + 1) * 128], MBt[ri][:], identb[:]), ["MBt", "identb"], [RP[5]])
                              for gl in range(2):
                                  for ri in range(2):
                                      P.op("act", Rec("activation", out=MBp[gl * 2 + ri][:, gl * 64:gl * 64 + 64], in_=tb[:, ri * 128 + gl * 64:ri * 128 + gl * 64 + 64], func=AF.Copy),
                                           [RP[5]], ["MBp%d" % (jj % 2)])
                              for gl in range(2):
                                  sl_ = slice(gl * 64, gl * 64 + 64)
                                  P.op("pe", Rec("matmul", ps[3][:, gl * 128:(gl + 1) * 128], lhsT=MBt[0][sl_, :], rhs=MCm[0][sl_, 0:128], start=True, stop=False), ["MBt", "MCm"], [RP[3]])
                                  P.op("pe", Rec("matmul", ps[3][:, gl * 128:(gl + 1) * 128], lhsT=MBt[1][sl_, :], rhs=MCm[1][sl_, 0:128], start=False, stop=True), ["MBt", "MCm"], [RP[3]])
                                  P.op("dve", Rec("tensor_tensor", out=TgS[jj * 2 + gl][:], in0=ps[3][:, gl * 128:(gl + 1) * 128], in1=tmask[:], op=ALU.mult), [RP[3], "tmask"], ["TgS"])
                          mreg = "scrM_%d_%d" % (l, j)
                          msrc = dap(scrM, (l * 32 + j) * 128 * 512, [[512, 128], [1, 512]])
                          if pi == 0:
                              P.dma("sp", Rec("dma_start", out=msrc, in_=MBpT[:, jj % 2].rearrange("p a b -> p (a b)")), ["MBp%d" % (jj % 2)], [mreg])
                          else:
                              P.dma("sp", Rec("dma_start", out=MBpT[:, jj % 2].rearrange("p a b -> p (a b)"), in_=msrc), [mreg], ["MBp%d" % (jj % 2)])
                          xbk = 6 if jj % 2 == 0 else 4
                          for ri in range(2):
                              for gl in range(2):
                                  g = jj * 2 + gl
                                  P.op("pe", Rec("matmul", ps[xbk][:, ri * 128:ri * 128 + NCH1], lhsT=MBp[gl * 2 + ri][:], rhs=Ug[g][:], start=(gl == 0), stop=(gl == 1)),
                                       ["MBp%d" % (jj % 2), "Ug%d" % g], [RP[xbk]])
                          xr = ps[xbk][:, 0:NCHK]; xi = ps[xbk][:, 128:128 + NCHK]
                          tcj = Tc[:, jj, 1:65]; tsj = Ts[:, jj, 1:65]
                          rdm = [RP[xbk], "tr2c", "tr2s"]
                          P.op("dve", Rec("tensor_tensor", out=Dr[:, jj, 1:65], in0=xr, in1=tcj, op=ALU.mult), rdm, ["Dm"])
                          P.op("dve", Rec("tensor_tensor", out=u1[:, jj, 1:65], in0=xi, in1=tsj, op=ALU.mult), rdm + ["tr2a1"], ["tr2a1"])
                          P.op("dve", Rec("tensor_tensor", out=Di[:, jj, 1:65], in0=xi, in1=tcj, op=ALU.mult), rdm, ["Dm"])
                          P.op("dve", Rec("tensor_tensor", out=u2[:, jj, 1:65], in0=xr, in1=tsj, op=ALU.mult), rdm + ["tr2a2"], ["tr2a2"])
                          P.op("act", Rec("activation", out=Xs[:, 0, jj:jj + 1], in_=ps[xbk][:, NCHK:NCHK + 1], func=AF.Copy), [RP[xbk]], ["Xs"])
                          P.op("act", Rec("activation", out=Xs[:, 1, jj:jj + 1], in_=ps[xbk][:, 128 + NCHK:128 + NCHK + 1], func=AF.Copy), [RP[xbk]], ["Xs"])
                      if pi == 0:
                          P.dma("sp", Rec("dma_start", out=csrc, in_=MCBt[:].rearrange("p a b c -> p (a b c)")), ["MCm"], [creg])
                          P.dma("sp", Rec("dma_start", out=gsrc, in_=TgSt[:].rearrange("p a b -> p (a b)")), ["TgS"], [creg])
                      dm = ["Dm", "tr2a1", "tr2a2"]
                      P.op("dve", Rec("tensor_tensor", out=Dr[:, :, 1:65], in0=Dr[:, :, 1:65], in1=u1[:, :, 1:65], op=ALU.add), dm, ["Dm"])
                      P.op("dve", Rec("tensor_tensor", out=Di[:, :, 1:65], in0=Di[:, :, 1:65], in1=u2[:, :, 1:65], op=ALU.subtract), dm, ["Dm"])
                      P.op("dve", Rec("tensor_copy", out=Dr[:, :, 0], in_=Hr[:, l, j0:j0 + NPB]), ["H", "Dm"], ["Dm"])
                      P.op("dve", Rec("tensor_copy", out=Di[:, :, 0], in_=Hi[:, l, j0:j0 + NPB]), ["H", "Dm"], ["Dm"])
                      fl = lambda a: a[:].rearrange("p a b -> p (a b)")
                      P.op("dve", Rec("tensor_tensor_scan", out=fl(Dr), data0=fl(d0), data1=fl(Dr), initial=0.0, op0=ALU.mult, op1=ALU.add), ["Dm", "d0"], ["Dm"])
                      P.op("dve", Rec("tensor_tensor_scan", out=fl(Di), data0=fl(d0), data1=fl(Di), initial=0.0, op0=ALU.mult, op1=ALU.add), ["Dm", "d0"], ["Dm"])
                      md = ["Dm", "tr2c", "tr2s", "tr2a1", "tr2a2", "trb"]
                      P.op("dve", Rec("tensor_tensor", out=u1[:], in0=Dr[:], in1=Tc[:], op=ALU.mult), md, ["tr2a1"])
                      P.op("dve", Rec("tensor_tensor", out=u2[:], in0=Di[:], in1=Ts[:], op=ALU.mult), md, ["tr2a2"])
                      P.op("dve", Rec("tensor_tensor", out=u1[:], in0=u1[:], in1=u2[:], op=ALU.subtract), md, ["tr2a1"])
                      P.op("dve", Rec("tensor_tensor", out=u2[:], in0=Dr[:], in1=Ts[:], op=ALU.mult), md, ["tr2a2"])
                      P.op("dve", Rec("tensor_tensor", out=ang[:], in0=Di[:], in1=Tc[:], op=ALU.mult), md, ["trb"])
                      P.op("dve", Rec("tensor_tensor", out=u2[:], in0=u2[:], in1=ang[:], op=ALU.add), md, ["tr2a2"])
                      P.op("act", Rec("activation", out=Sbr[:, :, 0:64], in_=u1[:, :, 0:64], func=AF.Copy), ["tr2a1"], ["Sb"])
                      P.op("act", Rec("activation", out=Sbi[:, :, 0:64], in_=u2[:, :, 0:64], func=AF.Copy), ["tr2a2"], ["Sb"])
                      P.op("act", Rec("activation", out=Sbr[:, :, 64], in_=h0r[:, j0:j0 + NPB], func=AF.Copy), ["h0"], ["Sb"])
                      P.op("act", Rec("activation", out=Sbi[:, :, 64], in_=h0i[:, j0:j0 + NPB], func=AF.Copy), ["h0"], ["Sb"])
                      P.op("dve", Rec("tensor_copy", out=Hr[:, l, j0:j0 + NPB], in_=u1[:, :, 64]), ["tr2a1"], ["H"])
                      P.op("dve", Rec("tensor_copy", out=Hi[:, l, j0:j0 + NPB], in_=u2[:, :, 64]), ["tr2a2"], ["H"])
                      cmul("dve", tA[:, 0:NPB], tA[:, NPB:2 * NPB], PCr[:, j0:j0 + NPB, 3], PCi[:, j0:j0 + NPB, 3], Xs[:, 0, :], Xs[:, 1, :], tB[:, 0:NPB], tB[:, NPB:2 * NPB],
                           ["PC", "Xs", "tA", "tB"], ["tA", "tB"])
                      P.op("dve", Rec("tensor_tensor", out=hend[:, 0, j0:j0 + NPB], in0=hend[:, 0, j0:j0 + NPB], in1=tA[:, 0:NPB], op=ALU.add), ["tA", "hend"], ["hend"])
                      P.op("dve", Rec("tensor_tensor", out=hend[:, 1, j0:j0 + NPB], in0=hend[:, 1, j0:j0 + NPB], in1=tA[:, NPB:2 * NPB], op=ALU.add), ["tA", "hend"], ["hend"])
                      for gq in range(4):
                          ybk = 7 if gq % 2 == 0 else 2
                          for gg in range(4):
                              g = gq * 4 + gg
                              jj = g // 2; gl = g % 2
                              sl_ = slice(gl * 64, gl * 64 + 64)
                              oc = slice(gg * 128, (gg + 1) * 128)
                              P.op("pe", Rec("matmul", ps[ybk][0:NCH1, oc], lhsT=Ug[g][:], rhs=TgS[g][:], start=True, stop=False), ["Ug%d" % g, "TgS"], [RP[ybk]])
                              P.op("pe", Rec("matmul", ps[ybk][0:NCH1, oc], lhsT=Sbr[sl_, jj, :], rhs=MCB[jj][0][sl_, 128:256], start=False, stop=False), ["Sb", "MCm"], [RP[ybk]])
                              P.op("pe", Rec("matmul", ps[ybk][0:NCH1, oc], lhsT=Sbi[sl_, jj, :], rhs=MCB[jj][1][sl_, 128:256], start=False, stop=True), ["Sb", "MCm"], [RP[ybk]])
                          ch0 = sl * 256 + gq * 64
                          P.op("dve", Rec("tensor_tensor", out=vd[0:NCH1], in0=Vu[0:NCH1, gq * 4:(gq + 1) * 4, :, :], in1=sap(Drow[0:NCH1, ch0:ch0 + 1], [[16, 4], [0, 8], [1, 16]]), op=ALU.mult),
                               ["Vu", "Drow"], ["vd"])
                          P.op("dve", Rec("tensor_tensor", out=ypre[0:NCH1], in0=ps[ybk][0:NCH1, :].rearrange("p (g i c) -> p g i c", g=4, i=8),
                                                                in1=vd[0:NCH1], op=ALU.add), [RP[ybk], "vd"], ["ypre"])
                          half = gq % 2
                          P.op("act", Rec("activation", out=zcm[0:NCH1, :, half * 64:(half + 1) * 64].rearrange("p i (g c) -> p g i c", g=4), in_=ypre[0:NCH1], func=AF.Gelu), ["ypre"], ["zcm"])
                          if half == 1:
                              mz = sl * 2 + gq // 2
                              zb = psb(5)
                              for i in range(8):
                                  P.op("pe", Rec("transpose", zb[:, i * 128:i * 128 + NCH1], zcm[0:NCH1, i, :], identb[0:NCH1, 0:NCH1]), ["zcm", "identb"], [RP[5]])
                              zv = zb[:, 0:1024].rearrange("p (i n) -> p i n", i=8)
                              P.op("dve", Rec("tensor_copy", out=sap(Zf[:, mz, 0:1], [[1, 4], [8, NCH1]]), in_=zv[:, 0:4, 0:NCH1]), [RP[5]], ["fb%d" % mz])
                              P.op("dve", Rec("tensor_copy", out=sap(Zf[:, mz, 4:5], [[1, 4], [8, NCHK]]), in_=zv[:, 4:8, 0:NCHK]), [RP[5]], ["fb%d" % mz])
                  ckpt(7)
                  for ri, (dst_p, dst_s, Hx) in enumerate([(ohrp, ohrs, Hr), (ohip, ohis, Hi)]):
                      P.dma("sp", Rec("dma_start", out=dap(dst_p, l * 4096, [[1, 128], [128, 32]]), in_=Hx[:, l, :]), ["H"], ["ohp%d" % ri])
                      if HS[0]:
                          P.dma("sp", Rec("dma_start", out=dap(dst_s, (l * 4 + sidx) * 4096, [[1, 128], [128, 32]]), in_=hend[:, ri, :]), ["hend"], ["ohs%d" % ri])
                  for j in range(4):
                      slab, rslab = w_next("w_glu", l, j * 256)
                      for mm in range(2):
                          m = j * 2 + mm
                          def ev_g(pm, psm, rpm, rps, m=m):
                              P.op("act", Rec("activation", out=Pf[0][:, 0:256], in_=pm[:, 0:256], func=AF.Sigmoid), rpm, ["Pf0", "Pf1"])
                              P.op("act", Rec("activation", out=Pf[1][:, 0:256], in_=pm[:, 256:512], func=AF.Sigmoid), rpm, ["Pf2"])
                              P.op("dve", Rec("tensor_tensor", out=Hb[:, 8 + m, 0:256], in0=Pf[0][:, 0:256], in1=Zf[:, m, 0:256], op=ALU.mult), ["Pf0", "Pf1", "fb%d" % m], ["Hb%d" % (8 + m)])
                              P.op("dve", Rec("tensor_tensor", out=Hb[:, 8 + m, 256:512], in0=Pf[1][:, 0:256], in1=Zf[:, m, 256:512], op=ALU.mult), ["Pf2", "fb%d" % m], ["Hb%d" % (8 + m)])
                              if psm is not None:
                                  P.op("act", Rec("activation", out=sm[:, 0:NS], in_=psm, func=AF.Sigmoid), rps, ["sm0", "sm1"])
                              P.op("dve", Rec("tensor_tensor", out=Hb[:, 8 + m, NP:NT], in0=sm[:, 0:NS], in1=Zf[:, m, NP:NT], op=ALU.mult), ["sm0", "sm1", "fb%d" % m], ["Hb%d" % (8 + m)])
                          proj_chunk(slab, rslab, 8, mm * 128, lambda k: Zf[:, k, :], lambda k: ["fb%d" % k], ev_g)
                  ckpt(8)
                  rmsnorm([Ao[:, k, :] for k in range(8)], ["Ao%d" % k for k in range(8)], lambda k, l=l: gains[:, l, 32 + k:33 + k],
                          [Ao[:, k, :] for k in range(8)], ["Ao%d" % k for k in range(8)], 1024)
                  rmsnorm([Hb[:, 8 + k, :] for k in range(8)], ["Hb%d" % (8 + k) for k in range(8)], lambda k, l=l: gains[:, l, 40 + k:41 + k],
                          [Hb[:, 8 + k, :] for k in range(8)], ["Hb%d" % (8 + k) for k in range(8)], 1024)
                  mix_rhs = lambda k: (Ao[:, k, :] if k < 8 else Hb[:, k, :])
                  mix_regs = lambda k: ["Ao%d" % k] if k < 8 else ["Hb%d" % k]
                  for j in range(8):
                      slab, rslab = w_next("w_out", l, j * 256)
                      for mm in range(2):
                          m = j * 2 + mm
                          def ev_o(pm, psm, rpm, rps, m=m):
                              P.op("dve", Rec("tensor_tensor", out=X[:, m, 0:NP], in0=pm, in1=X[:, m, 0:NP], op=ALU.add), rpm + ["X%d" % m], ["X%d" % m])
                              if psm is not None:
                                  P.op("dve", Rec("tensor_tensor", out=X[:, m, NP:NT], in0=psm, in1=X[:, m, NP:NT], op=ALU.add), rps + ["X%d" % m], ["X%d" % m])
                          proj_chunk(slab, rslab, 16, mm * 128, mix_rhs, mix_regs, ev_o)
                  ckpt(9)
                  rmsnorm([X[:, k, :] for k in range(16)], ["X%d" % k for k in range(16)], lambda k, l=l: gains[:, l, 16 + k:17 + k],
                          [Hb[:, k, :] for k in range(16)], ["Hb%d" % k for k in range(16)], D)
                  for fp in range(0 if int(_os.environ.get("SKIPFFN", "0")) else 4):
                      c0 = fp * 1536
                      nsl = 6 if fp < 3 else 4
                      for j in range(nsl):
                          slg, rslg = w_next("w_gate", l, c0 + j * 256)
                          slu, rslu = w_next("w_up", l, c0 + j * 256)
                          for mm in range(2):
                              ma = j * 2 + mm
                              def ev_gate(pm, psm, rpm, rps, ma=ma):
                                  P.op("act", Rec("activation", out=Pf[0][:, 0:256], in_=pm[:, 0:256], func=AF.Silu), rpm, ["Pf0", "Pf1"])
                                  P.op("act", Rec("activation", out=Pf[1][:, 0:256], in_=pm[:, 256:512], func=AF.Silu), rpm, ["Pf2"])
                                  if psm is not None:
                                      P.op("act", Rec("activation", out=sm[:, 0:NS], in_=psm, func=AF.Silu), rps, ["sm0", "sm1"])
                              def ev_up(pm, psm, rpm, rps, ma=ma):
                                  P.op("dve", Rec("tensor_tensor", out=ACT_[:, ma, 0:256], in0=pm[:, 0:256], in1=Pf[0][:, 0:256], op=ALU.mult), rpm + ["Pf0", "Pf1"], ["fb%d" % ma])
                                  P.op("dve", Rec("tensor_tensor", out=ACT_[:, ma, 256:512], in0=pm[:, 256:512], in1=Pf[1][:, 0:256], op=ALU.mult), rpm + ["Pf2"], ["fb%d" % ma])
                                  if psm is not None:
                                      P.op("dve", Rec("tensor_tensor", out=ACT_[:, ma, NP:NT], in0=psm, in1=sm[:, 0:NS], op=ALU.mult), rps + ["sm0", "sm1"], ["fb%d" % ma])
                              proj_chunk(slg, rslg, 16, mm * 128, hb_rhs, hb_regs, ev_gate)
                              proj_chunk(slu, rslu, 16, mm * 128, hb_rhs, hb_regs, ev_up)
                      nk = nsl * 2
                      for j in range(8):
                          slab, rslab = w_next("w_down", l, j * 256)
                          for mm in range(2):
                              m = j * 2 + mm
                              def ev_d(pm, psm, rpm, rps, m=m):
                                  P.op("dve", Rec("tensor_tensor", out=X[:, m, 0:NP], in0=pm, in1=X[:, m, 0:NP], op=ALU.add), rpm + ["X%d" % m], ["X%d" % m])
                                  if psm is not None:
                                      P.op("dve", Rec("tensor_tensor", out=X[:, m, NP:NT], in0=psm, in1=X[:, m, NP:NT], op=ALU.add), rps + ["X%d" % m], ["X%d" % m])
                              proj_chunk(slab, rslab, nk, mm * 128, lambda k: ACT_[:, k, :], lambda k: ["fb%d" % k], ev_d)

              ckpt(10)
              for k in range(16):
                  q = sqs[k % 2]
                  P.op("act", Rec("activation", out=q[:], in_=X[:, k, :], func=AF.Square), ["X%d" % k], ["sq%d" % (k % 2)])
                  P.op("pe", Rec("matmul", ps[3][:, 0:NP], lhsT=onesb[:], rhs=q[:, 0:NP], start=(k == 0), stop=(k == 15)), ["sq%d" % (k % 2), "onesb"], [RP[3]])
                  if HS[0]:
                      P.op("pe", Rec("matmul", ps[2][:, 480:480 + NS], lhsT=onesb[:], rhs=q[:, NP:NT], start=(k == 0), stop=(k == 15)), ["sq%d" % (k % 2), "onesb"], [RP[2]])
              P.op("act", Rec("activation", out=rstd[:, 0:NP], in_=ps[3][:, 0:NP], func=AF.Sqrt, scale=1.0 / D, bias=epsb[:, 0:1]), [RP[3], "epsb"], ["rstd"])
              if HS[0]:
                  P.op("act", Rec("activation", out=rstd[:, NP:NT], in_=ps[2][:, 480:480 + NS], func=AF.Sqrt, scale=1.0 / D, bias=epsb[:, 0:1]), [RP[2], "epsb"], ["rstd"])
              P.op("dve", Rec("reciprocal", out=rstd[:], in_=rstd[:]), ["rstd"], ["rstd"])
              for k in range(16):
                  P.op("dve", Rec("scalar_tensor_tensor", out=X[:, k, :], in0=X[:, k, :], scalar=gfin[:, k:k + 1], in1=rstd[:], op0=ALU.mult, op1=ALU.mult),
                       ["X%d" % k, "rstd", "gfin"], ["X%d" % k])
              for blk in range(NB + (1 if HS[0] else 0)):
                  if blk < NB:
                      cols = slice(blk * 128, (blk + 1) * 128); nr = 128
                  else:
                      cols = slice(NP, NT); nr = NS
                  for kq in range(4):
                      for kk in range(4):
                          k = kq * 4 + kk
                          P.op("pe", Rec("transpose", ps[4][0:nr, kk * 128:(kk + 1) * 128], X[:, k, cols], identf), ["X%d" % k, "cs"], [RP[4]])
                      P.op("dve", Rec("tensor_copy", out=stage[0:nr, kq * 512:(kq + 1) * 512], in_=ps[4][0:nr, :]), [RP[4]], STG)
                  if blk < NB:
                      r0 = pi * NP + blk * 128
                      P.dma("sp", Rec("dma_start", out=dap(yp, r0 * D, [[D, 128], [1, D]]), in_=stage[:]), STG, ["yp"])
                  else:
                      P.dma("sp", Rec("dma_start", out=dap(ys, sidx * 4 * D, [[D, 4], [1, D]]), in_=stage[0:4, :]), STG, ["ys"])

        except _Stop:
            pass
        if dbg:
            P.dma("sp", Rec("dma_start", out=dap(dbg_t, 0, [[16 * NT, 128], [1, 8 * NT]]), in_=Ao[:].rearrange("p a b -> p (a b)")), ["Ao%d" % k for k in range(8)], ["dbg"])
            P.dma("sp", Rec("dma_start", out=dap(dbg_t, 8 * NT, [[16 * NT, 128], [1, 8 * NT]]), in_=Hb[:, 8:16, :].rearrange("p a b -> p (a b)")), ["Hb%d" % k for k in range(8, 16)], ["dbg"])
        P.op("sp", Rec("nop", ), ["yp", "ys", "okp", "ovp", "oks", "ovs", "ohp0", "ohp1", "ohs0", "ohs1"] + (["dbg"] if dbg else []), [])
        P.emit()
    return nc


def make_consts():
    c = np.zeros((128, 1032), np.float32)
    c[:, 0:128] = np.eye(128, dtype=np.float32)
    i = np.arange(128)[:, None]
    j = np.arange(128)[None, :]
    c[:, 128:256] = np.where(j > i, 0.0, NEG)
    c[:, 256:384] = np.where(j <= i, 0.0, NEG)
    c[:, 384:512] = NEG
    c[:, 512:640] = c[:, 256:384]
    js = np.arange(132)[None, :]
    c[:, 640:772] = np.where((js > i) & (js <= i + 128), 0.0, NEG)
    r = np.arange(128)[:, None] // 16
    cc = np.arange(128)[None, :] // 16
    c[:, 772:900] = (cc >= r).astype(np.float32)
    c[:, 900:965] = np.arange(65, dtype=np.float32)[None, :]
    c[:, 965:1030] = 1.0
    c[:, 965] = 0.0
    return c


_NC_CACHE = {}


def kernel(**inputs):
    inp = {k: np.ascontiguousarray(np.asarray(v)) for k, v in inputs.items()}
    npass = SEQ // NP
    key = (npass, DEPTH)
    if key not in _NC_CACHE:
        _NC_CACHE[key] = build(npass, DEPTH)
    nc = _NC_CACHE[key]
    cst = make_consts()
    wnames = ["norm_mix", "w_in", "attn_sink", "ssm_a_re", "ssm_a_im", "ssm_log_dt", "ssm_b_re", "ssm_b_im",
              "ssm_c_re", "ssm_c_im", "ssm_d", "w_glu", "norm_attn_out", "norm_ssm_out", "w_out", "norm_ffn",
              "w_gate", "w_up", "w_down", "norm_final"]
    in_maps = []
    for c in range(8):
        m = {n: inp[n] for n in wnames}
        m["cst"] = cst
        m["xp"] = inp["x_prompt"][c % 2]
        m["xs"] = inp["x_sample"][4 * c:4 * c + 4].reshape(16, D)
        m["ck"] = inp["cache_k"][:, 4 * c:4 * c + 4].reshape(DEPTH, 4, 128, 128)
        m["cv"] = inp["cache_v"][:, 4 * c:4 * c + 4].reshape(DEPTH, 4, 128, 128)
        m["hr0"] = inp["state_ssm_re"][:, 4 * c:4 * c + 4]
        m["hi0"] = inp["state_ssm_im"][:, 4 * c:4 * c + 4]
        in_maps.append({k: np.ascontiguousarray(v, dtype=np.float32) for k, v in m.items()})
    res = run_bass_kernel_spmd(nc, in_maps, core_ids=list(range(8))).results
    f = np.float32
    y_prompt = np.stack([res[0]["yp"], res[1]["yp"]]).astype(f)
    y_sample = np.concatenate([res[c]["ys"].reshape(4, 4, D) for c in range(8)], 0).astype(f)
    k_p = np.stack([res[0]["okp"], res[1]["okp"]], 1).reshape(DEPTH, 2, 128, 2, 64).astype(f)
    v_p = np.stack([res[0]["ovp"], res[1]["ovp"]], 1).reshape(DEPTH, 2, 128, 2, 64).astype(f)
    hr_p = np.stack([res[0]["ohrp"], res[1]["ohrp"]], 1).astype(f)
    hi_p = np.stack([res[0]["ohip"], res[1]["ohip"]], 1).astype(f)
    k_s = np.concatenate([res[c]["oks"] for c in range(8)], 1).reshape(DEPTH, 32, 128, 2, 64).astype(f)
    v_s = np.concatenate([res[c]["ovs"] for c in range(8)], 1).reshape(DEPTH, 32, 128, 2, 64).astype(f)
    hr_s = np.concatenate([res[c]["ohrs"] for c in range(8)], 1).astype(f)
    hi_s = np.concatenate([res[c]["ohis"] for c in range(8)], 1).astype(f)
    return (y_prompt, y_sample, k_p, v_p, hr_p, hi_p, k_s, v_s, hr_s, hi_s)
```

```python
import contextlib
import math
import numpy as np
import concourse.bass as bass
import concourse.mybir as mybir
from concourse.bass import AP
from concourse.bass_utils import run_bass_kernel_spmd

F32 = mybir.dt.float32
BF16 = mybir.dt.bfloat16
I32 = mybir.dt.int32
AF = mybir.ActivationFunctionType
ALU = mybir.AluOpType
AX = mybir.AxisListType

D = 2048
DEPTH = 4
SEQ = 4096
DIN = 2304
DFF = 5632
NP = 512
NS = 4
NT = NP + NS
NB = NP // 128
NCHK = NP // 8
NCH1 = NCHK + 1
EPS = 1e-5
NEG = -30000.0
TWO_PI = 2.0 * math.pi


class Reg:
    __slots__ = ("name", "w", "rs")

    def __init__(self, name):
        self.name = name
        self.w = None
        self.rs = []


class Op:
    __slots__ = ("eng", "fn", "deps", "marked", "count", "is_dma", "sem", "semval")

    def __init__(self, eng, fn, is_dma=False):
        self.eng = eng
        self.fn = fn
        self.deps = []
        self.marked = False
        self.count = 0
        self.is_dma = is_dma
        self.sem = None
        self.semval = 0


ENGS = ("pe", "act", "dve", "pool", "sp")
N_DMA_SEMS = 32


class Prog:
    def __init__(self, nc):
        self.nc = nc
        self.ops = {e: [] for e in ENGS}
        self.dma_last = [None] * N_DMA_SEMS
        self.dma_cum = [0] * N_DMA_SEMS
        self.dma_rr = 0
        self.regs = {}

    def R(self, name):
        r = self.regs.get(name)
        if r is None:
            r = self.regs[name] = Reg(name)
        return r

    def _deps(self, op, reads, writes):
        deps = []
        for r in reads:
            if r.w is not None:
                deps.append(r.w)
        for w in writes:
            if w.w is not None:
                deps.append(w.w)
            deps.extend(w.rs)
        for r in reads:
            r.rs.append(op)
        for w in writes:
            w.w = op
            w.rs = []
        seen = set()
        out = []
        for d in deps:
            if id(d) in seen or d is op:
                continue
            seen.add(id(d))
            if op.eng == "pe" and d.eng == "pe" and not d.is_dma and not op.is_dma:
                continue
            out.append(d)
            d.marked = True
        op.deps = out

    def op(self, eng, fn, reads=(), writes=()):
        o = Op(eng, fn)
        writes = list(writes) + [x for x in reads if isinstance(x, str) and x.startswith("ps")]
        reads = [x for x in reads if not (isinstance(x, str) and x.startswith("ps"))]
        self._deps(o, [self.R(x) if isinstance(x, str) else x for x in reads],
                   [self.R(x) if isinstance(x, str) else x for x in writes])
        self.ops[eng].append(o)
        return o

    def dma(self, queue, fn, reads=(), writes=()):
        o = Op(queue, fn, is_dma=True)
        s = self.dma_rr
        self.dma_rr = (self.dma_rr + 1) % N_DMA_SEMS
        o.sem = s
        self.dma_cum[s] += 16
        o.semval = self.dma_cum[s]
        self._deps(o, [self.R(x) if isinstance(x, str) else x for x in reads],
                   [self.R(x) if isinstance(x, str) else x for x in writes])
        if self.dma_last[s] is not None:
            o.deps.append(self.dma_last[s])
        self.dma_last[s] = o
        self.ops[queue].append(o)
        return o

    def emit(self):
        nc = self.nc
        for e in ENGS:
            c = 0
            for o in self.ops[e]:
                if o.is_dma:
                    continue
                if o.marked:
                    c += 1
                o.count = c
        with contextlib.ExitStack() as st:
            esem = {e: st.enter_context(nc.semaphore("es_" + e)) for e in ENGS}
            dsem = [st.enter_context(nc.semaphore("ds_%d" % i)) for i in range(N_DMA_SEMS)]
            block = st.enter_context(nc.Block())
            handles = {"pe": block.tensor, "act": block.scalar, "dve": block.vector,
                       "pool": block.gpsimd, "sp": block.sync}
            for e in ENGS:
                ops = self.ops[e]

                def body(eh, ops=ops, e=e):
                    waited = {}
                    for o in ops:
                        for d in o.deps:
                            if d.is_dma:
                                key, sem, val = ("d", d.sem), dsem[d.sem], d.semval
                            else:
                                key, sem, val = ("e", d.eng), esem[d.eng], d.count
                            if waited.get(key, 0) >= val:
                                continue
                            waited[key] = val
                            eh.wait_ge(sem, val)
                        inst = o.fn(eh)
                        if o.is_dma:
                            inst.then_inc(dsem[o.sem], 16)
                        elif o.marked:
                            inst.then_inc(esem[e], 1)

                handles[e](body)


def Rec(name, *a, **k):
    return lambda e: getattr(e, name)(*a, **k)


def sap(base, dims):
    return AP(tensor=base.tensor, offset=base.offset, ap=[list(base.ap[0])] + [[int(a), int(b)] for a, b in dims])


def dap(t, offset, dims):
    return AP(tensor=t, offset=int(offset), ap=[[int(a), int(b)] for a, b in dims])


class _Stop(Exception):
    pass


def build(npass=8, depth=DEPTH, dbg=None, stop=None):
    nc = bass.Bass("TRN2", target_bir_lowering=False)
    P = Prog(nc)
    dt_in = lambda n, s: nc.dram_tensor(n, list(s), F32, kind="ExternalInput")
    dt_out = lambda n, s: nc.dram_tensor(n, list(s), F32, kind="ExternalOutput")
    xp = dt_in("xp", [SEQ, D])
    xs = dt_in("xs", [16, D])
    ck = dt_in("ck", [DEPTH, 4, 128, 128])
    cv = dt_in("cv", [DEPTH, 4, 128, 128])
    hr0 = dt_in("hr0", [DEPTH, 4, 64, 64])
    hi0 = dt_in("hi0", [DEPTH, 4, 64, 64])
    W = {}
    for n, s in [("norm_mix", [DEPTH, D]), ("w_in", [DEPTH, D, DIN]), ("attn_sink", [DEPTH, 16]),
                 ("ssm_a_re", [DEPTH, 64, 64]), ("ssm_a_im", [DEPTH, 64, 64]), ("ssm_log_dt", [DEPTH, 64]),
                 ("ssm_b_re", [DEPTH, 64, 64, 16]), ("ssm_b_im", [DEPTH, 64, 64, 16]),
                 ("ssm_c_re", [DEPTH, 64, 16, 64]), ("ssm_c_im", [DEPTH, 64, 16, 64]),
                 ("ssm_d", [DEPTH, 1024]), ("w_glu", [DEPTH, 1024, 1024]),
                 ("norm_attn_out", [DEPTH, 1024]), ("norm_ssm_out", [DEPTH, 1024]),
                 ("w_out", [DEPTH, D, D]), ("norm_ffn", [DEPTH, D]),
                 ("w_gate", [DEPTH, D, DFF]), ("w_up", [DEPTH, D, DFF]), ("w_down", [DEPTH, DFF, D]),
                 ("norm_final", [D])]:
        W[n] = dt_in(n, s)
    cst = dt_in("cst", [128, 1032])
    yp = dt_out("yp", [SEQ, D])
    ys = dt_out("ys", [16, D])
    okp = dt_out("okp", [DEPTH, 128, 128])
    ovp = dt_out("ovp", [DEPTH, 128, 128])
    ohrp = dt_out("ohrp", [DEPTH, 64, 64])
    ohip = dt_out("ohip", [DEPTH, 64, 64])
    oks = dt_out("oks", [DEPTH, 4, 128, 128])
    ovs = dt_out("ovs", [DEPTH, 4, 128, 128])
    ohrs = dt_out("ohrs", [DEPTH, 4, 64, 64])
    ohis = dt_out("ohis", [DEPTH, 4, 64, 64])
    scrT = nc.dram_tensor("scrT", [DEPTH, 128, 1088], F32, kind="Internal")
    scrR = nc.dram_tensor("scrR", [DEPTH * 4 * 3, 128, 520], F32, kind="Internal")
    scrM = nc.dram_tensor("scrM", [DEPTH * 32, 128, 512], BF16, kind="Internal")
    scrC = nc.dram_tensor("scrC", [DEPTH * 4, 128, 4096], BF16, kind="Internal")
    scrG = nc.dram_tensor("scrG", [DEPTH * 4, 128, 2048], BF16, kind="Internal")
    dbg_t = nc.dram_tensor("dbg", [128, 16 * NT], BF16, kind="ExternalOutput") if dbg else None

    st = contextlib.ExitStack()
    with st:
        st.enter_context(nc.allow_non_contiguous_dma(reason="small strided parameter loads"))
        sb = lambda n, s, d=F32: st.enter_context(nc.sbuf_tensor(n, list(s), d))
        X = sb("X", [128, 16, NT])
        Hb = sb("Hb", [128, 16, NT], BF16)
        Ao = sb("Ao", [128, 8, NT], BF16)
        FB = sb("FB", [128, 6 * NT])
        ACT_ = FB[:].bitcast(BF16).rearrange("p (a b) -> p a b", a=12)
        Zf = ACT_
        stage = FB[:, 0:D]
        rstd = sb("rstd", [128, NT])
        sqs = [sb("sq%d" % i, [128, NT], BF16) for i in range(2)]
        NSLOT = 4
        ring = [sb("ring%d" % i, [128, 16, 256], BF16) for i in range(NSLOT)]
        cs = sb("cs", [128, 1032])
        identb = sb("identb", [128, 128], BF16)
        onesb = sb("onesb", [128, 128], BF16)
        maskb = sb("maskb", [128, 256], BF16)
        maskfb = sb("maskfb", [128, 256], BF16)
        masksb = sb("masksb", [128, 132], BF16)
        gains = sb("gains", [128, DEPTH, 64])
        gfin = sb("gfin", [128, 16])
        sinkb = sb("sinkb", [128, DEPTH * 16])
        Kb = [sb("Kb%d" % i, [128, 128 + NP], BF16) for i in range(2)]
        KS = [sb("KS%d" % i, [128, 132], BF16) for i in range(2)]
        Vb = sb("Vb", [128, NB + 1, 4, 128], BF16)
        VSc = sb("VSc", [128, 4, 128], BF16)
        VSn = sb("VSn", [4, 4, 128], BF16)
        Khalo = sb("Khalo", [128, DEPTH, 2, 128], BF16)
        Vhalo = sb("Vhalo", [128, DEPTH, 4, 128], BF16)
        kvst = sb("kvst", [128, 256])
        kvss = kvst[0:4, :]
        cpy = kvst
        cks = sb("cks", [128, 128], BF16)
        PFB = sb("PFB", [128, 512])
        Pf = [PFB[:, 0:256], PFB[:, 256:512]]
        Pe = [PFB[:].bitcast(BF16)[:, i * 256:(i + 1) * 256] for i in range(3)]
        Pn = [sb("Pn%d" % i, [128, 256], BF16) for i in range(3)]
        PT = [sb("PT%d" % i, [128, 2, 128], BF16) for i in range(3)]
        sm = sb("sm", [128, 24])
        S1 = lambda n: sb(n, [128, 32])
        lam_r, lam_i, dtt, mag, th, abr, abi, t1, t2, t3, t4, crr, cii, rho, phi, iar, iai = [
            S1(n) for n in "lam_r lam_i dtt mag th abr abi t1 t2 t3 t4 crr cii rho phi iar iai".split()]
        ti = sb("ti", [128, 32], I32)
        Br = sb("Br", [128, 32, 16]); Bi = sb("Bi", [128, 32, 16])
        Bbr = Br; Bbi = Bi
        Cn = sb("Cn", [16, 16, 64])
        Cr = sb("Cr", [128, 32, 16]); Ci = sb("Ci", [128, 32, 16])
        PCr = sb("PCr", [128, 32, 16]); PCi = sb("PCi", [128, 32, 16])
        PBr = sb("PBr", [128, 32, 8]); PBi = sb("PBi", [128, 32, 8])
        Drow = sb("Drow", [128, 1024])
        Hr = sb("Hr", [128, DEPTH, 32]); Hi = sb("Hi", [128, DEPTH, 32])
        h0r = sb("h0r", [128, 32]); h0i = sb("h0i", [128, 32])
        Vu = sb("Vu", [128, 16, 8, 16], BF16)
        UgT = sb("UgT", [128, 16, NCH1], BF16)
        Ug = [UgT[:, i, :] for i in range(16)]
        MBt = [sb("MBt%d" % i, [128, 128], BF16) for i in range(2)]
        MBpT = sb("MBpT", [128, 2, 4, 128], BF16)
        TgSt = sb("TgSt", [128, 16, 128], BF16)
        MCBt = sb("MCBt", [128, 8, 2, 256], BF16)
        TgS = [TgSt[:, i, :] for i in range(16)]
        MCB = [[MCBt[:, i, r, :] for r in range(2)] for i in range(8)]
        tA = sb("tA", [128, 256]); tB = sb("tB", [128, 256])
        tmask = sb("tmask", [128, 128])
        NPB = 8
        Tc = sb("Tc", [128, NPB, 65]); Ts = sb("Ts", [128, NPB, 65])
        Dr = sb("Dr", [128, NPB, 65]); Di = sb("Di", [128, NPB, 65])
        d0 = sb("d0", [128, NPB, 65])
        ang = sb("ang", [128, NPB, 65])
        u1 = sb("u1", [128, NPB, 65]); u2 = sb("u2", [128, NPB, 65])
        Sbr = sb("Sbr", [128, NPB, 65], BF16); Sbi = sb("Sbi", [128, NPB, 65], BF16)
        Xs = sb("Xs", [128, 2, NPB])
        hend = sb("hend", [128, 2, 32])
        vd = sb("vd", [128, 4, 8, 16])
        ypre = sb("ypre", [128, 4, 8, 16])
        zcm = sb("zcm", [128, 8, 128], BF16)
        ps = [st.enter_context(nc.psum_tensor("ps%d" % i, [128, 512], F32)) for i in range(8)]
        RP = ["ps%d" % i for i in range(8)]
        STG = ["fb%d" % i for i in range(8)]

        def psb(b):
            return ps[b][:].bitcast(BF16)

        P.dma("sp", Rec("dma_start", out=cs[:], in_=cst.ap()), [], ["cs"])
        P.op("dve", Rec("tensor_copy", out=identb[:], in_=cs[:, 0:128]), ["cs"], ["identb"])
        P.op("dve", Rec("tensor_copy", out=maskb[:], in_=cs[:, 128:384]), ["cs"], ["maskb"])
        P.op("dve", Rec("tensor_copy", out=maskfb[:], in_=cs[:, 384:640]), ["cs"], ["maskfb"])
        P.op("dve", Rec("tensor_copy", out=masksb[:], in_=cs[:, 640:772]), ["cs"], ["masksb"])
        P.op("dve", Rec("tensor_copy", out=tmask[:], in_=cs[:, 772:900]), ["cs"], ["tmask"])
        P.op("pool", Rec("memset", onesb[:], 1.0), [], ["onesb"])
        identf = cs[:, 0:128]
        posf = cs[:, 900:965]
        for l in range(depth):
            for nm, off, nk in [("norm_mix", 0, 16), ("norm_ffn", 16, 16), ("norm_attn_out", 32, 8), ("norm_ssm_out", 40, 8)]:
                src = dap(W[nm], l * nk * 128, [[1, 128], [128, nk]])
                P.dma("sp", Rec("dma_start", out=gains[:, l, off:off + nk], in_=src), [], ["gains"])
        P.dma("sp", Rec("dma_start", out=gfin[:], in_=dap(W["norm_final"], 0, [[1, 128], [128, 16]])), [], ["gfin"])
        P.dma("sp", Rec("dma_start", out=sinkb[:], in_=dap(W["attn_sink"], 0, [[0, 128], [1, DEPTH * 16]])), [], ["sinkb"])
        P.op("pool", Rec("memset", Hr[:], 0.0), [], ["H"])
        P.op("pool", Rec("memset", Hi[:], 0.0), [], ["H"])
        P.op("pool", Rec("memset", Khalo[:], 0.0), [], ["Khalo"])
        P.op("pool", Rec("memset", Vhalo[:], 0.0), [], ["Vhalo"])
        P.op("pool", Rec("memset", Vb[:], 0.0), [], ["Vb"])
        P.op("pool", Rec("memset", VSc[:], 0.0), [], ["VSc"])
        P.op("pool", Rec("memset", VSn[:], 0.0), [], ["VSn"])
        P.op("pool", Rec("memset", Vu[:], 0.0), [], ["Vu"])
        P.op("pool", Rec("memset", MBpT[:], 0.0), [], ["MBp0", "MBp1"])

        sched = []
        for ps_i in range(npass):
            for l in range(depth):
                sched.append(("w_in", l, 0, 16, 1024, 256))
                for j in range(4):
                    sched.append(("w_in", l, 0, 16, j * 256, 256))
                for j in range(4):
                    sched.append(("w_in", l, 0, 16, 1280 + j * 256, 256))
                for j in range(4):
                    sched.append(("w_glu", l, 0, 8, j * 256, 256))
                for j in range(8):
                    sched.append(("w_out", l, 0, 16, j * 256, 256))
                for fp in range(4):
                    c0 = fp * 1536
                    nsl = 6 if fp < 3 else 4
                    for j in range(nsl):
                        sched.append(("w_gate", l, 0, 16, c0 + j * 256, 256))
                        sched.append(("w_up", l, 0, 16, c0 + j * 256, 256))
                    nk = nsl * 2
                    for j in range(8):
                        sched.append(("w_down", l, c0, nk, j * 256, 256))
        wcur = [0, 0]
        NCOLS = {"w_in": DIN, "w_glu": 1024, "w_out": D, "w_gate": DFF, "w_up": DFF, "w_down": D}
        NROWS = {"w_in": D, "w_glu": 1024, "w_out": D, "w_gate": D, "w_up": D, "w_down": DFF}

        def w_issue(upto):
            while wcur[1] < min(upto, len(sched)):
                i = wcur[1]
                nm, l, r0, nk, c0, ncl = sched[i]
                slot = i % NSLOT
                if nm == "w_krep":
                    for kvh in range(2):
                        for r_ in range(2):
                            src = dap(W["w_in"], l * D * DIN + 1024 + kvh * 64, [[DIN, 128], [128 * DIN, 16], [1, 64]])
                            c_ = kvh * 128 + r_ * 64
                            P.dma("pool", Rec("dma_start", out=ring[slot][:, :, c_:c_ + 64], in_=src),
                                  [], ["ring%d" % slot])
                    wcur[1] += 1
                    continue
                ncols = NCOLS[nm]
                src = dap(W[nm], l * NROWS[nm] * ncols + r0 * ncols + c0, [[ncols, 128], [128 * ncols, nk], [1, ncl]])
                P.dma("pool", Rec("dma_start", out=ring[slot][:, 0:nk, 0:ncl], in_=src),
                      [], ["ring%d" % slot])
                wcur[1] += 1

        def w_next(nm, l, c0):
            i = wcur[0]
            assert sched[i][0] == nm and sched[i][1] == l and sched[i][4] == c0, (sched[i], nm, l, c0)
            w_issue(i + NSLOT - 1)
            wcur[0] += 1
            return ring[i % NSLOT], "ring%d" % (i % NSLOT)

        pcnt = [0]
        HS = [True]

        def proj_chunk(slab, rslab, nk, mcol, rhs_fn, rhs_regs, evac_fn, lhs_rep=False):
            b = pcnt[0] % 2
            sbk = 2 + pcnt[0] % 2
            so = ((pcnt[0] // 2) % 8) * 4
            pcnt[0] += 1
            for k in range(nk):
                lhs = slab[:, k, mcol:mcol + 128]
                r = rhs_fn(k)
                P.op("pe", Rec("matmul", ps[b][:, 0:NP], lhsT=lhs, rhs=r[:, 0:NP], start=(k == 0), stop=(k == nk - 1)),
                     [rslab] + rhs_regs(k), [RP[b]])
                if HS[0]:
                    P.op("pe", Rec("matmul", ps[sbk][:, so:so + NS], lhsT=lhs, rhs=r[:, NP:NT], start=(k == 0), stop=(k == nk - 1)),
                         [rslab] + rhs_regs(k), [RP[sbk]])
            evac_fn(ps[b][:, 0:NP], ps[sbk][:, so:so + NS] if HS[0] else None, [RP[b]], [RP[sbk]])

        def rmsnorm(srcs, src_regs, gain_fn, dsts, dst_regs, dim):
            nk = len(srcs)
            for k in range(nk):
                q = sqs[k % 2]
                P.op("act", Rec("activation", out=q[:], in_=srcs[k], func=AF.Square), [src_regs[k]], ["sq%d" % (k % 2)])
                P.op("pe", Rec("matmul", ps[3][:, 0:NP], lhsT=onesb[:], rhs=q[:, 0:NP], start=(k == 0), stop=(k == nk - 1)),
                     ["sq%d" % (k % 2), "onesb"], [RP[3]])
                if HS[0]:
                    P.op("pe", Rec("matmul", ps[2][:, 480:480 + NS], lhsT=onesb[:], rhs=q[:, NP:NT], start=(k == 0), stop=(k == nk - 1)),
                         ["sq%d" % (k % 2), "onesb"], [RP[2]])
            P.op("act", Rec("activation", out=rstd[:, 0:NP], in_=ps[3][:, 0:NP], func=AF.Sqrt, scale=1.0 / dim, bias=epsb[:, 0:1]), [RP[3], "epsb"], ["rstd"])
            if HS[0]:
                P.op("act", Rec("activation", out=rstd[:, NP:NT], in_=ps[2][:, 480:480 + NS], func=AF.Sqrt, scale=1.0 / dim, bias=epsb[:, 0:1]), [RP[2], "epsb"], ["rstd"])
            P.op("dve", Rec("reciprocal", out=rstd[:], in_=rstd[:]), ["rstd"], ["rstd"])
            for k in range(nk):
                P.op("dve", Rec("scalar_tensor_tensor", out=dsts[k], in0=srcs[k], scalar=gain_fn(k), in1=rstd[:], op0=ALU.mult, op1=ALU.mult),
                     [src_regs[k], "rstd", "gains", "gfin"], [dst_regs[k]])

        epsb = sb("epsb", [128, 1])
        P.op("pool", Rec("memset", epsb[:], EPS), [], ["epsb"])
        halfpi = sb("halfpi", [128, 1])
        P.op("pool", Rec("memset", halfpi[:], math.pi / 2), [], ["halfpi"])

        def sincos(eng_v, angle, itmp, a1, a2, out_c, out_s, regs_in, rtag):
            P.op("dve", Rec("tensor_scalar", out=a1, in0=angle, scalar1=1.0 / TWO_PI, scalar2=None, op0=ALU.mult), regs_in, [rtag + "a1"])
            P.op("dve", Rec("tensor_copy", out=itmp, in_=a1), [rtag + "a1"], [rtag + "a2"])
            P.op("dve", Rec("tensor_copy", out=a1, in_=itmp), [rtag + "a2"], [rtag + "a1"])
            P.op("dve", Rec("scalar_tensor_tensor", out=angle, in0=a1, scalar=-TWO_PI, in1=angle, op0=ALU.mult, op1=ALU.add), [rtag + "a1"] + regs_in, regs_in)
            P.op("act", Rec("activation", out=a1, in_=angle, func=AF.Sin, scale=0.5), regs_in, [rtag + "a1"])
            P.op("act", Rec("activation", out=a2, in_=angle, func=AF.Sin, scale=0.5, bias=halfpi[:, 0:1]), regs_in + ["halfpi"], [rtag + "a2"])
            P.op("dve", Rec("scalar_tensor_tensor", out=out_s, in0=a1, scalar=2.0, in1=a2, op0=ALU.mult, op1=ALU.mult), [rtag + "a1", rtag + "a2"], [rtag + "s"])
            P.op("dve", Rec("tensor_tensor", out=a2, in0=a1, in1=a1, op=ALU.mult), [rtag + "a1"], [rtag + "a2"])
            P.op("dve", Rec("tensor_scalar", out=out_c, in0=a2, scalar1=-2.0, scalar2=1.0, op0=ALU.mult, op1=ALU.add), [rtag + "a2"], [rtag + "c"])

        def cmul(eng, o_r, o_i, a_r, a_i, b_r, b_i, tmp1, tmp2, rin, rout):
            P.op(eng, Rec("tensor_tensor", out=tmp1, in0=a_i, in1=b_i, op=ALU.mult), rin, rout)
            P.op(eng, Rec("tensor_tensor", out=tmp2, in0=a_i, in1=b_r, op=ALU.mult), rin, rout)
            P.op(eng, Rec("tensor_tensor", out=o_r, in0=a_r, in1=b_r, op=ALU.mult), rin, rout)
            P.op(eng, Rec("tensor_tensor", out=o_i, in0=a_r, in1=b_i, op=ALU.mult), rin, rout)
            P.op(eng, Rec("tensor_tensor", out=o_r, in0=o_r, in1=tmp1, op=ALU.subtract), rin, rout)
            P.op(eng, Rec("tensor_tensor", out=o_i, in0=o_i, in1=tmp2, op=ALU.add), rin, rout)

        def ckpt(n):
            if stop is not None and n >= stop:
                raise _Stop()

        try:
          for pi in range(npass):
              sidx = pi % 4
              HS[0] = pi < 4
              for blk in range(NB):
                  r0 = pi * NP + blk * 128
                  P.dma("sp", Rec("dma_start", out=stage[:], in_=dap(xp, r0 * D, [[D, 128], [1, D]])), [], STG)
                  for kq in range(4):
                      for kk in range(4):
                          k = kq * 4 + kk
                          P.op("pe", Rec("transpose", ps[4][:, kk * 128:(kk + 1) * 128], stage[:, k * 128:(k + 1) * 128], identf),
                               STG + ["cs"], [RP[4]])
                      P.op("dve", Rec("tensor_copy", out=X[:, kq * 4:kq * 4 + 4, blk * 128:(blk + 1) * 128],
                                                                      in_=ps[4][:].rearrange("p (a b) -> p a b", a=4)),
                           [RP[4]], ["X%d" % k for k in range(kq * 4, kq * 4 + 4)])
              P.dma("sp", Rec("dma_start", out=stage[0:4, :], in_=dap(xs, sidx * 4 * D, [[D, 4], [1, D]])), [], STG)
              for k in range(16 if HS[0] else 0):
                  P.op("pe", Rec("transpose", ps[4][:, k * 4:(k + 1) * 4], stage[0:4, k * 128:(k + 1) * 128], identf[0:4, 0:4]),
                       STG + ["cs"], [RP[4]])
              if HS[0]:
                  P.op("dve", Rec("tensor_copy", out=X[:, :, NP:NT], in_=ps[4][:, 0:64].rearrange("p (a b) -> p a b", a=16)),
                       [RP[4]], ["X%d" % k for k in range(16)])

              ckpt(1)
              for l in range(depth):
                  rmsnorm([X[:, k, :] for k in range(16)], ["X%d" % k for k in range(16)], lambda k, l=l: gains[:, l, k:k + 1],
                          [Hb[:, k, :] for k in range(16)], ["Hb%d" % k for k in range(16)], D)
                  hb_rhs = lambda k: Hb[:, k, :]
                  hb_regs = lambda k: ["Hb%d" % k]
                  ckpt(2)
                  for kvh in range(2):
                      P.op("pool", Rec("tensor_copy", out=Kb[kvh][:, 0:128], in_=Khalo[:, l, kvh, :]), ["Khalo"], ["Kb%d" % kvh])
                  P.op("pool", Rec("tensor_copy", out=Vb[:, 0, :, :], in_=Vhalo[:, l, :, :]), ["Vhalo"], ["Vb"])
                  ckpt(2.1)
                  slab, rslab = w_next("w_in", l, 1024)

                  def ev_k(pm, psm, rpm, rps):
                      for kvh in range(2):
                          hs = slice(kvh * 64, kvh * 64 + 64)
                          P.op("act", Rec("activation", out=Kb[kvh][hs, 128:128 + NP], in_=pm[hs, :], func=AF.Copy), rpm, ["Kb%d" % kvh])
                          if psm is not None:
                              P.op("act", Rec("activation", out=KS[kvh][hs, 128:132], in_=psm[hs, :], func=AF.Copy), rps, ["KS%d" % kvh])
                  proj_chunk(slab, rslab, 16, 0, hb_rhs, hb_regs, ev_k)
                  for kvh in range(2):
                      hs = slice(kvh * 64, kvh * 64 + 64)
                      ho = slice((1 - kvh) * 64, (1 - kvh) * 64 + 64)
                      P.dma("sp", Rec("dma_start", out=Kb[kvh][ho, 128:128 + NP], in_=Kb[kvh][hs, 128:128 + NP]), ["Kb%d" % kvh], ["Kb%d" % kvh])
                  ckpt(2.2)
                  import os as _os
                  for blk in range(NB + (1 if HS[0] else 0)):
                      if blk < NB:
                          cols = slice(blk * 128, (blk + 1) * 128); mrows = 128
                      else:
                          cols = slice(NP, NT); mrows = NS
                      kvb = 5 if blk % 2 == 0 else 4
                      for k in range(16):
                          P.op("pe", Rec("matmul", ps[kvb][0:mrows, 0:256], lhsT=Hb[:, k, cols], rhs=slab[:, k, 0:256], start=(k == 0), stop=(k == 15)),
                               [rslab, "Hb%d" % k], [RP[kvb]])
                      if blk < NB:
                          for kvh in range(2 - 2 * int(_os.environ.get("SKIP_A", "0"))):
                              for odd in range(2):
                                  P.op("dve", Rec("tensor_copy", out=Vb[:, blk + 1, kvh * 2 + odd, odd * 64:odd * 64 + 64], in_=ps[kvb][:, 128 + kvh * 64:128 + kvh * 64 + 64]),
                                       [RP[kvb]], ["Vb"])
                          if blk == NB - 1 and not int(_os.environ.get("SKIP_B", "0")):
                              P.op("act", Rec("activation", out=kvst[:], in_=ps[kvb][:, 0:256], func=AF.Copy), [RP[kvb]], ["kvst"])
                              P.dma("sp", Rec("dma_start", out=okp.ap()[l], in_=kvst[:, 0:128]), ["kvst"], ["okp"])
                              P.dma("sp", Rec("dma_start", out=ovp.ap()[l], in_=kvst[:, 128:256]), ["kvst"], ["ovp"])
                      else:
                          for kvh in range(2):
                              for odd in range(2):
                                  P.op("dve", Rec("tensor_copy", out=VSn[0:4, kvh * 2 + odd, odd * 64:odd * 64 + 64], in_=ps[kvb][0:4, 128 + kvh * 64:128 + kvh * 64 + 64]),
                                       [RP[kvb]], ["VSn"])
                          P.op("act", Rec("activation", out=kvss[:], in_=ps[kvb][0:4, 0:256], func=AF.Copy), [RP[kvb]], ["kvst"])
                          P.dma("sp", Rec("dma_start", out=oks.ap()[l, sidx, 124:128, :], in_=kvss[:, 0:128]), ["kvst"], ["oks"])
                          P.dma("sp", Rec("dma_start", out=ovs.ap()[l, sidx, 124:128, :], in_=kvss[:, 128:256]), ["kvst"], ["ovs"])
                          P.dma("sp", Rec("dma_start", out=cpy[0:124, 0:128], in_=ck.ap()[l, sidx, 4:128, :]), [], ["kvst"])
                          P.dma("sp", Rec("dma_start", out=cpy[0:124, 128:256], in_=cv.ap()[l, sidx, 4:128, :]), [], ["kvst"])
                          P.dma("sp", Rec("dma_start", out=oks.ap()[l, sidx, 0:124, :], in_=cpy[0:124, 0:128]), ["kvst"], ["oks"])
                          P.dma("sp", Rec("dma_start", out=ovs.ap()[l, sidx, 0:124, :], in_=cpy[0:124, 128:256]), ["kvst"], ["ovs"])
                  ckpt(2.3)
                  for kvh in range(2):
                      P.op("pool", Rec("tensor_copy", out=Khalo[:, l, kvh, :], in_=Kb[kvh][:, NP:NP + 128]), ["Kb%d" % kvh], ["Khalo"])
                  P.op("pool", Rec("tensor_copy", out=Vhalo[:, l, :, :], in_=Vb[:, NB, :, :]), ["Vb"], ["Vhalo"])
                  ckpt(2.4)
                  if HS[0]:
                      P.dma("pool", Rec("dma_start", out=cks[:, 0:128], in_=ck.ap()[l, sidx]), [], ["cks"])
                      P.op("pe", Rec("matmul", ps[5][:, 256:384], lhsT=cks[:, 0:128], rhs=identb[:], start=True, stop=True), ["cks", "identb"], [RP[5]])
                  for kvh in range(2 if HS[0] else 0):
                      hs = slice(kvh * 64, kvh * 64 + 64)
                      ho = slice((1 - kvh) * 64, (1 - kvh) * 64 + 64)
                      P.op("act", Rec("activation", out=KS[kvh][hs, 0:128], in_=ps[5][hs, 256:384], func=AF.Copy), [RP[5]], ["KS%d" % kvh])
                      P.dma("sp", Rec("dma_start", out=KS[kvh][ho, :], in_=KS[kvh][hs, :]), ["KS%d" % kvh], ["KS%d" % kvh])
                  for kvh in range(2 if HS[0] else 0):
                      for odd in range(2):
                          P.dma("pool", Rec("dma_start", out=VSc[:, kvh * 2 + odd, odd * 64:odd * 64 + 64], in_=cv.ap()[l, sidx, :, kvh * 64:kvh * 64 + 64]), [], ["VSc"])
                  ckpt(3)
                  for j in range(4):
                      slab, rslab = w_next("w_in", l, j * 256)
                      for mm in range(2):
                          m = j * 2 + mm
                          def ev_q(pm, psm, rpm, rps, m=m):
                              P.op("act", Rec("activation", out=Ao[:, m, 0:NP], in_=pm, func=AF.Copy, scale=0.125), rpm, ["Ao%d" % m])
                              if psm is not None:
                                  P.op("act", Rec("activation", out=Ao[:, m, NP:NT], in_=psm, func=AF.Copy, scale=0.125), rps, ["Ao%d" % m])
                          proj_chunk(slab, rslab, 16, mm * 128, hb_rhs, hb_regs, ev_q)
                  ckpt(4)
                  units = []
                  for nb in range(NB):
                      for m in range(8):
                          mk = maskfb if (pi == 0 and nb == 0) else maskb
                          for odd in range(2):
                              units.append((m, odd, 128, slice(nb * 128, (nb + 1) * 128), (lambda kvh, nb=nb: Kb[kvh][:, nb * 128:nb * 128 + 256]), 256, mk,
                                            [(128, (lambda v, nb=nb: Vb[:, nb, v, :])), (128, (lambda v, nb=nb: Vb[:, nb + 1, v, :]))]))
                  for m in range(8 if HS[0] else 0):
                      for odd in range(2):
                          units.append((m, odd, NS, slice(NP, NT), (lambda kvh: KS[kvh][:, 0:132]), 132, masksb,
                                        [(128, (lambda v: VSc[:, v, :])), (NS, (lambda v: VSn[:, v, :]))]))

                  def U_(i):
                      m, odd, nq, qcols, keyfn, nkeys, mk, segs = units[i]
                      h = 2 * m + odd
                      return m, odd, nq, qcols, keyfn, nkeys, mk, segs, h, h // 8, odd * 64, i % 3, 5 + i % 3, (i % 3) * 8

                  def sS(i):
                      m, odd, nq, qcols, keyfn, nkeys, mk, segs, h, kvh, hp, u, sbk, c0 = U_(i)
                      P.op("pe", Rec("matmul", ps[sbk][0:nq, 0:nkeys], lhsT=Ao[hp:hp + 64, m, qcols], rhs=keyfn(kvh)[hp:hp + 64, :], start=True, stop=False),
                           ["Ao%d" % m, "Kb%d" % kvh, "KS%d" % kvh], [RP[sbk]])
                      P.op("pe", Rec("matmul", ps[sbk][0:nq, 0:nkeys], lhsT=identb[0:nq, 0:nq], rhs=mk[0:nq, 0:nkeys], start=False, stop=True),
                           ["identb", "maskb", "maskfb", "masksb"], [RP[sbk]])

                  def sPre1(i):
                      m, odd, nq, qcols, keyfn, nkeys, mk, segs, h, kvh, hp, u, sbk, c0 = U_(i)
                      P.op("dve", Rec("reduce_max", out=sm[0:nq, c0:c0 + 1], in_=ps[sbk][0:nq, 0:nkeys], axis=AX.X), [RP[sbk]], ["sm%d" % u])

                  def sPre2(i, l=l):
                      m, odd, nq, qcols, keyfn, nkeys, mk, segs, h, kvh, hp, u, sbk, c0 = U_(i)
                      scol = sinkb[0:nq, l * 16 + h:l * 16 + h + 1]
                      P.op("dve", Rec("tensor_scalar", out=sm[0:nq, c0 + 1:c0 + 2], in0=sm[0:nq, c0:c0 + 1], scalar1=scol, scalar2=-1.0, op0=ALU.max, op1=ALU.mult),
                           ["sm%d" % u, "sinkb"], ["sm%d" % u])

                  def sExp(i, l=l):
                      m, odd, nq, qcols, keyfn, nkeys, mk, segs, h, kvh, hp, u, sbk, c0 = U_(i)
                      scol = sinkb[0:nq, l * 16 + h:l * 16 + h + 1]
                      P.op("act", Rec("activation", out=Pe[u][0:nq, 0:nkeys], in_=ps[sbk][0:nq, 0:nkeys], func=AF.Exp, bias=sm[0:nq, c0 + 1:c0 + 2], accum_out=sm[0:nq, c0 + 2:c0 + 3]),
                           [RP[sbk], "sm%d" % u], ["Pf%d" % u, "sm%d" % u])
                      P.op("act", Rec("activation", out=sm[0:nq, c0 + 3:c0 + 4], in_=sm[0:nq, c0 + 1:c0 + 2], func=AF.Exp, bias=scol),
                           ["sm%d" % u, "sinkb"], ["sm%d" % u])

                  def sPost1(i):
                      m, odd, nq, qcols, keyfn, nkeys, mk, segs, h, kvh, hp, u, sbk, c0 = U_(i)
                      P.op("dve", Rec("tensor_tensor", out=sm[0:nq, c0 + 4:c0 + 5], in0=sm[0:nq, c0 + 2:c0 + 3], in1=sm[0:nq, c0 + 3:c0 + 4], op=ALU.add), ["sm%d" % u], ["sm%d" % u])

                  def sPost2(i):
                      m, odd, nq, qcols, keyfn, nkeys, mk, segs, h, kvh, hp, u, sbk, c0 = U_(i)
                      P.op("dve", Rec("reciprocal", out=sm[0:nq, c0 + 5:c0 + 6], in_=sm[0:nq, c0 + 4:c0 + 5]), ["sm%d" % u], ["sm%d" % u])

                  def sPost3(i):
                      m, odd, nq, qcols, keyfn, nkeys, mk, segs, h, kvh, hp, u, sbk, c0 = U_(i)
                      P.op("dve", Rec("tensor_scalar", out=Pn[u][0:nq, 0:nkeys], in0=Pe[u][0:nq, 0:nkeys], scalar1=sm[0:nq, c0 + 5:c0 + 6], scalar2=None, op0=ALU.mult),
                           ["Pf%d" % u, "sm%d" % u], ["Pn%d" % u])

                  def sT(i):
                      m, odd, nq, qcols, keyfn, nkeys, mk, segs, h, kvh, hp, u, sbk, c0 = U_(i)
                      ptb = psb(i % 2)
                      ko = 0
                      for si, (nk_, vfn) in enumerate(segs):
                          P.op("pe", Rec("transpose", ptb[0:nk_, si * 128:si * 128 + nq], Pn[u][0:nq, ko:ko + nk_], identb[0:nq, 0:nq]),
                               ["Pn%d" % u, "identb"], [RP[i % 2]])
                          ko += nk_

                  def sEv(i):
                      m, odd, nq, qcols, keyfn, nkeys, mk, segs, h, kvh, hp, u, sbk, c0 = U_(i)
                      ptb = psb(i % 2)
                      for si, (nk_, vfn) in enumerate(segs):
                          P.op("act", Rec("activation", out=PT[u][0:nk_, si, 0:nq], in_=ptb[0:nk_, si * 128:si * 128 + nq], func=AF.Copy),
                               [RP[i % 2]], ["PT%d" % u])

                  def sPV(i):
                      m, odd, nq, qcols, keyfn, nkeys, mk, segs, h, kvh, hp, u, sbk, c0 = U_(i)
                      for si, (nk_, vfn) in enumerate(segs):
                          first = (odd == 0 and si == 0)
                          last = (odd == 1 and si == len(segs) - 1)
                          P.op("pe", Rec("matmul", ps[3][:, 0:nq], lhsT=vfn(kvh * 2 + odd)[0:nk_, :], rhs=PT[u][0:nk_, si, 0:nq], start=first, stop=last),
                               ["PT%d" % u, "Vb", "VSc", "VSn"], [RP[3]])

                  def sO(i):
                      m, odd, nq, qcols, keyfn, nkeys, mk, segs, h, kvh, hp, u, sbk, c0 = U_(i)
                      if odd == 1:
                          P.op("dve", Rec("tensor_copy", out=Ao[:, m, qcols], in_=ps[3][:, 0:nq]), [RP[3]], ["Ao%d" % m])

                  NU = len(units)
                  ok_ = lambda i: 0 <= i < NU
                  for t in range(NU + 8):
                      if ok_(t): sS(t)
                      if ok_(t - 1): sPre1(t - 1)
                      if ok_(t - 3): sPost1(t - 3)
                      if ok_(t - 1): sPre2(t - 1)
                      if ok_(t - 3): sPost2(t - 3)
                      if ok_(t - 7): sO(t - 7)
                      if ok_(t - 3): sPost3(t - 3)
                      if ok_(t - 2): sExp(t - 2)
                      if ok_(t - 4): sT(t - 4)
                      if ok_(t - 5): sEv(t - 5)
                      if ok_(t - 6): sPV(t - 6)

                  ckpt(5)
                  if pi == 0:
                      for nm, dst in [("ssm_a_re", lam_r), ("ssm_a_im", lam_i)]:
                          P.dma("sp", Rec("dma_start", out=dst[:], in_=dap(W[nm], l * 4096, [[1, 128], [128, 32]])), [], ["ssmp"])
                      for gl in range(2):
                          P.dma("sp", Rec("dma_start", out=dtt[gl * 64:(gl + 1) * 64, :], in_=dap(W["ssm_log_dt"], l * 64 + gl, [[0, 64], [2, 32]])), [], ["ssmp"])
                      for nm, dst in [("ssm_b_re", Br), ("ssm_b_im", Bi)]:
                          P.dma("sp", Rec("dma_start", out=dst[:], in_=dap(W[nm], l * 65536, [[16, 128], [2048, 32], [1, 16]])), [], ["ssmB"])
                      for nm, dst, rc in [("ssm_c_re", Cr, "Cr"), ("ssm_c_im", Ci, "Ci")]:
                          for jq in range(4):
                              P.dma("sp", Rec("dma_start", out=Cn[:], in_=dap(W[nm], l * 65536 + jq * 16384, [[64, 16], [1024, 16], [1, 64]])), [], ["Cn"])
                              for jj in range(8):
                                  P.op("pe", Rec("transpose", ps[4][:, jj * 16:(jj + 1) * 16], Cn[:, 2 * jj:2 * jj + 2, :], identf[0:16, 0:16]), ["Cn", "cs"], [RP[4]])
                              P.op("dve", Rec("tensor_copy", out=dst[:, jq * 8:(jq + 1) * 8, :], in_=ps[4][:, 0:128].rearrange("p (a b) -> p a b", a=8)), [RP[4]], ["ssmC"])
                  P.dma("sp", Rec("dma_start", out=Drow[:], in_=dap(W["ssm_d"], l * 1024, [[0, 128], [1, 1024]])), [], ["Drow"])
                  for src_t, dst in [(hr0, h0r), (hi0, h0i)]:
                      P.dma("sp", Rec("dma_start", out=dst[:], in_=dap(src_t, (l * 4 + sidx) * 4096, [[1, 128], [128, 32]])), [], ["h0"])
                  if pi == 0:
                      sp_ = ["ssmp"]
                      P.op("act", Rec("activation", out=dtt[:], in_=dtt[:], func=AF.Exp), sp_, sp_)
                      P.op("dve", Rec("tensor_tensor", out=t1[:], in0=lam_r[:], in1=dtt[:], op=ALU.mult), sp_, sp_)
                      P.op("act", Rec("activation", out=mag[:], in_=t1[:], func=AF.Exp), sp_, sp_)
                      P.op("act", Rec("activation", out=rho[:], in_=t1[:], func=AF.Exp, scale=8.0), sp_, sp_)
                      P.op("dve", Rec("tensor_tensor", out=th[:], in0=lam_i[:], in1=dtt[:], op=ALU.mult), sp_, sp_)
                      P.op("dve", Rec("tensor_scalar", out=phi[:], in0=th[:], scalar1=8.0, scalar2=None, op0=ALU.mult), sp_, sp_)
                      sincos("dve", th[:], t3[:].bitcast(I32), t2[:], t3[:], abr[:], abi[:], sp_, "tr1")
                      P.op("dve", Rec("tensor_tensor", out=abr[:], in0=abr[:], in1=mag[:], op=ALU.mult), sp_ + ["tr1c"], sp_)
                      P.op("dve", Rec("tensor_tensor", out=abi[:], in0=abi[:], in1=mag[:], op=ALU.mult), sp_ + ["tr1s"], sp_)
                      P.op("dve", Rec("tensor_scalar", out=t2[:], in0=phi[:], scalar1=1.0 / TWO_PI, scalar2=None, op0=ALU.mult), sp_ + ["tr1a1"], sp_ + ["tr1a1"])
                      P.op("dve", Rec("tensor_copy", out=ti[:], in_=t2[:]), sp_ + ["tr1a1", "tr1a2"], sp_ + ["tr1a2"])
                      P.op("dve", Rec("tensor_copy", out=t2[:], in_=ti[:]), sp_ + ["tr1a1", "tr1a2"], sp_ + ["tr1a1"])
                      P.op("dve", Rec("scalar_tensor_tensor", out=phi[:], in0=t2[:], scalar=-TWO_PI, in1=phi[:], op0=ALU.mult, op1=ALU.add), sp_ + ["tr1a1"], sp_)
                      P.op("dve", Rec("tensor_scalar", out=t1[:], in0=abr[:], scalar1=-1.0, scalar2=None, op0=ALU.add), sp_, sp_)
                      P.op("dve", Rec("tensor_tensor", out=t2[:], in0=lam_r[:], in1=lam_r[:], op=ALU.mult), sp_ + ["tr1a1"], sp_ + ["tr1a1"])
                      P.op("dve", Rec("tensor_tensor", out=t3[:], in0=lam_i[:], in1=lam_i[:], op=ALU.mult), sp_ + ["tr1a2"], sp_ + ["tr1a2"])
                      P.op("dve", Rec("tensor_tensor", out=t2[:], in0=t2[:], in1=t3[:], op=ALU.add), sp_ + ["tr1a1", "tr1a2"], sp_ + ["tr1a1"])
                      P.op("dve", Rec("reciprocal", out=t2[:], in_=t2[:]), sp_ + ["tr1a1"], sp_ + ["tr1a1"])
                      P.op("dve", Rec("tensor_tensor", out=crr[:], in0=t1[:], in1=lam_r[:], op=ALU.mult), sp_, sp_)
                      P.op("dve", Rec("tensor_tensor", out=t3[:], in0=abi[:], in1=lam_i[:], op=ALU.mult), sp_ + ["tr1a2"], sp_ + ["tr1a2"])
                      P.op("dve", Rec("tensor_tensor", out=crr[:], in0=crr[:], in1=t3[:], op=ALU.add), sp_ + ["tr1a2"], sp_)
                      P.op("dve", Rec("tensor_tensor", out=crr[:], in0=crr[:], in1=t2[:], op=ALU.mult), sp_ + ["tr1a1"], sp_)
                      P.op("dve", Rec("tensor_tensor", out=cii[:], in0=abi[:], in1=lam_r[:], op=ALU.mult), sp_, sp_)
                      P.op("dve", Rec("tensor_tensor", out=t3[:], in0=t1[:], in1=lam_i[:], op=ALU.mult), sp_ + ["tr1a2"], sp_ + ["tr1a2"])
                      P.op("dve", Rec("tensor_tensor", out=cii[:], in0=cii[:], in1=t3[:], op=ALU.subtract), sp_ + ["tr1a2"], sp_)
                      P.op("dve", Rec("tensor_tensor", out=cii[:], in0=cii[:], in1=t2[:], op=ALU.mult), sp_ + ["tr1a1"], sp_)
                      bc = lambda a: a[:].unsqueeze(2).broadcast_to([128, 32, 16])
                      cmul("dve", Bbr[:], Bbi[:], bc(crr), bc(cii), Br[:], Bi[:], PCr[:], PCi[:], sp_ + ["ssmB", "PC"], ["ssmB", "PC"])
                      P.op("dve", Rec("tensor_tensor", out=t1[:], in0=mag[:], in1=mag[:], op=ALU.mult), sp_, sp_)
                      P.op("dve", Rec("reciprocal", out=t1[:], in_=t1[:]), sp_, sp_)
                      P.op("dve", Rec("tensor_tensor", out=iar[:], in0=abr[:], in1=t1[:], op=ALU.mult), sp_, sp_)
                      P.op("dve", Rec("scalar_tensor_tensor", out=iai[:], in0=abi[:], scalar=-1.0, in1=t1[:], op0=ALU.mult, op1=ALU.mult), sp_, sp_)
                      pr = ["PC", "ssmp"]
                      P.op("pool", Rec("memset", PCr[:, :, 7:8], 1.0), pr, pr)
                      P.op("pool", Rec("memset", PCi[:, :, 7:8], 0.0), pr, pr)
                      for kk in range(8, 16):
                          cmul("dve", PCr[:, :, kk], PCi[:, :, kk], PCr[:, :, kk - 1], PCi[:, :, kk - 1], abr[:], abi[:], t2[:], t3[:], pr + ["tr1a1", "tr1a2"], pr + ["tr1a1", "tr1a2"])
                      for kk in range(6, -1, -1):
                          cmul("dve", PCr[:, :, kk], PCi[:, :, kk], PCr[:, :, kk + 1], PCi[:, :, kk + 1], iar[:], iai[:], t2[:], t3[:], pr + ["tr1a1", "tr1a2"], pr + ["tr1a1", "tr1a2"])
                      for i in range(8):
                          P.op("pool", Rec("tensor_copy", out=PBr[:, :, i], in_=PCr[:, :, 14 - i]), pr, ["PB"])
                          P.op("pool", Rec("tensor_copy", out=PBi[:, :, i], in_=PCi[:, :, 14 - i]), pr, ["PB"])
                  if pi == 0:
                      for t_, o_, n_ in [(PCr, 0, 512), (PCi, 512, 512)]:
                          P.dma("sp", Rec("dma_start", out=dap(scrT, l * 128 * 1088 + o_, [[1088, 128], [1, n_]]), in_=t_[:].rearrange("p a b -> p (a b)")), ["PC"], ["scrT%d" % l])
                      for t_, o_ in [(rho, 1024), (phi, 1056)]:
                          P.dma("sp", Rec("dma_start", out=dap(scrT, l * 128 * 1088 + o_, [[1088, 128], [1, 32]]), in_=t_[:]), ["ssmp"], ["scrT%d" % l])
                  else:
                      for t_, o_, n_ in [(PCr, 0, 512), (PCi, 512, 512)]:
                          P.dma("sp", Rec("dma_start", out=t_[:].rearrange("p a b -> p (a b)"), in_=dap(scrT, l * 128 * 1088 + o_, [[1088, 128], [1, n_]])), ["scrT%d" % l], ["PC"])
                      for t_, o_ in [(rho, 1024), (phi, 1056)]:
                          P.dma("sp", Rec("dma_start", out=t_[:], in_=dap(scrT, l * 128 * 1088 + o_, [[1088, 128], [1, 32]])), ["scrT%d" % l], ["ssmp"])
                  cmul("dve", hend[:, 0, :], hend[:, 1, :], PCr[:, :, 11], PCi[:, :, 11], h0r[:], h0i[:], t2[:], t3[:], pr + ["h0", "hend", "tr1a1", "tr1a2"], ["hend", "tr1a1", "tr1a2"])

                  ckpt(6)
                  for sl in range(4):
                      slab, rslab = w_next("w_in", l, 1280 + sl * 256)
                      for i in range(8):
                          nrow = NCH1 if i < 4 else NCHK
                          for k in range(16):
                              lhs = sap(Hb[:, k, i:i + 1], [[8, nrow]])
                              P.op("pe", Rec("matmul", ps[i % 2][0:nrow, 0:256], lhsT=lhs, rhs=slab[:, k, 0:256], start=(k == 0), stop=(k == 15)),
                                   [rslab, "Hb%d" % k], [RP[i % 2]])
                          P.op("act", Rec("activation", out=Vu[0:nrow, :, i, :], in_=ps[i % 2][0:nrow, 0:256].rearrange("p (g c) -> p g c", g=16), func=AF.Copy), [RP[i % 2]], ["Vu"])
                      for gq in range(4):
                          ubk = 4 if gq % 2 == 0 else 5
                          ub = psb(ubk)
                          for gg in range(4):
                              g = gq * 4 + gg
                              P.op("pe", Rec("transpose", ub[:, gg * 128:gg * 128 + NCH1], Vu[0:NCH1, g, :, :], identb[0:NCH1, 0:NCH1]),
                                   ["Vu", "identb"], [RP[ubk]])
                          P.op("dve", Rec("tensor_copy", out=UgT[:, gq * 4:(gq + 1) * 4, :], in_=ub[:, 0:512].rearrange("p (g n) -> p g n", g=4)[:, :, 0:NCH1]),
                               [RP[ubk]], ["Ug%d" % g_ for g_ in range(gq * 4, gq * 4 + 4)])
                      j0 = sl * NPB
                      bcp = lambda a: a[:, j0:j0 + NPB].unsqueeze(2).broadcast_to([128, NPB, 65])
                      posb = posf.unsqueeze(1).broadcast_to([128, NPB, 65])
                      treg = "scrR_%d_%d" % (l, sl)
                      tsrc = [dap(scrR, ((l * 4 + sl) * 3 + q_) * 128 * 520, [[520, 128], [1, 520]]) for q_ in range(3)]
                      flt = lambda a: a[:].rearrange("p a b -> p (a b)")
                      if pi == 0:
                          P.op("dve", Rec("tensor_tensor", out=ang[:], in0=bcp(phi), in1=posb, op=ALU.mult), ["ssmp", "cs", "trb"], ["trb"])
                          sincos("dve", ang[:], u2[:].bitcast(I32), u1[:], u2[:], Tc[:], Ts[:], ["trb"], "tr2")
                          P.op("dve", Rec("tensor_tensor", out=d0[:], in0=bcp(rho), in1=cs[:, 965:1030].unsqueeze(1).broadcast_to([128, NPB, 65]), op=ALU.mult), ["ssmp", "cs"], ["d0"])
                          P.dma("sp", Rec("dma_start", out=tsrc[0], in_=flt(Tc)), ["tr2c"], [treg])
                          P.dma("sp", Rec("dma_start", out=tsrc[1], in_=flt(Ts)), ["tr2s"], [treg])
                          P.dma("sp", Rec("dma_start", out=tsrc[2], in_=flt(d0)), ["d0"], [treg])
                      else:
                          P.dma("sp", Rec("dma_start", out=flt(Tc), in_=tsrc[0]), [treg], ["tr2c"])
                          P.dma("sp", Rec("dma_start", out=flt(Ts), in_=tsrc[1]), [treg], ["tr2s"])
                          P.dma("sp", Rec("dma_start", out=flt(d0), in_=tsrc[2]), [treg], ["d0"])
                      creg = "scrC_%d_%d" % (l, sl)
                      csrc = dap(scrC, (l * 4 + sl) * 128 * 4096, [[4096, 128], [1, 4096]])
                      gsrc = dap(scrG, (l * 4 + sl) * 128 * 2048, [[2048, 128], [1, 2048]])
                      if pi > 0:
                          P.dma("sp", Rec("dma_start", out=MCBt[:].rearrange("p a b c -> p (a b c)"), in_=csrc), [creg], ["MCm"])
                          P.dma("sp", Rec("dma_start", out=TgSt[:].rearrange("p a b -> p (a b)"), in_=gsrc), [creg], ["TgS"])
                      for jj in range(NPB):
                          j = j0 + jj
                          MBp = [MBpT[:, jj % 2, q_, :] for q_ in range(4)]
                          if pi == 0:
                              rsp = ["ssmB", "PB", "PC", "ssmC", "ssmp"]
                              pb_r = PBr[:, j, :].unsqueeze(2).broadcast_to([128, 8, 16]); pb_i = PBi[:, j, :].unsqueeze(2).broadcast_to([128, 8, 16])
                              bb_r = Bbr[:, j, :].unsqueeze(1).broadcast_to([128, 8, 16]); bb_i = Bbi[:, j, :].unsqueeze(1).broadcast_to([128, 8, 16])
                              v3 = lambda a, n: a[:, 0:n * 16].rearrange("p (a b) -> p a b", b=16)
                              P.op("pool", Rec("tensor_tensor", out=v3(tA, 8), in0=pb_i, in1=bb_i, op=ALU.mult), rsp + ["tA"], ["tA"])
                              P.op("pool", Rec("tensor_tensor", out=v3(tB, 8), in0=pb_r, in1=bb_r, op=ALU.mult), rsp + ["tB"], ["tB"])
                              P.op("pool", Rec("tensor_tensor", out=v3(MBt[0], 8), in0=v3(tB, 8), in1=v3(tA, 8), op=ALU.subtract), ["tA", "tB"], ["MBt"])
                              P.op("pool", Rec("tensor_tensor", out=v3(tA, 8), in0=pb_r, in1=bb_i, op=ALU.mult), rsp + ["tA"], ["tA"])
                              P.op("pool", Rec("tensor_tensor", out=v3(tB, 8), in0=pb_i, in1=bb_r, op=ALU.mult), rsp + ["tB"], ["tB"])
                              P.op("pool", Rec("tensor_tensor", out=v3(MBt[1], 8), in0=v3(tA, 8), in1=v3(tB, 8), op=ALU.add), ["tA", "tB"], ["MBt"])
                              pc_r = PCr[:, j, :].unsqueeze(2).broadcast_to([128, 16, 16]); pc_i = PCi[:, j, :].unsqueeze(2).broadcast_to([128, 16, 16])
                              cc_r = Cr[:, j, :].unsqueeze(1).broadcast_to([128, 16, 16]); cc_i = Ci[:, j, :].unsqueeze(1).broadcast_to([128, 16, 16])
                              P.op("dve", Rec("tensor_tensor", out=v3(tA, 16), in0=pc_i, in1=cc_i, op=ALU.mult), rsp + ["tA"], ["tA"])
                              P.op("dve", Rec("tensor_tensor", out=v3(tB, 16), in0=pc_r, in1=cc_r, op=ALU.mult), rsp + ["tB"], ["tB"])
                              MCm = MCB[jj]
                              P.op("dve", Rec("tensor_tensor", out=v3(MCm[0], 16), in0=v3(tB, 16), in1=v3(tA, 16), op=ALU.subtract), ["tA", "tB"], ["MCm"])
                              P.op("dve", Rec("tensor_tensor", out=v3(tA, 16), in0=pc_r, in1=cc_i, op=ALU.mult), rsp + ["tA"], ["tA"])
                              P.op("dve", Rec("tensor_tensor", out=v3(tB, 16), in0=pc_i, in1=cc_r, op=ALU.mult), rsp + ["tB"], ["tB"])
                              P.op("dve", Rec("scalar_tensor_tensor", out=v3(MCm[1], 16), in0=v3(tA, 16), scalar=-1.0, in1=v3(tB, 16), op0=ALU.mult, op1=ALU.subtract), ["tA", "tB"], ["MCm"])
                              tb = psb(5)
                              for ri in range(2):
                                  P.op("pe", Rec("transpose", tb[:, ri * 128:(ri + 1) * 128], MBt[ri][:], identb[:]), ["MBt", "identb"], [RP[5]])
                              for gl in range(2):
                                  for ri in range(2):
                                      P.op("act", Rec("activation", out=MBp[gl * 2 + ri][:, gl * 64:gl * 64 + 64], in_=tb[:, ri * 128 + gl * 64:ri * 128 + gl * 64 + 64], func=AF.Copy),
                                           [RP[5]], ["MBp%d" % (jj % 2)])
                              for gl in range(2):
                                  sl_ = slice(gl * 64, gl * 64 + 64)
                                  P.op("pe", Rec("matmul", ps[3][:, gl * 128:(gl + 1) * 128], lhsT=MBt[0][sl_, :], rhs=MCm[0][sl_, 0:128], start=True, stop=False), ["MBt", "MCm"], [RP[3]])
                                  P.op("pe", Rec("matmul", ps[3][:, gl * 128:(gl + 1) * 128], lhsT=MBt[1][sl_, :], rhs=MCm[1][sl_, 0:128], start=False, stop=True), ["MBt", "MCm"], [RP[3]])
                                  P.op("dve", Rec("tensor_tensor", out=TgS[jj * 2 + gl][:], in0=ps[3][:, gl * 128:(gl + 1) * 128], in1=tmask[:], op=ALU.mult), [RP[3], "tmask"], ["TgS"])
                          mreg = "scrM_%d_%d" % (l, j)
                          msrc = dap(scrM, (l * 32 + j) * 128 * 512, [[512, 128], [1, 512]])
                          if pi == 0:
                              P.dma("sp", Rec("dma_start", out=msrc, in_=MBpT[:, jj % 2].rearrange("p a b -> p (a b)")), ["MBp%d" % (jj % 2)], [mreg])
                          else:
                              P.dma("sp", Rec("dma_start", out=MBpT[:, jj % 2].rearrange("p a b -> p (a b)"), in_=msrc), [mreg], ["MBp%d" % (jj % 2)])
                          xbk = 6 if jj % 2 == 0 else 4
                          for ri in range(2):
                              for gl in range(2):
                                  g = jj * 2 + gl
                                  P.op("pe", Rec("matmul", ps[xbk][:, ri * 128:ri * 128 + NCH1], lhsT=MBp[gl * 2 + ri][:], rhs=Ug[g][:], start=(gl == 0), stop=(gl == 1)),
                                       ["MBp%d" % (jj % 2), "Ug%d" % g], [RP[xbk]])
                          xr = ps[xbk][:, 0:NCHK]; xi = ps[xbk][:, 128:128 + NCHK]
                          tcj = Tc[:, jj, 1:65]; tsj = Ts[:, jj, 1:65]
                          rdm = [RP[xbk], "tr2c", "tr2s"]
                          P.op("dve", Rec("tensor_tensor", out=Dr[:, jj, 1:65], in0=xr, in1=tcj, op=ALU.mult), rdm, ["Dm"])
                          P.op("dve", Rec("tensor_tensor", out=u1[:, jj, 1:65], in0=xi, in1=tsj, op=ALU.mult), rdm + ["tr2a1"], ["tr2a1"])
                          P.op("dve", Rec("tensor_tensor", out=Di[:, jj, 1:65], in0=xi, in1=tcj, op=ALU.mult), rdm, ["Dm"])
                          P.op("dve", Rec("tensor_tensor", out=u2[:, jj, 1:65], in0=xr, in1=tsj, op=ALU.mult), rdm + ["tr2a2"], ["tr2a2"])
                          P.op("act", Rec("activation", out=Xs[:, 0, jj:jj + 1], in_=ps[xbk][:, NCHK:NCHK + 1], func=AF.Copy), [RP[xbk]], ["Xs"])
                          P.op("act", Rec("activation", out=Xs[:, 1, jj:jj + 1], in_=ps[xbk][:, 128 + NCHK:128 + NCHK + 1], func=AF.Copy), [RP[xbk]], ["Xs"])
                      if pi == 0:
                          P.dma("sp", Rec("dma_start", out=csrc, in_=MCBt[:].rearrange("p a b c -> p (a b c)")), ["MCm"], [creg])
                          P.dma("sp", Rec("dma_start", out=gsrc, in_=TgSt[:].rearrange("p a b -> p (a b)")), ["TgS"], [creg])
                      dm = ["Dm", "tr2a1", "tr2a2"]
                      P.op("dve", Rec("tensor_tensor", out=Dr[:, :, 1:65], in0=Dr[:, :, 1:65], in1=u1[:, :, 1:65], op=ALU.add), dm, ["Dm"])
                      P.op("dve", Rec("tensor_tensor", out=Di[:, :, 1:65], in0=Di[:, :, 1:65], in1=u2[:, :, 1:65], op=ALU.subtract), dm, ["Dm"])
                      P.op("dve", Rec("tensor_copy", out=Dr[:, :, 0], in_=Hr[:, l, j0:j0 + NPB]), ["H", "Dm"], ["Dm"])
                      P.op("dve", Rec("tensor_copy", out=Di[:, :, 0], in_=Hi[:, l, j0:j0 + NPB]), ["H", "Dm"], ["Dm"])
                      fl = lambda a: a[:].rearrange("p a b -> p (a b)")
                      P.op("dve", Rec("tensor_tensor_scan", out=fl(Dr), data0=fl(d0), data1=fl(Dr), initial=0.0, op0=ALU.mult, op1=ALU.add), ["Dm", "d0"], ["Dm"])
                      P.op("dve", Rec("tensor_tensor_scan", out=fl(Di), data0=fl(d0), data1=fl(Di), initial=0.0, op0=ALU.mult, op1=ALU.add), ["Dm", "d0"], ["Dm"])
                      md = ["Dm", "tr2c", "tr2s", "tr2a1", "tr2a2", "trb"]
                      P.op("dve", Rec("tensor_tensor", out=u1[:], in0=Dr[:], in1=Tc[:], op=ALU.mult), md, ["tr2a1"])
                      P.op("dve", Rec("tensor_tensor", out=u2[:], in0=Di[:], in1=Ts[:], op=ALU.mult), md, ["tr2a2"])
                      P.op("dve", Rec("tensor_tensor", out=u1[:], in0=u1[:], in1=u2[:], op=ALU.subtract), md, ["tr2a1"])
                      P.op("dve", Rec("tensor_tensor", out=u2[:], in0=Dr[:], in1=Ts[:], op=ALU.mult), md, ["tr2a2"])
                      P.op("dve", Rec("tensor_tensor", out=ang[:], in0=Di[:], in1=Tc[:], op=ALU.mult), md, ["trb"])
                      P.op("dve", Rec("tensor_tensor", out=u2[:], in0=u2[:], in1=ang[:], op=ALU.add), md, ["tr2a2"])
                      P.op("act", Rec("activation", out=Sbr[:, :, 0:64], in_=u1[:, :, 0:64], func=AF.Copy), ["tr2a1"], ["Sb"])
                      P.op("act", Rec("activation", out=Sbi[:, :, 0:64], in_=u2[:, :, 0:64], func=AF.Copy), ["tr2a2"], ["Sb"])
                      P.op("act", Rec("activation", out=Sbr[:, :, 64], in_=h0r[:, j0:j0 + NPB], func=AF.Copy), ["h0"], ["Sb"])
                      P.op("act", Rec("activation", out=Sbi[:, :, 64], in_=h0i[:, j0:j0 + NPB], func=AF.Copy), ["h0"], ["Sb"])
                      P.op("dve", Rec("tensor_copy", out=Hr[:, l, j0:j0 + NPB], in_=u1[:, :, 64]), ["tr2a1"], ["H"])
                      P.op("dve", Rec("tensor_copy", out=Hi[:, l, j0:j0 + NPB], in_=u2[:, :, 64]), ["tr2a2"], ["H"])
                      cmul("dve", tA[:, 0:NPB], tA[:, NPB:2 * NPB], PCr[:, j0:j0 + NPB, 3], PCi[:, j0:j0 + NPB, 3], Xs[:, 0, :], Xs[:, 1, :], tB[:, 0:NPB], tB[:, NPB:2 * NPB],
                           ["PC", "Xs", "tA", "tB"], ["tA", "tB"])
                      P.op("dve", Rec("tensor_tensor", out=hend[:, 0, j0:j0 + NPB], in0=hend[:, 0, j0:j0 + NPB], in1=tA[:, 0:NPB], op=ALU.add), ["tA", "hend"], ["hend"])
                      P.op("dve", Rec("tensor_tensor", out=hend[:, 1, j0:j0 + NPB], in0=hend[:, 1, j0:j0 + NPB], in1=tA[:, NPB:2 * NPB], op=ALU.add), ["tA", "hend"], ["hend"])
                      for gq in range(4):
                          ybk = 7 if gq % 2 == 0 else 2
                          for gg in range(4):
                              g = gq * 4 + gg
                              jj = g // 2; gl = g % 2
                              sl_ = slice(gl * 64, gl * 64 + 64)
                              oc = slice(gg * 128, (gg + 1) * 128)
                              P.op("pe", Rec("matmul", ps[ybk][0:NCH1, oc], lhsT=Ug[g][:], rhs=TgS[g][:], start=True, stop=False), ["Ug%d" % g, "TgS"], [RP[ybk]])
                              P.op("pe", Rec("matmul", ps[ybk][0:NCH1, oc], lhsT=Sbr[sl_, jj, :], rhs=MCB[jj][0][sl_, 128:256], start=False, stop=False), ["Sb", "MCm"], [RP[ybk]])
                              P.op("pe", Rec("matmul", ps[ybk][0:NCH1, oc], lhsT=Sbi[sl_, jj, :], rhs=MCB[jj][1][sl_, 128:256], start=False, stop=True), ["Sb", "MCm"], [RP[ybk]])
                          ch0 = sl * 256 + gq * 64
                          P.op("dve", Rec("tensor_tensor", out=vd[0:NCH1], in0=Vu[0:NCH1, gq * 4:(gq + 1) * 4, :, :], in1=sap(Drow[0:NCH1, ch0:ch0 + 1], [[16, 4], [0, 8], [1, 16]]), op=ALU.mult),
                               ["Vu", "Drow"], ["vd"])
                          P.op("dve", Rec("tensor_tensor", out=ypre[0:NCH1], in0=ps[ybk][0:NCH1, :].rearrange("p (g i c) -> p g i c", g=4, i=8),
                                                                in1=vd[0:NCH1], op=ALU.add), [RP[ybk], "vd"], ["ypre"])
                          half = gq % 2
                          P.op("act", Rec("activation", out=zcm[0:NCH1, :, half * 64:(half + 1) * 64].rearrange("p i (g c) -> p g i c", g=4), in_=ypre[0:NCH1], func=AF.Gelu), ["ypre"], ["zcm"])
                          if half == 1:
                              mz = sl * 2 + gq // 2
                              zb = psb(5)
                              for i in range(8):
                                  P.op("pe", Rec("transpose", zb[:, i * 128:i * 128 + NCH1], zcm[0:NCH1, i, :], identb[0:NCH1, 0:NCH1]), ["zcm", "identb"], [RP[5]])
                              zv = zb[:, 0:1024].rearrange("p (i n) -> p i n", i=8)
                              P.op("dve", Rec("tensor_copy", out=sap(Zf[:, mz, 0:1], [[1, 4], [8, NCH1]]), in_=zv[:, 0:4, 0:NCH1]), [RP[5]], ["fb%d" % mz])
                              P.op("dve", Rec("tensor_copy", out=sap(Zf[:, mz, 4:5], [[1, 4], [8, NCHK]]), in_=zv[:, 4:8, 0:NCHK]), [RP[5]], ["fb%d" % mz])
                  ckpt(7)
                  for ri, (dst_p, dst_s, Hx) in enumerate([(ohrp, ohrs, Hr), (ohip, ohis, Hi)]):
                      P.dma("sp", Rec("dma_start", out=dap(dst_p, l * 4096, [[1, 128], [128, 32]]), in_=Hx[:, l, :]), ["H"], ["ohp%d" % ri])
                      if HS[0]:
                          P.dma("sp", Rec("dma_start", out=dap(dst_s, (l * 4 + sidx) * 4096, [[1, 128], [128, 32]]), in_=hend[:, ri, :]), ["hend"], ["ohs%d" % ri])
                  for j in range(4):
                      slab, rslab = w_next("w_glu", l, j * 256)
                      for mm in range(2):
                          m = j * 2 + mm
                          def ev_g(pm, psm, rpm, rps, m=m):
                              P.op("act", Rec("activation", out=Pf[0][:, 0:256], in_=pm[:, 0:256], func=AF.Sigmoid), rpm, ["Pf0", "Pf1"])
                              P.op("act", Rec("activation", out=Pf[1][:, 0:256], in_=pm[:, 256:512], func=AF.Sigmoid), rpm, ["Pf2"])
                              P.op("dve", Rec("tensor_tensor", out=Hb[:, 8 + m, 0:256], in0=Pf[0][:, 0:256], in1=Zf[:, m, 0:256], op=ALU.mult), ["Pf0", "Pf1", "fb%d" % m], ["Hb%d" % (8 + m)])
                              P.op("dve", Rec("tensor_tensor", out=Hb[:, 8 + m, 256:512], in0=Pf[1][:, 0:256], in1=Zf[:, m, 256:512], op=ALU.mult), ["Pf2", "fb%d" % m], ["Hb%d" % (8 + m)])
                              if psm is not None:
                                  P.op("act", Rec("activation", out=sm[:, 0:NS], in_=psm, func=AF.Sigmoid), rps, ["sm0", "sm1"])
                              P.op("dve", Rec("tensor_tensor", out=Hb[:, 8 + m, NP:NT], in0=sm[:, 0:NS], in1=Zf[:, m, NP:NT], op=ALU.mult), ["sm0", "sm1", "fb%d" % m], ["Hb%d" % (8 + m)])
                          proj_chunk(slab, rslab, 8, mm * 128, lambda k: Zf[:, k, :], lambda k: ["fb%d" % k], ev_g)
                  ckpt(8)
                  rmsnorm([Ao[:, k, :] for k in range(8)], ["Ao%d" % k for k in range(8)], lambda k, l=l: gains[:, l, 32 + k:33 + k],
                          [Ao[:, k, :] for k in range(8)], ["Ao%d" % k for k in range(8)], 1024)
                  rmsnorm([Hb[:, 8 + k, :] for k in range(8)], ["Hb%d" % (8 + k) for k in range(8)], lambda k, l=l: gains[:, l, 40 + k:41 + k],
                          [Hb[:, 8 + k, :] for k in range(8)], ["Hb%d" % (8 + k) for k in range(8)], 1024)
                  mix_rhs = lambda k: (Ao[:, k, :] if k < 8 else Hb[:, k, :])
                  mix_regs = lambda k: ["Ao%d" % k] if k < 8 else ["Hb%d" % k]
                  for j in range(8):
                      slab, rslab = w_next("w_out", l, j * 256)
                      for mm in range(2):
                          m = j * 2 + mm
                          def ev_o(pm, psm, rpm, rps, m=m):
                              P.op("dve", Rec("tensor_tensor", out=X[:, m, 0:NP], in0=pm, in1=X[:, m, 0:NP], op=ALU.add), rpm + ["X%d" % m], ["X%d" % m])
                              if psm is not None:
                                  P.op("dve", Rec("tensor_tensor", out=X[:, m, NP:NT], in0=psm, in1=X[:, m, NP:NT], op=ALU.add), rps + ["X%d" % m], ["X%d" % m])
                          proj_chunk(slab, rslab, 16, mm * 128, mix_rhs, mix_regs, ev_o)
                  ckpt(9)
                  rmsnorm([X[:, k, :] for k in range(16)], ["X%d" % k for k in range(16)], lambda k, l=l: gains[:, l, 16 + k:17 + k],
                          [Hb[:, k, :] for k in range(16)], ["Hb%d" % k for k in range(16)], D)
                  for fp in range(0 if int(_os.environ.get("SKIPFFN", "0")) else 4):
                      c0 = fp * 1536
                      nsl = 6 if fp < 3 else 4
                      for j in range(nsl):
                          slg, rslg = w_next("w_gate", l, c0 + j * 256)
                          slu, rslu = w_next("w_up", l, c0 + j * 256)
                          for mm in range(2):
                              ma = j * 2 + mm
                              def ev_gate(pm, psm, rpm, rps, ma=ma):
                                  P.op("act", Rec("activation", out=Pf[0][:, 0:256], in_=pm[:, 0:256], func=AF.Silu), rpm, ["Pf0", "Pf1"])
                                  P.op("act", Rec("activation", out=Pf[1][:, 0:256], in_=pm[:, 256:512], func=AF.Silu), rpm, ["Pf2"])
                                  if psm is not None:
                                      P.op("act", Rec("activation", out=sm[:, 0:NS], in_=psm, func=AF.Silu), rps, ["sm0", "sm1"])
                              def ev_up(pm, psm, rpm, rps, ma=ma):
                                  P.op("dve", Rec("tensor_tensor", out=ACT_[:, ma, 0:256], in0=pm[:, 0:256], in1=Pf[0][:, 0:256], op=ALU.mult), rpm + ["Pf0", "Pf1"], ["fb%d" % ma])
                                  P.op("dve", Rec("tensor_tensor", out=ACT_[:, ma, 256:512], in0=pm[:, 256:512], in1=Pf[1][:, 0:256], op=ALU.mult), rpm + ["Pf2"], ["fb%d" % ma])
                                  if psm is not None:
                                      P.op("dve", Rec("tensor_tensor", out=ACT_[:, ma, NP:NT], in0=psm, in1=sm[:, 0:NS], op=ALU.mult), rps + ["sm0", "sm1"], ["fb%d" % ma])
                              proj_chunk(slg, rslg, 16, mm * 128, hb_rhs, hb_regs, ev_gate)
                              proj_chunk(slu, rslu, 16, mm * 128, hb_rhs, hb_regs, ev_up)
                      nk = nsl * 2
                      for j in range(8):
                          slab, rslab = w_next("w_down", l, j * 256)
                          for mm in range(2):
                              m = j * 2 + mm
                              def ev_d(pm, psm, rpm, rps, m=m):
                                  P.op("dve", Rec("tensor_tensor", out=X[:, m, 0:NP], in0=pm, in1=X[:, m, 0:NP], op=ALU.add), rpm + ["X%d" % m], ["X%d" % m])
                                  if psm is not None:
                                      P.op("dve", Rec("tensor_tensor", out=X[:, m, NP:NT], in0=psm, in1=X[:, m, NP:NT], op=ALU.add), rps + ["X%d" % m], ["X%d" % m])
                              proj_chunk(slab, rslab, nk, mm * 128, lambda k: ACT_[:, k, :], lambda k: ["fb%d" % k], ev_d)

              ckpt(10)
              for k in range(16):
                  q = sqs[k % 2]
                  P.op("act", Rec("activation", out=q[:], in_=X[:, k, :], func=AF.Square), ["X%d" % k], ["sq%d" % (k % 2)])
                  P.op("pe", Rec("matmul", ps[3][:, 0:NP], lhsT=onesb[:], rhs=q[:, 0:NP], start=(k == 0), stop=(k == 15)), ["sq%d" % (k % 2), "onesb"], [RP[3]])
                  if HS[0]:
                      P.op("pe", Rec("matmul", ps[2][:, 480:480 + NS], lhsT=onesb[:], rhs=q[:, NP:NT], start=(k == 0), stop=(k == 15)), ["sq%d" % (k % 2), "onesb"], [RP[2]])
              P.op("act", Rec("activation", out=rstd[:, 0:NP], in_=ps[3][:, 0:NP], func=AF.Sqrt, scale=1.0 / D, bias=epsb[:, 0:1]), [RP[3], "epsb"], ["rstd"])
              if HS[0]:
                  P.op("act", Rec("activation", out=rstd[:, NP:NT], in_=ps[2][:, 480:480 + NS], func=AF.Sqrt, scale=1.0 / D, bias=epsb[:, 0:1]), [RP[2], "epsb"], ["rstd"])
              P.op("dve", Rec("reciprocal", out=rstd[:], in_=rstd[:]), ["rstd"], ["rstd"])
              for k in range(16):
                  P.op("dve", Rec("scalar_tensor_tensor", out=X[:, k, :], in0=X[:, k, :], scalar=gfin[:, k:k + 1], in1=rstd[:], op0=ALU.mult, op1=ALU.mult),
                       ["X%d" % k, "rstd", "gfin"], ["X%d" % k])
              for blk in range(NB + (1 if HS[0] else 0)):
                  if blk < NB:
                      cols = slice(blk * 128, (blk + 1) * 128); nr = 128
                  else:
                      cols = slice(NP, NT); nr = NS
                  for kq in range(4):
                      for kk in range(4):
                          k = kq * 4 + kk
                          P.op("pe", Rec("transpose", ps[4][0:nr, kk * 128:(kk + 1) * 128], X[:, k, cols], identf), ["X%d" % k, "cs"], [RP[4]])
                      P.op("dve", Rec("tensor_copy", out=stage[0:nr, kq * 512:(kq + 1) * 512], in_=ps[4][0:nr, :]), [RP[4]], STG)
                  if blk < NB:
                      r0 = pi * NP + blk * 128
                      P.dma("sp", Rec("dma_start", out=dap(yp, r0 * D, [[D, 128], [1, D]]), in_=stage[:]), STG, ["yp"])
                  else:
                      P.dma("sp", Rec("dma_start", out=dap(ys, sidx * 4 * D, [[D, 4], [1, D]]), in_=stage[0:4, :]), STG, ["ys"])

        except _Stop:
            pass
        if dbg:
            P.dma("sp", Rec("dma_start", out=dap(dbg_t, 0, [[16 * NT, 128], [1, 8 * NT]]), in_=Ao[:].rearrange("p a b -> p (a b)")), ["Ao%d" % k for k in range(8)], ["dbg"])
            P.dma("sp", Rec("dma_start", out=dap(dbg_t, 8 * NT, [[16 * NT, 128], [1, 8 * NT]]), in_=Hb[:, 8:16, :].rearrange("p a b -> p (a b)")), ["Hb%d" % k for k in range(8, 16)], ["dbg"])
        P.op("sp", Rec("nop", ), ["yp", "ys", "okp", "ovp", "oks", "ovs", "ohp0", "ohp1", "ohs0", "ohs1"] + (["dbg"] if dbg else []), [])
        P.emit()
    return nc


def make_consts():
    c = np.zeros((128, 1032), np.float32)
    c[:, 0:128] = np.eye(128, dtype=np.float32)
    i = np.arange(128)[:, None]
    j = np.arange(128)[None, :]
    c[:, 128:256] = np.where(j > i, 0.0, NEG)
    c[:, 256:384] = np.where(j <= i, 0.0, NEG)
    c[:, 384:512] = NEG
    c[:, 512:640] = c[:, 256:384]
    js = np.arange(132)[None, :]
    c[:, 640:772] = np.where((js > i) & (js <= i + 128), 0.0, NEG)
    r = np.arange(128)[:, None] // 16
    cc = np.arange(128)[None, :] // 16
    c[:, 772:900] = (cc >= r).astype(np.float32)
    c[:, 900:965] = np.arange(65, dtype=np.float32)[None, :]
    c[:, 965:1030] = 1.0
    c[:, 965] = 0.0
    return c


_NC_CACHE = {}


def kernel(**inputs):
    inp = {k: np.ascontiguousarray(np.asarray(v)) for k, v in inputs.items()}
    npass = SEQ // NP
    key = (npass, DEPTH)
    if key not in _NC_CACHE:
        _NC_CACHE[key] = build(npass, DEPTH)
    nc = _NC_CACHE[key]
    cst = make_consts()
    wnames = ["norm_mix", "w_in", "attn_sink", "ssm_a_re", "ssm_a_im", "ssm_log_dt", "ssm_b_re", "ssm_b_im",
              "ssm_c_re", "ssm_c_im", "ssm_d", "w_glu", "norm_attn_out", "norm_ssm_out", "w_out", "norm_ffn",
              "w_gate", "w_up", "w_down", "norm_final"]
    in_maps = []
    for c in range(8):
        m = {n: inp[n] for n in wnames}
        m["cst"] = cst
        m["xp"] = inp["x_prompt"][c % 2]
        m["xs"] = inp["x_sample"][4 * c:4 * c + 4].reshape(16, D)
        m["ck"] = inp["cache_k"][:, 4 * c:4 * c + 4].reshape(DEPTH, 4, 128, 128)
        m["cv"] = inp["cache_v"][:, 4 * c:4 * c + 4].reshape(DEPTH, 4, 128, 128)
        m["hr0"] = inp["state_ssm_re"][:, 4 * c:4 * c + 4]
        m["hi0"] = inp["state_ssm_im"][:, 4 * c:4 * c + 4]
        in_maps.append({k: np.ascontiguousarray(v, dtype=np.float32) for k, v in m.items()})
    res = run_bass_kernel_spmd(nc, in_maps, core_ids=list(range(8))).results
    f = np.float32
    y_prompt = np.stack([res[0]["yp"], res[1]["yp"]]).astype(f)
    y_sample = np.concatenate([res[c]["ys"].reshape(4, 4, D) for c in range(8)], 0).astype(f)
    k_p = np.stack([res[0]["okp"], res[1]["okp"]], 1).reshape(DEPTH, 2, 128, 2, 64).astype(f)
    v_p = np.stack([res[0]["ovp"], res[1]["ovp"]], 1).reshape(DEPTH, 2, 128, 2, 64).astype(f)
    hr_p = np.stack([res[0]["ohrp"], res[1]["ohrp"]], 1).astype(f)
    hi_p = np.stack([res[0]["ohip"], res[1]["ohip"]], 1).astype(f)
    k_s = np.concatenate([res[c]["oks"] for c in range(8)], 1).reshape(DEPTH, 32, 128, 2, 64).astype(f)
    v_s = np.concatenate([res[c]["ovs"] for c in range(8)], 1).reshape(DEPTH, 32, 128, 2, 64).astype(f)
    hr_s = np.concatenate([res[c]["ohrs"] for c in range(8)], 1).astype(f)
    hi_s = np.concatenate([res[c]["ohis"] for c in range(8)], 1).astype(f)
    return (y_prompt, y_sample, k_p, v_p, hr_p, hi_p, k_s, v_s, hr_s, hi_s)
```
